# Optimizing a Trainium2 kernel written in Bass

```python
import math, functools
import jax, jax.numpy as jnp
from jax import lax
import numpy as np

D_MODEL = 1024
BATCH = 8
SEQ = 2048
DEPTH = 2
DEC_BATCH = 128
DEC_SEQ = 8
PAST_LEN = 8192
PAGE_SIZE = 128

N_AB_LAYERS = (DEPTH + 1) // 2
N_C_LAYERS = DEPTH // 2
H_A = 4
DK_A = 128
DV_A = 128
CONV_W = 4
QKV_A = H_A * (2 * DK_A + DV_A)
H_B = 8
D_NOPE = 64
D_ROPE = 32
DV_B = 64
Q_LORA = 256
KV_LORA = 256
ROPE_BASE = 10000.0
MLA_SCALE = (D_NOPE + D_ROPE) ** -0.5
H_C = 8
DK_C = 128
DV_C = D_MODEL // H_C
D_FF = 4 * D_MODEL
CHUNK = 64
Q_BLOCK = 128
EPS = 1e-6
SPLIT_AB = (QKV_A,
            QKV_A + H_A * DV_A,
            QKV_A + H_A * DV_A + H_A,
            QKV_A + H_A * DV_A + 2 * H_A,
            QKV_A + H_A * DV_A + 2 * H_A + Q_LORA,
            QKV_A + H_A * DV_A + 2 * H_A + Q_LORA + KV_LORA)
IN_AB = SPLIT_AB[-1] + D_ROPE
OUT_AB = H_A * DV_A + H_B * DV_B
IN_C = H_C * (2 * DK_C + 2 * DV_C)
SPLIT_C = (H_C * DK_C, 2 * H_C * DK_C, 2 * H_C * DK_C + H_C * DV_C)

kernel_name = 'hybrid_gdn_mla_hgrn2_step'


def rmsnorm(x, g):
    xf = x.astype(jnp.float32)
    y = xf * lax.rsqrt(jnp.mean(xf * xf, axis=-1, keepdims=True) + EPS)
    return (y * g.astype(jnp.float32)).astype(x.dtype)


def l2norm(x):
    xf = x.astype(jnp.float32)
    return xf * lax.rsqrt(jnp.sum(xf * xf, axis=-1, keepdims=True) + EPS)


def rope(x, pos):
    half = D_ROPE // 2
    inv = ROPE_BASE ** (-jnp.arange(half, dtype=jnp.float32) / half)
    ang = pos.astype(jnp.float32)[:, None] * inv
    shape = (ang.shape[0],) + (1,) * (x.ndim - 3) + (half,)
    cos, sin = jnp.cos(ang).reshape(shape), jnp.sin(ang).reshape(shape)
    x1, x2 = x[..., :half].astype(jnp.float32), x[..., half:].astype(jnp.float32)
    return jnp.concatenate([x1 * cos - x2 * sin, x2 * cos + x1 * sin], axis=-1).astype(x.dtype)


def to_chunks(a, c):
    b, t = a.shape[:2]
    a = a.astype(jnp.float32).reshape((b, t // c, c) + a.shape[2:])
    return jnp.swapaxes(jnp.moveaxis(a, 1, 0), 2, 3)


def from_chunks(o):
    n, b, h, c, d = o.shape
    return jnp.moveaxis(jnp.swapaxes(o, 2, 3), 0, 1).reshape(b, n * c, h, d)


def gated_delta_rule(q, k, v, g, beta, s0):
    c = math.gcd(q.shape[1], CHUNK)
    q = to_chunks(q, c) * (DK_A ** -0.5)
    k, v, g, beta = (to_chunks(a, c) for a in (k, v, g, beta))
    gc = jnp.cumsum(g, axis=-1)
    incl = jnp.tril(jnp.ones((c, c), dtype=bool))
    strict = jnp.tril(jnp.ones((c, c), dtype=bool), -1)
    diff = gc[..., :, None] - gc[..., None, :]
    decay = jnp.where(incl, jnp.exp(jnp.where(incl, diff, 0.0)), 0.0)
    kb = k * beta[..., None]
    lower = jnp.where(strict, jnp.einsum('nbhid,nbhjd->nbhij', kb, k) * decay, 0.0)
    eye = jnp.eye(c, dtype=jnp.float32)
    t_inv = lax.linalg.triangular_solve(eye + lower, jnp.broadcast_to(eye, lower.shape),
                                        left_side=True, lower=True, unit_diagonal=True)
    u = jnp.einsum('nbhij,nbhje->nbhie', t_inv, v * beta[..., None])
    w = jnp.einsum('nbhij,nbhjd->nbhid', t_inv, kb * jnp.exp(gc)[..., None])

    def step(s, xs):
        qc, kc, uc, wc, gcc, dc = xs
        v_new = uc - jnp.einsum('bhid,bhde->bhie', wc, s)
        intra = jnp.einsum('bhid,bhjd->bhij', qc, kc) * dc
        o = (jnp.einsum('bhid,bhde->bhie', qc * jnp.exp(gcc)[..., None], s)
             + jnp.einsum('bhij,bhje->bhie', intra, v_new))
        g_last = gcc[..., -1]
        s = (s * jnp.exp(g_last)[..., None, None]
             + jnp.einsum('bhid,bhie->bhde', kc * jnp.exp(g_last[..., None] - gcc)[..., None], v_new))
        return s, o

    s, o = lax.scan(step, s0.astype(jnp.float32), (q, k, u, w, gc, decay))
    return from_chunks(o), s


def gated_linear_recurrence(q, k, v, logf, s0):
    c = math.gcd(q.shape[1], CHUNK)
    q = to_chunks(q, c) * (DK_C ** -0.5)
    k, v, logf = (to_chunks(a, c) for a in (k, v, logf))
    gc = jnp.cumsum(logf, axis=-2)
    incl = jnp.tril(jnp.ones((c, c), dtype=bool))[:, :, None]

    def step(s, xs):
        qc, kc, vc, gcc = xs
        diff = gcc[:, :, :, None, :] - gcc[:, :, None, :, :]
        dec = jnp.where(incl, jnp.exp(jnp.where(incl, diff, 0.0)), 0.0)
        intra = jnp.einsum('bhid,bhjd,bhijd->bhij', qc, kc, dec)
        o = (jnp.einsum('bhid,bhde->bhie', qc * jnp.exp(gcc), s)
             + jnp.einsum('bhij,bhje->bhie', intra, vc))
        g_last = gcc[:, :, -1]
        s = (s * jnp.exp(g_last)[..., None]
             + jnp.einsum('bhid,bhie->bhde', kc * jnp.exp(g_last[:, :, None] - gcc), vc))
        return s, o

    s, o = lax.scan(step, s0.astype(jnp.float32), (q, k, v, gc))
    return from_chunks(o), s


def mla_prompt(q_nope, q_pe, c_kv, k_pe, w_uk, w_uv):
    b, s, h, _ = q_nope.shape
    qb = math.gcd(s, Q_BLOCK)
    k_nope = jnp.einsum('bsl,lhd->bshd', c_kv, w_uk)
    v = jnp.einsum('bsl,lhd->bshd', c_kv, w_uv)
    kpos = jnp.arange(s)

    def block(args):
        i, qn, qp = args
        sc = (jnp.einsum('bqhd,bkhd->bhqk', qn, k_nope)
              + jnp.einsum('bqhd,bkd->bhqk', qp, k_pe)).astype(jnp.float32) * MLA_SCALE
        qpos = i * qb + jnp.arange(qb)
        sc = jnp.where(qpos[:, None] >= kpos[None, :], sc, -jnp.inf)
        p = jax.nn.softmax(sc, axis=-1).astype(v.dtype)
        return jnp.einsum('bhqk,bkhd->bqhd', p, v)

    def split(a):
        return jnp.swapaxes(a.reshape((b, s // qb, qb) + a.shape[2:]), 0, 1)

    o = lax.map(block, (jnp.arange(s // qb), split(q_nope), split(q_pe)))
    return jnp.swapaxes(o, 0, 1).reshape(b, s, h, DV_B)


def mla_sample(q_nope, q_pe, c_kv, k_pe, w_uk, w_uv, cache_lat, cache_kpe, page_table):
    b, t, h, _ = q_nope.shape
    lat = jnp.concatenate([cache_lat[page_table].reshape(b, -1, KV_LORA), c_kv], axis=1)
    kpe = jnp.concatenate([cache_kpe[page_table].reshape(b, -1, D_ROPE), k_pe], axis=1)
    past = lat.shape[1] - t
    q_lat = jnp.einsum('bthd,lhd->bthl', q_nope, w_uk)
    sc = (jnp.einsum('bthl,bkl->bhtk', q_lat, lat)
          + jnp.einsum('bthd,bkd->bhtk', q_pe, kpe)).astype(jnp.float32) * MLA_SCALE
    mask = jnp.arange(past + t)[None, :] <= past + jnp.arange(t)[:, None]
    sc = jnp.where(mask, sc, -jnp.inf)
    p = jax.nn.softmax(sc, axis=-1).astype(lat.dtype)
    o_lat = jnp.einsum('bhtk,bkl->bthl', p, lat)
    return jnp.einsum('bthl,lhd->bthd', o_lat, w_uv)


def gdn_mla_mixer(h, pos, conv_buf, s0, attend, w_in, conv_w, a_log, dt_bias, gdn_norm,
                  q_norm, w_uq, kv_norm, w_uk, w_uv, w_out):
    b, t, _ = h.shape
    qkv, z, b_raw, a_raw, cq, ckv, kpe = jnp.split(h @ w_in, SPLIT_AB, axis=-1)
    ext = jnp.concatenate([conv_buf.astype(qkv.dtype), qkv], axis=1)
    new_buf = ext[:, -(CONV_W - 1):]
    qkv = jax.nn.silu(sum(ext[:, i:i + t] * conv_w[i] for i in range(CONV_W)))
    q, k, v = jnp.split(qkv, (H_A * DK_A, 2 * H_A * DK_A), axis=-1)
    q = l2norm(q.reshape(b, t, H_A, DK_A))
    k = l2norm(k.reshape(b, t, H_A, DK_A))
    beta = jax.nn.sigmoid(b_raw.astype(jnp.float32))
    g = -jnp.exp(a_log.astype(jnp.float32)) * jax.nn.softplus(a_raw.astype(jnp.float32) + dt_bias)
    o_a, s_new = gated_delta_rule(q, k, v.reshape(b, t, H_A, DV_A), g, beta, s0)
    o_a = rmsnorm(o_a.astype(h.dtype), gdn_norm) * jax.nn.silu(z.reshape(b, t, H_A, DV_A))
    qh = jnp.einsum('btl,lhd->bthd', rmsnorm(cq, q_norm), w_uq)
    q_nope, q_pe = qh[..., :D_NOPE], rope(qh[..., D_NOPE:], pos)
    ckv = rmsnorm(ckv, kv_norm)
    kpe = rope(kpe, pos)
    o_b = attend(q_nope, q_pe, ckv, kpe, w_uk, w_uv)
    o = jnp.concatenate([o_a.reshape(b, t, -1), o_b.reshape(b, t, -1)], axis=-1)
    return o @ w_out, new_buf, s_new, ckv, kpe


def hgrn2_mixer(h, s0, lb, w_in, g_norm, w_out):
    b, t, _ = h.shape
    q, f, i, gate = jnp.split(h @ w_in, SPLIT_C, axis=-1)
    f = lb + (1.0 - lb) * jax.nn.sigmoid(f.astype(jnp.float32))

    def heads(a, d):
        return a.reshape(b, t, H_C, d)

    o, s_new = gated_linear_recurrence(heads(jax.nn.silu(q), DK_C), heads(1.0 - f, DK_C),
                                       heads(i, DV_C), heads(jnp.log(f), DK_C), s0)
    o = rmsnorm(o.astype(h.dtype), g_norm) * jax.nn.silu(heads(gate, DV_C))
    return o.reshape(b, t, -1) @ w_out, s_new


def sq_relu_mlp(h, w_up, w_down):
    return jnp.square(jax.nn.relu(h @ w_up)) @ w_down


def setup_inputs(seed: int = 0) -> dict:
    key = jax.random.key(seed)
    ks = iter(jax.random.split(key, 32))

    def nrm(shape, scale=1.0):
        return jax.random.normal(next(ks), shape, jnp.float32) * scale

    def gain(shape):
        return 1.0 + nrm(shape, 0.02)

    n_pages = PAST_LEN // PAGE_SIZE
    n_pool = (DEC_BATCH * n_pages * 5) // 4
    page_table = jax.random.permutation(next(ks), n_pool)[:DEC_BATCH * n_pages]
    page_table = page_table.reshape(DEC_BATCH, n_pages).astype(jnp.int32)
    dt = jnp.exp(jax.random.uniform(next(ks), (N_AB_LAYERS, H_A), jnp.float32,
                                    math.log(1e-3), math.log(1e-1)))
    a_log = jnp.log(jax.random.uniform(next(ks), (N_AB_LAYERS, H_A), jnp.float32, 1.0, 16.0))
    return {
        'x_prompt': nrm((BATCH, SEQ, D_MODEL)),
        'x_sample': nrm((DEC_BATCH, DEC_SEQ, D_MODEL)),
        'state_gdn': nrm((N_AB_LAYERS, DEC_BATCH, H_A, DK_A, DV_A), 0.5),
        'state_gdn_conv': nrm((N_AB_LAYERS, DEC_BATCH, CONV_W - 1, QKV_A)),
        'cache_mla_latent': nrm((N_AB_LAYERS, n_pool, PAGE_SIZE, KV_LORA)),
        'cache_mla_krope': nrm((N_AB_LAYERS, n_pool, PAGE_SIZE, D_ROPE)),
        'state_hgrn': nrm((N_C_LAYERS, DEC_BATCH, H_C, DK_C, DV_C), 0.5),
        'page_table': page_table,
        'mix_norm': gain((DEPTH, D_MODEL)),
        'mlp_norm': gain((DEPTH, D_MODEL)),
        'final_norm': gain((D_MODEL,)),
        'w_up': nrm((DEPTH, D_MODEL, D_FF), D_MODEL ** -0.5),
        'w_down': nrm((DEPTH, D_FF, D_MODEL), 0.5 * D_FF ** -0.5),
        'w_in_ab': nrm((N_AB_LAYERS, D_MODEL, IN_AB), D_MODEL ** -0.5),
        'conv_w_ab': nrm((N_AB_LAYERS, CONV_W, QKV_A), CONV_W ** -0.5),
        'a_log_ab': a_log,
        'dt_bias_ab': dt + jnp.log(-jnp.expm1(-dt)),
        'gdn_norm_ab': gain((N_AB_LAYERS, DV_A)),
        'q_norm_ab': gain((N_AB_LAYERS, Q_LORA)),
        'w_uq_ab': nrm((N_AB_LAYERS, Q_LORA, H_B, D_NOPE + D_ROPE), Q_LORA ** -0.5),
        'kv_norm_ab': gain((N_AB_LAYERS, KV_LORA)),
        'w_uk_ab': nrm((N_AB_LAYERS, KV_LORA, H_B, D_NOPE), KV_LORA ** -0.5),
        'w_uv_ab': nrm((N_AB_LAYERS, KV_LORA, H_B, DV_B), KV_LORA ** -0.5),
        'w_out_ab': nrm((N_AB_LAYERS, OUT_AB, D_MODEL), OUT_AB ** -0.5),
        'w_in_c': nrm((N_C_LAYERS, D_MODEL, IN_C), D_MODEL ** -0.5),
        'lb_logits_c': nrm((DEPTH, H_C * DK_C), 0.5),
        'g_norm_c': gain((N_C_LAYERS, DV_C)),
        'w_out_c': nrm((N_C_LAYERS, H_C * DV_C, D_MODEL), (H_C * DV_C) ** -0.5),
    }


def reference(x_prompt, x_sample, state_gdn, state_gdn_conv, cache_mla_latent, cache_mla_krope,
              state_hgrn, page_table, mix_norm, mlp_norm, final_norm, w_up, w_down,
              w_in_ab, conv_w_ab, a_log_ab, dt_bias_ab, gdn_norm_ab, q_norm_ab, w_uq_ab,
              kv_norm_ab, w_uk_ab, w_uv_ab, w_out_ab, w_in_c, lb_logits_c, g_norm_c, w_out_c):
    f32 = jnp.float32
    p_lb = jax.nn.softmax(lb_logits_c.astype(f32), axis=0)
    lower_bounds = jnp.cumsum(p_lb, axis=0) - p_lb[0]

    def trunk(x, pos, conv0, gdn0, hgrn0, attend):
        convs, gdns, lats, kpes, hgrns = [], [], [], [], []
        for layer in range(DEPTH):
            j = layer // 2
            h = rmsnorm(x, mix_norm[layer])
            if layer % 2 == 0:
                o, buf, s, ckv, kpe = gdn_mla_mixer(
                    h, pos, conv0[j], gdn0[j], functools.partial(attend, j), w_in_ab[j], conv_w_ab[j],
                    a_log_ab[j], dt_bias_ab[j], gdn_norm_ab[j], q_norm_ab[j], w_uq_ab[j],
                    kv_norm_ab[j], w_uk_ab[j], w_uv_ab[j], w_out_ab[j])
                convs.append(buf)
                gdns.append(s)
                lats.append(ckv)
                kpes.append(kpe)
            else:
                o, s = hgrn2_mixer(h, hgrn0[j], lower_bounds[layer], w_in_c[j], g_norm_c[j], w_out_c[j])
                hgrns.append(s)
            x = x + o
            x = x + sq_relu_mlp(rmsnorm(x, mlp_norm[layer]), w_up[layer], w_down[layer])
        return (rmsnorm(x, final_norm), jnp.stack(gdns), jnp.stack(convs), jnp.stack(lats),
                jnp.stack(kpes), jnp.stack(hgrns))

    def attend_prompt(j, q_nope, q_pe, c_kv, k_pe, w_uk, w_uv):
        return mla_prompt(q_nope, q_pe, c_kv, k_pe, w_uk, w_uv)

    def attend_sample(j, q_nope, q_pe, c_kv, k_pe, w_uk, w_uv):
        return mla_sample(q_nope, q_pe, c_kv, k_pe, w_uk, w_uv,
                          cache_mla_latent[j], cache_mla_krope[j], page_table)

    b, s_len = x_prompt.shape[:2]
    y_prompt, gdn_p, conv_p, lat_p, kpe_p, hgrn_p = trunk(
        x_prompt, jnp.arange(s_len),
        jnp.zeros((N_AB_LAYERS, b, CONV_W - 1, QKV_A), x_prompt.dtype),
        jnp.zeros((N_AB_LAYERS, b, H_A, DK_A, DV_A), f32),
        jnp.zeros((N_C_LAYERS, b, H_C, DK_C, DV_C), f32),
        attend_prompt)
    past_len = page_table.shape[1] * cache_mla_latent.shape[2]
    y_sample, gdn_s, conv_s, lat_s, kpe_s, hgrn_s = trunk(
        x_sample, past_len + jnp.arange(x_sample.shape[1]),
        state_gdn_conv, state_gdn, state_hgrn, attend_sample)
    return (y_prompt, y_sample, gdn_p, conv_p, lat_p, kpe_p, hgrn_p, gdn_s, conv_s, lat_s, kpe_s, hgrn_s)
```

```python
from contextlib import ExitStack

import numpy as np
import concourse.bass as bass
import concourse.mybir as mybir
from concourse.bass_utils import run_bass_kernel_spmd

F32 = mybir.dt.float32
BF16 = mybir.dt.bfloat16
I32 = mybir.dt.int32
AF = mybir.ActivationFunctionType
ALU = mybir.AluOpType
AX = mybir.AxisListType

D = 1024
DFF = 4096
KC = 8
EPS = 1e-6
NCORES = 8


class Sched:
    def __init__(self, nc, stack, ndma=4):
        self.nc = nc
        self.stack = stack
        self.eng = {'pe': nc.tensor, 'act': nc.scalar, 'dve': nc.vector,
                    'pool': nc.gpsimd, 'sp': nc.sync}
        self.sem = {}
        self.cnt = {}
        self.seen = {e: {} for e in self.eng}
        self.lastw = {}
        self.reads = {}
        self.ndma = ndma
        self.dma_rr = {}
        self.n_inst = 0
        self.alias_map = {}
        self.log = {e: [] for e in self.eng}
        for e in ('pe', 'act', 'dve', 'pool'):
            self._mk(e)

    def _mk(self, q):
        self.sem[q] = self.stack.enter_context(self.nc.semaphore("s_" + q))
        self.cnt[q] = 0

    def _wait(self, e, q, c):
        if self.seen[e].get(q, 0) >= c:
            return
        self.seen[e][q] = c
        self.log[e].append(('w', q, c))
        self.eng[e].wait_ge(self.sem[q], c)

    def alias(self, a, canon):
        self.alias_map[a] = canon

    def _canon(self, keys):
        return [self.alias_map.get(k, k) for k in keys]

    def _deps(self, e, myq, reads, writes, same_war=False, skip_waw_q=None):
        reads = self._canon(reads)
        writes = self._canon(writes)
        need = {}

        def add(t):
            if t is None:
                return
            q, c = t
            if need.get(q, 0) < c:
                need[q] = c
        for k in reads:
            add(self.lastw.get(k))
        for k in writes:
            t = self.lastw.get(k)
            if not (t is not None and skip_waw_q is not None and t[0] == skip_waw_q):
                add(t)
            for q, c in self.reads.get(k, {}).items():
                if q == myq and not same_war:
                    continue
                add((q, c))
        for q, c in need.items():
            self._wait(e, q, c)

    def _commit(self, ticket, reads, writes):
        reads = self._canon(reads)
        writes = self._canon(writes)
        q, c = ticket
        for k in reads:
            d = self.reads.setdefault(k, {})
            if d.get(q, 0) < c:
                d[q] = c
        for k in writes:
            self.lastw[k] = ticket
            self.reads[k] = {}

    def op(self, e, fn, reads=(), writes=(), inc=True, pe_acc=False):
        self._deps(e, e, reads, writes, skip_waw_q=('pe' if pe_acc else None))
        ins = fn(self.eng[e])
        self.n_inst += 1
        ticket = (e, self.cnt[e] + 1)
        if inc:
            ins.then_inc(self.sem[e], 1)
            self.cnt[e] += 1
            self.log[e].append(('i', e, 1))
        self._commit(ticket, reads, writes)
        return ins

    def dma(self, e, out, in_, reads=(), writes=(), stream='d', nrr=None, **kw):
        return self.dma_custom(e, lambda eng: eng.dma_start(out=out, in_=in_, **kw),
                               reads, writes, stream, nrr)

    def dma_custom(self, e, fn, reads=(), writes=(), stream='g', nrr=None):
        i = self.dma_rr.get(stream, 0)
        self.dma_rr[stream] = i + 1
        q = "dma_%s_%d" % (stream, i % (nrr or self.ndma))
        if q not in self.sem:
            self._mk(q)
        self._deps(e, q, reads, writes, same_war=True)
        ins = fn(self.eng[e])
        ins.then_inc(self.sem[q], 16)
        self.n_inst += 1
        self.cnt[q] += 16
        self.log[e].append(('i', q, 16))
        self._commit((q, self.cnt[q]), reads, writes)
        return ins

    def barrier(self):
        for e in self.eng:
            for q, c in self.cnt.items():
                if c > 0:
                    self._wait(e, q, c)

    def check_deadlock(self):
        pos = {e: 0 for e in self.log}
        val = {}
        progress = True
        while progress:
            progress = False
            for e, lg in self.log.items():
                while pos[e] < len(lg):
                    k, q, c = lg[pos[e]]
                    if k == 'w':
                        if val.get(q, 0) >= c:
                            pos[e] += 1
                            progress = True
                        else:
                            break
                    else:
                        val[q] = val.get(q, 0) + c
                        pos[e] += 1
                        progress = True
        stuck = {e: (pos[e], lg[pos[e]], val.get(lg[pos[e]][1], 0)) for e, lg in self.log.items()
                 if pos[e] < len(lg)}
        return stuck

    def finish(self, e='sp'):
        for q, c in self.cnt.items():
            if c > 0:
                self._wait(e, q, c)


def blocks(total, step):
    return [(s, min(step, total - s)) for s in range(0, total, step)]


def build(cfg):
    NT = cfg['NT']
    TP = 128 * NT
    TT = TP + 128
    PHASES = cfg.get('phases', 'ABCD')
    nc = bass.Bass("TRN2", target_bir_lowering=False)

    def din(name, shape, dt=F32):
        return nc.dram_tensor(name, list(shape), dt, kind="ExternalInput").ap()

    def dout(name, shape):
        return nc.dram_tensor(name, list(shape), F32, kind="ExternalOutput").ap()

    x_p = din("x_p", [TP, D])
    x_s = din("x_s", [128, D])
    w_up = din("w_up", [2, D, DFF])
    w_down = din("w_down", [2, DFF, D])
    mlp_norm_fm = din("mlp_norm_fm", [2, 128, KC])
    mix_norm_fm = din("mix_norm_fm", [2, 128, KC])
    final_norm = din("final_norm", [1, D])
    w_in_c = din("w_in_c", [D, 4096])
    w_out_c = din("w_out_c", [D, D])
    lb_fm = din("lb_fm", [2, 128, 8])
    g_norm_c = din("g_norm_c", [128, 1])
    hgrn_in = din("hgrn_in", [16, 8, 128, 128])
    w_in_ab = din("w_in_ab", [D, 2600])
    w_out_ab = din("w_out_ab", [D, D])
    w_uq = din("w_uq", [256, 768])
    w_uk = din("w_uk", [256, 512])
    w_uv = din("w_uv", [256, 512])
    conv_w_fm = din("conv_w_fm", [128, 12, 4])
    a_log = din("a_log", [1, 4])
    dt_bias = din("dt_bias", [1, 4])
    gdn_norm = din("gdn_norm", [128, 1])
    q_norm = din("q_norm", [1, 256])
    kv_norm = din("kv_norm", [1, 256])
    rope_cos = din("rope_cos", [TT, 16])
    rope_sin = din("rope_sin", [TT, 16])
    conv_s_in = din("conv_s_in", [48, 1536])
    gdn_s_in = din("gdn_s_in", [16, 4, 128, 128])
    NPG = cfg['NPG']
    NPOOL = cfg['NPOOL']
    NGRP = NPG // min(8, NPG)
    cache_lat = din("cache_lat", [NPOOL, 128, 256])
    cache_kr = din("cache_kr", [NPOOL, 128, 32])
    page_table = din("page_table", [16, NPG], I32)
    w_ukT = din("w_ukT", [128, 2048])
    rep_in = din("rep_in", [NPG, 128])
    gbase_in = din("gbase_in", [128, NGRP + 1])
    cm8_in = din("cm8_in", [128, 64])
    NPG = cfg['NPG']
    if cfg.get('dbg_s'):
        dbg_mst = dout("dbg_mst", [128, 8])
        dbg_oacc = dout("dbg_oacc", [64, 257])
        dbg_lb = nc.dram_tensor("dbg_lb", [128, min(8, NPG) * 256], BF16, kind="ExternalOutput").ap()
        dbg_kb = nc.dram_tensor("dbg_kb", [128, NPG * 32], BF16, kind="ExternalOutput").ap()
        dbg_olat = dout("dbg_olat", [64, 256])
    gdn_p_o = dout("gdn_p", [4, 128, 128])
    conv_p_o = dout("conv_p", [3, 1536])
    lat_p_o = dout("lat_p", [TP, 256])
    kpe_p_o = dout("kpe_p", [TP, 32])
    gdn_s_o = dout("gdn_s", [16, 4, 128, 128])
    conv_s_o = dout("conv_s", [16, 3, 1536])
    lat_s_o = dout("lat_s", [128, 256])
    kpe_s_o = dout("kpe_s", [128, 32])
    xsp = nc.dram_tensor("xsp", [NT + 1, 128, KC, 128], F32, kind="Internal").ap()
    y_p = dout("y_p", [TP, D])
    y_s = dout("y_s", [128, D])
    hgrn_p = dout("hgrn_p", [8, 128, 128])
    hgrn_s = dout("hgrn_s", [16, 8, 128, 128])

    with ExitStack() as st:
        S = Sched(nc, st, ndma=4)

        uniq = [0]

        def mk_sb(stack):
            uniq[0] += 1
            pfx = "p%d_" % uniq[0]

            def sb(name, shape, dt=F32):
                nb = int(np.prod(shape[1:])) * (2 if dt == BF16 else 4)
                if cfg.get('memlog'):
                    print("  alloc %-10s %7d B" % (pfx + name, nb))
                return stack.enter_context(nc.sbuf_tensor(pfx + name, list(shape), dt))
            return sb
        sb = mk_sb(st)

        pb = [st.enter_context(nc.psum_tensor("pb%d" % i, [128, 512], F32)) for i in range(8)]

        def PK(i):
            return [('pbq', i, j) for j in range(4)]

        def pbv(i, a):
            return pb[i][:].rearrange("p (a b) -> p a b", a=a)

        def pbbf(i):
            return pb[i][:].bitcast(BF16)

        def A(fn, r, w):
            return S.op('act', fn, r, w)

        def V(fn, r, w):
            return S.op('dve', fn, r, w)

        def G(fn, r, w):
            return S.op('pool', fn, r, w)

        def MM(out, lhsT, rhs, start, stop, r, w, inc=None):
            return S.op('pe', lambda e: e.matmul(out, lhsT, rhs, start=start, stop=stop), r, w,
                        inc=(stop if inc is None else inc), pe_acc=(not start))

        def TR(out, in_, idt, r, w):
            return S.op('pe', lambda e: e.transpose(out, in_, idt), r, w)

        ident = sb("ident", [128, 128], F32)
        ident_bf = sb("ident_bf", [128, 128], BF16)
        ones_bf = sb("ones_bf", [128, 128], BF16)
        G(lambda e: e.memset(ident[:], 0.0), [], ['ident'])
        G(lambda e: e.affine_select(out=ident[:], in_=ident[:], pattern=[[-1, 128]],
                                    compare_op=ALU.not_equal, fill=1.0, base=0,
                                    channel_multiplier=1), ['ident'], ['ident'])
        G(lambda e: e.tensor_copy(out=ident_bf[:], in_=ident[:]), ['ident'], ['ident_bf'])
        G(lambda e: e.memset(ones_bf[:], 1.0), [], ['ones_bf'])

        gmlp = sb("gmlp", [128, 2, KC], F32)
        S.dma('sp', gmlp[:], mlp_norm_fm.rearrange("l p k -> p l k"), writes=['gmlp'], stream='c')
        gmix = sb("gmix", [128, 2, KC], F32)
        S.dma('sp', gmix[:], mix_norm_fm.rearrange("l p k -> p l k"), writes=['gmix'], stream='c')

        xTh = [None]

        def tile_rows(t):
            return x_s if t == NT else x_p[t * 128:(t + 1) * 128, :]

        def tile_cols(t):
            return slice(t * 128, (t + 1) * 128)

        def phase0():
          xT = xTh[0]
          with ExitStack() as ph:
            psb = mk_sb(ph)
            xin = [psb("xin%d" % i, [128, D], F32) for i in range(2)]
            for t in range(NT + 1):
                b = t % 2
                S.dma('sp', xin[b][:], tile_rows(t), writes=[('xin', b)], stream='x')
                for half in range(2):
                    pi = half
                    for j in range(4):
                        kc = half * 4 + j
                        TR(pb[pi][:, j * 128:(j + 1) * 128], xin[b][:, kc * 128:(kc + 1) * 128], ident[:],
                           [('xin', b), 'ident'], [('pbq', pi, j)])
                    A(lambda e, half=half, pi=pi: e.copy(out=xT[:, half * 4:half * 4 + 4, tile_cols(t)],
                                                        in_=pbv(pi, 4)),
                      [('pbq', pi, j) for j in range(4)], [('xT', t)] + [('pbq', pi, j) for j in range(4)])
            S.barrier()

        def xkeys(s0, n):
            return [('xT', t) for t in range(s0 // 128, (s0 + n + 127) // 128)]

        def hkeys(s0, n):
            return [('hT', t) for t in range(s0 // 128, (s0 + n + 127) // 128)]

        def make_norm(psb, W, nbuf=2):
            sq = [psb("sq%d" % i, [128, KC, W], BF16) for i in range(nbuf)]
            ntmp = [psb("ntmp%d" % i, [128, KC, W], F32) for i in range(nbuf)]
            rs = [psb("rs%d" % i, [128, W], F32) for i in range(nbuf)]
            ctr = [0]

            def norm_range(gain, gkey, s0, n, hdst, hk, xv=None, xk=None):
                i = ctr[0] % nbuf
                ctr[0] += 1
                if xv is None:
                    xv = xTh[0][:, :, s0:s0 + n]
                    xk = xkeys(s0, n)
                A(lambda e: e.activation(out=sq[i][:, :, :n], in_=xv, func=AF.Square),
                  xk, [('sq', i)])
                for kc in range(KC):
                    MM(pb[5][:, :n], ones_bf[:], sq[i][:, kc, :n], kc == 0, kc == KC - 1,
                       [('sq', i), 'ones_bf'], PK(5))
                A(lambda e: e.activation(out=rs[i][:, :n], in_=pb[5][:, :n], func=AF.Sqrt,
                                         bias=EPS, scale=1.0 / D), PK(5), [('rs', i)])
                V(lambda e: e.reciprocal(out=rs[i][:, :n], in_=rs[i][:, :n]), [('rs', i)], [('rs', i)])
                G(lambda e: e.tensor_tensor(out=ntmp[i][:, :, :n], in0=xv,
                                            in1=rs[i][:, :n].unsqueeze(1).to_broadcast([128, KC, n]),
                                            op=ALU.mult), xk + [('rs', i)], [('ntmp', i)])
                V(lambda e: e.tensor_tensor(out=hdst, in0=ntmp[i][:, :, :n],
                                            in1=gain.unsqueeze(2).to_broadcast([128, KC, n]),
                                            op=ALU.mult), [('ntmp', i), gkey], hk)
            return norm_range

        def mlp_layer(l):
            xT = xTh[0]
            with ExitStack() as ph:
                psb = mk_sb(ph)
                hT = psb("hT", [128, KC, TT], BF16)
                norm_range = make_norm(psb, 256)
                wu = [psb("wu%d" % i, [128, KC, 512], BF16) for i in range(2)]
                wd = [psb("wd%d" % i, [128, 4, D], BF16) for i in range(2)]
                h1 = [psb("h1_%d" % i, [128, 4, 512], BF16) for i in range(2)]
                rtmp = [psb("rtmp%d" % i, [128, 512], F32) for i in range(2)]
                ctr = {'up': 0, 'dn': 0, 'h1': 0, 'w': 0}
                for (s0, n) in blocks(TT, 256):
                    norm_range(gmlp[:, l, :], 'gmlp', s0, n, hT[:, :, s0:s0 + n], hkeys(s0, n))
                wup_v = w_up[l].rearrange("(kc p) f -> p kc f", p=128)
                wdn_v = w_down[l].rearrange("(fc p) d -> p fc d", p=128)
                for g in range(8):
                    wb = ctr['w'] % 2
                    ctr['w'] += 1
                    S.dma('pool', wu[wb][:], wup_v[:, :, g * 512:(g + 1) * 512], writes=[('wu', wb)], stream='w')
                    S.dma('pool', wd[wb][:], wdn_v[:, g * 4:(g + 1) * 4, :], writes=[('wd', wb)], stream='w')
                    for (s0, n) in blocks(TT, 512):
                        hb = ctr['h1'] % 2
                        ctr['h1'] += 1
                        tl = list(range(s0 // 128, (s0 + n + 127) // 128))
                        for fc in range(4):
                            pi = ctr['up'] % 2
                            ctr['up'] += 1
                            for kc in range(KC):
                                MM(pb[pi][:, :n], wu[wb][:, kc, fc * 128:(fc + 1) * 128], hT[:, kc, s0:s0 + n],
                                   kc == 0, kc == KC - 1, [('wu', wb)] + hkeys(s0, n), PK(pi))
                            A(lambda e, pi=pi: e.activation(out=rtmp[pi][:, :n], in_=pb[pi][:, :n], func=AF.Relu),
                              PK(pi), [('rtmp', pi)])
                            G(lambda e, pi=pi, fc=fc: e.tensor_tensor(out=h1[hb][:, fc, :n], in0=rtmp[pi][:, :n],
                                                                      in1=rtmp[pi][:, :n], op=ALU.mult),
                              [('rtmp', pi)], [('h1', hb, fc)])
                        for dc in range(KC):
                            pi = 2 + ctr['dn'] % 3
                            ctr['dn'] += 1
                            for fc in range(4):
                                MM(pb[pi][:, :n], wd[wb][:, fc, dc * 128:(dc + 1) * 128], h1[hb][:, fc, :n],
                                   fc == 0, fc == 3, [('wd', wb), ('h1', hb, fc)], PK(pi))
                            V(lambda e, dc=dc, pi=pi: e.tensor_tensor(out=xT[:, dc, s0:s0 + n],
                                                                      in0=xT[:, dc, s0:s0 + n],
                                                                      in1=pb[pi][:, :n], op=ALU.add),
                              PK(pi) + [('xTd', dc, t) for t in tl], [('xTd', dc, t) for t in tl])
                S.barrier()

        def hgrn_layer():
            xT = xTh[0]
            with ExitStack() as ph:
                psb = mk_sb(ph)
                norm_range = make_norm(psb, 128, 1)
                hTt = psb("hTt", [128, KC, 128], BF16)
                wic = psb("wic", [128, KC, 4096], BF16)
                woc = psb("woc", [128, KC, D], BF16)
                wic_v = w_in_c.rearrange("(kc p) f -> p kc f", p=128)
                woc_v = w_out_c.rearrange("(kc p) f -> p kc f", p=128)
                for kc in range(KC):
                    S.dma('pool', wic[:, kc, :], wic_v[:, kc, :], writes=['wic'], stream='w')
                S.dma('pool', woc[:], woc_v, writes=['woc'], stream='w')
                lbt = psb("lbt", [128, 2, 8], F32)
                lb = psb("lb", [128, 8], F32)
                oml = psb("oml", [128, 8], F32)
                gnc = psb("gnc", [128, 1], F32)
                S.dma('sp', lbt[:], lb_fm.rearrange("l p k -> p l k"), writes=['lbt'], stream='c')
                S.dma('sp', gnc[:], g_norm_c, writes=['gnc'], stream='c')
                V(lambda e: e.tensor_tensor(out=lb[:], in0=lbt[:, 1, :], in1=lbt[:, 0, :], op=ALU.subtract),
                  ['lbt'], ['lb'])
                A(lambda e: e.activation(out=lb[:], in_=lb[:], func=AF.Sigmoid), ['lb'], ['lb'])
                V(lambda e: e.tensor_scalar(out=oml[:], in0=lb[:], scalar1=-1.0, scalar2=1.0, op0=ALU.mult,
                                            op1=ALU.add), ['lb'], ['oml'])
                SC = 128 ** -0.5
                mask_p = psb("mask_p", [128, 128], F32)
                mask_s = psb("mask_s", [128, 128], F32)
                for mk, nm in ((mask_p, 'mask_p'), (mask_s, 'mask_s')):
                    G(lambda e, mk=mk: e.memset(mk[:], SC), [], [nm])
                    G(lambda e, mk=mk: e.affine_select(out=mk[:], in_=mk[:], pattern=[[1, 128]],
                                                       compare_op=ALU.is_ge, fill=0.0, base=0,
                                                       channel_multiplier=-1), [nm], [nm])
                G(lambda e: e.affine_select(out=mask_s[:].rearrange("p (a b) -> p a b", a=16),
                                            in_=mask_s[:].rearrange("p (a b) -> p a b", a=16),
                                            pattern=[[-8, 16], [0, 8]], compare_op=ALU.is_ge, fill=0.0,
                                            base=0, channel_multiplier=1), ['mask_s'], ['mask_s'])
                rm_p = psb("rm_p", [128, 8, 128], F32)
                rm_s = rm_p
                S.alias('rm_s', 'rm_p')
                G(lambda e: e.memset(rm_p[:], 1.0), [], ['rm_p'])
                G(lambda e: e.memset(rm_p[:, :, 0:1], 0.0), ['rm_p'], ['rm_p'])
                seqmask = psb("seqmask", [128, 16], F32)
                G(lambda e: e.memset(seqmask[:], 1.0), [], ['seqmask'])
                G(lambda e: e.affine_select(out=seqmask[:], in_=seqmask[:], pattern=[[-8, 16]],
                                            compare_op=ALU.is_ge, fill=0.0, base=0, channel_multiplier=1),
                  ['seqmask'], ['seqmask'])
                G(lambda e: e.affine_select(out=seqmask[:], in_=seqmask[:], pattern=[[8, 16]],
                                            compare_op=ALU.is_ge, fill=0.0, base=7, channel_multiplier=-1),
                  ['seqmask'], ['seqmask'])

                f4 = [128, 8, 128]
                qs = psb("qs", f4)
                ff = psb("ff", f4)
                kk = psb("kk", f4)
                gc = psb("gc", f4)
                sgate = psb("sgate", f4)
                v_tm = psb("v_tm", f4, BF16)
                tA = psb("tA", [128, 8, 32])
                tB = ff
                S.alias('tB', 'ff')
                qI = psb("qI", [128, 8, 32], BF16)
                kI = psb("kI", f4, BF16)
                intra = psb("intra", f4, BF16)
                qg = psb("qg", f4, BF16)
                khT = psb("khT", f4, BF16)
                kh_tm = psb("kh_tm", f4, BF16)
                khm = kI
                S.alias('khm', 'kI')
                egl = psb("egl", [128, 8, 16])
                Sst = psb("Sst", f4)
                Sbf = psb("Sbf", f4, BF16)
                S0 = ff
                S.alias('S0', 'ff')
                osq = khT
                S.alias('osq', 'khT')
                rn = qs[:].rearrange("p h t -> p (h t)")
                S.alias('rn', 'qs')
                onb = psb("onb", f4, BF16)
                G(lambda e: e.memset(intra[:], 0.0), [], ['intra'])
                G(lambda e: e.memset(Sst[:], 0.0), [], ['Sst'])
                G(lambda e: e.memset(Sbf[:], 0.0), [], ['Sbf'])

                def proj_fm(col0, banks):
                    for h in range(8):
                        bi = banks[h // 4]
                        for kc in range(KC):
                            MM(pb[bi][:, (h % 4) * 128:(h % 4 + 1) * 128],
                               wic[:, kc, col0 + h * 128:col0 + (h + 1) * 128], hTt[:, kc, :],
                               kc == 0, kc == KC - 1, ['wic', 'hTt'], [('pbq', bi, h % 4)])

                bq = PK

                order = list(range(NT)) + [NT]
                for t in order:
                    smp = (t == NT)
                    cols = tile_cols(t)
                    mask = mask_s if smp else mask_p
                    mkey = 'mask_s' if smp else 'mask_p'
                    rm = rm_s if smp else rm_p
                    if smp:
                        G(lambda e: e.memset(rm_s[:].rearrange("p h (s t) -> p h s t", t=8)[:, :, :, 0:1], 0.0),
                          ['rm_s'], ['rm_s'])
                    norm_range(gmix[:, 1, :], 'gmix', t * 128, 128, hTt[:], ['hTt'])
                    proj_fm(0, (0, 1))
                    for b2 in range(2):
                        A(lambda e, b2=b2: e.activation(out=qs[:, b2 * 4:b2 * 4 + 4, :], in_=pbv(b2, 4), func=AF.Silu),
                          bq(b2), ['qs'] + bq(b2))
                    if cfg.get('cstop', 99) <= 1:
                        continue
                    proj_fm(1024, (2, 3))
                    for b2 in range(2):
                        A(lambda e, b2=b2: e.activation(out=ff[:, b2 * 4:b2 * 4 + 4, :], in_=pbv(2 + b2, 4),
                                                        func=AF.Sigmoid), bq(2 + b2), ['ff'] + bq(2 + b2))
                    V(lambda e: e.tensor_tensor(out=ff[:], in0=ff[:], in1=oml[:].unsqueeze(2).to_broadcast(f4),
                                                op=ALU.mult), ['ff', 'oml'], ['ff'])
                    V(lambda e: e.tensor_tensor(out=ff[:], in0=ff[:], in1=lb[:].unsqueeze(2).to_broadcast(f4),
                                                op=ALU.add), ['ff', 'lb'], ['ff'])
                    V(lambda e: e.tensor_scalar(out=kk[:], in0=ff[:], scalar1=-1.0, scalar2=1.0, op0=ALU.mult,
                                                op1=ALU.add), ['ff'], ['kk'])
                    logf = ff
                    A(lambda e: e.activation(out=ff[:], in_=ff[:], func=AF.Ln), ['ff'], ['ff'])
                    V(lambda e: e.tensor_tensor_scan(out=gc[:].rearrange("p h t -> p (h t)"),
                                                     data0=rm[:].rearrange("p h t -> p (h t)"),
                                                     data1=logf[:].rearrange("p h t -> p (h t)"),
                                                     initial=0.0, op0=ALU.mult, op1=ALU.add),
                      ['ff', 'rm_p'], ['gc'])
                    if cfg.get('cstop', 99) <= 2:
                        continue
                    proj_fm(3072, (4, 5))
                    for b2 in range(2):
                        A(lambda e, b2=b2: e.activation(out=sgate[:, b2 * 4:b2 * 4 + 4, :], in_=pbv(4 + b2, 4),
                                                        func=AF.Silu), bq(4 + b2), ['sgate'] + bq(4 + b2))
                    for b2 in range(2):
                        for kc in range(KC):
                            MM(pb[6 + b2][:], hTt[:, kc, :], wic[:, kc, 2048 + b2 * 512:2048 + (b2 + 1) * 512],
                               kc == 0, kc == KC - 1, ['wic', 'hTt'], PK(6 + b2))
                        A(lambda e, b2=b2: e.copy(out=v_tm[:, b2 * 4:b2 * 4 + 4, :], in_=pbv(6 + b2, 4)),
                          PK(6 + b2), ['v_tm'])
                    if cfg.get('cstop', 99) <= 3:
                        continue
                    for I in range(4):
                        blk = slice(32 * I, 32 * I + 32)
                        nj = 32 * (I + 1)
                        if I == 0:
                            A(lambda e: e.activation(out=tA[:], in_=gc[:, :, blk], func=AF.Exp), ['gc'], ['tA'])
                            A(lambda e: e.activation(out=tB[:, :, :nj], in_=gc[:, :, :nj], func=AF.Exp, scale=-1.0),
                              ['gc'], ['tB'])
                        else:
                            ref = gc[:, :, 32 * I - 1:32 * I]
                            V(lambda e: e.tensor_tensor(out=tA[:], in0=gc[:, :, blk],
                                                        in1=ref.to_broadcast([128, 8, 32]), op=ALU.subtract),
                              ['gc'], ['tA'])
                            A(lambda e: e.activation(out=tA[:], in_=tA[:], func=AF.Exp), ['tA'], ['tA'])
                            V(lambda e: e.tensor_tensor(out=tB[:, :, :nj], in0=ref.to_broadcast([128, 8, nj]),
                                                        in1=gc[:, :, :nj], op=ALU.subtract), ['gc'], ['tB'])
                            A(lambda e: e.activation(out=tB[:, :, :nj], in_=tB[:, :, :nj], func=AF.Exp),
                              ['tB'], ['tB'])
                        V(lambda e: e.tensor_tensor(out=qI[:], in0=qs[:, :, blk], in1=tA[:], op=ALU.mult),
                          ['qs', 'tA'], ['qI'])
                        G(lambda e: e.tensor_tensor(out=kI[:, :, :nj], in0=kk[:, :, :nj], in1=tB[:, :, :nj],
                                                    op=ALU.mult), ['kk', 'tB'], ['kI'])
                        for h in range(8):
                            bi = h // 4
                            c0 = (h % 4) * 128 + 32 * I
                            MM(pb[bi][:nj, c0:c0 + 32], kI[:, h, :nj], qI[:, h, :], True, True,
                               ['kI', 'qI'], [('pbq', bi, h % 4)])
                        for b2 in range(2):
                            V(lambda e, b2=b2: e.tensor_tensor(
                                out=intra[:nj, b2 * 4:b2 * 4 + 4, blk], in0=pbv(b2, 4)[:nj, :, blk],
                                in1=mask[:nj, blk].unsqueeze(1).to_broadcast([nj, 4, 32]), op=ALU.mult),
                                bq(b2) + [mkey], ['intra'] + bq(b2))
                    if cfg.get('cstop', 99) <= 4:
                        continue
                    A(lambda e: e.activation(out=tB[:], in_=gc[:], func=AF.Exp), ['gc'], ['tB'])
                    V(lambda e: e.scalar_tensor_tensor(out=qg[:], in0=qs[:], scalar=SC, in1=tB[:], op0=ALU.mult, op1=ALU.mult),
                      ['qs', 'tB'], ['qg'])
                    if smp:
                        gl = gc[:].rearrange("p h (s t) -> p h s t", t=8)[:, :, :, 7:8]
                        V(lambda e: e.tensor_tensor(out=tB[:].rearrange("p h (s t) -> p h s t", t=8),
                                                    in0=gl.to_broadcast([128, 8, 16, 8]),
                                                    in1=gc[:].rearrange("p h (s t) -> p h s t", t=8),
                                                    op=ALU.subtract), ['gc'], ['tB'])
                        A(lambda e: e.activation(out=egl[:], in_=gc[:].rearrange("p h (s t) -> p h s t", t=8)[:, :, :, 7],
                                                 func=AF.Exp), ['gc'], ['egl'])
                    else:
                        gl = gc[:, :, 127:128]
                        V(lambda e: e.tensor_tensor(out=tB[:], in0=gl.to_broadcast(f4), in1=gc[:],
                                                    op=ALU.subtract), ['gc'], ['tB'])
                        A(lambda e: e.activation(out=egl[:, :, 0], in_=gc[:, :, 127], func=AF.Exp), ['gc'], ['egl'])
                    A(lambda e: e.activation(out=tB[:], in_=tB[:], func=AF.Exp), ['tB'], ['tB'])
                    G(lambda e: e.tensor_tensor(out=khT[:], in0=kk[:], in1=tB[:], op=ALU.mult), ['kk', 'tB'], ['khT'])
                    for h in range(8):
                        TR(pbbf(2)[:, h * 128:(h + 1) * 128], khT[:, h, :], ident_bf[:], ['khT', 'ident_bf'],
                           [('pbq', 2, h // 2)])
                    A(lambda e: e.copy(out=kh_tm[:].rearrange("p h t -> p (h t)"), in_=pbbf(2)),
                      PK(2), ['kh_tm'] + PK(2))
                    if cfg.get('cstop', 99) <= 5:
                        continue
                    if not smp and cfg.get('dbg', '') == 'nop':
                        pass
                    elif smp and cfg.get('dbg', '') == 'nos':
                        pass
                    elif not smp:
                        for h in range(8):
                            bi = 4 + h // 4
                            o_out = pb[bi][:, (h % 4) * 128:(h % 4 + 1) * 128]
                            MM(o_out, Sbf[:, h, :], qg[:, h, :], True, False, ['Sbf', 'qg'], [('pbq', bi, h % 4)])
                            MM(o_out, v_tm[:, h, :], intra[:, h, :], False, True, ['v_tm', 'intra'],
                               [('pbq', bi, h % 4)])
                        for h in range(8):
                            if cfg.get('sub', 9) < 2:
                                break
                            bi = 6 + h // 4
                            MM(pb[bi][:, (h % 4) * 128:(h % 4 + 1) * 128], kh_tm[:, h, :], v_tm[:, h, :], True, True,
                               ['kh_tm', 'v_tm'], [('pbq', bi, h % 4)])
                        V(lambda e: e.tensor_tensor(out=Sst[:], in0=Sst[:], in1=egl[:, :, 0:1].to_broadcast(f4),
                                                    op=ALU.mult), ['Sst', 'egl'], ['Sst'])
                        for b2 in range(2):
                            V(lambda e, b2=b2: e.tensor_tensor(out=Sst[:, b2 * 4:b2 * 4 + 4, :],
                                                               in0=Sst[:, b2 * 4:b2 * 4 + 4, :], in1=pbv(6 + b2, 4),
                                                               op=ALU.add), ['Sst'] + PK(6 + b2), ['Sst'] + PK(6 + b2))
                        if cfg.get('sub', 9) >= 4:
                            A(lambda e: e.copy(out=Sbf[:], in_=Sst[:]), ['Sst'], ['Sbf'])
                        if t == NT - 1 and cfg.get('sub', 9) >= 5:
                            S.dma('sp', hgrn_p.rearrange("h k v -> k h v"), Sst[:], reads=['Sst'], stream='o')
                    else:
                        for b2 in range(2):
                            V(lambda e, b2=b2: e.memset(pb[4 + b2][:], 0.0), [], PK(4 + b2))
                        for s in range(16):
                            S.dma('sp', S0[:], hgrn_in[s].rearrange("h k v -> k h v"), writes=['S0'], stream='s0')
                            A(lambda e: e.copy(out=Sbf[:], in_=S0[:]), ['S0'], ['Sbf'])
                            for h in range(8):
                                bi = 4 + h // 4
                                c0 = (h % 4) * 128 + s * 8
                                MM(pb[bi][:, c0:c0 + 8], Sbf[:, h, :], qg[:, h, s * 8:(s + 1) * 8], False, False,
                                   ['Sbf', 'qg'], [('pbq', bi, h % 4)])
                            V(lambda e, s=s: e.tensor_scalar(out=khm[:], in0=kh_tm[:], scalar1=seqmask[:, s:s + 1],
                                                            scalar2=None, op0=ALU.mult),
                              ['kh_tm', 'seqmask'], ['khm'])
                            for h in range(8):
                                bi = 6 + h // 4
                                MM(pb[bi][:, (h % 4) * 128:(h % 4 + 1) * 128], khm[:, h, :], v_tm[:, h, :],
                                   True, True, ['khm', 'v_tm'], [('pbq', bi, h % 4)])
                            V(lambda e, s=s: e.tensor_tensor(out=S0[:], in0=S0[:],
                                                             in1=egl[:, :, s:s + 1].to_broadcast(f4), op=ALU.mult),
                              ['S0', 'egl'], ['S0'])
                            for b2 in range(2):
                                V(lambda e, b2=b2: e.tensor_tensor(out=S0[:, b2 * 4:b2 * 4 + 4, :],
                                                                   in0=S0[:, b2 * 4:b2 * 4 + 4, :], in1=pbv(6 + b2, 4),
                                                                   op=ALU.add), ['S0'] + PK(6 + b2), ['S0'] + PK(6 + b2))
                            S.dma('sp', hgrn_s[s].rearrange("h k v -> k h v"), S0[:], reads=['S0'], stream='o')
                        for h in range(8):
                            bi = 4 + h // 4
                            MM(pb[bi][:, (h % 4) * 128:(h % 4 + 1) * 128], v_tm[:, h, :], intra[:, h, :],
                               False, True, ['v_tm', 'intra'], [('pbq', bi, h % 4)])
                    if cfg.get('cstop', 99) <= 6:
                        continue
                    for b2 in range(2):
                        A(lambda e, b2=b2: e.activation(out=osq[:, b2 * 4:b2 * 4 + 4, :], in_=pbv(4 + b2, 4),
                                                        func=AF.Square), bq(4 + b2), ['osq'])
                    for b2 in range(2):
                        MM(pb[b2][:], ones_bf[:], osq[:, b2 * 4:b2 * 4 + 4, :].rearrange("p h t -> p (h t)"),
                           True, True, ['osq', 'ones_bf'], bq(b2))
                        A(lambda e, b2=b2: e.activation(out=rn[:, b2 * 512:(b2 + 1) * 512], in_=pb[b2][:],
                                                        func=AF.Sqrt, bias=EPS, scale=1.0 / 128),
                          bq(b2), ['rn'] + bq(b2))
                    V(lambda e: e.reciprocal(out=rn[:], in_=rn[:]), ['rn'], ['rn'])
                    for b2 in range(2):
                        V(lambda e, b2=b2: e.scalar_tensor_tensor(
                            out=rn[:, b2 * 512:(b2 + 1) * 512].rearrange("p (h t) -> p h t", h=4),
                            in0=pbv(4 + b2, 4), scalar=gnc[:, 0:1],
                            in1=rn[:, b2 * 512:(b2 + 1) * 512].rearrange("p (h t) -> p h t", h=4),
                            op0=ALU.mult, op1=ALU.mult), bq(4 + b2) + ['gnc', 'rn'], ['rn'] + bq(4 + b2))
                    G(lambda e: e.tensor_tensor(out=onb[:], in0=rn[:].rearrange("p (h t) -> p h t", h=8), in1=sgate[:],
                                                op=ALU.mult), ['rn', 'sgate'], ['onb'])
                    if cfg.get('cstop', 99) <= 7:
                        continue
                    for dc in range(KC):
                        bi = 2 + dc // 4
                        for kc in range(KC):
                            MM(pb[bi][:, (dc % 4) * 128:(dc % 4 + 1) * 128], woc[:, kc, dc * 128:(dc + 1) * 128],
                               onb[:, kc, :], kc == 0, kc == KC - 1, ['woc', 'onb'],
                               [('pbq', bi, dc % 4)])
                    for b2 in range(2):
                        V(lambda e, b2=b2: e.tensor_tensor(out=xT[:, b2 * 4:b2 * 4 + 4, cols],
                                                           in0=xT[:, b2 * 4:b2 * 4 + 4, cols],
                                                           in1=pbv(2 + b2, 4), op=ALU.add),
                          bq(2 + b2) + [('xT', t)], [('xT', t)] + bq(2 + b2))
                S.barrier()

        def ab_layer():
            with ExitStack() as ph:
                psb = mk_sb(ph)
                norm_range = make_norm(psb, 128, 1)
                STG = cfg.get('astage', 99)
                NEG = -1.0e30
                SCq = 128 ** -0.5
                SCm = 96 ** -0.5
                f4 = [128, 4, 128]
                wia = psb("wia", [128, KC, 2600], BF16)
                woa = psb("woa", [128, KC, D], BF16)
                wuq = psb("wuq", [128, 2, 768], BF16)
                wuk = psb("wuk", [128, 2, 512], BF16)
                wuv = psb("wuv", [128, 2, 512], BF16)
                wia_v = w_in_ab.rearrange("(kc p) f -> p kc f", p=128)
                for kc in range(KC):
                    for (c0, cn) in blocks(2600, 1300):
                        S.dma('pool', wia[:, kc, c0:c0 + cn], wia_v[:, kc, c0:c0 + cn], writes=['wia'], stream='w')
                S.dma('pool', woa[:], w_out_ab.rearrange("(kc p) f -> p kc f", p=128), writes=['woa'], stream='w')
                S.dma('pool', wuq[:], w_uq.rearrange("(c p) f -> p c f", p=128), writes=['wuq'], stream='w')
                S.dma('pool', wuk[:], w_uk.rearrange("(c p) f -> p c f", p=128), writes=['wuk'], stream='w')
                S.dma('pool', wuv[:], w_uv.rearrange("(c p) f -> p c f", p=128), writes=['wuv'], stream='w')
                cw = psb("cw", [128, 12, 4])
                S.dma('sp', cw[:], conv_w_fm, writes=['cw'], stream='c')
                negA = psb("negA", [128, 4])
                dtb = psb("dtb", [128, 4])
                S.dma('sp', negA[:], a_log.partition_broadcast(128), writes=['negA'], stream='c')
                S.dma('sp', dtb[:], dt_bias.partition_broadcast(128), writes=['dtb'], stream='c')
                A(lambda e: e.activation(out=negA[:], in_=negA[:], func=AF.Exp), ['negA'], ['negA'])
                V(lambda e: e.tensor_scalar(out=negA[:], in0=negA[:], scalar1=-1.0, scalar2=None, op0=ALU.mult),
                  ['negA'], ['negA'])
                gdnn = psb("gdnn", [128, 1])
                S.dma('sp', gdnn[:], gdn_norm, writes=['gdnn'], stream='c')
                qnb = psb("qnb", [128, 256])
                kvnb = psb("kvnb", [128, 256])
                S.dma('sp', qnb[:], q_norm.partition_broadcast(128), writes=['qnb'], stream='c')
                S.dma('sp', kvnb[:], kv_norm.partition_broadcast(128), writes=['kvnb'], stream='c')
                cosb = psb("cosb", [128, NT + 1, 16])
                sinb = psb("sinb", [128, NT + 1, 16])
                S.dma('sp', cosb[:], rope_cos.rearrange("(t p) f -> p t f", p=128), writes=['cosb'], stream='c')
                S.dma('sp', sinb[:], rope_sin.rearrange("(t p) f -> p t f", p=128), writes=['sinb'], stream='c')
                ones_f = psb("ones_f", [128, 128])
                G(lambda e: e.memset(ones_f[:], 1.0), [], ['ones_f'])

                def build_masks(tag, blk8, xsb):
                    m = {}
                    for nm in ('tri', 'blk', 'nms', 'nmi', 'nmsT', 'cmask'):
                        m[nm] = xsb(nm + tag, [128, 128])

                    def sel(t, pattern, op, fill, base, cm, view=None):
                        key = m_key(t)
                        ap = t[:] if view is None else t[:].rearrange("p (a b) -> p a b", a=16)
                        G(lambda e: e.affine_select(out=ap, in_=ap, pattern=pattern, compare_op=op, fill=fill,
                                                    base=base, channel_multiplier=cm), [key], [key])

                    def m_key(t):
                        return 'mask' + tag

                    def same_block(t, fill):
                        sel(t, [[-8, 16], [0, 8]], ALU.is_ge, fill, 0, 1, view=True)
                        sel(t, [[8, 16], [0, 8]], ALU.is_ge, fill, 7, -1, view=True)
                    G(lambda e: e.memset(m['tri'][:], 1.0), [], ['mask' + tag])
                    sel(m['tri'], [[1, 128]], ALU.is_ge, 0.0, 0, -1)
                    G(lambda e: e.memset(m['blk'][:], 1.0), ['mask' + tag], ['mask' + tag])
                    G(lambda e: e.memset(m['nms'][:], 0.0), ['mask' + tag], ['mask' + tag])
                    sel(m['nms'], [[1, 128]], ALU.is_gt, NEG, 0, -1)
                    G(lambda e: e.memset(m['nmi'][:], 0.0), ['mask' + tag], ['mask' + tag])
                    sel(m['nmi'], [[1, 128]], ALU.is_ge, NEG, 0, -1)
                    G(lambda e: e.memset(m['nmsT'][:], 0.0), ['mask' + tag], ['mask' + tag])
                    sel(m['nmsT'], [[-1, 128]], ALU.is_gt, NEG, 0, 1)
                    G(lambda e: e.memset(m['cmask'][:], 1.0), ['mask' + tag], ['mask' + tag])
                    sel(m['cmask'], [[1, 128]], ALU.is_ge, 0.0, 0, -1)
                    if blk8:
                        same_block(m['tri'], 0.0)
                        same_block(m['blk'], 0.0)
                        same_block(m['nms'], NEG)
                        same_block(m['nmi'], NEG)
                        same_block(m['nmsT'], NEG)
                        same_block(m['cmask'], 0.0)
                    return m
                seqmask = psb("seqmask", [128, 16], F32)
                G(lambda e: e.memset(seqmask[:], 1.0), [], ['seqmask'])
                G(lambda e: e.affine_select(out=seqmask[:], in_=seqmask[:], pattern=[[-8, 16]],
                                            compare_op=ALU.is_ge, fill=0.0, base=0, channel_multiplier=1),
                  ['seqmask'], ['seqmask'])
                G(lambda e: e.affine_select(out=seqmask[:], in_=seqmask[:], pattern=[[8, 16]],
                                            compare_op=ALU.is_ge, fill=0.0, base=7, channel_multiplier=-1),
                  ['seqmask'], ['seqmask'])

                H = {}
                xTt = psb("xTt", [128, KC, 128])
                hTt = psb("hTt", [128, KC, 128], BF16)
                acc = psb("acc", [128, 12, 128])
                ctmp = psb("ctmp", [128, 12, 128])
                qkvs = acc
                S.alias('qkvs', 'acc')
                qkvtm = ctmp[:].rearrange("p c t -> p (c t)")
                S.alias('qkvtm', 'ctmp')
                cst = ctmp[:].rearrange("p c t -> p (c t)")[0:48, :]
                S.alias('cst', 'ctmp')
                xin = ctmp[:].rearrange("p c t -> p (c t)")[:, 0:D]
                S.alias('xin', 'ctmp')
                zs = psb("zs", f4)
                tm = psb("tm", [128, 552])
                sq8 = psb("sq8", [128, 8, 128], BF16)
                rn8 = psb("rn8", [128, 8, 128])
                qkn = rn8
                S.alias('qkn', 'rn8')
                ktm = psb("ktm", f4)
                vtm = psb("vtm", f4)
                gt = psb("gt", [128, 16])
                gcc = psb("gcc", [128, 8])
                G4 = psb("G4", f4)
                gcr = psb("gcr", f4)
                br = psb("br", f4)
                a1 = psb("a1", f4)
                E1s = psb("E1s", f4)
                E1i = psb("E1i", f4)
                E2s = psb("E2s", f4)
                Xa = [psb("Xa%d" % i, f4) for i in range(2)]
                XTa = [psb("XTa%d" % i, f4) for i in range(2)]
                Pm = psb("Pm", f4)
                itr = psb("itr", f4)
                qg = psb("qg", f4)
                vb, kbg, kdec, uT, wT = E1s, E2s, E1i, br, G4
                vnT, vn, kdm, ona = Xa[0], Xa[1], XTa[0], XTa[1]
                for al, cn in (('vb', 'E1s'), ('kbg', 'E2s'), ('kdec', 'E1i'), ('uT', 'br'), ('wT', 'G4'),
                               ('vnT', ('Xa', 0)), ('vn', ('Xa', 1)), ('kdm', ('XTa', 0)), ('ona', ('XTa', 1))):
                    S.alias(al, cn)
                eglr = psb("eglr", [128, 4, 16])
                Sg = psb("Sg", f4)
                Sgs = psb("Sgs", f4)
                osq = psb("osq", f4, BF16)
                rno = psb("rno", [128, 512])
                onb = psb("onb", [128, KC, 128], BF16)
                ssn = psb("ssn", [128, 4])
                cqn = psb("cqn", [128, 256])
                ckvn = psb("ckvn", [128, 256])
                kst = psb("kst", [128, 128])
                rt = psb("rt", [128, 8, 64])
                cqnT = psb("cqnT", [128, 2, 128], BF16)
                ckvnT = psb("ckvnT", [128, 2, 128], BF16)
                qtm = psb("qtm", [128, 8, 96])
                qn2 = psb("qn2", [128, 4, 128], BF16)
                qp2 = psb("qp2", [128, 8, 32], BF16)
                QTn = psb("QTn", [128, 4, 128], BF16)
                QTp = psb("QTp", [32, 8, 128], BF16)
                G(lambda e: e.memset(Sg[:], 0.0), [], ['Sg'])
                G(lambda e: e.memset(kst[:], 0.0), [], ['kst'])
                G(lambda e: e.memset(onb[:], 0.0), [], [('onb', 0), ('onb', 1)])


                def sample_attention():
                    NPG = cfg['NPG']
                    NPOOL = cfg['NPOOL']
                    R = NPG
                    SB = 128 // R
                    RG = min(8, R)
                    NG = R // RG
                    BPP = 128 // RG
                    ssb = H['ssb2']
                    rep = ssb("rep", [NPG, 128])
                    gb = ssb("gb", [128, NG + 1])
                    cm8 = ssb("cm8", [128, 64])
                    S.dma('sp', rep[:], rep_in, writes=['rep'], stream='c')
                    S.dma('sp', gb[:], gbase_in, writes=['gb'], stream='c')
                    S.dma('sp', cm8[:], cm8_in, writes=['cm8'], stream='c')
                    pti = ssb("pti", [16, NPG], I32)
                    ptf = ssb("ptf", [16, NPG])
                    ptT = ssb("ptT", [NPG, 16])
                    ptrep = ssb("ptrep", [128, 16])
                    S.dma('sp', pti[:], page_table, writes=['pti'], stream='c')
                    V(lambda e: e.tensor_copy(out=ptf[:], in_=pti[:]), ['pti'], ['ptf'])
                    TR(pb[0][0:NPG, 0:16], ptf[:], ident[0:16, 0:16], ['ptf', 'ident'], PK(0))
                    A(lambda e: e.copy(out=ptT[:], in_=pb[0][0:NPG, 0:16]), PK(0), ['ptT'] + PK(0))
                    MM(pb[0][:, 0:16], rep[:], ptT[:], True, True, ['rep', 'ptT'], PK(0))
                    A(lambda e: e.copy(out=ptrep[:], in_=pb[0][:, 0:16]), PK(0), ['ptrep'] + PK(0))
                    idxf = ssb("idxf", [128, 16, NG + 1])
                    idx = ssb("idx", [128, 16, NG + 1], I32)
                    for g in range(NG + 1):
                        V(lambda e, g=g: e.tensor_scalar(out=idxf[:, :, g], in0=ptrep[:],
                                                        scalar1=float(BPP if g < NG else SB), scalar2=gb[:, g:g + 1],
                                                        op0=ALU.mult, op1=ALU.add), ['ptrep', 'gb', 'idxf'], ['idxf'])
                    V(lambda e: e.tensor_copy(out=idx[:], in_=idxf[:]), ['idxf'], ['idx'])
                    SST = cfg.get('sstop', 99)
                    if SST <= 1:
                        return
                    lat_blk = cache_lat.rearrange("n (b r) d -> (n b) (r d)", r=RG)
                    kr_blk = cache_kr.rearrange("n (b r) d -> (n b) (r d)", r=R)
                    wk_ol = ssb("wk_ol", [128, 2048], BF16)
                    wukT = wk_ol[:, :].rearrange("p (h l) -> p h l", h=8)
                    S.alias('wukT', 'wk_ol')
                    S.alias('olatT', 'wk_ol')
                    for hh in range(2):
                        S.dma('pool', wk_ol[:, hh * 1024:(hh + 1) * 1024], w_ukT[:, hh * 1024:(hh + 1) * 1024], writes=['wukT'], stream='w')
                    if cfg.get('sq', 9) <= 0:
                        return
                    qlatT = ssb("qlatT", [128, 2, 8, 128], BF16)
                    for c in range(2):
                        for h in range(8):
                            bi = h // 4
                            MM(pb[bi][:, (h % 4) * 128:(h % 4 + 1) * 128], wukT[:, h, c * 128:(c + 1) * 128],
                               QTn[:, h // 2, :], True, True, ['wukT', 'QT'], [('pbq', bi, h % 4)])
                        for b2 in range(2):
                            A(lambda e, c=c, b2=b2: e.copy(out=qlatT[:, c, b2 * 4:b2 * 4 + 4, :], in_=pbv(b2, 4)),
                              PK(b2), ['qlatT'] + PK(b2))
                    if cfg.get('sq', 9) <= 1:
                        return
                    qpeT = QTp
                    S.alias('qpeT', 'QT')
                    if cfg.get('sq', 9) <= 2:
                        return
                    kpTn = ssb("kpTn", [32, 128], BF16)
                    TR(pb[3][0:32, 0:128], kst[:, 0:32], ident[:], ['kst', 'ident'], PK(3))
                    A(lambda e: e.copy(out=kpTn[:], in_=pb[3][0:32, 0:128]), PK(3), ['kpTn'] + PK(3))
                    Lnew = ssb("Lnew", [128, 257], BF16)
                    V(lambda e: e.memset(Lnew[:, 256:257], 1.0), [], ['Lnew'])
                    V(lambda e: e.tensor_copy(out=Lnew[:, 0:256], in_=ckvn[:]), ['ckvn', 'Lnew'], ['Lnew'])
                    if SST <= 2:
                        return
                    Lb = [ssb("Lb%d" % i, [128, RG, 256], BF16) for i in range(2)]
                    Kb = [ssb("Kb0", [128, R, 32], BF16)] * 2
                    latT = [ssb("latT0", [128, 2, RG, 128], BF16)] * 2
                    kpT = [ssb("kpT0", [32, RG, 128], BF16)] * 2
                    PTs = [ssb("PTs%d" % i, [128, RG, 64], BF16) for i in range(2)]
                    oacc = ssb("oacc", [64, 257])
                    olat = ssb("olat", [64, 256])
                    olatT = wk_ol[:, :].rearrange("p (c s q) -> p c s q", c=2, s=16)
                    mst = ssb("mst", [128, 8])
                    m11 = ssb("m11", [1, 4])
                    gq = []
                    for s in range(16):
                        for g in range(NG + 1):
                            gq.append((s, g))
                    ctr = {'n': 0}

                    def stage1(s, g):
                        n = ctr['n']
                        ctr['n'] += 1
                        b = n % 2
                        sbk = 6 + b
                        if g == 0:
                            kb_ = s % 2
                            S.dma_custom('pool', lambda e: e.indirect_dma_start(
                                out=Kb[kb_][:].rearrange("p r d -> p (r d)"), out_offset=None, in_=kr_blk,
                                in_offset=bass.IndirectOffsetOnAxis(ap=idx[:, s, NG:NG + 1], axis=0)),
                                ['idx'], [('Kb', 0)], stream='gk', nrr=2)
                        if g < NG:
                            S.dma_custom('pool', lambda e: e.indirect_dma_start(
                                out=Lb[b][:].rearrange("p r d -> p (r d)"), out_offset=None, in_=lat_blk,
                                in_offset=bass.IndirectOffsetOnAxis(ap=idx[:, s, g:g + 1], axis=0)),
                                ['idx'], [('Lb', b)], stream='gl', nrr=2)
                            for r in range(RG):
                                rr = g * RG + r
                                for c in range(2):
                                    blk = (r % 4) * 2 + c
                                    TR(pbbf(4 + (r // 4) % 2)[:, blk * 128:(blk + 1) * 128], Lb[b][:, r, c * 128:(c + 1) * 128],
                                       ident_bf[:], [('Lb', b), 'ident_bf'], [('pbq', 4 + (r // 4) % 2, blk // 2)])
                                TR(pbbf(3)[0:32, r * 128:(r + 1) * 128], Kb[s % 2][:, rr, :], ident_bf[:],
                                   [('Kb', 0), 'ident_bf'], [('pbq', 3, r // 2)])
                                if r % 4 == 3:
                                    bk = 4 + (r // 4) % 2
                                    r0 = r - 3
                                    V(lambda e, bk=bk, r0=r0: e.tensor_copy(
                                        out=latT[b][:, :, r0:r0 + 4, :].rearrange("p c r k -> p r c k"),
                                        in_=pbbf(bk).rearrange("p (r c k) -> p r c k", r=4, c=2)),
                                        PK(bk), [('latT', 0)] + PK(bk))
                            A(lambda e: e.copy(out=kpT[b][:].rearrange("p r k -> p (r k)"), in_=pbbf(3)[0:32, 0:RG * 128]),
                              PK(3), [('kpT', 0)] + PK(3))
                            for r in range(RG):
                                o = pb[sbk][:, r * 64:(r + 1) * 64]
                                MM(o, latT[b][:, 0, r, :], qlatT[:, 0, :, s * 8:(s + 1) * 8], True, False, [('latT', 0), 'qlatT'],
                                   [('pbq', sbk, r // 2)])
                                MM(o, latT[b][:, 1, r, :], qlatT[:, 1, :, s * 8:(s + 1) * 8], False, False, [('latT', 0), 'qlatT'],
                                   [('pbq', sbk, r // 2)])
                                MM(o, kpT[b][:, r, :], qpeT[:, :, s * 8:(s + 1) * 8], False, True, [('kpT', 0), 'qpeT'],
                                   [('pbq', sbk, r // 2)])
                            ncol = RG * 64
                        else:
                            o = pb[sbk][:, 0:64]
                            MM(o, ckvnT[:, 0, :], qlatT[:, 0, :, s * 8:(s + 1) * 8], True, False, ['ckvnT', 'qlatT'], [('pbq', sbk, 0)])
                            MM(o, ckvnT[:, 1, :], qlatT[:, 1, :, s * 8:(s + 1) * 8], False, False, ['ckvnT', 'qlatT'], [('pbq', sbk, 0)])
                            MM(o, kpTn[:], qpeT[:, :, s * 8:(s + 1) * 8], False, True, ['kpTn', 'qpeT'], [('pbq', sbk, 0)])
                            ncol = 64
                        return (s, g, b, sbk, ncol)

                    def stage2(st_):
                        s, g, b, sbk, ncol = st_
                        keys = [('pbq', sbk, j) for j in range((ncol + 127) // 128)]
                        V(lambda e: e.reduce_max(out=mst[:, 0:1], in_=pb[sbk][:, :ncol], axis=AX.X), keys, ['mst0'])
                        TR(pb[2][0:1, 0:128], mst[:, 0:1], ident[:], ['mst0', 'ident'], PK(2))
                        V(lambda e: e.reduce_max(out=m11[:, 0:1], in_=pb[2][0:1, 0:128], axis=AX.X), PK(2), ['m11'] + PK(2))
                        if g == 0:
                            V(lambda e: e.tensor_copy(out=m11[:, 1:2], in_=m11[:, 0:1]), ['m11'], ['m11'])
                        else:
                            V(lambda e: e.tensor_tensor(out=m11[:, 1:2], in0=m11[:, 1:2], in1=m11[:, 0:1], op=ALU.max),
                              ['m11'], ['m11'])
                        MM(pb[2][:, 128:129], ones_f[0:1, :], m11[:, 1:2], True, True, ['ones_f', 'm11'], PK(2))
                        A(lambda e: e.copy(out=mst[:, 2:3], in_=pb[2][:, 128:129]), PK(2), ['mst2'] + PK(2))
                        A(lambda e: e.mul(out=mst[:, 3:4], in_=mst[:, 2:3], mul=-SCm), ['mst2'], ['mst3'])
                        if g > 0:
                            V(lambda e: e.tensor_tensor(out=mst[:, 4:5], in0=mst[:, 1:2], in1=mst[:, 2:3], op=ALU.subtract),
                              ['mst1', 'mst2'], ['mst4'])
                            A(lambda e: e.activation(out=mst[:, 4:5], in_=mst[:, 4:5], func=AF.Exp, scale=SCm),
                              ['mst4'], ['mst4'])
                        V(lambda e: e.tensor_copy(out=mst[:, 1:2], in_=mst[:, 2:3]), ['mst2', 'mst4'], ['mst1'])
                        if g < NG:
                            A(lambda e: e.activation(out=PTs[b][:].rearrange("p r q -> p (r q)"), in_=pb[sbk][:, :ncol],
                                                     func=AF.Exp, bias=mst[:, 3:4], scale=SCm),
                              keys + ['mst3'], [('PTs', b)] + keys)
                            for r in range(RG):
                                MM(pb[1][0:64, 0:256], PTs[b][:, r, :], Lb[b][:, r, :], r == 0, r == RG - 1,
                                   [('PTs', b), ('Lb', b)], PK(1))
                                MM(pb[1][0:64, 256:257], PTs[b][:, r, :], ones_bf[:, 0:1], False, r == RG - 1,
                                   [('PTs', b), 'ones_bf'], PK(1))
                        else:
                            A(lambda e: e.activation(out=PTs[b][:, 0, :], in_=pb[sbk][:, :64], func=AF.Exp,
                                                     bias=mst[:, 3:4], scale=SCm), keys + ['mst3'], [('PTs', b)] + keys)
                            V(lambda e: e.scalar_tensor_tensor(out=PTs[b][:, 0, :], in0=PTs[b][:, 0, :],
                                                               scalar=seqmask[:, s:s + 1], in1=cm8[:], op0=ALU.mult,
                                                               op1=ALU.mult), [('PTs', b), 'seqmask', 'cm8'], [('PTs', b)])
                            MM(pb[1][0:64, 0:257], PTs[b][:, 0, :], Lnew[:], True, True, [('PTs', b), 'Lnew'], PK(1))
                        if g == 0:
                            V(lambda e: e.tensor_copy(out=oacc[:], in_=pb[1][0:64, 0:257]), PK(1), ['oacc'] + PK(1))
                        else:
                            V(lambda e: e.tensor_scalar(out=oacc[:], in0=oacc[:], scalar1=mst[0:64, 4:5], scalar2=None,
                                                        op0=ALU.mult), ['oacc', 'mst4'], ['oacc'])
                            V(lambda e: e.tensor_tensor(out=oacc[:], in0=oacc[:], in1=pb[1][0:64, 0:257], op=ALU.add),
                              ['oacc'] + PK(1), ['oacc'] + PK(1))
                        if g == NG:
                            V(lambda e: e.reciprocal(out=mst[0:64, 5:6], in_=oacc[:, 256:257]), ['oacc'], ['mst5'])
                            V(lambda e: e.tensor_scalar(out=olat[:], in0=oacc[:, 0:256], scalar1=mst[0:64, 5:6],
                                                        scalar2=None, op0=ALU.mult), ['oacc', 'mst5'], ['olat'])
                            for c in range(2):
                                TR(pb[0][:, c * 64:(c + 1) * 64], olat[:, c * 128:(c + 1) * 128], ident[0:64, 0:64],
                                   ['olat', 'ident'], PK(0))
                            A(lambda e: e.copy(out=olatT[:, :, s, :], in_=pb[0][:, 0:128].rearrange("p (c q) -> p c q", c=2)),
                              PK(0), ['olatT'] + PK(0))

                    prev = None
                    if SST <= 5:
                        gq = gq[:SST - 2]
                    for (s, g) in gq:
                        cur = stage1(s, g)
                        if prev is not None and SST != 3:
                            stage2(prev)
                        prev = cur
                    if SST != 3:
                        stage2(prev)
                    if cfg.get('dbg_s'):
                        S.dma('sp', dbg_mst, mst[:], reads=['mst0', 'mst1', 'mst2', 'mst3', 'mst4', 'mst5'], stream='o')
                        S.dma('sp', dbg_oacc, oacc[:], reads=['oacc'], stream='o')
                        S.dma('sp', dbg_lb, Lb[0][:, :, :].rearrange('p r d -> p (r d)'), reads=[('Lb', 0)], stream='o')
                        S.dma('sp', dbg_kb, Kb[0][:].rearrange('p r d -> p (r d)'), reads=[('Kb', 0)], stream='o')
                        S.dma('sp', dbg_olat, olat[:], reads=['olat'], stream='o')
                    if SST <= 6:
                        return
                    for h in range(8):
                        rh = (h % 2) * 64
                        pr = (h // 2) * 128
                        for c in range(2):
                            MM(pb[0][:, 0:128], wuv[:, c, pr:pr + 128],
                               olatT[:, c, :, h * 8:(h + 1) * 8], c == 0, c == 1,
                               ['wuv', 'olatT'], PK(0))
                        V(lambda e, h=h, rh=rh: e.tensor_copy(out=onb[rh:rh + 64, 4 + h // 2, :], in_=pb[0][rh:rh + 64, 0:128]),
                          PK(0), [('onb', 1)] + PK(0))


                def tile_body(t):
                    smp = (t == NT)
                    M = H['M']
                    mk = 'maskS' if smp else 'maskP'
                    S.dma('sp', xin, tile_rows(t), writes=['xin'], stream='x')
                    for half in range(2):
                        for j in range(4):
                            kc = half * 4 + j
                            TR(pb[half][:, j * 128:(j + 1) * 128], xin[:, kc * 128:(kc + 1) * 128], ident[:],
                               ['xin', 'ident'], [('pbq', half, j)])
                        A(lambda e, half=half: e.copy(out=xTt[:, half * 4:half * 4 + 4, :], in_=pbv(half, 4)),
                          PK(half), ['xTt'] + PK(half))
                    norm_range(gmix[:, 0, :], 'gmix', 0, 128, hTt[:], ['hTt'], xv=xTt[:], xk=['xTt'])
                    for c in range(16):
                        bi = c // 4
                        for kc in range(KC):
                            MM(pb[bi][:, (c % 4) * 128:(c % 4 + 1) * 128], wia[:, kc, c * 128:(c + 1) * 128],
                               hTt[:, kc, :], kc == 0, kc == KC - 1, ['wia', 'hTt'], [('pbq', bi, c % 4)])
                    for b2 in range(3):
                        if smp:
                            A(lambda e, b2=b2: e.copy(
                                out=H['cbufS'][:, b2 * 4:b2 * 4 + 4, :, 3:11],
                                in_=pb[b2][:].rearrange("p (c s t) -> p c s t", c=4, s=16)),
                              PK(b2), ['cbufS'] + PK(b2))
                        else:
                            A(lambda e, b2=b2: e.copy(out=H['cbuf'][:, b2 * 4:b2 * 4 + 4, 3:131], in_=pbv(b2, 4)),
                              PK(b2), ['cbuf'] + PK(b2))
                    A(lambda e: e.activation(out=zs[:], in_=pbv(3, 4), func=AF.Silu), PK(3), ['zs'] + PK(3))
                    for (bi, c0, cn) in ((4, 2048, 512), (5, 2560, 40)):
                        for kc in range(KC):
                            MM(pb[bi][:, :cn], hTt[:, kc, :], wia[:, kc, c0:c0 + cn], kc == 0, kc == KC - 1,
                               ['wia', 'hTt'], PK(bi))
                        A(lambda e, bi=bi, c0=c0, cn=cn: e.copy(out=tm[:, c0 - 2048:c0 - 2048 + cn], in_=pb[bi][:, :cn]),
                          PK(bi), ['tm'] + PK(bi))
                    if smp or t == NT - 1:
                        for j3 in range(3):
                            bi = 5 + j3
                            for kc in range(KC):
                                MM(pb[bi][:], hTt[:, kc, :], wia[:, kc, j3 * 512:(j3 + 1) * 512], kc == 0, kc == KC - 1,
                                   ['wia', 'hTt'], PK(bi))
                            A(lambda e, bi=bi, j3=j3: e.copy(out=qkvtm[:, j3 * 512:(j3 + 1) * 512], in_=pb[bi][:]),
                              PK(bi), ['qkvtm'] + PK(bi))
                        if smp:
                            for s in range(16):
                                S.dma('sp', conv_s_o[s], qkvtm[s * 8 + 5:s * 8 + 8, :], reads=['qkvtm'], stream='o')
                        else:
                            S.dma('sp', conv_p_o, qkvtm[125:128, :], reads=['qkvtm'], stream='o')
                    if STG <= 1:
                        return
                    if smp:
                        S.dma('sp', cst, conv_s_in, writes=['cst'], stream='c')
                        for c in range(12):
                            bi = c // 8
                            cc = c % 8
                            TR(pb[bi][:, cc * 48:(cc + 1) * 48], cst[:, c * 128:(c + 1) * 128], ident[0:48, 0:48],
                               ['cst', 'ident'], PK(bi))
                        A(lambda e: e.copy(out=H['cbufS'][:, 0:8, :, 0:3],
                                           in_=pb[0][:, 0:384].rearrange("p (c s r) -> p c s r", c=8, s=16)),
                          PK(0), ['cbufS'] + PK(0))
                        A(lambda e: e.copy(out=H['cbufS'][:, 8:12, :, 0:3],
                                           in_=pb[1][:, 0:192].rearrange("p (c s r) -> p c s r", c=4, s=16)),
                          PK(1), ['cbufS'] + PK(1))
                        accv = acc[:].rearrange("p c (s t) -> p c s t", s=16)
                        ctv = ctmp[:].rearrange("p c (s t) -> p c s t", s=16)

                        def win(i):
                            return H['cbufS'][:, :, :, i:i + 8]

                        def cwb(i):
                            return cw[:, :, i:i + 1].unsqueeze(3).to_broadcast([128, 12, 16, 8])
                        ck = 'cbufS'
                    else:
                        accv = acc[:]
                        ctv = ctmp[:]

                        def win(i):
                            return H['cbuf'][:, :, i:i + 128]

                        def cwb(i):
                            return cw[:, :, i:i + 1].to_broadcast([128, 12, 128])
                        ck = 'cbuf'
                    V(lambda e: e.tensor_tensor(out=accv, in0=win(0), in1=cwb(0), op=ALU.mult), [ck, 'cw'], ['acc'])
                    for i in range(1, 4):
                        G(lambda e, i=i: e.tensor_tensor(out=ctv, in0=win(i), in1=cwb(i), op=ALU.mult),
                          [ck, 'cw'], ['ctmp'])
                        V(lambda e: e.tensor_tensor(out=accv, in0=accv, in1=ctv, op=ALU.add), ['acc', 'ctmp'], ['acc'])
                    if not smp:
                        V(lambda e: e.tensor_copy(out=H['cbuf'][:, :, 0:3], in_=H['cbuf'][:, :, 128:131]), ['cbuf'], ['cbuf'])
                    A(lambda e: e.activation(out=acc[:], in_=acc[:], func=AF.Silu), ['acc'], ['acc'])
                    A(lambda e: e.activation(out=sq8[:], in_=qkvs[:, 0:8, :], func=AF.Square), ['qkvs'], ['sq8'])
                    for b2 in range(2):
                        MM(pb[b2][:], ones_bf[:], sq8[:, b2 * 4:b2 * 4 + 4, :].rearrange("p h t -> p (h t)"), True, True,
                           ['sq8', 'ones_bf'], PK(b2))
                        A(lambda e, b2=b2: e.activation(out=rn8[:, b2 * 4:b2 * 4 + 4, :], in_=pbv(b2, 4), func=AF.Sqrt,
                                                        bias=EPS, scale=1.0), PK(b2), ['rn8'] + PK(b2))
                    V(lambda e: e.reciprocal(out=rn8[:], in_=rn8[:]), ['rn8'], ['rn8'])
                    V(lambda e: e.tensor_tensor(out=qkn[:], in0=qkvs[:, 0:8, :], in1=rn8[:], op=ALU.mult),
                      ['qkvs', 'rn8'], ['qkn'])
                    for h in range(4):
                        TR(pb[2][:, h * 128:(h + 1) * 128], qkn[:, 4 + h, :], ident[:], ['qkn', 'ident'], [('pbq', 2, h)])
                        TR(pb[3][:, h * 128:(h + 1) * 128], qkvs[:, 8 + h, :], ident[:], ['qkvs', 'ident'],
                           [('pbq', 3, h)])
                    A(lambda e: e.copy(out=ktm[:], in_=pbv(2, 4)), PK(2), ['ktm'] + PK(2))
                    A(lambda e: e.copy(out=vtm[:], in_=pbv(3, 4)), PK(3), ['vtm'] + PK(3))
                    A(lambda e: e.activation(out=gt[:, 0:4], in_=tm[:, 0:4], func=AF.Sigmoid), ['tm'], ['gt'])
                    V(lambda e: e.tensor_tensor(out=gt[:, 8:12], in0=tm[:, 4:8], in1=dtb[:], op=ALU.add),
                      ['tm', 'dtb', 'gt'], ['gt'])
                    A(lambda e: e.activation(out=gt[:, 8:12], in_=gt[:, 8:12], func=AF.Exp), ['gt'], ['gt'])
                    A(lambda e: e.activation(out=gt[:, 8:12], in_=gt[:, 8:12], func=AF.Ln, bias=1.0), ['gt'], ['gt'])
                    V(lambda e: e.tensor_tensor(out=gt[:, 4:8], in0=gt[:, 8:12], in1=negA[:], op=ALU.mult),
                      ['gt', 'negA'], ['gt'])
                    MM(pb[4][:, 0:4], M['tri'][:], gt[:, 4:8], True, True, [mk, 'gt'], PK(4))
                    A(lambda e: e.copy(out=gcc[:, 0:4], in_=pb[4][:, 0:4]), PK(4), ['gcc'] + PK(4))
                    MM(pb[4][:, 0:4], M['blk'][:], gt[:, 4:8], True, True, [mk, 'gt'], PK(4))
                    A(lambda e: e.copy(out=gcc[:, 4:8], in_=pb[4][:, 0:4]), PK(4), ['gcc'] + PK(4))
                    V(lambda e: e.tensor_tensor(out=G4[:], in0=M['tri'][:].unsqueeze(1).to_broadcast(f4),
                                                in1=gt[:, 4:8].unsqueeze(2).to_broadcast(f4), op=ALU.mult),
                      [mk, 'gt'], ['G4'])
                    MM(pb[5][:], ones_f[:], G4[:].rearrange("p h t -> p (h t)"), True, True, ['ones_f', 'G4'], PK(5))
                    A(lambda e: e.copy(out=gcr[:], in_=pbv(5, 4)), PK(5), ['gcr'] + PK(5))
                    V(lambda e: e.tensor_tensor(out=G4[:], in0=ident[:].unsqueeze(1).to_broadcast(f4),
                                                in1=gt[:, 0:4].unsqueeze(2).to_broadcast(f4), op=ALU.mult),
                      ['ident', 'gt', 'G4'], ['G4'])
                    MM(pb[6][:], ones_f[:], G4[:].rearrange("p h t -> p (h t)"), True, True, ['ones_f', 'G4'], PK(6))
                    A(lambda e: e.copy(out=br[:], in_=pbv(6, 4)), PK(6), ['br'] + PK(6))
                    V(lambda e: e.tensor_tensor(out=a1[:], in0=gcr[:], in1=gcc[:, 0:4].unsqueeze(2).to_broadcast(f4),
                                                op=ALU.subtract), ['gcr', 'gcc'], ['a1'])
                    V(lambda e: e.tensor_tensor(out=E1s[:], in0=a1[:], in1=M['nms'][:].unsqueeze(1).to_broadcast(f4),
                                                op=ALU.add), ['a1', mk], ['E1s'])
                    A(lambda e: e.activation(out=E1s[:], in_=E1s[:], func=AF.Exp), ['E1s'], ['E1s'])
                    V(lambda e: e.tensor_tensor(out=E1i[:], in0=a1[:], in1=M['nmi'][:].unsqueeze(1).to_broadcast(f4),
                                                op=ALU.add), ['a1', mk], ['E1i'])
                    A(lambda e: e.activation(out=E1i[:], in_=E1i[:], func=AF.Exp), ['E1i'], ['E1i'])
                    V(lambda e: e.tensor_tensor(out=a1[:], in0=gcc[:, 0:4].unsqueeze(2).to_broadcast(f4), in1=gcr[:],
                                                op=ALU.subtract), ['gcr', 'gcc', 'a1'], ['a1'])
                    V(lambda e: e.tensor_tensor(out=E2s[:], in0=a1[:], in1=M['nmsT'][:].unsqueeze(1).to_broadcast(f4),
                                                op=ALU.add), ['a1', mk], ['E2s'])
                    A(lambda e: e.activation(out=E2s[:], in_=E2s[:], func=AF.Exp), ['E2s'], ['E2s'])
                    V(lambda e: e.scalar_tensor_tensor(out=E1s[:], in0=E1s[:], scalar=-1.0, in1=br[:], op0=ALU.mult,
                                                       op1=ALU.mult), ['E1s', 'br'], ['E1s'])
                    V(lambda e: e.scalar_tensor_tensor(out=E2s[:], in0=E2s[:], scalar=-1.0,
                                                       in1=gt[:, 0:4].unsqueeze(2).to_broadcast(f4), op0=ALU.mult,
                                                       op1=ALU.mult), ['E2s', 'gt'], ['E2s'])
                    for h in range(4):
                        MM(pb[0][:, h * 128:(h + 1) * 128], qkn[:, 4 + h, :], qkn[:, 4 + h, :], True, True, ['qkn'],
                           [('pbq', 0, h)])
                        MM(pb[1][:, h * 128:(h + 1) * 128], qkn[:, 4 + h, :], qkn[:, h, :], True, True, ['qkn'],
                           [('pbq', 1, h)])
                    V(lambda e: e.tensor_tensor(out=Xa[0][:], in0=pbv(0, 4), in1=E1s[:], op=ALU.mult),
                      PK(0) + ['E1s'], [('Xa', 0)])
                    V(lambda e: e.tensor_tensor(out=XTa[0][:], in0=pbv(0, 4), in1=E2s[:], op=ALU.mult),
                      PK(0) + ['E2s'], [('XTa', 0)] + PK(0))
                    V(lambda e: e.scalar_tensor_tensor(out=itr[:], in0=pbv(1, 4), scalar=SCq, in1=E1i[:], op0=ALU.mult,
                                                       op1=ALU.mult), PK(1) + ['E1i'], ['itr'] + PK(1))
                    V(lambda e: e.tensor_tensor(out=Pm[:], in0=Xa[0][:], in1=ident[:].unsqueeze(1).to_broadcast(f4),
                                                op=ALU.add), [('Xa', 0), 'ident'], ['Pm'])
                    nlev = 2 if smp else 6
                    for lv in range(nlev):
                        a, b = lv % 2, (lv + 1) % 2
                        last = (lv == nlev - 1)
                        for h in range(4):
                            hs = slice(h * 128, (h + 1) * 128)
                            if not last:
                                MM(pb[2][:, hs], XTa[a][:, h, :], Xa[a][:, h, :], True, True, [('XTa', a), ('Xa', a)],
                                   [('pbq', 2, h)])
                            MM(pb[3][:, hs], Xa[a][:, h, :], XTa[a][:, h, :], True, True, [('XTa', a), ('Xa', a)],
                               [('pbq', 3, h)])
                        if not last:
                            A(lambda e, b=b: e.copy(out=Xa[b][:], in_=pbv(2, 4)), PK(2), [('Xa', b)] + PK(2))
                        A(lambda e, b=b: e.copy(out=XTa[b][:], in_=pbv(3, 4)), PK(3), [('XTa', b)] + PK(3))
                        for h in range(4):
                            hs = slice(h * 128, (h + 1) * 128)
                            MM(pb[4][:, hs], XTa[b][:, h, :], Pm[:, h, :], True, True, [('XTa', b), 'Pm'], [('pbq', 4, h)])
                        V(lambda e: e.tensor_tensor(out=Pm[:], in0=Pm[:], in1=pbv(4, 4), op=ALU.add),
                          ['Pm'] + PK(4), ['Pm'] + PK(4))
                    V(lambda e: e.tensor_tensor(out=vb[:], in0=vtm[:], in1=gt[:, 0:4].unsqueeze(2).to_broadcast(f4),
                                                op=ALU.mult), ['vtm', 'gt'], ['vb'])
                    A(lambda e: e.activation(out=gt[:, 8:12], in_=gcc[:, 0:4], func=AF.Exp), ['gcc', 'gt'], ['gt'])
                    V(lambda e: e.tensor_tensor(out=gt[:, 12:16], in0=gt[:, 8:12], in1=gt[:, 0:4], op=ALU.mult),
                      ['gt'], ['gt'])
                    V(lambda e: e.tensor_tensor(out=kbg[:], in0=ktm[:], in1=gt[:, 12:16].unsqueeze(2).to_broadcast(f4),
                                                op=ALU.mult), ['ktm', 'gt'], ['kbg'])
                    V(lambda e: e.tensor_tensor(out=gt[:, 8:12], in0=gcc[:, 4:8], in1=gcc[:, 0:4], op=ALU.subtract),
                      ['gcc', 'gt'], ['gt'])
                    A(lambda e: e.activation(out=gt[:, 8:12], in_=gt[:, 8:12], func=AF.Exp), ['gt'], ['gt'])
                    V(lambda e: e.tensor_tensor(out=kdec[:], in0=ktm[:], in1=gt[:, 8:12].unsqueeze(2).to_broadcast(f4),
                                                op=ALU.mult), ['ktm', 'gt'], ['kdec'])
                    for h in range(4):
                        hs = slice(h * 128, (h + 1) * 128)
                        MM(pb[5][:, hs], vb[:, h, :], Pm[:, h, :], True, True, ['vb', 'Pm'], [('pbq', 5, h)])
                        MM(pb[6][:, hs], kbg[:, h, :], Pm[:, h, :], True, True, ['kbg', 'Pm'], [('pbq', 6, h)])
                    A(lambda e: e.copy(out=uT[:], in_=pbv(5, 4)), PK(5), ['uT'] + PK(5))
                    A(lambda e: e.copy(out=wT[:], in_=pbv(6, 4)), PK(6), ['wT'] + PK(6))
                    A(lambda e: e.activation(out=a1[:], in_=gcr[:], func=AF.Exp), ['gcr', 'a1'], ['a1'])
                    V(lambda e: e.scalar_tensor_tensor(out=qg[:], in0=qkn[:, 0:4, :], scalar=SCq, in1=a1[:], op0=ALU.mult,
                                                       op1=ALU.mult), ['qkn', 'a1'], ['qg'])
                    if smp:
                        A(lambda e: e.activation(out=eglr[:], in_=gcr[:].rearrange("p h (s t) -> p h s t", t=8)[:, :, :, 7],
                                                 func=AF.Exp), ['gcr'], ['eglr'])
                        V(lambda e: e.memset(pb[7][:], 0.0), [], PK(7))
                        V(lambda e: e.memset(pb[3][:], 0.0), [], PK(3))
                        for s in range(16):
                            ss_ = slice(s * 8, (s + 1) * 8)
                            S.dma('sp', Sgs[:], gdn_s_in[s].rearrange("h k v -> k h v"), writes=['Sgs'], stream='s0')
                            for h in range(4):
                                MM(pb[7][:, h * 128 + s * 8:h * 128 + s * 8 + 8], Sgs[:, h, :], wT[:, h, ss_],
                                   False, False, ['Sgs', 'wT'], [('pbq', 7, h)])
                                MM(pb[3][:, h * 128 + s * 8:h * 128 + s * 8 + 8], Sgs[:, h, :], qg[:, h, ss_],
                                   False, False, ['Sgs', 'qg'], [('pbq', 3, h)], inc=(h == 3))
                    else:
                        A(lambda e: e.activation(out=eglr[:, :, 0], in_=gcr[:, :, 127], func=AF.Exp), ['gcr'], ['eglr'])
                        V(lambda e: e.memset(pb[3][:], 0.0), [], PK(3))
                        for h in range(4):
                            hs = slice(h * 128, (h + 1) * 128)
                            MM(pb[7][:, hs], Sg[:, h, :], wT[:, h, :], True, True, ['Sg', 'wT'], [('pbq', 7, h)])
                            MM(pb[3][:, hs], Sg[:, h, :], qg[:, h, :], False, False, ['Sg', 'qg'], [('pbq', 3, h)])
                    V(lambda e: e.tensor_tensor(out=vnT[:], in0=uT[:], in1=pbv(7, 4), op=ALU.subtract),
                      ['uT'] + PK(7), ['vnT'] + PK(7))
                    for h in range(4):
                        TR(pb[2][:, h * 128:(h + 1) * 128], vnT[:, h, :], ident[:], ['vnT', 'ident'], [('pbq', 2, h)])
                    A(lambda e: e.copy(out=vn[:], in_=pbv(2, 4)), PK(2), ['vn'] + PK(2))
                    for h in range(4):
                        hs = slice(h * 128, (h + 1) * 128)
                        MM(pb[3][:, hs], vn[:, h, :], itr[:, h, :], False, True, ['vn', 'itr'], [('pbq', 3, h)])
                    if smp:
                        for s in range(16):
                            V(lambda e, s=s: e.tensor_scalar(out=kdm[:], in0=kdec[:], scalar1=seqmask[:, s:s + 1],
                                                            scalar2=None, op0=ALU.mult), ['kdec', 'seqmask'], ['kdm'])
                            for h in range(4):
                                MM(pb[4][:, h * 128:(h + 1) * 128], kdm[:, h, :], vn[:, h, :], True, True, ['kdm', 'vn'],
                                   [('pbq', 4, h)])
                            S.dma('sp', Sgs[:], gdn_s_in[s].rearrange("h k v -> k h v"), writes=['Sgs'], stream='s0')
                            V(lambda e, s=s: e.tensor_tensor(out=Sgs[:], in0=Sgs[:],
                                                             in1=eglr[:, :, s:s + 1].to_broadcast(f4), op=ALU.mult),
                              ['Sgs', 'eglr'], ['Sgs'])
                            V(lambda e: e.tensor_tensor(out=Sgs[:], in0=Sgs[:], in1=pbv(4, 4), op=ALU.add),
                              ['Sgs'] + PK(4), ['Sgs'] + PK(4))
                            S.dma('sp', gdn_s_o[s].rearrange("h k v -> k h v"), Sgs[:], reads=['Sgs'], stream='o')
                    else:
                        for h in range(4):
                            MM(pb[4][:, h * 128:(h + 1) * 128], kdec[:, h, :], vn[:, h, :], True, True, ['kdec', 'vn'],
                               [('pbq', 4, h)])
                        V(lambda e: e.tensor_tensor(out=Sg[:], in0=Sg[:], in1=eglr[:, :, 0:1].to_broadcast(f4),
                                                    op=ALU.mult), ['Sg', 'eglr'], ['Sg'])
                        V(lambda e: e.tensor_tensor(out=Sg[:], in0=Sg[:], in1=pbv(4, 4), op=ALU.add),
                          ['Sg'] + PK(4), ['Sg'] + PK(4))
                        if t == NT - 1:
                            S.dma('sp', gdn_p_o.rearrange("h k v -> k h v"), Sg[:], reads=['Sg'], stream='o')
                    A(lambda e: e.activation(out=osq[:], in_=pbv(3, 4), func=AF.Square), PK(3), ['osq'])
                    MM(pb[0][:], ones_bf[:], osq[:].rearrange("p h t -> p (h t)"), True, True, ['osq', 'ones_bf'], PK(0))
                    A(lambda e: e.activation(out=rno[:], in_=pb[0][:], func=AF.Sqrt, bias=EPS, scale=1.0 / 128),
                      PK(0), ['rno'] + PK(0))
                    V(lambda e: e.reciprocal(out=rno[:], in_=rno[:]), ['rno'], ['rno'])
                    V(lambda e: e.scalar_tensor_tensor(out=ona[:], in0=pbv(3, 4), scalar=gdnn[:, 0:1],
                                                       in1=rno[:].rearrange("p (h t) -> p h t", h=4), op0=ALU.mult,
                                                       op1=ALU.mult), PK(3) + ['gdnn', 'rno'], ['ona'] + PK(3))
                    G(lambda e: e.tensor_tensor(out=onb[:, 0:4, :], in0=ona[:], in1=zs[:], op=ALU.mult),
                      ['ona', 'zs'], [('onb', 0)])
                    if STG <= 2:
                        return
                    A(lambda e: e.activation(out=rt[:, 0:4, :].rearrange("p a b -> p (a b)"), in_=tm[:, 8:264],
                                             func=AF.Square, accum_out=ssn[:, 0:1]), ['tm'], ['rt', 'ssn'])
                    A(lambda e: e.activation(out=rt[:, 4:8, :].rearrange("p a b -> p (a b)"), in_=tm[:, 264:520],
                                             func=AF.Square, accum_out=ssn[:, 1:2]), ['tm'], ['rt', 'ssn'])
                    A(lambda e: e.activation(out=ssn[:, 2:4], in_=ssn[:, 0:2], func=AF.Sqrt, bias=EPS, scale=1.0 / 256),
                      ['ssn'], ['ssn'])
                    V(lambda e: e.reciprocal(out=ssn[:, 2:4], in_=ssn[:, 2:4]), ['ssn'], ['ssn'])
                    V(lambda e: e.scalar_tensor_tensor(out=cqn[:], in0=tm[:, 8:264], scalar=ssn[:, 2:3], in1=qnb[:],
                                                       op0=ALU.mult, op1=ALU.mult), ['tm', 'ssn', 'qnb'], ['cqn'])
                    V(lambda e: e.scalar_tensor_tensor(out=ckvn[:], in0=tm[:, 264:520], scalar=ssn[:, 3:4], in1=kvnb[:],
                                                       op0=ALU.mult, op1=ALU.mult), ['tm', 'ssn', 'kvnb'], ['ckvn'])
                    lat_dst = lat_s_o if smp else lat_p_o[t * 128:(t + 1) * 128, :]
                    kpe_dst = kpe_s_o if smp else kpe_p_o[t * 128:(t + 1) * 128, :]
                    S.dma('sp', lat_dst, ckvn[:], reads=['ckvn'], stream='o')
                    cs = cosb[:, t, :]
                    sn = sinb[:, t, :]
                    x1 = tm[:, 520:536]
                    x2 = tm[:, 536:552]
                    V(lambda e: e.tensor_tensor(out=kst[:, 0:16], in0=x1, in1=cs, op=ALU.mult), ['tm', 'cosb', 'kst'], ['kst'])
                    V(lambda e: e.tensor_tensor(out=rt[:, 0, 0:16], in0=x2, in1=sn, op=ALU.mult), ['tm', 'sinb', 'rt'], ['rt'])
                    V(lambda e: e.tensor_tensor(out=kst[:, 0:16], in0=kst[:, 0:16], in1=rt[:, 0, 0:16], op=ALU.subtract),
                      ['kst', 'rt'], ['kst'])
                    V(lambda e: e.tensor_tensor(out=kst[:, 16:32], in0=x2, in1=cs, op=ALU.mult), ['tm', 'cosb', 'kst'], ['kst'])
                    V(lambda e: e.tensor_tensor(out=rt[:, 0, 16:32], in0=x1, in1=sn, op=ALU.mult), ['tm', 'sinb', 'rt'], ['rt'])
                    V(lambda e: e.tensor_tensor(out=kst[:, 16:32], in0=kst[:, 16:32], in1=rt[:, 0, 16:32], op=ALU.add),
                      ['kst', 'rt'], ['kst'])
                    S.dma('sp', kpe_dst, kst[:, 0:32], reads=['kst'], stream='o')
                    if STG <= 3:
                        return
                    for c in range(2):
                        TR(pb[6][:, c * 128:(c + 1) * 128], cqn[:, c * 128:(c + 1) * 128], ident[:], ['cqn', 'ident'],
                           [('pbq', 6, c)])
                        TR(pb[6][:, 256 + c * 128:256 + (c + 1) * 128], ckvn[:, c * 128:(c + 1) * 128], ident[:],
                           ['ckvn', 'ident'], [('pbq', 6, 2 + c)])
                    TR(pb[7][:, 0:128], kst[:], ident[:], ['kst', 'ident'], [('pbq', 7, 0)])
                    A(lambda e: e.copy(out=cqnT[:], in_=pb[6][:, 0:256].rearrange("p (c t) -> p c t", c=2)),
                      PK(6), ['cqnT'])
                    A(lambda e: e.copy(out=ckvnT[:], in_=pb[6][:, 256:512].rearrange("p (c t) -> p c t", c=2)),
                      PK(6), ['ckvnT'] + PK(6))
                    kcols = tile_cols(t)
                    kkey = ('KT', t)
                    if not smp:
                        A(lambda e: e.copy(out=H['KTp'][:, kcols], in_=pb[7][0:32, 0:128]), PK(7), [kkey] + PK(7))
                    for (bi, c0, cn) in ((0, 0, 512), (1, 512, 256)):
                        for c in range(2):
                            MM(pb[bi][:, :cn], cqnT[:, c, :], wuq[:, c, c0:c0 + cn], c == 0, c == 1, ['cqnT', 'wuq'], PK(bi))
                        A(lambda e, bi=bi, c0=c0, cn=cn: e.copy(out=qtm[:].rearrange("p h d -> p (h d)")[:, c0:c0 + cn],
                                                                in_=pb[bi][:, :cn]), PK(bi), ['qtm'] + PK(bi))
                    csb = cs.unsqueeze(1).to_broadcast([128, 8, 16])
                    snb = sn.unsqueeze(1).to_broadcast([128, 8, 16])
                    q1 = qtm[:, :, 64:80]
                    q2 = qtm[:, :, 80:96]
                    V(lambda e: e.tensor_copy(out=qn2[:].rearrange("p m (a d) -> p (m a) d", a=2), in_=qtm[:, :, 0:64]),
                      ['qtm'], ['qtb'])
                    V(lambda e: e.tensor_tensor(out=rt[:, :, 0:16], in0=q1, in1=csb, op=ALU.mult), ['qtm', 'cosb', 'rt'], ['rt'])
                    V(lambda e: e.tensor_tensor(out=rt[:, :, 16:32], in0=q2, in1=snb, op=ALU.mult), ['qtm', 'sinb', 'rt'], ['rt'])
                    V(lambda e: e.tensor_tensor(out=qp2[:, :, 0:16], in0=rt[:, :, 0:16], in1=rt[:, :, 16:32],
                                                op=ALU.subtract), ['rt', 'qtb'], ['qtb'])
                    V(lambda e: e.tensor_tensor(out=rt[:, :, 32:48], in0=q2, in1=csb, op=ALU.mult), ['qtm', 'cosb', 'rt'], ['rt'])
                    V(lambda e: e.tensor_tensor(out=rt[:, :, 48:64], in0=q1, in1=snb, op=ALU.mult), ['qtm', 'sinb', 'rt'], ['rt'])
                    V(lambda e: e.tensor_tensor(out=qp2[:, :, 16:32], in0=rt[:, :, 32:48], in1=rt[:, :, 48:64],
                                                op=ALU.add), ['rt', 'qtb'], ['qtb'])
                    for m in range(4):
                        TR(pbbf(2)[:, m * 128:(m + 1) * 128], qn2[:, m, :], ident_bf[:], ['qtb', 'ident_bf'],
                           [('pbq', 2, m // 2)])
                    for h in range(8):
                        TR(pbbf(1)[0:32, h * 128:(h + 1) * 128], qp2[:, h, :], ident_bf[:], ['qtb', 'ident_bf'],
                           [('pbq', 1, h // 2)])
                    A(lambda e: e.copy(out=QTn[:].rearrange("p m t -> p (m t)"), in_=pbbf(2)[:, 0:512]), PK(2),
                      ['QT'] + PK(2))
                    A(lambda e: e.copy(out=QTp[:].rearrange("p h t -> p (h t)"), in_=pbbf(1)[0:32, :]), PK(1),
                      ['QT'] + PK(1))
                    if not smp:
                        for m in range(4):
                            for c in range(2):
                                MM(pb[3][:, m * 128:(m + 1) * 128], wuk[:, c, m * 128:(m + 1) * 128], ckvnT[:, c, :],
                                   c == 0, c == 1, ['wuk', 'ckvnT'], [('pbq', 3, m)])
                        A(lambda e: e.copy(out=H['KTn'][:, :, kcols], in_=pbv(3, 4)), PK(3), [kkey] + PK(3))
                        for c in range(2):
                            MM(pb[5][:], ckvnT[:, c, :], wuv[:, c, :], c == 0, c == 1, ['ckvnT', 'wuv'], PK(5))
                        A(lambda e: e.copy(out=H['Vt'][:, t, :], in_=pb[5][:]), PK(5), [('Vt', t)] + PK(5))
                    if STG <= 4:
                        return
                    if not smp:
                        for h in range(8):
                            ob = h // 4
                            hs = slice((h % 4) * 128, (h % 4 + 1) * 128)
                            pr = (h // 2) * 128
                            rh = (h % 2) * 64
                            nkt = t + 1
                            for g0 in range(0, nkt, 4):
                                gn = min(4, nkt - g0)
                                sbk = 6 + (g0 // 4) % 2
                                for j in range(gn):
                                    kt = g0 + j
                                    MM(pb[sbk][:, j * 128:(j + 1) * 128], H['KTn'][rh:rh + 64, h // 2, kt * 128:(kt + 1) * 128],
                                       QTn[rh:rh + 64, h // 2, :], True, False, [('KT', kt), 'QT'], [('pbq', sbk, j)])
                                    MM(pb[sbk][:, j * 128:(j + 1) * 128], H['KTp'][:, kt * 128:(kt + 1) * 128],
                                       QTp[:, h, :], False, True, [('KT', kt), 'QT'], [('pbq', sbk, j)])
                                pi = (g0 // 4) % 2
                                A(lambda e, sbk=sbk, gn=gn, pi=pi: e.activation(out=H['PT'][pi][:, :gn * 128],
                                                                                in_=pb[sbk][:, :gn * 128], func=AF.Exp,
                                                                                scale=SCm),
                                  PK(sbk), [('PT', pi)] + PK(sbk))
                                if g0 + gn == nkt:
                                    jd = gn - 1
                                    G(lambda e, pi=pi, jd=jd: e.tensor_tensor(out=H['PT'][pi][:, jd * 128:(jd + 1) * 128],
                                                                              in0=H['PT'][pi][:, jd * 128:(jd + 1) * 128],
                                                                              in1=H['M']['cmask'][:], op=ALU.mult),
                                      [('PT', pi), 'maskP'], [('PT', pi)])
                                for j in range(gn):
                                    kt = g0 + j
                                    first = (kt == 0)
                                    lastk = (kt == nkt - 1)
                                    MM(pb[ob][:, hs], H['Vt'][:, kt, pr:pr + 128], H['PT'][pi][:, j * 128:(j + 1) * 128], first, lastk,
                                       [('Vt', kt), ('PT', pi)], [('pbq', ob, h % 4)])
                                    MM(pb[2 + ob][:, hs], ones_bf[:], H['PT'][pi][:, j * 128:(j + 1) * 128], first, lastk,
                                       [('PT', pi), 'ones_bf'], [('pbq', 2 + ob, h % 4)])
                            V(lambda e, ob=ob, hs=hs, rh=rh: e.reciprocal(out=H['rsum'][rh:rh + 64, :],
                                                                          in_=pb[2 + ob][rh:rh + 64, hs]),
                              [('pbq', 2 + ob, h % 4)], ['rsum', ('pbq', 2 + ob, h % 4)])
                            V(lambda e, ob=ob, hs=hs, rh=rh, h=h: e.tensor_tensor(out=onb[rh:rh + 64, 4 + h // 2, :],
                                                                                  in0=pb[ob][rh:rh + 64, hs],
                                                                                  in1=H['rsum'][rh:rh + 64, :], op=ALU.mult),
                              [('pbq', ob, h % 4), 'rsum'], [('onb', 1), ('pbq', ob, h % 4)])
                    else:
                        sample_attention()
                    if STG <= 5:
                        return
                    for dc in range(KC):
                        bi = 4 + dc // 4
                        for kc in range(KC):
                            MM(pb[bi][:, (dc % 4) * 128:(dc % 4 + 1) * 128], woa[:, kc, dc * 128:(dc + 1) * 128],
                               onb[:, kc, :], kc == 0, kc == KC - 1, ['woa', ('onb', 0), ('onb', 1)],
                               [('pbq', bi, dc % 4)])
                    for b2 in range(2):
                        V(lambda e, b2=b2: e.tensor_tensor(out=xTt[:, b2 * 4:b2 * 4 + 4, :], in0=xTt[:, b2 * 4:b2 * 4 + 4, :],
                                                           in1=pbv(4 + b2, 4), op=ALU.add),
                          PK(4 + b2) + ['xTt'], ['xTt'] + PK(4 + b2))

                with ExitStack() as sph:
                    ssb = mk_sb(sph)
                    H['ssb'] = ssb
                    H['M'] = build_masks('S', True, ssb)
                    H['cbufS'] = ssb("cbufS", [128, 12, 16, 11])
                    H['ssb2'] = ssb
                    tile_body(NT)
                    S.dma('sp', xsp[NT], xTt[:], reads=['xTt'], stream='xs')
                    S.barrier()
                with ExitStack() as pph:
                    qsb = mk_sb(pph)
                    H['M'] = build_masks('P', False, qsb)
                    H['cbuf'] = qsb("cbuf", [128, 12, 131])
                    H['KTn'] = qsb("KTn", [128, 4, TP], BF16)
                    H['KTp'] = qsb("KTp", [32, TP], BF16)
                    H['Vt'] = qsb("Vt", [128, NT, 512], BF16)
                    H['PT'] = [qsb("PT%d" % i, [128, 512], BF16) for i in range(2)]
                    H['rsum'] = qsb("rsum", [128, 128])
                    G(lambda e: e.memset(H['cbuf'][:], 0.0), [], ['cbuf'])
                    for t in range(NT):
                        tile_body(t)
                        S.dma('sp', xsp[t], xTt[:], reads=['xTt'], stream='xs')
                    S.barrier()

        def fold_x():
            for t in range(NT + 1):
                ks = [S.lastw.get(('xTd', dc, t)) for dc in range(KC)]
                ks = [k for k in ks if k is not None]
                if ks:
                    S.lastw[('xT', t)] = max(ks, key=lambda tk: tk[1])
                    S.reads[('xT', t)] = {}

        if 'A' in PHASES:
            ab_layer()
            xTh[0] = sb("xT", [128, KC, TT], F32)
            for t in range(NT + 1):
                S.dma('sp', xTh[0][:, :, tile_cols(t)], xsp[t], writes=[('xT', t)], stream='x')
        else:
            xTh[0] = sb("xT", [128, KC, TT], F32)
            phase0()
        xT = xTh[0]

        for ph_ in PHASES:
            if ph_ == 'B':
                mlp_layer(0)
                fold_x()
            elif ph_ == 'D':
                mlp_layer(1)
                fold_x()
            elif ph_ == 'C':
                hgrn_layer()

        gfin = sb("gfin", [128, D], F32)
        S.dma('sp', gfin[:], final_norm.partition_broadcast(128), writes=['gfin'], stream='c')
        yo = [sb("yo%d" % i, [128, D], F32) for i in range(2)]
        ss2 = [sb("ss2_%d" % i, [128, 4], F32) for i in range(2)]
        junk = sb("junk", [128, 512], F32)
        for t in range(NT + 1):
            b = t % 2
            for half in range(2):
                pi = 6 + half
                for j in range(4):
                    kc = half * 4 + j
                    TR(pb[pi][:, j * 128:(j + 1) * 128], xT[:, kc, tile_cols(t)], ident[:],
                       [('xT', t), 'ident'], [('pbq', pi, j)])
                A(lambda e, half=half, pi=pi: e.activation(out=junk[:], in_=pb[pi][:], func=AF.Square,
                                                           accum_out=ss2[b][:, half:half + 1]),
                  [('pbq', pi, j) for j in range(4)], ['junk', ('ss2', b, half)])
            V(lambda e: e.tensor_tensor(out=ss2[b][:, 2:3], in0=ss2[b][:, 0:1], in1=ss2[b][:, 1:2], op=ALU.add),
              [('ss2', b, 0), ('ss2', b, 1)], [('ss2', b, 2)])
            A(lambda e: e.activation(out=ss2[b][:, 3:4], in_=ss2[b][:, 2:3], func=AF.Sqrt, bias=EPS,
                                     scale=1.0 / D), [('ss2', b, 2)], [('ss2', b, 3)])
            V(lambda e: e.reciprocal(out=ss2[b][:, 3:4], in_=ss2[b][:, 3:4]), [('ss2', b, 3)], [('ss2', b, 3)])
            for half in range(2):
                pi = 6 + half
                V(lambda e, half=half, pi=pi: e.scalar_tensor_tensor(
                    out=yo[b][:, half * 512:(half + 1) * 512], in0=pb[pi][:], scalar=ss2[b][:, 3:4],
                    in1=gfin[:, half * 512:(half + 1) * 512], op0=ALU.mult, op1=ALU.mult),
                    [('pbq', pi, j) for j in range(4)] + [('ss2', b, 3), 'gfin'],
                    [('yo', b, half)] + [('pbq', pi, j) for j in range(4)])
            dst = y_s if t == NT else y_p[t * 128:(t + 1) * 128, :]
            S.dma('sp', dst, yo[b][:], reads=[('yo', b, 0), ('yo', b, 1)], stream='y')
        S.finish('sp')
        dl = S.check_deadlock()
        print("instructions:", S.n_inst, "deadlock:", dl)
        assert not dl, "semaphore deadlock in the emitted program"
    return nc


def core_inputs(cfg, c, inp):
    m = {}
    m["x_p"] = np.ascontiguousarray(inp['x_prompt'][c])
    m["x_s"] = np.ascontiguousarray(inp['x_sample'][16 * c:16 * c + 16].reshape(128, D))
    m["w_up"] = inp['w_up']
    m["w_down"] = inp['w_down']
    m["mlp_norm_fm"] = np.ascontiguousarray(inp['mlp_norm'].reshape(2, KC, 128).transpose(0, 2, 1))
    m["mix_norm_fm"] = np.ascontiguousarray(inp['mix_norm'].reshape(2, KC, 128).transpose(0, 2, 1))
    m["final_norm"] = np.ascontiguousarray(inp['final_norm'].reshape(1, D))
    m["w_in_c"] = inp['w_in_c'][0]
    m["w_out_c"] = inp['w_out_c'][0]
    m["lb_fm"] = np.ascontiguousarray(inp['lb_logits_c'].reshape(2, 8, 128).transpose(0, 2, 1))
    m["g_norm_c"] = np.ascontiguousarray(inp['g_norm_c'][0].reshape(128, 1))
    m["hgrn_in"] = np.ascontiguousarray(inp['state_hgrn'][0, 16 * c:16 * c + 16])
    m["w_in_ab"] = inp['w_in_ab'][0]
    m["w_out_ab"] = inp['w_out_ab'][0]
    m["w_uq"] = np.ascontiguousarray(inp['w_uq_ab'][0].reshape(256, 768))
    m["w_uk"] = np.ascontiguousarray(inp['w_uk_ab'][0].reshape(256, 512))
    m["w_uv"] = np.ascontiguousarray(inp['w_uv_ab'][0].reshape(256, 512))
    m["conv_w_fm"] = np.ascontiguousarray(inp['conv_w_ab'][0].reshape(4, 12, 128).transpose(2, 1, 0))
    m["a_log"] = np.ascontiguousarray(inp['a_log_ab'].reshape(1, 4))
    m["dt_bias"] = np.ascontiguousarray(inp['dt_bias_ab'].reshape(1, 4))
    m["gdn_norm"] = np.ascontiguousarray(inp['gdn_norm_ab'].reshape(128, 1))
    m["q_norm"] = np.ascontiguousarray(inp['q_norm_ab'].reshape(1, 256))
    m["kv_norm"] = np.ascontiguousarray(inp['kv_norm_ab'].reshape(1, 256))
    TP = cfg['NT'] * 128
    past = inp['page_table'].shape[1] * inp['cache_mla_latent'].shape[2]
    pos = np.concatenate([np.arange(TP), np.tile(past + np.arange(8), 16)]).astype(np.float32)
    inv = (np.float32(10000.0) ** (-np.arange(16, dtype=np.float32) / np.float32(16))).astype(np.float32)
    ang = (pos[:, None] * inv[None, :]).astype(np.float32)
    m["rope_cos"] = np.cos(ang).astype(np.float32)
    m["rope_sin"] = np.sin(ang).astype(np.float32)
    NPG = cfg['NPG']
    R = NPG
    SB = 128 // R
    RG = min(8, R)
    NG = R // RG
    m["cache_lat"] = inp['cache_mla_latent'][0]
    m["cache_kr"] = inp['cache_mla_krope'][0]
    m["page_table"] = np.ascontiguousarray(inp['page_table'][16 * c:16 * c + 16]).astype(np.int32)
    wkt = np.zeros((2, 64, 8, 256), np.float32)
    wu = inp['w_uk_ab'][0]
    for h in range(8):
        wkt[h % 2, :, h, :] = wu[:, h, :].T
    m["w_ukT"] = wkt.reshape(128, 2048)
    p = np.arange(128)
    m["rep_in"] = (p[None, :] // SB == np.arange(NPG)[:, None]).astype(np.float32)
    gbase = np.zeros((128, NG + 1), np.float32)
    for g in range(NG):
        gbase[:, g] = (p % SB) * (R // RG) + g
    gbase[:, NG] = p % SB
    m["gbase_in"] = gbase
    tq = np.arange(64) % 8
    m["cm8_in"] = ((p % 8)[:, None] <= tq[None, :]).astype(np.float32)
    m["conv_s_in"] = np.ascontiguousarray(inp['state_gdn_conv'][0, 16 * c:16 * c + 16].reshape(48, 1536))
    m["gdn_s_in"] = np.ascontiguousarray(inp['state_gdn'][0, 16 * c:16 * c + 16])
    return m


def kernel(**inputs):
    inp = {k: np.asarray(v) for k, v in inputs.items()}
    B, SEQ = inp['x_prompt'].shape[:2]
    cfg = {'NT': SEQ // 128, 'NPG': inp['page_table'].shape[1], 'NPOOL': inp['cache_mla_latent'].shape[1]}
    nc = build(cfg)
    in_maps = [core_inputs(cfg, c, inp) for c in range(NCORES)]
    res = run_bass_kernel_spmd(nc, in_maps, core_ids=list(range(NCORES)))
    r = res.results

    def stk(name):
        return np.stack([r[c][name] for c in range(NCORES)], 0)

    def cat(name):
        return np.concatenate([r[c][name] for c in range(NCORES)], 0)
    y_p = stk("y_p")
    y_s = cat("y_s").reshape(16 * NCORES, 8, D)
    return (y_p, y_s,
            stk("gdn_p")[None], stk("conv_p")[None], stk("lat_p")[None], stk("kpe_p")[None], stk("hgrn_p")[None],
            cat("gdn_s")[None], cat("conv_s")[None], cat("lat_s").reshape(1, 16 * NCORES, 8, 256),
            cat("kpe_s").reshape(1, 16 * NCORES, 8, 32), cat("hgrn_s")[None])
```

```python
from contextlib import ExitStack

import numpy as np
import concourse.bass as bass
import concourse.mybir as mybir
from concourse.bass_utils import run_bass_kernel_spmd

F32 = mybir.dt.float32
BF16 = mybir.dt.bfloat16
I32 = mybir.dt.int32
AF = mybir.ActivationFunctionType
ALU = mybir.AluOpType
AX = mybir.AxisListType

D = 1024
DFF = 4096
KC = 8
EPS = 1e-6
NCORES = 8


class Sched:
    def __init__(self, nc, stack, ndma=4):
        self.nc = nc
        self.stack = stack
        self.eng = {'pe': nc.tensor, 'act': nc.scalar, 'dve': nc.vector,
                    'pool': nc.gpsimd, 'sp': nc.sync}
        self.sem = {}
        self.cnt = {}
        self.seen = {e: {} for e in self.eng}
        self.lastw = {}
        self.reads = {}
        self.ndma = ndma
        self.dma_rr = {}
        self.n_inst = 0
        self.alias_map = {}
        self.log = {e: [] for e in self.eng}
        for e in ('pe', 'act', 'dve', 'pool'):
            self._mk(e)

    def _mk(self, q):
        self.sem[q] = self.stack.enter_context(self.nc.semaphore("s_" + q))
        self.cnt[q] = 0

    def _wait(self, e, q, c):
        if self.seen[e].get(q, 0) >= c:
            return
        self.seen[e][q] = c
        self.log[e].append(('w', q, c))
        self.eng[e].wait_ge(self.sem[q], c)

    def alias(self, a, canon):
        self.alias_map[a] = canon

    def _canon(self, keys):
        return [self.alias_map.get(k, k) for k in keys]

    def _deps(self, e, myq, reads, writes, same_war=False, skip_waw_q=None):
        reads = self._canon(reads)
        writes = self._canon(writes)
        need = {}

        def add(t):
            if t is None:
                return
            q, c = t
            if need.get(q, 0) < c:
                need[q] = c
        for k in reads:
            add(self.lastw.get(k))
        for k in writes:
            t = self.lastw.get(k)
            if not (t is not None and skip_waw_q is not None and t[0] == skip_waw_q):
                add(t)
            for q, c in self.reads.get(k, {}).items():
                if q == myq and not same_war:
                    continue
                add((q, c))
        for q, c in need.items():
            self._wait(e, q, c)

    def _commit(self, ticket, reads, writes):
        reads = self._canon(reads)
        writes = self._canon(writes)
        q, c = ticket
        for k in reads:
            d = self.reads.setdefault(k, {})
            if d.get(q, 0) < c:
                d[q] = c
        for k in writes:
            self.lastw[k] = ticket
            self.reads[k] = {}

    def op(self, e, fn, reads=(), writes=(), inc=True, pe_acc=False):
        self._deps(e, e, reads, writes, skip_waw_q=('pe' if pe_acc else None))
        ins = fn(self.eng[e])
        self.n_inst += 1
        ticket = (e, self.cnt[e] + 1)
        if inc:
            ins.then_inc(self.sem[e], 1)
            self.cnt[e] += 1
            self.log[e].append(('i', e, 1))
        self._commit(ticket, reads, writes)
        return ins

    def dma(self, e, out, in_, reads=(), writes=(), stream='d', nrr=None, **kw):
        return self.dma_custom(e, lambda eng: eng.dma_start(out=out, in_=in_, **kw),
                               reads, writes, stream, nrr)

    def dma_custom(self, e, fn, reads=(), writes=(), stream='g', nrr=None):
        i = self.dma_rr.get(stream, 0)
        self.dma_rr[stream] = i + 1
        q = "dma_%s_%d" % (stream, i % (nrr or self.ndma))
        if q not in self.sem:
            self._mk(q)
        self._deps(e, q, reads, writes, same_war=True)
        ins = fn(self.eng[e])
        ins.then_inc(self.sem[q], 16)
        self.n_inst += 1
        self.cnt[q] += 16
        self.log[e].append(('i', q, 16))
        self._commit((q, self.cnt[q]), reads, writes)
        return ins

    def barrier(self):
        for e in self.eng:
            for q, c in self.cnt.items():
                if c > 0:
                    self._wait(e, q, c)

    def check_deadlock(self):
        pos = {e: 0 for e in self.log}
        val = {}
        progress = True
        while progress:
            progress = False
            for e, lg in self.log.items():
                while pos[e] < len(lg):
                    k, q, c = lg[pos[e]]
                    if k == 'w':
                        if val.get(q, 0) >= c:
                            pos[e] += 1
                            progress = True
                        else:
                            break
                    else:
                        val[q] = val.get(q, 0) + c
                        pos[e] += 1
                        progress = True
        stuck = {e: (pos[e], lg[pos[e]], val.get(lg[pos[e]][1], 0)) for e, lg in self.log.items()
                 if pos[e] < len(lg)}
        return stuck

    def finish(self, e='sp'):
        for q, c in self.cnt.items():
            if c > 0:
                self._wait(e, q, c)


def blocks(total, step):
    return [(s, min(step, total - s)) for s in range(0, total, step)]


def build(cfg):
    NT = cfg['NT']
    TP = 128 * NT
    TT = TP + 128
    PHASES = cfg.get('phases', 'ABCD')
    nc = bass.Bass("TRN2", target_bir_lowering=False)

    def din(name, shape, dt=F32):
        return nc.dram_tensor(name, list(shape), dt, kind="ExternalInput").ap()

    def dout(name, shape):
        return nc.dram_tensor(name, list(shape), F32, kind="ExternalOutput").ap()

    x_p = din("x_p", [TP, D])
    x_s = din("x_s", [128, D])
    w_up = din("w_up", [2, D, DFF])
    w_down = din("w_down", [2, DFF, D])
    mlp_norm_fm = din("mlp_norm_fm", [2, 128, KC])
    mix_norm_fm = din("mix_norm_fm", [2, 128, KC])
    final_norm = din("final_norm", [1, D])
    w_in_c = din("w_in_c", [D, 4096])
    w_out_c = din("w_out_c", [D, D])
    lb_fm = din("lb_fm", [2, 128, 8])
    g_norm_c = din("g_norm_c", [128, 1])
    hgrn_in = din("hgrn_in", [16, 8, 128, 128])
    w_in_ab = din("w_in_ab", [D, 2600])
    w_out_ab = din("w_out_ab", [D, D])
    w_uq = din("w_uq", [256, 768])
    w_uk = din("w_uk", [256, 512])
    w_uv = din("w_uv", [256, 512])
    conv_w_fm = din("conv_w_fm", [128, 12, 4])
    a_log = din("a_log", [1, 4])
    dt_bias = din("dt_bias", [1, 4])
    gdn_norm = din("gdn_norm", [128, 1])
    q_norm = din("q_norm", [1, 256])
    kv_norm = din("kv_norm", [1, 256])
    rope_cos = din("rope_cos", [TT, 16])
    rope_sin = din("rope_sin", [TT, 16])
    conv_s_in = din("conv_s_in", [48, 1536])
    gdn_s_in = din("gdn_s_in", [16, 4, 128, 128])
    NPG = cfg['NPG']
    NPOOL = cfg['NPOOL']
    NGRP = NPG // min(8, NPG)
    cache_lat = din("cache_lat", [NPOOL, 128, 256])
    cache_kr = din("cache_kr", [NPOOL, 128, 32])
    page_table = din("page_table", [16, NPG], I32)
    w_ukT = din("w_ukT", [128, 2048])
    rep_in = din("rep_in", [NPG, 128])
    gbase_in = din("gbase_in", [128, NGRP + 1])
    cm8_in = din("cm8_in", [128, 64])
    NPG = cfg['NPG']
    if cfg.get('dbg_s'):
        dbg_mst = dout("dbg_mst", [128, 8])
        dbg_oacc = dout("dbg_oacc", [64, 257])
        dbg_lb = nc.dram_tensor("dbg_lb", [128, min(8, NPG) * 256], BF16, kind="ExternalOutput").ap()
        dbg_kb = nc.dram_tensor("dbg_kb", [128, NPG * 32], BF16, kind="ExternalOutput").ap()
        dbg_olat = dout("dbg_olat", [64, 256])
    gdn_p_o = dout("gdn_p", [4, 128, 128])
    conv_p_o = dout("conv_p", [3, 1536])
    lat_p_o = dout("lat_p", [TP, 256])
    kpe_p_o = dout("kpe_p", [TP, 32])
    gdn_s_o = dout("gdn_s", [16, 4, 128, 128])
    conv_s_o = dout("conv_s", [16, 3, 1536])
    lat_s_o = dout("lat_s", [128, 256])
    kpe_s_o = dout("kpe_s", [128, 32])
    xsp = nc.dram_tensor("xsp", [NT + 1, 128, KC, 128], F32, kind="Internal").ap()
    y_p = dout("y_p", [TP, D])
    y_s = dout("y_s", [128, D])
    hgrn_p = dout("hgrn_p", [8, 128, 128])
    hgrn_s = dout("hgrn_s", [16, 8, 128, 128])

    with ExitStack() as st:
        S = Sched(nc, st, ndma=4)

        uniq = [0]

        def mk_sb(stack):
            uniq[0] += 1
            pfx = "p%d_" % uniq[0]

            def sb(name, shape, dt=F32):
                nb = int(np.prod(shape[1:])) * (2 if dt == BF16 else 4)
                if cfg.get('memlog'):
                    print("  alloc %-10s %7d B" % (pfx + name, nb))
                return stack.enter_context(nc.sbuf_tensor(pfx + name, list(shape), dt))
            return sb
        sb = mk_sb(st)

        pb = [st.enter_context(nc.psum_tensor("pb%d" % i, [128, 512], F32)) for i in range(8)]

        def PK(i):
            return [('pbq', i, j) for j in range(4)]

        def pbv(i, a):
            return pb[i][:].rearrange("p (a b) -> p a b", a=a)

        def pbbf(i):
            return pb[i][:].bitcast(BF16)

        def A(fn, r, w):
            return S.op('act', fn, r, w)

        def V(fn, r, w):
            return S.op('dve', fn, r, w)

        def G(fn, r, w):
            return S.op('pool', fn, r, w)

        def MM(out, lhsT, rhs, start, stop, r, w, inc=None):
            return S.op('pe', lambda e: e.matmul(out, lhsT, rhs, start=start, stop=stop), r, w,
                        inc=(stop if inc is None else inc), pe_acc=(not start))

        def TR(out, in_, idt, r, w):
            return S.op('pe', lambda e: e.transpose(out, in_, idt), r, w)

        ident = sb("ident", [128, 128], F32)
        ident_bf = sb("ident_bf", [128, 128], BF16)
        ones_bf = sb("ones_bf", [128, 128], BF16)
        G(lambda e: e.memset(ident[:], 0.0), [], ['ident'])
        G(lambda e: e.affine_select(out=ident[:], in_=ident[:], pattern=[[-1, 128]],
                                    compare_op=ALU.not_equal, fill=1.0, base=0,
                                    channel_multiplier=1), ['ident'], ['ident'])
        G(lambda e: e.tensor_copy(out=ident_bf[:], in_=ident[:]), ['ident'], ['ident_bf'])
        G(lambda e: e.memset(ones_bf[:], 1.0), [], ['ones_bf'])

        gmlp = sb("gmlp", [128, 2, KC], F32)
        S.dma('sp', gmlp[:], mlp_norm_fm.rearrange("l p k -> p l k"), writes=['gmlp'], stream='c')
        gmix = sb("gmix", [128, 2, KC], F32)
        S.dma('sp', gmix[:], mix_norm_fm.rearrange("l p k -> p l k"), writes=['gmix'], stream='c')

        xTh = [None]

        def tile_rows(t):
            return x_s if t == NT else x_p[t * 128:(t + 1) * 128, :]

        def tile_cols(t):
            return slice(t * 128, (t + 1) * 128)

        def phase0():
          xT = xTh[0]
          with ExitStack() as ph:
            psb = mk_sb(ph)
            xin = [psb("xin%d" % i, [128, D], F32) for i in range(2)]
            for t in range(NT + 1):
                b = t % 2
                S.dma('sp', xin[b][:], tile_rows(t), writes=[('xin', b)], stream='x')
                for half in range(2):
                    pi = half
                    for j in range(4):
                        kc = half * 4 + j
                        TR(pb[pi][:, j * 128:(j + 1) * 128], xin[b][:, kc * 128:(kc + 1) * 128], ident[:],
                           [('xin', b), 'ident'], [('pbq', pi, j)])
                    A(lambda e, half=half, pi=pi: e.copy(out=xT[:, half * 4:half * 4 + 4, tile_cols(t)],
                                                        in_=pbv(pi, 4)),
                      [('pbq', pi, j) for j in range(4)], [('xT', t)] + [('pbq', pi, j) for j in range(4)])
            S.barrier()

        def xkeys(s0, n):
            return [('xT', t) for t in range(s0 // 128, (s0 + n + 127) // 128)]

        def hkeys(s0, n):
            return [('hT', t) for t in range(s0 // 128, (s0 + n + 127) // 128)]

        def make_norm(psb, W, nbuf=2):
            sq = [psb("sq%d" % i, [128, KC, W], BF16) for i in range(nbuf)]
            ntmp = [psb("ntmp%d" % i, [128, KC, W], F32) for i in range(nbuf)]
            rs = [psb("rs%d" % i, [128, W], F32) for i in range(nbuf)]
            ctr = [0]

            def norm_range(gain, gkey, s0, n, hdst, hk, xv=None, xk=None):
                i = ctr[0] % nbuf
                ctr[0] += 1
                if xv is None:
                    xv = xTh[0][:, :, s0:s0 + n]
                    xk = xkeys(s0, n)
                A(lambda e: e.activation(out=sq[i][:, :, :n], in_=xv, func=AF.Square),
                  xk, [('sq', i)])
                for kc in range(KC):
                    MM(pb[5][:, :n], ones_bf[:], sq[i][:, kc, :n], kc == 0, kc == KC - 1,
                       [('sq', i), 'ones_bf'], PK(5))
                A(lambda e: e.activation(out=rs[i][:, :n], in_=pb[5][:, :n], func=AF.Sqrt,
                                         bias=EPS, scale=1.0 / D), PK(5), [('rs', i)])
                V(lambda e: e.reciprocal(out=rs[i][:, :n], in_=rs[i][:, :n]), [('rs', i)], [('rs', i)])
                G(lambda e: e.tensor_tensor(out=ntmp[i][:, :, :n], in0=xv,
                                            in1=rs[i][:, :n].unsqueeze(1).to_broadcast([128, KC, n]),
                                            op=ALU.mult), xk + [('rs', i)], [('ntmp', i)])
                V(lambda e: e.tensor_tensor(out=hdst, in0=ntmp[i][:, :, :n],
                                            in1=gain.unsqueeze(2).to_broadcast([128, KC, n]),
                                            op=ALU.mult), [('ntmp', i), gkey], hk)
            return norm_range

        def mlp_layer(l):
            xT = xTh[0]
            with ExitStack() as ph:
                psb = mk_sb(ph)
                hT = psb("hT", [128, KC, TT], BF16)
                norm_range = make_norm(psb, 256)
                wu = [psb("wu%d" % i, [128, KC, 512], BF16) for i in range(2)]
                wd = [psb("wd%d" % i, [128, 4, D], BF16) for i in range(2)]
                h1 = [psb("h1_%d" % i, [128, 4, 512], BF16) for i in range(2)]
                rtmp = [psb("rtmp%d" % i, [128, 512], F32) for i in range(2)]
                ctr = {'up': 0, 'dn': 0, 'h1': 0, 'w': 0}
                for (s0, n) in blocks(TT, 256):
                    norm_range(gmlp[:, l, :], 'gmlp', s0, n, hT[:, :, s0:s0 + n], hkeys(s0, n))
                wup_v = w_up[l].rearrange("(kc p) f -> p kc f", p=128)
                wdn_v = w_down[l].rearrange("(fc p) d -> p fc d", p=128)
                for g in range(8):
                    wb = ctr['w'] % 2
                    ctr['w'] += 1
                    S.dma('pool', wu[wb][:], wup_v[:, :, g * 512:(g + 1) * 512], writes=[('wu', wb)], stream='w')
                    S.dma('pool', wd[wb][:], wdn_v[:, g * 4:(g + 1) * 4, :], writes=[('wd', wb)], stream='w')
                    for (s0, n) in blocks(TT, 512):
                        hb = ctr['h1'] % 2
                        ctr['h1'] += 1
                        tl = list(range(s0 // 128, (s0 + n + 127) // 128))
                        for fc in range(4):
                            pi = ctr['up'] % 2
                            ctr['up'] += 1
                            for kc in range(KC):
                                MM(pb[pi][:, :n], wu[wb][:, kc, fc * 128:(fc + 1) * 128], hT[:, kc, s0:s0 + n],
                                   kc == 0, kc == KC - 1, [('wu', wb)] + hkeys(s0, n), PK(pi))
                            A(lambda e, pi=pi: e.activation(out=rtmp[pi][:, :n], in_=pb[pi][:, :n], func=AF.Relu),
                              PK(pi), [('rtmp', pi)])
                            G(lambda e, pi=pi, fc=fc: e.tensor_tensor(out=h1[hb][:, fc, :n], in0=rtmp[pi][:, :n],
                                                                      in1=rtmp[pi][:, :n], op=ALU.mult),
                              [('rtmp', pi)], [('h1', hb, fc)])
                        for dc in range(KC):
                            pi = 2 + ctr['dn'] % 3
                            ctr['dn'] += 1
                            for fc in range(4):
                                MM(pb[pi][:, :n], wd[wb][:, fc, dc * 128:(dc + 1) * 128], h1[hb][:, fc, :n],
                                   fc == 0, fc == 3, [('wd', wb), ('h1', hb, fc)], PK(pi))
                            V(lambda e, dc=dc, pi=pi: e.tensor_tensor(out=xT[:, dc, s0:s0 + n],
                                                                      in0=xT[:, dc, s0:s0 + n],
                                                                      in1=pb[pi][:, :n], op=ALU.add),
                              PK(pi) + [('xTd', dc, t) for t in tl], [('xTd', dc, t) for t in tl])
                S.barrier()

        def hgrn_layer():
            xT = xTh[0]
            with ExitStack() as ph:
                psb = mk_sb(ph)
                norm_range = make_norm(psb, 128, 1)
                hTt = psb("hTt", [128, KC, 128], BF16)
                wic = psb("wic", [128, KC, 4096], BF16)
                woc = psb("woc", [128, KC, D], BF16)
                wic_v = w_in_c.rearrange("(kc p) f -> p kc f", p=128)
                woc_v = w_out_c.rearrange("(kc p) f -> p kc f", p=128)
                for kc in range(KC):
                    S.dma('pool', wic[:, kc, :], wic_v[:, kc, :], writes=['wic'], stream='w')
                S.dma('pool', woc[:], woc_v, writes=['woc'], stream='w')
                lbt = psb("lbt", [128, 2, 8], F32)
                lb = psb("lb", [128, 8], F32)
                oml = psb("oml", [128, 8], F32)
                gnc = psb("gnc", [128, 1], F32)
                S.dma('sp', lbt[:], lb_fm.rearrange("l p k -> p l k"), writes=['lbt'], stream='c')
                S.dma('sp', gnc[:], g_norm_c, writes=['gnc'], stream='c')
                V(lambda e: e.tensor_tensor(out=lb[:], in0=lbt[:, 1, :], in1=lbt[:, 0, :], op=ALU.subtract),
                  ['lbt'], ['lb'])
                A(lambda e: e.activation(out=lb[:], in_=lb[:], func=AF.Sigmoid), ['lb'], ['lb'])
                V(lambda e: e.tensor_scalar(out=oml[:], in0=lb[:], scalar1=-1.0, scalar2=1.0, op0=ALU.mult,
                                            op1=ALU.add), ['lb'], ['oml'])
                SC = 128 ** -0.5
                mask_p = psb("mask_p", [128, 128], F32)
                mask_s = psb("mask_s", [128, 128], F32)
                for mk, nm in ((mask_p, 'mask_p'), (mask_s, 'mask_s')):
                    G(lambda e, mk=mk: e.memset(mk[:], SC), [], [nm])
                    G(lambda e, mk=mk: e.affine_select(out=mk[:], in_=mk[:], pattern=[[1, 128]],
                                                       compare_op=ALU.is_ge, fill=0.0, base=0,
                                                       channel_multiplier=-1), [nm], [nm])
                G(lambda e: e.affine_select(out=mask_s[:].rearrange("p (a b) -> p a b", a=16),
                                            in_=mask_s[:].rearrange("p (a b) -> p a b", a=16),
                                            pattern=[[-8, 16], [0, 8]], compare_op=ALU.is_ge, fill=0.0,
                                            base=0, channel_multiplier=1), ['mask_s'], ['mask_s'])
                rm_p = psb("rm_p", [128, 8, 128], F32)
                rm_s = rm_p
                S.alias('rm_s', 'rm_p')
                G(lambda e: e.memset(rm_p[:], 1.0), [], ['rm_p'])
                G(lambda e: e.memset(rm_p[:, :, 0:1], 0.0), ['rm_p'], ['rm_p'])
                seqmask = psb("seqmask", [128, 16], F32)
                G(lambda e: e.memset(seqmask[:], 1.0), [], ['seqmask'])
                G(lambda e: e.affine_select(out=seqmask[:], in_=seqmask[:], pattern=[[-8, 16]],
                                            compare_op=ALU.is_ge, fill=0.0, base=0, channel_multiplier=1),
                  ['seqmask'], ['seqmask'])
                G(lambda e: e.affine_select(out=seqmask[:], in_=seqmask[:], pattern=[[8, 16]],
                                            compare_op=ALU.is_ge, fill=0.0, base=7, channel_multiplier=-1),
                  ['seqmask'], ['seqmask'])

                f4 = [128, 8, 128]
                qs = psb("qs", f4)
                ff = psb("ff", f4)
                kk = psb("kk", f4)
                gc = psb("gc", f4)
                sgate = psb("sgate", f4)
                v_tm = psb("v_tm", f4, BF16)
                tA = psb("tA", [128, 8, 32])
                tB = ff
                S.alias('tB', 'ff')
                qI = psb("qI", [128, 8, 32], BF16)
                kI = psb("kI", f4, BF16)
                intra = psb("intra", f4, BF16)
                qg = psb("qg", f4, BF16)
                khT = psb("khT", f4, BF16)
                kh_tm = psb("kh_tm", f4, BF16)
                khm = kI
                S.alias('khm', 'kI')
                egl = psb("egl", [128, 8, 16])
                Sst = psb("Sst", f4)
                Sbf = psb("Sbf", f4, BF16)
                S0 = ff
                S.alias('S0', 'ff')
                osq = khT
                S.alias('osq', 'khT')
                rn = qs[:].rearrange("p h t -> p (h t)")
                S.alias('rn', 'qs')
                onb = psb("onb", f4, BF16)
                G(lambda e: e.memset(intra[:], 0.0), [], ['intra'])
                G(lambda e: e.memset(Sst[:], 0.0), [], ['Sst'])
                G(lambda e: e.memset(Sbf[:], 0.0), [], ['Sbf'])

                def proj_fm(col0, banks):
                    for h in range(8):
                        bi = banks[h // 4]
                        for kc in range(KC):
                            MM(pb[bi][:, (h % 4) * 128:(h % 4 + 1) * 128],
                               wic[:, kc, col0 + h * 128:col0 + (h + 1) * 128], hTt[:, kc, :],
                               kc == 0, kc == KC - 1, ['wic', 'hTt'], [('pbq', bi, h % 4)])

                bq = PK

                order = list(range(NT)) + [NT]
                for t in order:
                    smp = (t == NT)
                    cols = tile_cols(t)
                    mask = mask_s if smp else mask_p
                    mkey = 'mask_s' if smp else 'mask_p'
                    rm = rm_s if smp else rm_p
                    if smp:
                        G(lambda e: e.memset(rm_s[:].rearrange("p h (s t) -> p h s t", t=8)[:, :, :, 0:1], 0.0),
                          ['rm_s'], ['rm_s'])
                    norm_range(gmix[:, 1, :], 'gmix', t * 128, 128, hTt[:], ['hTt'])
                    proj_fm(0, (0, 1))
                    for b2 in range(2):
                        A(lambda e, b2=b2: e.activation(out=qs[:, b2 * 4:b2 * 4 + 4, :], in_=pbv(b2, 4), func=AF.Silu),
                          bq(b2), ['qs'] + bq(b2))
                    if cfg.get('cstop', 99) <= 1:
                        continue
                    proj_fm(1024, (2, 3))
                    for b2 in range(2):
                        A(lambda e, b2=b2: e.activation(out=ff[:, b2 * 4:b2 * 4 + 4, :], in_=pbv(2 + b2, 4),
                                                        func=AF.Sigmoid), bq(2 + b2), ['ff'] + bq(2 + b2))
                    V(lambda e: e.tensor_tensor(out=ff[:], in0=ff[:], in1=oml[:].unsqueeze(2).to_broadcast(f4),
                                                op=ALU.mult), ['ff', 'oml'], ['ff'])
                    V(lambda e: e.tensor_tensor(out=ff[:], in0=ff[:], in1=lb[:].unsqueeze(2).to_broadcast(f4),
                                                op=ALU.add), ['ff', 'lb'], ['ff'])
                    V(lambda e: e.tensor_scalar(out=kk[:], in0=ff[:], scalar1=-1.0, scalar2=1.0, op0=ALU.mult,
                                                op1=ALU.add), ['ff'], ['kk'])
                    logf = ff
                    A(lambda e: e.activation(out=ff[:], in_=ff[:], func=AF.Ln), ['ff'], ['ff'])
                    V(lambda e: e.tensor_tensor_scan(out=gc[:].rearrange("p h t -> p (h t)"),
                                                     data0=rm[:].rearrange("p h t -> p (h t)"),
                                                     data1=logf[:].rearrange("p h t -> p (h t)"),
                                                     initial=0.0, op0=ALU.mult, op1=ALU.add),
                      ['ff', 'rm_p'], ['gc'])
                    if cfg.get('cstop', 99) <= 2:
                        continue
                    proj_fm(3072, (4, 5))
                    for b2 in range(2):
                        A(lambda e, b2=b2: e.activation(out=sgate[:, b2 * 4:b2 * 4 + 4, :], in_=pbv(4 + b2, 4),
                                                        func=AF.Silu), bq(4 + b2), ['sgate'] + bq(4 + b2))
                    for b2 in range(2):
                        for kc in range(KC):
                            MM(pb[6 + b2][:], hTt[:, kc, :], wic[:, kc, 2048 + b2 * 512:2048 + (b2 + 1) * 512],
                               kc == 0, kc == KC - 1, ['wic', 'hTt'], PK(6 + b2))
                        A(lambda e, b2=b2: e.copy(out=v_tm[:, b2 * 4:b2 * 4 + 4, :], in_=pbv(6 + b2, 4)),
                          PK(6 + b2), ['v_tm'])
                    if cfg.get('cstop', 99) <= 3:
                        continue
                    for I in range(4):
                        blk = slice(32 * I, 32 * I + 32)
                        nj = 32 * (I + 1)
                        if I == 0:
                            A(lambda e: e.activation(out=tA[:], in_=gc[:, :, blk], func=AF.Exp), ['gc'], ['tA'])
                            A(lambda e: e.activation(out=tB[:, :, :nj], in_=gc[:, :, :nj], func=AF.Exp, scale=-1.0),
                              ['gc'], ['tB'])
                        else:
                            ref = gc[:, :, 32 * I - 1:32 * I]
                            V(lambda e: e.tensor_tensor(out=tA[:], in0=gc[:, :, blk],
                                                        in1=ref.to_broadcast([128, 8, 32]), op=ALU.subtract),
                              ['gc'], ['tA'])
                            A(lambda e: e.activation(out=tA[:], in_=tA[:], func=AF.Exp), ['tA'], ['tA'])
                            V(lambda e: e.tensor_tensor(out=tB[:, :, :nj], in0=ref.to_broadcast([128, 8, nj]),
                                                        in1=gc[:, :, :nj], op=ALU.subtract), ['gc'], ['tB'])
                            A(lambda e: e.activation(out=tB[:, :, :nj], in_=tB[:, :, :nj], func=AF.Exp),
                              ['tB'], ['tB'])
                        V(lambda e: e.tensor_tensor(out=qI[:], in0=qs[:, :, blk], in1=tA[:], op=ALU.mult),
                          ['qs', 'tA'], ['qI'])
                        G(lambda e: e.tensor_tensor(out=kI[:, :, :nj], in0=kk[:, :, :nj], in1=tB[:, :, :nj],
                                                    op=ALU.mult), ['kk', 'tB'], ['kI'])
                        for h in range(8):
                            bi = h // 4
                            c0 = (h % 4) * 128 + 32 * I
                            MM(pb[bi][:nj, c0:c0 + 32], kI[:, h, :nj], qI[:, h, :], True, True,
                               ['kI', 'qI'], [('pbq', bi, h % 4)])
                        for b2 in range(2):
                            V(lambda e, b2=b2: e.tensor_tensor(
                                out=intra[:nj, b2 * 4:b2 * 4 + 4, blk], in0=pbv(b2, 4)[:nj, :, blk],
                                in1=mask[:nj, blk].unsqueeze(1).to_broadcast([nj, 4, 32]), op=ALU.mult),
                                bq(b2) + [mkey], ['intra'] + bq(b2))
                    if cfg.get('cstop', 99) <= 4:
                        continue
                    A(lambda e: e.activation(out=tB[:], in_=gc[:], func=AF.Exp), ['gc'], ['tB'])
                    V(lambda e: e.scalar_tensor_tensor(out=qg[:], in0=qs[:], scalar=SC, in1=tB[:], op0=ALU.mult, op1=ALU.mult),
                      ['qs', 'tB'], ['qg'])
                    if smp:
                        gl = gc[:].rearrange("p h (s t) -> p h s t", t=8)[:, :, :, 7:8]
                        V(lambda e: e.tensor_tensor(out=tB[:].rearrange("p h (s t) -> p h s t", t=8),
                                                    in0=gl.to_broadcast([128, 8, 16, 8]),
                                                    in1=gc[:].rearrange("p h (s t) -> p h s t", t=8),
                                                    op=ALU.subtract), ['gc'], ['tB'])
                        A(lambda e: e.activation(out=egl[:], in_=gc[:].rearrange("p h (s t) -> p h s t", t=8)[:, :, :, 7],
                                                 func=AF.Exp), ['gc'], ['egl'])
                    else:
                        gl = gc[:, :, 127:128]
                        V(lambda e: e.tensor_tensor(out=tB[:], in0=gl.to_broadcast(f4), in1=gc[:],
                                                    op=ALU.subtract), ['gc'], ['tB'])
                        A(lambda e: e.activation(out=egl[:, :, 0], in_=gc[:, :, 127], func=AF.Exp), ['gc'], ['egl'])
                    A(lambda e: e.activation(out=tB[:], in_=tB[:], func=AF.Exp), ['tB'], ['tB'])
                    G(lambda e: e.tensor_tensor(out=khT[:], in0=kk[:], in1=tB[:], op=ALU.mult), ['kk', 'tB'], ['khT'])
                    for h in range(8):
                        TR(pbbf(2)[:, h * 128:(h + 1) * 128], khT[:, h, :], ident_bf[:], ['khT', 'ident_bf'],
                           [('pbq', 2, h // 2)])
                    A(lambda e: e.copy(out=kh_tm[:].rearrange("p h t -> p (h t)"), in_=pbbf(2)),
                      PK(2), ['kh_tm'] + PK(2))
                    if cfg.get('cstop', 99) <= 5:
                        continue
                    if not smp and cfg.get('dbg', '') == 'nop':
                        pass
                    elif smp and cfg.get('dbg', '') == 'nos':
                        pass
                    elif not smp:
                        for h in range(8):
                            bi = 4 + h // 4
                            o_out = pb[bi][:, (h % 4) * 128:(h % 4 + 1) * 128]
                            MM(o_out, Sbf[:, h, :], qg[:, h, :], True, False, ['Sbf', 'qg'], [('pbq', bi, h % 4)])
                            MM(o_out, v_tm[:, h, :], intra[:, h, :], False, True, ['v_tm', 'intra'],
                               [('pbq', bi, h % 4)])
                        for h in range(8):
                            if cfg.get('sub', 9) < 2:
                                break
                            bi = 6 + h // 4
                            MM(pb[bi][:, (h % 4) * 128:(h % 4 + 1) * 128], kh_tm[:, h, :], v_tm[:, h, :], True, True,
                               ['kh_tm', 'v_tm'], [('pbq', bi, h % 4)])
                        V(lambda e: e.tensor_tensor(out=Sst[:], in0=Sst[:], in1=egl[:, :, 0:1].to_broadcast(f4),
                                                    op=ALU.mult), ['Sst', 'egl'], ['Sst'])
                        for b2 in range(2):
                            V(lambda e, b2=b2: e.tensor_tensor(out=Sst[:, b2 * 4:b2 * 4 + 4, :],
                                                               in0=Sst[:, b2 * 4:b2 * 4 + 4, :], in1=pbv(6 + b2, 4),
                                                               op=ALU.add), ['Sst'] + PK(6 + b2), ['Sst'] + PK(6 + b2))
                        if cfg.get('sub', 9) >= 4:
                            A(lambda e: e.copy(out=Sbf[:], in_=Sst[:]), ['Sst'], ['Sbf'])
                        if t == NT - 1 and cfg.get('sub', 9) >= 5:
                            S.dma('sp', hgrn_p.rearrange("h k v -> k h v"), Sst[:], reads=['Sst'], stream='o')
                    else:
                        for b2 in range(2):
                            V(lambda e, b2=b2: e.memset(pb[4 + b2][:], 0.0), [], PK(4 + b2))
                        for s in range(16):
                            S.dma('sp', S0[:], hgrn_in[s].rearrange("h k v -> k h v"), writes=['S0'], stream='s0')
                            A(lambda e: e.copy(out=Sbf[:], in_=S0[:]), ['S0'], ['Sbf'])
                            for h in range(8):
                                bi = 4 + h // 4
                                c0 = (h % 4) * 128 + s * 8
                                MM(pb[bi][:, c0:c0 + 8], Sbf[:, h, :], qg[:, h, s * 8:(s + 1) * 8], False, False,
                                   ['Sbf', 'qg'], [('pbq', bi, h % 4)])
                            V(lambda e, s=s: e.tensor_scalar(out=khm[:], in0=kh_tm[:], scalar1=seqmask[:, s:s + 1],
                                                            scalar2=None, op0=ALU.mult),
                              ['kh_tm', 'seqmask'], ['khm'])
                            for h in range(8):
                                bi = 6 + h // 4
                                MM(pb[bi][:, (h % 4) * 128:(h % 4 + 1) * 128], khm[:, h, :], v_tm[:, h, :],
                                   True, True, ['khm', 'v_tm'], [('pbq', bi, h % 4)])
                            V(lambda e, s=s: e.tensor_tensor(out=S0[:], in0=S0[:],
                                                             in1=egl[:, :, s:s + 1].to_broadcast(f4), op=ALU.mult),
                              ['S0', 'egl'], ['S0'])
                            for b2 in range(2):
                                V(lambda e, b2=b2: e.tensor_tensor(out=S0[:, b2 * 4:b2 * 4 + 4, :],
                                                                   in0=S0[:, b2 * 4:b2 * 4 + 4, :], in1=pbv(6 + b2, 4),
                                                                   op=ALU.add), ['S0'] + PK(6 + b2), ['S0'] + PK(6 + b2))
                            S.dma('sp', hgrn_s[s].rearrange("h k v -> k h v"), S0[:], reads=['S0'], stream='o')
                        for h in range(8):
                            bi = 4 + h // 4
                            MM(pb[bi][:, (h % 4) * 128:(h % 4 + 1) * 128], v_tm[:, h, :], intra[:, h, :],
                               False, True, ['v_tm', 'intra'], [('pbq', bi, h % 4)])
                    if cfg.get('cstop', 99) <= 6:
                        continue
                    for b2 in range(2):
                        A(lambda e, b2=b2: e.activation(out=osq[:, b2 * 4:b2 * 4 + 4, :], in_=pbv(4 + b2, 4),
                                                        func=AF.Square), bq(4 + b2), ['osq'])
                    for b2 in range(2):
                        MM(pb[b2][:], ones_bf[:], osq[:, b2 * 4:b2 * 4 + 4, :].rearrange("p h t -> p (h t)"),
                           True, True, ['osq', 'ones_bf'], bq(b2))
                        A(lambda e, b2=b2: e.activation(out=rn[:, b2 * 512:(b2 + 1) * 512], in_=pb[b2][:],
                                                        func=AF.Sqrt, bias=EPS, scale=1.0 / 128),
                          bq(b2), ['rn'] + bq(b2))
                    V(lambda e: e.reciprocal(out=rn[:], in_=rn[:]), ['rn'], ['rn'])
                    for b2 in range(2):
                        V(lambda e, b2=b2: e.scalar_tensor_tensor(
                            out=rn[:, b2 * 512:(b2 + 1) * 512].rearrange("p (h t) -> p h t", h=4),
                            in0=pbv(4 + b2, 4), scalar=gnc[:, 0:1],
                            in1=rn[:, b2 * 512:(b2 + 1) * 512].rearrange("p (h t) -> p h t", h=4),
                            op0=ALU.mult, op1=ALU.mult), bq(4 + b2) + ['gnc', 'rn'], ['rn'] + bq(4 + b2))
                    G(lambda e: e.tensor_tensor(out=onb[:], in0=rn[:].rearrange("p (h t) -> p h t", h=8), in1=sgate[:],
                                                op=ALU.mult), ['rn', 'sgate'], ['onb'])
                    if cfg.get('cstop', 99) <= 7:
                        continue
                    for dc in range(KC):
                        bi = 2 + dc // 4
                        for kc in range(KC):
                            MM(pb[bi][:, (dc % 4) * 128:(dc % 4 + 1) * 128], woc[:, kc, dc * 128:(dc + 1) * 128],
                               onb[:, kc, :], kc == 0, kc == KC - 1, ['woc', 'onb'],
                               [('pbq', bi, dc % 4)])
                    for b2 in range(2):
                        V(lambda e, b2=b2: e.tensor_tensor(out=xT[:, b2 * 4:b2 * 4 + 4, cols],
                                                           in0=xT[:, b2 * 4:b2 * 4 + 4, cols],
                                                           in1=pbv(2 + b2, 4), op=ALU.add),
                          bq(2 + b2) + [('xT', t)], [('xT', t)] + bq(2 + b2))
                S.barrier()

        def ab_layer():
            with ExitStack() as ph:
                psb = mk_sb(ph)
                norm_range = make_norm(psb, 128, 1)
                STG = cfg.get('astage', 99)
                NEG = -1.0e30
                SCq = 128 ** -0.5
                SCm = 96 ** -0.5
                f4 = [128, 4, 128]
                wia = psb("wia", [128, KC, 2600], BF16)
                woa = psb("woa", [128, KC, D], BF16)
                wuq = psb("wuq", [128, 2, 768], BF16)
                wuk = psb("wuk", [128, 2, 512], BF16)
                wuv = psb("wuv", [128, 2, 512], BF16)
                wia_v = w_in_ab.rearrange("(kc p) f -> p kc f", p=128)
                for kc in range(KC):
                    for (c0, cn) in blocks(2600, 1300):
                        S.dma('pool', wia[:, kc, c0:c0 + cn], wia_v[:, kc, c0:c0 + cn], writes=['wia'], stream='w')
                S.dma('pool', woa[:], w_out_ab.rearrange("(kc p) f -> p kc f", p=128), writes=['woa'], stream='w')
                S.dma('pool', wuq[:], w_uq.rearrange("(c p) f -> p c f", p=128), writes=['wuq'], stream='w')
                S.dma('pool', wuk[:], w_uk.rearrange("(c p) f -> p c f", p=128), writes=['wuk'], stream='w')
                S.dma('pool', wuv[:], w_uv.rearrange("(c p) f -> p c f", p=128), writes=['wuv'], stream='w')
                cw = psb("cw", [128, 12, 4])
                S.dma('sp', cw[:], conv_w_fm, writes=['cw'], stream='c')
                negA = psb("negA", [128, 4])
                dtb = psb("dtb", [128, 4])
                S.dma('sp', negA[:], a_log.partition_broadcast(128), writes=['negA'], stream='c')
                S.dma('sp', dtb[:], dt_bias.partition_broadcast(128), writes=['dtb'], stream='c')
                A(lambda e: e.activation(out=negA[:], in_=negA[:], func=AF.Exp), ['negA'], ['negA'])
                V(lambda e: e.tensor_scalar(out=negA[:], in0=negA[:], scalar1=-1.0, scalar2=None, op0=ALU.mult),
                  ['negA'], ['negA'])
                gdnn = psb("gdnn", [128, 1])
                S.dma('sp', gdnn[:], gdn_norm, writes=['gdnn'], stream='c')
                qnb = psb("qnb", [128, 256])
                kvnb = psb("kvnb", [128, 256])
                S.dma('sp', qnb[:], q_norm.partition_broadcast(128), writes=['qnb'], stream='c')
                S.dma('sp', kvnb[:], kv_norm.partition_broadcast(128), writes=['kvnb'], stream='c')
                cosb = psb("cosb", [128, NT + 1, 16])
                sinb = psb("sinb", [128, NT + 1, 16])
                S.dma('sp', cosb[:], rope_cos.rearrange("(t p) f -> p t f", p=128), writes=['cosb'], stream='c')
                S.dma('sp', sinb[:], rope_sin.rearrange("(t p) f -> p t f", p=128), writes=['sinb'], stream='c')
                ones_f = psb("ones_f", [128, 128])
                G(lambda e: e.memset(ones_f[:], 1.0), [], ['ones_f'])

                def build_masks(tag, blk8, xsb):
                    m = {}
                    for nm in ('tri', 'blk', 'nms', 'nmi', 'nmsT', 'cmask'):
                        m[nm] = xsb(nm + tag, [128, 128])

                    def sel(t, pattern, op, fill, base, cm, view=None):
                        key = m_key(t)
                        ap = t[:] if view is None else t[:].rearrange("p (a b) -> p a b", a=16)
                        G(lambda e: e.affine_select(out=ap, in_=ap, pattern=pattern, compare_op=op, fill=fill,
                                                    base=base, channel_multiplier=cm), [key], [key])

                    def m_key(t):
                        return 'mask' + tag

                    def same_block(t, fill):
                        sel(t, [[-8, 16], [0, 8]], ALU.is_ge, fill, 0, 1, view=True)
                        sel(t, [[8, 16], [0, 8]], ALU.is_ge, fill, 7, -1, view=True)
                    G(lambda e: e.memset(m['tri'][:], 1.0), [], ['mask' + tag])
                    sel(m['tri'], [[1, 128]], ALU.is_ge, 0.0, 0, -1)
                    G(lambda e: e.memset(m['blk'][:], 1.0), ['mask' + tag], ['mask' + tag])
                    G(lambda e: e.memset(m['nms'][:], 0.0), ['mask' + tag], ['mask' + tag])
                    sel(m['nms'], [[1, 128]], ALU.is_gt, NEG, 0, -1)
                    G(lambda e: e.memset(m['nmi'][:], 0.0), ['mask' + tag], ['mask' + tag])
                    sel(m['nmi'], [[1, 128]], ALU.is_ge, NEG, 0, -1)
                    G(lambda e: e.memset(m['nmsT'][:], 0.0), ['mask' + tag], ['mask' + tag])
                    sel(m['nmsT'], [[-1, 128]], ALU.is_gt, NEG, 0, 1)
                    G(lambda e: e.memset(m['cmask'][:], 1.0), ['mask' + tag], ['mask' + tag])
                    sel(m['cmask'], [[1, 128]], ALU.is_ge, 0.0, 0, -1)
                    if blk8:
                        same_block(m['tri'], 0.0)
                        same_block(m['blk'], 0.0)
                        same_block(m['nms'], NEG)
                        same_block(m['nmi'], NEG)
                        same_block(m['nmsT'], NEG)
                        same_block(m['cmask'], 0.0)
                    return m
                seqmask = psb("seqmask", [128, 16], F32)
                G(lambda e: e.memset(seqmask[:], 1.0), [], ['seqmask'])
                G(lambda e: e.affine_select(out=seqmask[:], in_=seqmask[:], pattern=[[-8, 16]],
                                            compare_op=ALU.is_ge, fill=0.0, base=0, channel_multiplier=1),
                  ['seqmask'], ['seqmask'])
                G(lambda e: e.affine_select(out=seqmask[:], in_=seqmask[:], pattern=[[8, 16]],
                                            compare_op=ALU.is_ge, fill=0.0, base=7, channel_multiplier=-1),
                  ['seqmask'], ['seqmask'])

                H = {}
                xTt = psb("xTt", [128, KC, 128])
                hTt = psb("hTt", [128, KC, 128], BF16)
                acc = psb("acc", [128, 12, 128])
                ctmp = psb("ctmp", [128, 12, 128])
                qkvs = acc
                S.alias('qkvs', 'acc')
                qkvtm = ctmp[:].rearrange("p c t -> p (c t)")
                S.alias('qkvtm', 'ctmp')
                cst = ctmp[:].rearrange("p c t -> p (c t)")[0:48, :]
                S.alias('cst', 'ctmp')
                xin = ctmp[:].rearrange("p c t -> p (c t)")[:, 0:D]
                S.alias('xin', 'ctmp')
                zs = psb("zs", f4)
                tm = psb("tm", [128, 552])
                sq8 = psb("sq8", [128, 8, 128], BF16)
                rn8 = psb("rn8", [128, 8, 128])
                qkn = rn8
                S.alias('qkn', 'rn8')
                ktm = psb("ktm", f4)
                vtm = psb("vtm", f4)
                gt = psb("gt", [128, 16])
                gcc = psb("gcc", [128, 8])
                G4 = psb("G4", f4)
                gcr = psb("gcr", f4)
                br = psb("br", f4)
                a1 = psb("a1", f4)
                E1s = psb("E1s", f4)
                E1i = psb("E1i", f4)
                E2s = psb("E2s", f4)
                Xa = [psb("Xa%d" % i, f4) for i in range(2)]
                XTa = [psb("XTa%d" % i, f4) for i in range(2)]
                Pm = psb("Pm", f4)
                itr = psb("itr", f4)
                qg = psb("qg", f4)
                vb, kbg, kdec, uT, wT = E1s, E2s, E1i, br, G4
                vnT, vn, kdm, ona = Xa[0], Xa[1], XTa[0], XTa[1]
                for al, cn in (('vb', 'E1s'), ('kbg', 'E2s'), ('kdec', 'E1i'), ('uT', 'br'), ('wT', 'G4'),
                               ('vnT', ('Xa', 0)), ('vn', ('Xa', 1)), ('kdm', ('XTa', 0)), ('ona', ('XTa', 1))):
                    S.alias(al, cn)
                eglr = psb("eglr", [128, 4, 16])
                Sg = psb("Sg", f4)
                Sgs = psb("Sgs", f4)
                osq = psb("osq", f4, BF16)
                rno = psb("rno", [128, 512])
                onb = psb("onb", [128, KC, 128], BF16)
                ssn = psb("ssn", [128, 4])
                cqn = psb("cqn", [128, 256])
                ckvn = psb("ckvn", [128, 256])
                kst = psb("kst", [128, 128])
                rt = psb("rt", [128, 8, 64])
                cqnT = psb("cqnT", [128, 2, 128], BF16)
                ckvnT = psb("ckvnT", [128, 2, 128], BF16)
                qtm = psb("qtm", [128, 8, 96])
                qn2 = psb("qn2", [128, 4, 128], BF16)
                qp2 = psb("qp2", [128, 8, 32], BF16)
                QTn = psb("QTn", [128, 4, 128], BF16)
                QTp = psb("QTp", [32, 8, 128], BF16)
                G(lambda e: e.memset(Sg[:], 0.0), [], ['Sg'])
                G(lambda e: e.memset(kst[:], 0.0), [], ['kst'])
                G(lambda e: e.memset(onb[:], 0.0), [], [('onb', 0), ('onb', 1)])


                def sample_attention():
                    NPG = cfg['NPG']
                    NPOOL = cfg['NPOOL']
                    R = NPG
                    SB = 128 // R
                    RG = min(8, R)
                    NG = R // RG
                    BPP = 128 // RG
                    ssb = H['ssb2']
                    rep = ssb("rep", [NPG, 128])
                    gb = ssb("gb", [128, NG + 1])
                    cm8 = ssb("cm8", [128, 64])
                    S.dma('sp', rep[:], rep_in, writes=['rep'], stream='c')
                    S.dma('sp', gb[:], gbase_in, writes=['gb'], stream='c')
                    S.dma('sp', cm8[:], cm8_in, writes=['cm8'], stream='c')
                    pti = ssb("pti", [16, NPG], I32)
                    ptf = ssb("ptf", [16, NPG])
                    ptT = ssb("ptT", [NPG, 16])
                    ptrep = ssb("ptrep", [128, 16])
                    S.dma('sp', pti[:], page_table, writes=['pti'], stream='c')
                    V(lambda e: e.tensor_copy(out=ptf[:], in_=pti[:]), ['pti'], ['ptf'])
                    TR(pb[0][0:NPG, 0:16], ptf[:], ident[0:16, 0:16], ['ptf', 'ident'], PK(0))
                    A(lambda e: e.copy(out=ptT[:], in_=pb[0][0:NPG, 0:16]), PK(0), ['ptT'] + PK(0))
                    MM(pb[0][:, 0:16], rep[:], ptT[:], True, True, ['rep', 'ptT'], PK(0))
                    A(lambda e: e.copy(out=ptrep[:], in_=pb[0][:, 0:16]), PK(0), ['ptrep'] + PK(0))
                    idxf = ssb("idxf", [128, 16, NG + 1])
                    idx = ssb("idx", [128, 16, NG + 1], I32)
                    for g in range(NG + 1):
                        V(lambda e, g=g: e.tensor_scalar(out=idxf[:, :, g], in0=ptrep[:],
                                                        scalar1=float(BPP if g < NG else SB), scalar2=gb[:, g:g + 1],
                                                        op0=ALU.mult, op1=ALU.add), ['ptrep', 'gb', 'idxf'], ['idxf'])
                    V(lambda e: e.tensor_copy(out=idx[:], in_=idxf[:]), ['idxf'], ['idx'])
                    SST = cfg.get('sstop', 99)
                    if SST <= 1:
                        return
                    lat_blk = cache_lat.rearrange("n (b r) d -> (n b) (r d)", r=RG)
                    kr_blk = cache_kr.rearrange("n (b r) d -> (n b) (r d)", r=R)
                    wk_ol = ssb("wk_ol", [128, 2048], BF16)
                    wukT = wk_ol[:, :].rearrange("p (h l) -> p h l", h=8)
                    S.alias('wukT', 'wk_ol')
                    S.alias('olatT', 'wk_ol')
                    for hh in range(2):
                        S.dma('pool', wk_ol[:, hh * 1024:(hh + 1) * 1024], w_ukT[:, hh * 1024:(hh + 1) * 1024], writes=['wukT'], stream='w')
                    if cfg.get('sq', 9) <= 0:
                        return
                    qlatT = ssb("qlatT", [128, 2, 8, 128], BF16)
                    for c in range(2):
                        for h in range(8):
                            bi = h // 4
                            MM(pb[bi][:, (h % 4) * 128:(h % 4 + 1) * 128], wukT[:, h, c * 128:(c + 1) * 128],
                               QTn[:, h // 2, :], True, True, ['wukT', 'QT'], [('pbq', bi, h % 4)])
                        for b2 in range(2):
                            A(lambda e, c=c, b2=b2: e.copy(out=qlatT[:, c, b2 * 4:b2 * 4 + 4, :], in_=pbv(b2, 4)),
                              PK(b2), ['qlatT'] + PK(b2))
                    if cfg.get('sq', 9) <= 1:
                        return
                    qpeT = QTp
                    S.alias('qpeT', 'QT')
                    if cfg.get('sq', 9) <= 2:
                        return
                    kpTn = ssb("kpTn", [32, 128], BF16)
                    TR(pb[3][0:32, 0:128], kst[:, 0:32], ident[:], ['kst', 'ident'], PK(3))
                    A(lambda e: e.copy(out=kpTn[:], in_=pb[3][0:32, 0:128]), PK(3), ['kpTn'] + PK(3))
                    Lnew = ssb("Lnew", [128, 257], BF16)
                    V(lambda e: e.memset(Lnew[:, 256:257], 1.0), [], ['Lnew'])
                    V(lambda e: e.tensor_copy(out=Lnew[:, 0:256], in_=ckvn[:]), ['ckvn', 'Lnew'], ['Lnew'])
                    if SST <= 2:
                        return
                    Lb = [ssb("Lb%d" % i, [128, RG, 256], BF16) for i in range(2)]
                    Kb = [ssb("Kb0", [128, R, 32], BF16)] * 2
                    latT = [ssb("latT0", [128, 2, RG, 128], BF16)] * 2
                    kpT = [ssb("kpT0", [32, RG, 128], BF16)] * 2
                    PTs = [ssb("PTs%d" % i, [128, RG, 64], BF16) for i in range(2)]
                    oacc = ssb("oacc", [64, 257])
                    olat = ssb("olat", [64, 256])
                    olatT = wk_ol[:, :].rearrange("p (c s q) -> p c s q", c=2, s=16)
                    mst = ssb("mst", [128, 8])
                    m11 = ssb("m11", [1, 4])
                    gq = []
                    for s in range(16):
                        for g in range(NG + 1):
                            gq.append((s, g))
                    ctr = {'n': 0}

                    def stage1(s, g):
                        n = ctr['n']
                        ctr['n'] += 1
                        b = n % 2
                        sbk = 6 + b
                        if g == 0:
                            kb_ = s % 2
                            S.dma_custom('pool', lambda e: e.indirect_dma_start(
                                out=Kb[kb_][:].rearrange("p r d -> p (r d)"), out_offset=None, in_=kr_blk,
                                in_offset=bass.IndirectOffsetOnAxis(ap=idx[:, s, NG:NG + 1], axis=0)),
                                ['idx'], [('Kb', 0)], stream='gk', nrr=2)
                        if g < NG:
                            S.dma_custom('pool', lambda e: e.indirect_dma_start(
                                out=Lb[b][:].rearrange("p r d -> p (r d)"), out_offset=None, in_=lat_blk,
                                in_offset=bass.IndirectOffsetOnAxis(ap=idx[:, s, g:g + 1], axis=0)),
                                ['idx'], [('Lb', b)], stream='gl', nrr=2)
                            for r in range(RG):
                                rr = g * RG + r
                                for c in range(2):
                                    blk = (r % 4) * 2 + c
                                    TR(pbbf(4 + (r // 4) % 2)[:, blk * 128:(blk + 1) * 128], Lb[b][:, r, c * 128:(c + 1) * 128],
                                       ident_bf[:], [('Lb', b), 'ident_bf'], [('pbq', 4 + (r // 4) % 2, blk // 2)])
                                TR(pbbf(3)[0:32, r * 128:(r + 1) * 128], Kb[s % 2][:, rr, :], ident_bf[:],
                                   [('Kb', 0), 'ident_bf'], [('pbq', 3, r // 2)])
                                if r % 4 == 3:
                                    bk = 4 + (r // 4) % 2
                                    r0 = r - 3
                                    V(lambda e, bk=bk, r0=r0: e.tensor_copy(
                                        out=latT[b][:, :, r0:r0 + 4, :].rearrange("p c r k -> p r c k"),
                                        in_=pbbf(bk).rearrange("p (r c k) -> p r c k", r=4, c=2)),
                                        PK(bk), [('latT', 0)] + PK(bk))
                            A(lambda e: e.copy(out=kpT[b][:].rearrange("p r k -> p (r k)"), in_=pbbf(3)[0:32, 0:RG * 128]),
                              PK(3), [('kpT', 0)] + PK(3))
                            for r in range(RG):
                                o = pb[sbk][:, r * 64:(r + 1) * 64]
                                MM(o, latT[b][:, 0, r, :], qlatT[:, 0, :, s * 8:(s + 1) * 8], True, False, [('latT', 0), 'qlatT'],
                                   [('pbq', sbk, r // 2)])
                                MM(o, latT[b][:, 1, r, :], qlatT[:, 1, :, s * 8:(s + 1) * 8], False, False, [('latT', 0), 'qlatT'],
                                   [('pbq', sbk, r // 2)])
                                MM(o, kpT[b][:, r, :], qpeT[:, :, s * 8:(s + 1) * 8], False, True, [('kpT', 0), 'qpeT'],
                                   [('pbq', sbk, r // 2)])
                            ncol = RG * 64
                        else:
                            o = pb[sbk][:, 0:64]
                            MM(o, ckvnT[:, 0, :], qlatT[:, 0, :, s * 8:(s + 1) * 8], True, False, ['ckvnT', 'qlatT'], [('pbq', sbk, 0)])
                            MM(o, ckvnT[:, 1, :], qlatT[:, 1, :, s * 8:(s + 1) * 8], False, False, ['ckvnT', 'qlatT'], [('pbq', sbk, 0)])
                            MM(o, kpTn[:], qpeT[:, :, s * 8:(s + 1) * 8], False, True, ['kpTn', 'qpeT'], [('pbq', sbk, 0)])
                            ncol = 64
                        return (s, g, b, sbk, ncol)

                    def stage2(st_):
                        s, g, b, sbk, ncol = st_
                        keys = [('pbq', sbk, j) for j in range((ncol + 127) // 128)]
                        if g == 0:
                            V(lambda e: e.reduce_max(out=mst[:, 0:1], in_=pb[sbk][:, :ncol], axis=AX.X), keys, ['mst0'])
                            TR(pb[2][0:1, 0:128], mst[:, 0:1], ident[:], ['mst0', 'ident'], PK(2))
                            V(lambda e: e.reduce_max(out=m11[:, 1:2], in_=pb[2][0:1, 0:128], axis=AX.X), PK(2),
                              ['m11'] + PK(2))
                            MM(pb[2][:, 128:129], ones_f[0:1, :], m11[:, 1:2], True, True, ['ones_f', 'm11'], PK(2))
                            A(lambda e: e.mul(out=mst[:, 3:4], in_=pb[2][:, 128:129], mul=-SCm), PK(2), ['mst3'] + PK(2))
                        first = (g == 0)
                        if g < NG:
                            A(lambda e: e.activation(out=PTs[b][:].rearrange("p r q -> p (r q)"), in_=pb[sbk][:, :ncol],
                                                     func=AF.Exp, bias=mst[:, 3:4], scale=SCm),
                              keys + ['mst3'], [('PTs', b)] + keys)
                            for r in range(RG):
                                MM(pb[1][0:64, 0:256], PTs[b][:, r, :], Lb[b][:, r, :], first and r == 0, False,
                                   [('PTs', b), ('Lb', b)], PK(1), inc=(r == RG - 1))
                                MM(pb[1][0:64, 256:257], PTs[b][:, r, :], ones_bf[:, 0:1], False, False,
                                   [('PTs', b), 'ones_bf'], PK(1), inc=(r == RG - 1))
                        else:
                            A(lambda e: e.activation(out=PTs[b][:, 0, :], in_=pb[sbk][:, :64], func=AF.Exp,
                                                     bias=mst[:, 3:4], scale=SCm), keys + ['mst3'], [('PTs', b)] + keys)
                            V(lambda e: e.scalar_tensor_tensor(out=PTs[b][:, 0, :], in0=PTs[b][:, 0, :],
                                                               scalar=seqmask[:, s:s + 1], in1=cm8[:], op0=ALU.mult,
                                                               op1=ALU.mult), [('PTs', b), 'seqmask', 'cm8'], [('PTs', b)])
                            MM(pb[1][0:64, 0:257], PTs[b][:, 0, :], Lnew[:], False, True, [('PTs', b), 'Lnew'], PK(1))
                            V(lambda e: e.reciprocal(out=mst[0:64, 5:6], in_=pb[1][0:64, 256:257]), PK(1), ['mst5'])
                            V(lambda e: e.tensor_scalar(out=olat[:], in0=pb[1][0:64, 0:256], scalar1=mst[0:64, 5:6],
                                                        scalar2=None, op0=ALU.mult), PK(1) + ['mst5'], ['olat'] + PK(1))
                            for c in range(2):
                                TR(pb[0][:, c * 64:(c + 1) * 64], olat[:, c * 128:(c + 1) * 128], ident[0:64, 0:64],
                                   ['olat', 'ident'], PK(0))
                            A(lambda e: e.copy(out=olatT[:, :, s, :], in_=pb[0][:, 0:128].rearrange("p (c q) -> p c q", c=2)),
                              PK(0), ['olatT'] + PK(0))

                    prev = None
                    if SST <= 5:
                        gq = gq[:SST - 2]
                    for (s, g) in gq:
                        cur = stage1(s, g)
                        if prev is not None and SST != 3:
                            stage2(prev)
                        prev = cur
                    if SST != 3:
                        stage2(prev)
                    if cfg.get('dbg_s'):
                        S.dma('sp', dbg_mst, mst[:], reads=['mst0', 'mst3', 'mst5'], stream='o')
                        S.dma('sp', dbg_lb, Lb[0][:, :, :].rearrange('p r d -> p (r d)'), reads=[('Lb', 0)], stream='o')
                        S.dma('sp', dbg_kb, Kb[0][:].rearrange('p r d -> p (r d)'), reads=[('Kb', 0)], stream='o')
                        S.dma('sp', dbg_olat, olat[:], reads=['olat'], stream='o')
                    if SST <= 6:
                        return
                    for h in range(8):
                        rh = (h % 2) * 64
                        pr = (h // 2) * 128
                        for c in range(2):
                            MM(pb[0][:, 0:128], wuv[:, c, pr:pr + 128],
                               olatT[:, c, :, h * 8:(h + 1) * 8], c == 0, c == 1,
                               ['wuv', 'olatT'], PK(0))
                        V(lambda e, h=h, rh=rh: e.tensor_copy(out=onb[rh:rh + 64, 4 + h // 2, :], in_=pb[0][rh:rh + 64, 0:128]),
                          PK(0), [('onb', 1)] + PK(0))


                def tile_body(t):
                    smp = (t == NT)
                    M = H['M']
                    mk = 'maskS' if smp else 'maskP'
                    S.dma('sp', xin, tile_rows(t), writes=['xin'], stream='x')
                    for half in range(2):
                        for j in range(4):
                            kc = half * 4 + j
                            TR(pb[half][:, j * 128:(j + 1) * 128], xin[:, kc * 128:(kc + 1) * 128], ident[:],
                               ['xin', 'ident'], [('pbq', half, j)])
                        A(lambda e, half=half: e.copy(out=xTt[:, half * 4:half * 4 + 4, :], in_=pbv(half, 4)),
                          PK(half), ['xTt'] + PK(half))
                    norm_range(gmix[:, 0, :], 'gmix', 0, 128, hTt[:], ['hTt'], xv=xTt[:], xk=['xTt'])
                    for c in range(16):
                        bi = c // 4
                        for kc in range(KC):
                            MM(pb[bi][:, (c % 4) * 128:(c % 4 + 1) * 128], wia[:, kc, c * 128:(c + 1) * 128],
                               hTt[:, kc, :], kc == 0, kc == KC - 1, ['wia', 'hTt'], [('pbq', bi, c % 4)])
                    for b2 in range(3):
                        if smp:
                            A(lambda e, b2=b2: e.copy(
                                out=H['cbufS'][:, b2 * 4:b2 * 4 + 4, :, 3:11],
                                in_=pb[b2][:].rearrange("p (c s t) -> p c s t", c=4, s=16)),
                              PK(b2), ['cbufS'] + PK(b2))
                        else:
                            A(lambda e, b2=b2: e.copy(out=H['cbuf'][:, b2 * 4:b2 * 4 + 4, 3:131], in_=pbv(b2, 4)),
                              PK(b2), ['cbuf'] + PK(b2))
                    A(lambda e: e.activation(out=zs[:], in_=pbv(3, 4), func=AF.Silu), PK(3), ['zs'] + PK(3))
                    for (bi, c0, cn) in ((4, 2048, 512), (5, 2560, 40)):
                        for kc in range(KC):
                            MM(pb[bi][:, :cn], hTt[:, kc, :], wia[:, kc, c0:c0 + cn], kc == 0, kc == KC - 1,
                               ['wia', 'hTt'], PK(bi))
                        A(lambda e, bi=bi, c0=c0, cn=cn: e.copy(out=tm[:, c0 - 2048:c0 - 2048 + cn], in_=pb[bi][:, :cn]),
                          PK(bi), ['tm'] + PK(bi))
                    if smp or t == NT - 1:
                        for j3 in range(3):
                            bi = 5 + j3
                            for kc in range(KC):
                                MM(pb[bi][:], hTt[:, kc, :], wia[:, kc, j3 * 512:(j3 + 1) * 512], kc == 0, kc == KC - 1,
                                   ['wia', 'hTt'], PK(bi))
                            A(lambda e, bi=bi, j3=j3: e.copy(out=qkvtm[:, j3 * 512:(j3 + 1) * 512], in_=pb[bi][:]),
                              PK(bi), ['qkvtm'] + PK(bi))
                        if smp:
                            for s in range(16):
                                S.dma('sp', conv_s_o[s], qkvtm[s * 8 + 5:s * 8 + 8, :], reads=['qkvtm'], stream='o')
                        else:
                            S.dma('sp', conv_p_o, qkvtm[125:128, :], reads=['qkvtm'], stream='o')
                    if STG <= 1:
                        return
                    if smp:
                        S.dma('sp', cst, conv_s_in, writes=['cst'], stream='c')
                        for c in range(12):
                            bi = c // 8
                            cc = c % 8
                            TR(pb[bi][:, cc * 48:(cc + 1) * 48], cst[:, c * 128:(c + 1) * 128], ident[0:48, 0:48],
                               ['cst', 'ident'], PK(bi))
                        A(lambda e: e.copy(out=H['cbufS'][:, 0:8, :, 0:3],
                                           in_=pb[0][:, 0:384].rearrange("p (c s r) -> p c s r", c=8, s=16)),
                          PK(0), ['cbufS'] + PK(0))
                        A(lambda e: e.copy(out=H['cbufS'][:, 8:12, :, 0:3],
                                           in_=pb[1][:, 0:192].rearrange("p (c s r) -> p c s r", c=4, s=16)),
                          PK(1), ['cbufS'] + PK(1))
                        accv = acc[:].rearrange("p c (s t) -> p c s t", s=16)
                        ctv = ctmp[:].rearrange("p c (s t) -> p c s t", s=16)

                        def win(i):
                            return H['cbufS'][:, :, :, i:i + 8]

                        def cwb(i):
                            return cw[:, :, i:i + 1].unsqueeze(3).to_broadcast([128, 12, 16, 8])
                        ck = 'cbufS'
                    else:
                        accv = acc[:]
                        ctv = ctmp[:]

                        def win(i):
                            return H['cbuf'][:, :, i:i + 128]

                        def cwb(i):
                            return cw[:, :, i:i + 1].to_broadcast([128, 12, 128])
                        ck = 'cbuf'
                    V(lambda e: e.tensor_tensor(out=accv, in0=win(0), in1=cwb(0), op=ALU.mult), [ck, 'cw'], ['acc'])
                    for i in range(1, 4):
                        G(lambda e, i=i: e.tensor_tensor(out=ctv, in0=win(i), in1=cwb(i), op=ALU.mult),
                          [ck, 'cw'], ['ctmp'])
                        V(lambda e: e.tensor_tensor(out=accv, in0=accv, in1=ctv, op=ALU.add), ['acc', 'ctmp'], ['acc'])
                    if not smp:
                        V(lambda e: e.tensor_copy(out=H['cbuf'][:, :, 0:3], in_=H['cbuf'][:, :, 128:131]), ['cbuf'], ['cbuf'])
                    A(lambda e: e.activation(out=acc[:], in_=acc[:], func=AF.Silu), ['acc'], ['acc'])
                    A(lambda e: e.activation(out=sq8[:], in_=qkvs[:, 0:8, :], func=AF.Square), ['qkvs'], ['sq8'])
                    for b2 in range(2):
                        MM(pb[b2][:], ones_bf[:], sq8[:, b2 * 4:b2 * 4 + 4, :].rearrange("p h t -> p (h t)"), True, True,
                           ['sq8', 'ones_bf'], PK(b2))
                        A(lambda e, b2=b2: e.activation(out=rn8[:, b2 * 4:b2 * 4 + 4, :], in_=pbv(b2, 4), func=AF.Sqrt,
                                                        bias=EPS, scale=1.0), PK(b2), ['rn8'] + PK(b2))
                    V(lambda e: e.reciprocal(out=rn8[:], in_=rn8[:]), ['rn8'], ['rn8'])
                    V(lambda e: e.tensor_tensor(out=qkn[:], in0=qkvs[:, 0:8, :], in1=rn8[:], op=ALU.mult),
                      ['qkvs', 'rn8'], ['qkn'])
                    for h in range(4):
                        TR(pb[2][:, h * 128:(h + 1) * 128], qkn[:, 4 + h, :], ident[:], ['qkn', 'ident'], [('pbq', 2, h)])
                        TR(pb[3][:, h * 128:(h + 1) * 128], qkvs[:, 8 + h, :], ident[:], ['qkvs', 'ident'],
                           [('pbq', 3, h)])
                    A(lambda e: e.copy(out=ktm[:], in_=pbv(2, 4)), PK(2), ['ktm'] + PK(2))
                    A(lambda e: e.copy(out=vtm[:], in_=pbv(3, 4)), PK(3), ['vtm'] + PK(3))
                    A(lambda e: e.activation(out=gt[:, 0:4], in_=tm[:, 0:4], func=AF.Sigmoid), ['tm'], ['gt'])
                    V(lambda e: e.tensor_tensor(out=gt[:, 8:12], in0=tm[:, 4:8], in1=dtb[:], op=ALU.add),
                      ['tm', 'dtb', 'gt'], ['gt'])
                    A(lambda e: e.activation(out=gt[:, 8:12], in_=gt[:, 8:12], func=AF.Exp), ['gt'], ['gt'])
                    A(lambda e: e.activation(out=gt[:, 8:12], in_=gt[:, 8:12], func=AF.Ln, bias=1.0), ['gt'], ['gt'])
                    V(lambda e: e.tensor_tensor(out=gt[:, 4:8], in0=gt[:, 8:12], in1=negA[:], op=ALU.mult),
                      ['gt', 'negA'], ['gt'])
                    MM(pb[4][:, 0:4], M['tri'][:], gt[:, 4:8], True, True, [mk, 'gt'], PK(4))
                    A(lambda e: e.copy(out=gcc[:, 0:4], in_=pb[4][:, 0:4]), PK(4), ['gcc'] + PK(4))
                    MM(pb[4][:, 0:4], M['blk'][:], gt[:, 4:8], True, True, [mk, 'gt'], PK(4))
                    A(lambda e: e.copy(out=gcc[:, 4:8], in_=pb[4][:, 0:4]), PK(4), ['gcc'] + PK(4))
                    V(lambda e: e.tensor_tensor(out=G4[:], in0=M['tri'][:].unsqueeze(1).to_broadcast(f4),
                                                in1=gt[:, 4:8].unsqueeze(2).to_broadcast(f4), op=ALU.mult),
                      [mk, 'gt'], ['G4'])
                    MM(pb[5][:], ones_f[:], G4[:].rearrange("p h t -> p (h t)"), True, True, ['ones_f', 'G4'], PK(5))
                    A(lambda e: e.copy(out=gcr[:], in_=pbv(5, 4)), PK(5), ['gcr'] + PK(5))
                    V(lambda e: e.tensor_tensor(out=G4[:], in0=ident[:].unsqueeze(1).to_broadcast(f4),
                                                in1=gt[:, 0:4].unsqueeze(2).to_broadcast(f4), op=ALU.mult),
                      ['ident', 'gt', 'G4'], ['G4'])
                    MM(pb[6][:], ones_f[:], G4[:].rearrange("p h t -> p (h t)"), True, True, ['ones_f', 'G4'], PK(6))
                    A(lambda e: e.copy(out=br[:], in_=pbv(6, 4)), PK(6), ['br'] + PK(6))
                    V(lambda e: e.tensor_tensor(out=a1[:], in0=gcr[:], in1=gcc[:, 0:4].unsqueeze(2).to_broadcast(f4),
                                                op=ALU.subtract), ['gcr', 'gcc'], ['a1'])
                    V(lambda e: e.tensor_tensor(out=E1s[:], in0=a1[:], in1=M['nms'][:].unsqueeze(1).to_broadcast(f4),
                                                op=ALU.add), ['a1', mk], ['E1s'])
                    A(lambda e: e.activation(out=E1s[:], in_=E1s[:], func=AF.Exp), ['E1s'], ['E1s'])
                    V(lambda e: e.tensor_tensor(out=E1i[:], in0=a1[:], in1=M['nmi'][:].unsqueeze(1).to_broadcast(f4),
                                                op=ALU.add), ['a1', mk], ['E1i'])
                    A(lambda e: e.activation(out=E1i[:], in_=E1i[:], func=AF.Exp), ['E1i'], ['E1i'])
                    V(lambda e: e.tensor_tensor(out=a1[:], in0=gcc[:, 0:4].unsqueeze(2).to_broadcast(f4), in1=gcr[:],
                                                op=ALU.subtract), ['gcr', 'gcc', 'a1'], ['a1'])
                    V(lambda e: e.tensor_tensor(out=E2s[:], in0=a1[:], in1=M['nmsT'][:].unsqueeze(1).to_broadcast(f4),
                                                op=ALU.add), ['a1', mk], ['E2s'])
                    A(lambda e: e.activation(out=E2s[:], in_=E2s[:], func=AF.Exp), ['E2s'], ['E2s'])
                    V(lambda e: e.scalar_tensor_tensor(out=E1s[:], in0=E1s[:], scalar=-1.0, in1=br[:], op0=ALU.mult,
                                                       op1=ALU.mult), ['E1s', 'br'], ['E1s'])
                    V(lambda e: e.scalar_tensor_tensor(out=E2s[:], in0=E2s[:], scalar=-1.0,
                                                       in1=gt[:, 0:4].unsqueeze(2).to_broadcast(f4), op0=ALU.mult,
                                                       op1=ALU.mult), ['E2s', 'gt'], ['E2s'])
                    for h in range(4):
                        MM(pb[0][:, h * 128:(h + 1) * 128], qkn[:, 4 + h, :], qkn[:, 4 + h, :], True, True, ['qkn'],
                           [('pbq', 0, h)])
                        MM(pb[1][:, h * 128:(h + 1) * 128], qkn[:, 4 + h, :], qkn[:, h, :], True, True, ['qkn'],
                           [('pbq', 1, h)])
                    V(lambda e: e.tensor_tensor(out=Xa[0][:], in0=pbv(0, 4), in1=E1s[:], op=ALU.mult),
                      PK(0) + ['E1s'], [('Xa', 0)])
                    V(lambda e: e.tensor_tensor(out=XTa[0][:], in0=pbv(0, 4), in1=E2s[:], op=ALU.mult),
                      PK(0) + ['E2s'], [('XTa', 0)] + PK(0))
                    V(lambda e: e.scalar_tensor_tensor(out=itr[:], in0=pbv(1, 4), scalar=SCq, in1=E1i[:], op0=ALU.mult,
                                                       op1=ALU.mult), PK(1) + ['E1i'], ['itr'] + PK(1))
                    V(lambda e: e.tensor_tensor(out=Pm[:], in0=Xa[0][:], in1=ident[:].unsqueeze(1).to_broadcast(f4),
                                                op=ALU.add), [('Xa', 0), 'ident'], ['Pm'])
                    nlev = 2 if smp else 6
                    for lv in range(nlev):
                        a, b = lv % 2, (lv + 1) % 2
                        last = (lv == nlev - 1)
                        for h in range(4):
                            hs = slice(h * 128, (h + 1) * 128)
                            if not last:
                                MM(pb[2][:, hs], XTa[a][:, h, :], Xa[a][:, h, :], True, True, [('XTa', a), ('Xa', a)],
                                   [('pbq', 2, h)])
                            MM(pb[3][:, hs], Xa[a][:, h, :], XTa[a][:, h, :], True, True, [('XTa', a), ('Xa', a)],
                               [('pbq', 3, h)])
                        if not last:
                            A(lambda e, b=b: e.copy(out=Xa[b][:], in_=pbv(2, 4)), PK(2), [('Xa', b)] + PK(2))
                        A(lambda e, b=b: e.copy(out=XTa[b][:], in_=pbv(3, 4)), PK(3), [('XTa', b)] + PK(3))
                        for h in range(4):
                            hs = slice(h * 128, (h + 1) * 128)
                            MM(pb[4][:, hs], XTa[b][:, h, :], Pm[:, h, :], True, True, [('XTa', b), 'Pm'], [('pbq', 4, h)])
                        V(lambda e: e.tensor_tensor(out=Pm[:], in0=Pm[:], in1=pbv(4, 4), op=ALU.add),
                          ['Pm'] + PK(4), ['Pm'] + PK(4))
                    V(lambda e: e.tensor_tensor(out=vb[:], in0=vtm[:], in1=gt[:, 0:4].unsqueeze(2).to_broadcast(f4),
                                                op=ALU.mult), ['vtm', 'gt'], ['vb'])
                    A(lambda e: e.activation(out=gt[:, 8:12], in_=gcc[:, 0:4], func=AF.Exp), ['gcc', 'gt'], ['gt'])
                    V(lambda e: e.tensor_tensor(out=gt[:, 12:16], in0=gt[:, 8:12], in1=gt[:, 0:4], op=ALU.mult),
                      ['gt'], ['gt'])
                    V(lambda e: e.tensor_tensor(out=kbg[:], in0=ktm[:], in1=gt[:, 12:16].unsqueeze(2).to_broadcast(f4),
                                                op=ALU.mult), ['ktm', 'gt'], ['kbg'])
                    V(lambda e: e.tensor_tensor(out=gt[:, 8:12], in0=gcc[:, 4:8], in1=gcc[:, 0:4], op=ALU.subtract),
                      ['gcc', 'gt'], ['gt'])
                    A(lambda e: e.activation(out=gt[:, 8:12], in_=gt[:, 8:12], func=AF.Exp), ['gt'], ['gt'])
                    V(lambda e: e.tensor_tensor(out=kdec[:], in0=ktm[:], in1=gt[:, 8:12].unsqueeze(2).to_broadcast(f4),
                                                op=ALU.mult), ['ktm', 'gt'], ['kdec'])
                    for h in range(4):
                        hs = slice(h * 128, (h + 1) * 128)
                        MM(pb[5][:, hs], vb[:, h, :], Pm[:, h, :], True, True, ['vb', 'Pm'], [('pbq', 5, h)])
                        MM(pb[6][:, hs], kbg[:, h, :], Pm[:, h, :], True, True, ['kbg', 'Pm'], [('pbq', 6, h)])
                    A(lambda e: e.copy(out=uT[:], in_=pbv(5, 4)), PK(5), ['uT'] + PK(5))
                    A(lambda e: e.copy(out=wT[:], in_=pbv(6, 4)), PK(6), ['wT'] + PK(6))
                    A(lambda e: e.activation(out=a1[:], in_=gcr[:], func=AF.Exp), ['gcr', 'a1'], ['a1'])
                    V(lambda e: e.scalar_tensor_tensor(out=qg[:], in0=qkn[:, 0:4, :], scalar=SCq, in1=a1[:], op0=ALU.mult,
                                                       op1=ALU.mult), ['qkn', 'a1'], ['qg'])
                    if smp:
                        A(lambda e: e.activation(out=eglr[:], in_=gcr[:].rearrange("p h (s t) -> p h s t", t=8)[:, :, :, 7],
                                                 func=AF.Exp), ['gcr'], ['eglr'])
                        V(lambda e: e.memset(pb[7][:], 0.0), [], PK(7))
                        V(lambda e: e.memset(pb[3][:], 0.0), [], PK(3))
                        for s in range(16):
                            ss_ = slice(s * 8, (s + 1) * 8)
                            S.dma('sp', Sgs[:], gdn_s_in[s].rearrange("h k v -> k h v"), writes=['Sgs'], stream='s0')
                            for h in range(4):
                                MM(pb[7][:, h * 128 + s * 8:h * 128 + s * 8 + 8], Sgs[:, h, :], wT[:, h, ss_],
                                   False, False, ['Sgs', 'wT'], [('pbq', 7, h)])
                                MM(pb[3][:, h * 128 + s * 8:h * 128 + s * 8 + 8], Sgs[:, h, :], qg[:, h, ss_],
                                   False, False, ['Sgs', 'qg'], [('pbq', 3, h)], inc=(h == 3))
                    else:
                        A(lambda e: e.activation(out=eglr[:, :, 0], in_=gcr[:, :, 127], func=AF.Exp), ['gcr'], ['eglr'])
                        V(lambda e: e.memset(pb[3][:], 0.0), [], PK(3))
                        for h in range(4):
                            hs = slice(h * 128, (h + 1) * 128)
                            MM(pb[7][:, hs], Sg[:, h, :], wT[:, h, :], True, True, ['Sg', 'wT'], [('pbq', 7, h)])
                            MM(pb[3][:, hs], Sg[:, h, :], qg[:, h, :], False, False, ['Sg', 'qg'], [('pbq', 3, h)])
                    V(lambda e: e.tensor_tensor(out=vnT[:], in0=uT[:], in1=pbv(7, 4), op=ALU.subtract),
                      ['uT'] + PK(7), ['vnT'] + PK(7))
                    for h in range(4):
                        TR(pb[2][:, h * 128:(h + 1) * 128], vnT[:, h, :], ident[:], ['vnT', 'ident'], [('pbq', 2, h)])
                    A(lambda e: e.copy(out=vn[:], in_=pbv(2, 4)), PK(2), ['vn'] + PK(2))
                    for h in range(4):
                        hs = slice(h * 128, (h + 1) * 128)
                        MM(pb[3][:, hs], vn[:, h, :], itr[:, h, :], False, True, ['vn', 'itr'], [('pbq', 3, h)])
                    if smp:
                        for s in range(16):
                            V(lambda e, s=s: e.tensor_scalar(out=kdm[:], in0=kdec[:], scalar1=seqmask[:, s:s + 1],
                                                            scalar2=None, op0=ALU.mult), ['kdec', 'seqmask'], ['kdm'])
                            for h in range(4):
                                MM(pb[4][:, h * 128:(h + 1) * 128], kdm[:, h, :], vn[:, h, :], True, True, ['kdm', 'vn'],
                                   [('pbq', 4, h)])
                            S.dma('sp', Sgs[:], gdn_s_in[s].rearrange("h k v -> k h v"), writes=['Sgs'], stream='s0')
                            V(lambda e, s=s: e.tensor_tensor(out=Sgs[:], in0=Sgs[:],
                                                             in1=eglr[:, :, s:s + 1].to_broadcast(f4), op=ALU.mult),
                              ['Sgs', 'eglr'], ['Sgs'])
                            V(lambda e: e.tensor_tensor(out=Sgs[:], in0=Sgs[:], in1=pbv(4, 4), op=ALU.add),
                              ['Sgs'] + PK(4), ['Sgs'] + PK(4))
                            S.dma('sp', gdn_s_o[s].rearrange("h k v -> k h v"), Sgs[:], reads=['Sgs'], stream='o')
                    else:
                        for h in range(4):
                            MM(pb[4][:, h * 128:(h + 1) * 128], kdec[:, h, :], vn[:, h, :], True, True, ['kdec', 'vn'],
                               [('pbq', 4, h)])
                        V(lambda e: e.tensor_tensor(out=Sg[:], in0=Sg[:], in1=eglr[:, :, 0:1].to_broadcast(f4),
                                                    op=ALU.mult), ['Sg', 'eglr'], ['Sg'])
                        V(lambda e: e.tensor_tensor(out=Sg[:], in0=Sg[:], in1=pbv(4, 4), op=ALU.add),
                          ['Sg'] + PK(4), ['Sg'] + PK(4))
                        if t == NT - 1:
                            S.dma('sp', gdn_p_o.rearrange("h k v -> k h v"), Sg[:], reads=['Sg'], stream='o')
                    A(lambda e: e.activation(out=osq[:], in_=pbv(3, 4), func=AF.Square), PK(3), ['osq'])
                    MM(pb[0][:], ones_bf[:], osq[:].rearrange("p h t -> p (h t)"), True, True, ['osq', 'ones_bf'], PK(0))
                    A(lambda e: e.activation(out=rno[:], in_=pb[0][:], func=AF.Sqrt, bias=EPS, scale=1.0 / 128),
                      PK(0), ['rno'] + PK(0))
                    V(lambda e: e.reciprocal(out=rno[:], in_=rno[:]), ['rno'], ['rno'])
                    V(lambda e: e.scalar_tensor_tensor(out=ona[:], in0=pbv(3, 4), scalar=gdnn[:, 0:1],
                                                       in1=rno[:].rearrange("p (h t) -> p h t", h=4), op0=ALU.mult,
                                                       op1=ALU.mult), PK(3) + ['gdnn', 'rno'], ['ona'] + PK(3))
                    G(lambda e: e.tensor_tensor(out=onb[:, 0:4, :], in0=ona[:], in1=zs[:], op=ALU.mult),
                      ['ona', 'zs'], [('onb', 0)])
                    if STG <= 2:
                        return
                    A(lambda e: e.activation(out=rt[:, 0:4, :].rearrange("p a b -> p (a b)"), in_=tm[:, 8:264],
                                             func=AF.Square, accum_out=ssn[:, 0:1]), ['tm'], ['rt', 'ssn'])
                    A(lambda e: e.activation(out=rt[:, 4:8, :].rearrange("p a b -> p (a b)"), in_=tm[:, 264:520],
                                             func=AF.Square, accum_out=ssn[:, 1:2]), ['tm'], ['rt', 'ssn'])
                    A(lambda e: e.activation(out=ssn[:, 2:4], in_=ssn[:, 0:2], func=AF.Sqrt, bias=EPS, scale=1.0 / 256),
                      ['ssn'], ['ssn'])
                    V(lambda e: e.reciprocal(out=ssn[:, 2:4], in_=ssn[:, 2:4]), ['ssn'], ['ssn'])
                    V(lambda e: e.scalar_tensor_tensor(out=cqn[:], in0=tm[:, 8:264], scalar=ssn[:, 2:3], in1=qnb[:],
                                                       op0=ALU.mult, op1=ALU.mult), ['tm', 'ssn', 'qnb'], ['cqn'])
                    V(lambda e: e.scalar_tensor_tensor(out=ckvn[:], in0=tm[:, 264:520], scalar=ssn[:, 3:4], in1=kvnb[:],
                                                       op0=ALU.mult, op1=ALU.mult), ['tm', 'ssn', 'kvnb'], ['ckvn'])
                    lat_dst = lat_s_o if smp else lat_p_o[t * 128:(t + 1) * 128, :]
                    kpe_dst = kpe_s_o if smp else kpe_p_o[t * 128:(t + 1) * 128, :]
                    S.dma('sp', lat_dst, ckvn[:], reads=['ckvn'], stream='o')
                    cs = cosb[:, t, :]
                    sn = sinb[:, t, :]
                    x1 = tm[:, 520:536]
                    x2 = tm[:, 536:552]
                    V(lambda e: e.tensor_tensor(out=kst[:, 0:16], in0=x1, in1=cs, op=ALU.mult), ['tm', 'cosb', 'kst'], ['kst'])
                    V(lambda e: e.tensor_tensor(out=rt[:, 0, 0:16], in0=x2, in1=sn, op=ALU.mult), ['tm', 'sinb', 'rt'], ['rt'])
                    V(lambda e: e.tensor_tensor(out=kst[:, 0:16], in0=kst[:, 0:16], in1=rt[:, 0, 0:16], op=ALU.subtract),
                      ['kst', 'rt'], ['kst'])
                    V(lambda e: e.tensor_tensor(out=kst[:, 16:32], in0=x2, in1=cs, op=ALU.mult), ['tm', 'cosb', 'kst'], ['kst'])
                    V(lambda e: e.tensor_tensor(out=rt[:, 0, 16:32], in0=x1, in1=sn, op=ALU.mult), ['tm', 'sinb', 'rt'], ['rt'])
                    V(lambda e: e.tensor_tensor(out=kst[:, 16:32], in0=kst[:, 16:32], in1=rt[:, 0, 16:32], op=ALU.add),
                      ['kst', 'rt'], ['kst'])
                    S.dma('sp', kpe_dst, kst[:, 0:32], reads=['kst'], stream='o')
                    if STG <= 3:
                        return
                    for c in range(2):
                        TR(pb[6][:, c * 128:(c + 1) * 128], cqn[:, c * 128:(c + 1) * 128], ident[:], ['cqn', 'ident'],
                           [('pbq', 6, c)])
                        TR(pb[6][:, 256 + c * 128:256 + (c + 1) * 128], ckvn[:, c * 128:(c + 1) * 128], ident[:],
                           ['ckvn', 'ident'], [('pbq', 6, 2 + c)])
                    TR(pb[7][:, 0:128], kst[:], ident[:], ['kst', 'ident'], [('pbq', 7, 0)])
                    A(lambda e: e.copy(out=cqnT[:], in_=pb[6][:, 0:256].rearrange("p (c t) -> p c t", c=2)),
                      PK(6), ['cqnT'])
                    A(lambda e: e.copy(out=ckvnT[:], in_=pb[6][:, 256:512].rearrange("p (c t) -> p c t", c=2)),
                      PK(6), ['ckvnT'] + PK(6))
                    kcols = tile_cols(t)
                    kkey = ('KT', t)
                    if not smp:
                        A(lambda e: e.copy(out=H['KTp'][:, kcols], in_=pb[7][0:32, 0:128]), PK(7), [kkey] + PK(7))
                    for (bi, c0, cn) in ((0, 0, 512), (1, 512, 256)):
                        for c in range(2):
                            MM(pb[bi][:, :cn], cqnT[:, c, :], wuq[:, c, c0:c0 + cn], c == 0, c == 1, ['cqnT', 'wuq'], PK(bi))
                        A(lambda e, bi=bi, c0=c0, cn=cn: e.copy(out=qtm[:].rearrange("p h d -> p (h d)")[:, c0:c0 + cn],
                                                                in_=pb[bi][:, :cn]), PK(bi), ['qtm'] + PK(bi))
                    csb = cs.unsqueeze(1).to_broadcast([128, 8, 16])
                    snb = sn.unsqueeze(1).to_broadcast([128, 8, 16])
                    q1 = qtm[:, :, 64:80]
                    q2 = qtm[:, :, 80:96]
                    V(lambda e: e.tensor_copy(out=qn2[:].rearrange("p m (a d) -> p (m a) d", a=2), in_=qtm[:, :, 0:64]),
                      ['qtm'], ['qtb'])
                    V(lambda e: e.tensor_tensor(out=rt[:, :, 0:16], in0=q1, in1=csb, op=ALU.mult), ['qtm', 'cosb', 'rt'], ['rt'])
                    V(lambda e: e.tensor_tensor(out=rt[:, :, 16:32], in0=q2, in1=snb, op=ALU.mult), ['qtm', 'sinb', 'rt'], ['rt'])
                    V(lambda e: e.tensor_tensor(out=qp2[:, :, 0:16], in0=rt[:, :, 0:16], in1=rt[:, :, 16:32],
                                                op=ALU.subtract), ['rt', 'qtb'], ['qtb'])
                    V(lambda e: e.tensor_tensor(out=rt[:, :, 32:48], in0=q2, in1=csb, op=ALU.mult), ['qtm', 'cosb', 'rt'], ['rt'])
                    V(lambda e: e.tensor_tensor(out=rt[:, :, 48:64], in0=q1, in1=snb, op=ALU.mult), ['qtm', 'sinb', 'rt'], ['rt'])
                    V(lambda e: e.tensor_tensor(out=qp2[:, :, 16:32], in0=rt[:, :, 32:48], in1=rt[:, :, 48:64],
                                                op=ALU.add), ['rt', 'qtb'], ['qtb'])
                    for m in range(4):
                        TR(pbbf(2)[:, m * 128:(m + 1) * 128], qn2[:, m, :], ident_bf[:], ['qtb', 'ident_bf'],
                           [('pbq', 2, m // 2)])
                    for h in range(8):
                        TR(pbbf(1)[0:32, h * 128:(h + 1) * 128], qp2[:, h, :], ident_bf[:], ['qtb', 'ident_bf'],
                           [('pbq', 1, h // 2)])
                    A(lambda e: e.copy(out=QTn[:].rearrange("p m t -> p (m t)"), in_=pbbf(2)[:, 0:512]), PK(2),
                      ['QT'] + PK(2))
                    A(lambda e: e.copy(out=QTp[:].rearrange("p h t -> p (h t)"), in_=pbbf(1)[0:32, :]), PK(1),
                      ['QT'] + PK(1))
                    if not smp:
                        for m in range(4):
                            for c in range(2):
                                MM(pb[3][:, m * 128:(m + 1) * 128], wuk[:, c, m * 128:(m + 1) * 128], ckvnT[:, c, :],
                                   c == 0, c == 1, ['wuk', 'ckvnT'], [('pbq', 3, m)])
                        A(lambda e: e.copy(out=H['KTn'][:, :, kcols], in_=pbv(3, 4)), PK(3), [kkey] + PK(3))
                        for c in range(2):
                            MM(pb[5][:], ckvnT[:, c, :], wuv[:, c, :], c == 0, c == 1, ['ckvnT', 'wuv'], PK(5))
                        A(lambda e: e.copy(out=H['Vt'][:, t, :], in_=pb[5][:]), PK(5), [('Vt', t)] + PK(5))
                    if STG <= 4:
                        return
                    if not smp:
                        nkt = t + 1
                        items = [(h, g0) for h in range(8) for g0 in range(0, nkt, 4)]

                        def qk(idx_):
                            h, g0 = items[idx_]
                            rh = (h % 2) * 64
                            gn = min(4, nkt - g0)
                            sbk = 6 + idx_ % 2
                            for j in range(gn):
                                kt = g0 + j
                                MM(pb[sbk][:, j * 128:(j + 1) * 128], H['KTn'][rh:rh + 64, h // 2, kt * 128:(kt + 1) * 128],
                                   QTn[rh:rh + 64, h // 2, :], True, False, [('KT', kt), 'QT'], [('pbq', sbk, j)], inc=True)
                                MM(pb[sbk][:, j * 128:(j + 1) * 128], H['KTp'][:, kt * 128:(kt + 1) * 128],
                                   QTp[:, h, :], False, True, [('KT', kt), 'QT'], [('pbq', sbk, j)], inc=True)

                        def pv(idx_):
                            h, g0 = items[idx_]
                            ob = h // 4
                            hs = slice((h % 4) * 128, (h % 4 + 1) * 128)
                            pr = (h // 2) * 128
                            rh = (h % 2) * 64
                            gn = min(4, nkt - g0)
                            sbk = 6 + idx_ % 2
                            pi = idx_ % 2
                            A(lambda e: e.activation(out=H['PT'][pi][:, :gn * 128], in_=pb[sbk][:, :gn * 128], func=AF.Exp,
                                                     scale=SCm), PK(sbk), [('PT', pi)] + PK(sbk))
                            if g0 + gn == nkt:
                                jd = gn - 1
                                G(lambda e: e.tensor_tensor(out=H['PT'][pi][:, jd * 128:(jd + 1) * 128],
                                                            in0=H['PT'][pi][:, jd * 128:(jd + 1) * 128],
                                                            in1=H['M']['cmask'][:], op=ALU.mult),
                                  [('PT', pi), 'maskP'], [('PT', pi)])
                            for j in range(gn):
                                kt = g0 + j
                                first = (kt == 0)
                                lastk = (kt == nkt - 1)
                                MM(pb[ob][:, hs], H['Vt'][:, kt, pr:pr + 128], H['PT'][pi][:, j * 128:(j + 1) * 128], first, lastk,
                                   [('Vt', kt), ('PT', pi)], [('pbq', ob, h % 4)], inc=True)
                                MM(pb[2 + ob][:, hs], ones_bf[:], H['PT'][pi][:, j * 128:(j + 1) * 128], first, lastk,
                                   [('PT', pi), 'ones_bf'], [('pbq', 2 + ob, h % 4)], inc=True)
                            if g0 + gn == nkt:
                                V(lambda e: e.reciprocal(out=H['rsum'][rh:rh + 64, :], in_=pb[2 + ob][rh:rh + 64, hs]),
                                  [('pbq', 2 + ob, h % 4)], ['rsum', ('pbq', 2 + ob, h % 4)])
                                V(lambda e: e.tensor_tensor(out=onb[rh:rh + 64, 4 + h // 2, :], in0=pb[ob][rh:rh + 64, hs],
                                                            in1=H['rsum'][rh:rh + 64, :], op=ALU.mult),
                                  [('pbq', ob, h % 4), 'rsum'], [('onb', 1), ('pbq', ob, h % 4)])
                        if not cfg.get('lookahead'):
                            for i_ in range(len(items)):
                                qk(i_)
                                pv(i_)
                        else:
                            qk(0)
                            for i_ in range(len(items)):
                                if i_ + 1 < len(items):
                                    qk(i_ + 1)
                                pv(i_)
                    else:
                        sample_attention()
                    if STG <= 5:
                        return
                    for dc in range(KC):
                        bi = 4 + dc // 4
                        for kc in range(KC):
                            MM(pb[bi][:, (dc % 4) * 128:(dc % 4 + 1) * 128], woa[:, kc, dc * 128:(dc + 1) * 128],
                               onb[:, kc, :], kc == 0, kc == KC - 1, ['woa', ('onb', 0), ('onb', 1)],
                               [('pbq', bi, dc % 4)])
                    for b2 in range(2):
                        V(lambda e, b2=b2: e.tensor_tensor(out=xTt[:, b2 * 4:b2 * 4 + 4, :], in0=xTt[:, b2 * 4:b2 * 4 + 4, :],
                                                           in1=pbv(4 + b2, 4), op=ALU.add),
                          PK(4 + b2) + ['xTt'], ['xTt'] + PK(4 + b2))

                with ExitStack() as sph:
                    ssb = mk_sb(sph)
                    H['ssb'] = ssb
                    H['M'] = build_masks('S', True, ssb)
                    H['cbufS'] = ssb("cbufS", [128, 12, 16, 11])
                    H['ssb2'] = ssb
                    tile_body(NT)
                    S.dma('sp', xsp[NT], xTt[:], reads=['xTt'], stream='xs')
                    S.barrier()
                with ExitStack() as pph:
                    qsb = mk_sb(pph)
                    H['M'] = build_masks('P', False, qsb)
                    H['cbuf'] = qsb("cbuf", [128, 12, 131])
                    H['KTn'] = qsb("KTn", [128, 4, TP], BF16)
                    H['KTp'] = qsb("KTp", [32, TP], BF16)
                    H['Vt'] = qsb("Vt", [128, NT, 512], BF16)
                    H['PT'] = [qsb("PT%d" % i, [128, 512], BF16) for i in range(2)]
                    H['rsum'] = qsb("rsum", [128, 128])
                    G(lambda e: e.memset(H['cbuf'][:], 0.0), [], ['cbuf'])
                    for t in range(NT):
                        tile_body(t)
                        S.dma('sp', xsp[t], xTt[:], reads=['xTt'], stream='xs')
                    S.barrier()

        def fold_x():
            for t in range(NT + 1):
                ks = [S.lastw.get(('xTd', dc, t)) for dc in range(KC)]
                ks = [k for k in ks if k is not None]
                if ks:
                    S.lastw[('xT', t)] = max(ks, key=lambda tk: tk[1])
                    S.reads[('xT', t)] = {}

        if 'A' in PHASES:
            ab_layer()
            xTh[0] = sb("xT", [128, KC, TT], F32)
            for t in range(NT + 1):
                S.dma('sp', xTh[0][:, :, tile_cols(t)], xsp[t], writes=[('xT', t)], stream='x')
        else:
            xTh[0] = sb("xT", [128, KC, TT], F32)
            phase0()
        xT = xTh[0]

        for ph_ in PHASES:
            if ph_ == 'B':
                mlp_layer(0)
                fold_x()
            elif ph_ == 'D':
                mlp_layer(1)
                fold_x()
            elif ph_ == 'C':
                hgrn_layer()

        gfin = sb("gfin", [128, D], F32)
        S.dma('sp', gfin[:], final_norm.partition_broadcast(128), writes=['gfin'], stream='c')
        yo = [sb("yo%d" % i, [128, D], F32) for i in range(2)]
        ss2 = [sb("ss2_%d" % i, [128, 4], F32) for i in range(2)]
        junk = sb("junk", [128, 512], F32)
        for t in range(NT + 1):
            b = t % 2
            for half in range(2):
                pi = 6 + half
                for j in range(4):
                    kc = half * 4 + j
                    TR(pb[pi][:, j * 128:(j + 1) * 128], xT[:, kc, tile_cols(t)], ident[:],
                       [('xT', t), 'ident'], [('pbq', pi, j)])
                A(lambda e, half=half, pi=pi: e.activation(out=junk[:], in_=pb[pi][:], func=AF.Square,
                                                           accum_out=ss2[b][:, half:half + 1]),
                  [('pbq', pi, j) for j in range(4)], ['junk', ('ss2', b, half)])
            V(lambda e: e.tensor_tensor(out=ss2[b][:, 2:3], in0=ss2[b][:, 0:1], in1=ss2[b][:, 1:2], op=ALU.add),
              [('ss2', b, 0), ('ss2', b, 1)], [('ss2', b, 2)])
            A(lambda e: e.activation(out=ss2[b][:, 3:4], in_=ss2[b][:, 2:3], func=AF.Sqrt, bias=EPS,
                                     scale=1.0 / D), [('ss2', b, 2)], [('ss2', b, 3)])
            V(lambda e: e.reciprocal(out=ss2[b][:, 3:4], in_=ss2[b][:, 3:4]), [('ss2', b, 3)], [('ss2', b, 3)])
            for half in range(2):
                pi = 6 + half
                V(lambda e, half=half, pi=pi: e.scalar_tensor_tensor(
                    out=yo[b][:, half * 512:(half + 1) * 512], in0=pb[pi][:], scalar=ss2[b][:, 3:4],
                    in1=gfin[:, half * 512:(half + 1) * 512], op0=ALU.mult, op1=ALU.mult),
                    [('pbq', pi, j) for j in range(4)] + [('ss2', b, 3), 'gfin'],
                    [('yo', b, half)] + [('pbq', pi, j) for j in range(4)])
            dst = y_s if t == NT else y_p[t * 128:(t + 1) * 128, :]
            S.dma('sp', dst, yo[b][:], reads=[('yo', b, 0), ('yo', b, 1)], stream='y')
        S.finish('sp')
        dl = S.check_deadlock()
        print("instructions:", S.n_inst, "deadlock:", dl)
        assert not dl, "semaphore deadlock in the emitted program"
    return nc


def core_inputs(cfg, c, inp):
    m = {}
    m["x_p"] = np.ascontiguousarray(inp['x_prompt'][c])
    m["x_s"] = np.ascontiguousarray(inp['x_sample'][16 * c:16 * c + 16].reshape(128, D))
    m["w_up"] = inp['w_up']
    m["w_down"] = inp['w_down']
    m["mlp_norm_fm"] = np.ascontiguousarray(inp['mlp_norm'].reshape(2, KC, 128).transpose(0, 2, 1))
    m["mix_norm_fm"] = np.ascontiguousarray(inp['mix_norm'].reshape(2, KC, 128).transpose(0, 2, 1))
    m["final_norm"] = np.ascontiguousarray(inp['final_norm'].reshape(1, D))
    m["w_in_c"] = inp['w_in_c'][0]
    m["w_out_c"] = inp['w_out_c'][0]
    m["lb_fm"] = np.ascontiguousarray(inp['lb_logits_c'].reshape(2, 8, 128).transpose(0, 2, 1))
    m["g_norm_c"] = np.ascontiguousarray(inp['g_norm_c'][0].reshape(128, 1))
    m["hgrn_in"] = np.ascontiguousarray(inp['state_hgrn'][0, 16 * c:16 * c + 16])
    m["w_in_ab"] = inp['w_in_ab'][0]
    m["w_out_ab"] = inp['w_out_ab'][0]
    m["w_uq"] = np.ascontiguousarray(inp['w_uq_ab'][0].reshape(256, 768))
    m["w_uk"] = np.ascontiguousarray(inp['w_uk_ab'][0].reshape(256, 512))
    m["w_uv"] = np.ascontiguousarray(inp['w_uv_ab'][0].reshape(256, 512))
    m["conv_w_fm"] = np.ascontiguousarray(inp['conv_w_ab'][0].reshape(4, 12, 128).transpose(2, 1, 0))
    m["a_log"] = np.ascontiguousarray(inp['a_log_ab'].reshape(1, 4))
    m["dt_bias"] = np.ascontiguousarray(inp['dt_bias_ab'].reshape(1, 4))
    m["gdn_norm"] = np.ascontiguousarray(inp['gdn_norm_ab'].reshape(128, 1))
    m["q_norm"] = np.ascontiguousarray(inp['q_norm_ab'].reshape(1, 256))
    m["kv_norm"] = np.ascontiguousarray(inp['kv_norm_ab'].reshape(1, 256))
    TP = cfg['NT'] * 128
    past = inp['page_table'].shape[1] * inp['cache_mla_latent'].shape[2]
    pos = np.concatenate([np.arange(TP), np.tile(past + np.arange(8), 16)]).astype(np.float32)
    inv = (np.float32(10000.0) ** (-np.arange(16, dtype=np.float32) / np.float32(16))).astype(np.float32)
    ang = (pos[:, None] * inv[None, :]).astype(np.float32)
    m["rope_cos"] = np.cos(ang).astype(np.float32)
    m["rope_sin"] = np.sin(ang).astype(np.float32)
    NPG = cfg['NPG']
    R = NPG
    SB = 128 // R
    RG = min(8, R)
    NG = R // RG
    m["cache_lat"] = inp['cache_mla_latent'][0]
    m["cache_kr"] = inp['cache_mla_krope'][0]
    m["page_table"] = np.ascontiguousarray(inp['page_table'][16 * c:16 * c + 16]).astype(np.int32)
    wkt = np.zeros((2, 64, 8, 256), np.float32)
    wu = inp['w_uk_ab'][0]
    for h in range(8):
        wkt[h % 2, :, h, :] = wu[:, h, :].T
    m["w_ukT"] = wkt.reshape(128, 2048)
    p = np.arange(128)
    m["rep_in"] = (p[None, :] // SB == np.arange(NPG)[:, None]).astype(np.float32)
    gbase = np.zeros((128, NG + 1), np.float32)
    for g in range(NG):
        gbase[:, g] = (p % SB) * (R // RG) + g
    gbase[:, NG] = p % SB
    m["gbase_in"] = gbase
    tq = np.arange(64) % 8
    m["cm8_in"] = ((p % 8)[:, None] <= tq[None, :]).astype(np.float32)
    m["conv_s_in"] = np.ascontiguousarray(inp['state_gdn_conv'][0, 16 * c:16 * c + 16].reshape(48, 1536))
    m["gdn_s_in"] = np.ascontiguousarray(inp['state_gdn'][0, 16 * c:16 * c + 16])
    return m


def kernel(**inputs):
    inp = {k: np.asarray(v) for k, v in inputs.items()}
    B, SEQ = inp['x_prompt'].shape[:2]
    cfg = {'NT': SEQ // 128, 'NPG': inp['page_table'].shape[1], 'NPOOL': inp['cache_mla_latent'].shape[1]}
    nc = build(cfg)
    in_maps = [core_inputs(cfg, c, inp) for c in range(NCORES)]
    res = run_bass_kernel_spmd(nc, in_maps, core_ids=list(range(NCORES)))
    r = res.results

    def stk(name):
        return np.stack([r[c][name] for c in range(NCORES)], 0)

    def cat(name):
        return np.concatenate([r[c][name] for c in range(NCORES)], 0)
    y_p = stk("y_p")
    y_s = cat("y_s").reshape(16 * NCORES, 8, D)
    return (y_p, y_s,
            stk("gdn_p")[None], stk("conv_p")[None], stk("lat_p")[None], stk("kpe_p")[None], stk("hgrn_p")[None],
            cat("gdn_s")[None], cat("conv_s")[None], cat("lat_s").reshape(1, 16 * NCORES, 8, 256),
            cat("kpe_s").reshape(1, 16 * NCORES, 8, 32), cat("hgrn_s")[None])
```

```python
from contextlib import ExitStack

import numpy as np
import concourse.bass as bass
import concourse.mybir as mybir
from concourse.bass_utils import run_bass_kernel_spmd

F32 = mybir.dt.float32
BF16 = mybir.dt.bfloat16
I32 = mybir.dt.int32
AF = mybir.ActivationFunctionType
ALU = mybir.AluOpType
AX = mybir.AxisListType

D = 1024
DFF = 4096
KC = 8
EPS = 1e-6
NCORES = 8


class Sched:
    def __init__(self, nc, stack, ndma=4):
        self.nc = nc
        self.stack = stack
        self.eng = {'pe': nc.tensor, 'act': nc.scalar, 'dve': nc.vector,
                    'pool': nc.gpsimd, 'sp': nc.sync}
        self.sem = {}
        self.cnt = {}
        self.seen = {e: {} for e in self.eng}
        self.lastw = {}
        self.reads = {}
        self.ndma = ndma
        self.dma_rr = {}
        self.n_inst = 0
        self.alias_map = {}
        self.log = {e: [] for e in self.eng}
        for e in ('pe', 'act', 'dve', 'pool'):
            self._mk(e)

    def _mk(self, q):
        self.sem[q] = self.stack.enter_context(self.nc.semaphore("s_" + q))
        self.cnt[q] = 0

    def _wait(self, e, q, c):
        if self.seen[e].get(q, 0) >= c:
            return
        self.seen[e][q] = c
        self.log[e].append(('w', q, c))
        self.eng[e].wait_ge(self.sem[q], c)

    def alias(self, a, canon):
        self.alias_map[a] = canon

    def _canon(self, keys):
        return [self.alias_map.get(k, k) for k in keys]

    def _deps(self, e, myq, reads, writes, same_war=False, skip_waw_q=None):
        reads = self._canon(reads)
        writes = self._canon(writes)
        need = {}

        def add(t):
            if t is None:
                return
            q, c = t
            if need.get(q, 0) < c:
                need[q] = c
        for k in reads:
            add(self.lastw.get(k))
        for k in writes:
            t = self.lastw.get(k)
            if not (t is not None and skip_waw_q is not None and t[0] == skip_waw_q):
                add(t)
            for q, c in self.reads.get(k, {}).items():
                if q == myq and not same_war:
                    continue
                add((q, c))
        for q, c in need.items():
            self._wait(e, q, c)

    def _commit(self, ticket, reads, writes):
        reads = self._canon(reads)
        writes = self._canon(writes)
        q, c = ticket
        for k in reads:
            d = self.reads.setdefault(k, {})
            if d.get(q, 0) < c:
                d[q] = c
        for k in writes:
            self.lastw[k] = ticket
            self.reads[k] = {}

    def op(self, e, fn, reads=(), writes=(), inc=True, pe_acc=False):
        self._deps(e, e, reads, writes, skip_waw_q=('pe' if pe_acc else None))
        ins = fn(self.eng[e])
        self.n_inst += 1
        ticket = (e, self.cnt[e] + 1)
        if inc:
            ins.then_inc(self.sem[e], 1)
            self.cnt[e] += 1
            self.log[e].append(('i', e, 1))
        self._commit(ticket, reads, writes)
        return ins

    def dma(self, e, out, in_, reads=(), writes=(), stream='d', nrr=None, **kw):
        return self.dma_custom(e, lambda eng: eng.dma_start(out=out, in_=in_, **kw),
                               reads, writes, stream, nrr)

    def dma_custom(self, e, fn, reads=(), writes=(), stream='g', nrr=None):
        i = self.dma_rr.get(stream, 0)
        self.dma_rr[stream] = i + 1
        q = "dma_%s_%d" % (stream, i % (nrr or {'w': 12, 'c': 12}.get(stream, self.ndma)))
        if q not in self.sem:
            self._mk(q)
        self._deps(e, q, reads, writes, same_war=True)
        ins = fn(self.eng[e])
        ins.then_inc(self.sem[q], 16)
        self.n_inst += 1
        self.cnt[q] += 16
        self.log[e].append(('i', q, 16))
        self._commit((q, self.cnt[q]), reads, writes)
        return ins

    def barrier(self):
        for e in self.eng:
            for q, c in self.cnt.items():
                if c > 0:
                    self._wait(e, q, c)

    def check_deadlock(self):
        pos = {e: 0 for e in self.log}
        val = {}
        progress = True
        while progress:
            progress = False
            for e, lg in self.log.items():
                while pos[e] < len(lg):
                    k, q, c = lg[pos[e]]
                    if k == 'w':
                        if val.get(q, 0) >= c:
                            pos[e] += 1
                            progress = True
                        else:
                            break
                    else:
                        val[q] = val.get(q, 0) + c
                        pos[e] += 1
                        progress = True
        stuck = {e: (pos[e], lg[pos[e]], val.get(lg[pos[e]][1], 0)) for e, lg in self.log.items()
                 if pos[e] < len(lg)}
        return stuck

    def finish(self, e='sp'):
        for q, c in self.cnt.items():
            if c > 0:
                self._wait(e, q, c)


def blocks(total, step):
    return [(s, min(step, total - s)) for s in range(0, total, step)]


def build(cfg):
    NT = cfg['NT']
    TP = 128 * NT
    TT = TP + 128
    PHASES = cfg.get('phases', 'ABCD')
    nc = bass.Bass("TRN2", target_bir_lowering=False)

    def din(name, shape, dt=F32):
        return nc.dram_tensor(name, list(shape), dt, kind="ExternalInput").ap()

    def dout(name, shape):
        return nc.dram_tensor(name, list(shape), F32, kind="ExternalOutput").ap()

    x_p = din("x_p", [TP, D])
    x_s = din("x_s", [128, D])
    w_up = din("w_up", [2, D, DFF])
    w_down = din("w_down", [2, DFF, D])
    mlp_norm_fm = din("mlp_norm_fm", [2, 128, KC])
    mix_norm_fm = din("mix_norm_fm", [2, 128, KC])
    final_norm = din("final_norm", [1, D])
    w_in_c = din("w_in_c", [D, 4096])
    w_out_c = din("w_out_c", [D, D])
    lb_fm = din("lb_fm", [2, 128, 8])
    g_norm_c = din("g_norm_c", [128, 1])
    hgrn_in = din("hgrn_in", [16, 8, 128, 128])
    w_in_ab = din("w_in_ab", [D, 2600])
    w_out_ab = din("w_out_ab", [D, D])
    w_uq = din("w_uq", [256, 768])
    w_uk = din("w_uk", [256, 512])
    w_uv = din("w_uv", [256, 512])
    conv_w_fm = din("conv_w_fm", [128, 12, 4])
    a_log = din("a_log", [1, 4])
    dt_bias = din("dt_bias", [1, 4])
    gdn_norm = din("gdn_norm", [128, 1])
    q_norm = din("q_norm", [1, 256])
    kv_norm = din("kv_norm", [1, 256])
    rope_cos = din("rope_cos", [TT, 16])
    rope_sin = din("rope_sin", [TT, 16])
    conv_s_in = din("conv_s_in", [48, 1536])
    gdn_s_in = din("gdn_s_in", [16, 4, 128, 128])
    NPG = cfg['NPG']
    NPOOL = cfg['NPOOL']
    NGRP = NPG // min(8, NPG)
    cache_lat = din("cache_lat", [NPOOL, 128, 256])
    cache_kr = din("cache_kr", [NPOOL, 128, 32])
    page_table = din("page_table", [16, NPG], I32)
    w_ukT = din("w_ukT", [128, 2048])
    rep_in = din("rep_in", [NPG, 128])
    gbase_in = din("gbase_in", [128, NGRP + 1])
    cm8_in = din("cm8_in", [128, 64])
    NPG = cfg['NPG']
    if cfg.get('dbg_s'):
        dbg_mst = dout("dbg_mst", [128, 8])
        dbg_oacc = dout("dbg_oacc", [64, 257])
        dbg_lb = nc.dram_tensor("dbg_lb", [128, min(8, NPG) * 256], BF16, kind="ExternalOutput").ap()
        dbg_kb = nc.dram_tensor("dbg_kb", [128, NPG * 32], BF16, kind="ExternalOutput").ap()
        dbg_olat = dout("dbg_olat", [64, 256])
    gdn_p_o = dout("gdn_p", [4, 128, 128])
    conv_p_o = dout("conv_p", [3, 1536])
    lat_p_o = dout("lat_p", [TP, 256])
    kpe_p_o = dout("kpe_p", [TP, 32])
    gdn_s_o = dout("gdn_s", [16, 4, 128, 128])
    conv_s_o = dout("conv_s", [16, 3, 1536])
    lat_s_o = dout("lat_s", [128, 256])
    kpe_s_o = dout("kpe_s", [128, 32])
    xsp = nc.dram_tensor("xsp", [NT + 1, 128, KC, 128], F32, kind="Internal").ap()
    y_p = dout("y_p", [TP, D])
    y_s = dout("y_s", [128, D])
    hgrn_p = dout("hgrn_p", [8, 128, 128])
    hgrn_s = dout("hgrn_s", [16, 8, 128, 128])

    with ExitStack() as st:
        S = Sched(nc, st, ndma=4)

        uniq = [0]

        def mk_sb(stack):
            uniq[0] += 1
            pfx = "p%d_" % uniq[0]

            def sb(name, shape, dt=F32):
                nb = int(np.prod(shape[1:])) * (2 if dt == BF16 else 4)
                if cfg.get('memlog'):
                    print("  alloc %-10s %7d B" % (pfx + name, nb))
                return stack.enter_context(nc.sbuf_tensor(pfx + name, list(shape), dt))
            return sb
        sb = mk_sb(st)

        pb = [st.enter_context(nc.psum_tensor("pb%d" % i, [128, 512], F32)) for i in range(8)]

        def PK(i):
            return [('pbq', i, j) for j in range(4)]

        def pbv(i, a):
            return pb[i][:].rearrange("p (a b) -> p a b", a=a)

        def pbbf(i):
            return pb[i][:].bitcast(BF16)

        def A(fn, r, w):
            return S.op('act', fn, r, w)

        def V(fn, r, w):
            return S.op('dve', fn, r, w)

        def G(fn, r, w):
            return S.op('pool', fn, r, w)

        def MM(out, lhsT, rhs, start, stop, r, w, inc=None):
            return S.op('pe', lambda e: e.matmul(out, lhsT, rhs, start=start, stop=stop), r, w,
                        inc=(stop if inc is None else inc), pe_acc=(not start))

        def TR(out, in_, idt, r, w):
            return S.op('pe', lambda e: e.transpose(out, in_, idt), r, w)

        ident = sb("ident", [128, 128], F32)
        ident_bf = sb("ident_bf", [128, 128], BF16)
        ones_bf = sb("ones_bf", [128, 128], BF16)
        G(lambda e: e.memset(ident[:], 0.0), [], ['ident'])
        G(lambda e: e.affine_select(out=ident[:], in_=ident[:], pattern=[[-1, 128]],
                                    compare_op=ALU.not_equal, fill=1.0, base=0,
                                    channel_multiplier=1), ['ident'], ['ident'])
        G(lambda e: e.tensor_copy(out=ident_bf[:], in_=ident[:]), ['ident'], ['ident_bf'])
        G(lambda e: e.memset(ones_bf[:], 1.0), [], ['ones_bf'])

        gmlp = sb("gmlp", [128, 2, KC], F32)
        S.dma('sp', gmlp[:], mlp_norm_fm.rearrange("l p k -> p l k"), writes=['gmlp'], stream='c')
        gmix = sb("gmix", [128, 2, KC], F32)
        S.dma('sp', gmix[:], mix_norm_fm.rearrange("l p k -> p l k"), writes=['gmix'], stream='c')

        xTh = [None]

        def tile_rows(t):
            return x_s if t == NT else x_p[t * 128:(t + 1) * 128, :]

        def tile_cols(t):
            return slice(t * 128, (t + 1) * 128)

        def phase0():
          xT = xTh[0]
          with ExitStack() as ph:
            psb = mk_sb(ph)
            xin = [psb("xin%d" % i, [128, D], F32) for i in range(2)]
            for t in range(NT + 1):
                b = t % 2
                S.dma('sp', xin[b][:], tile_rows(t), writes=[('xin', b)], stream='x')
                for half in range(2):
                    pi = half
                    for j in range(4):
                        kc = half * 4 + j
                        TR(pb[pi][:, j * 128:(j + 1) * 128], xin[b][:, kc * 128:(kc + 1) * 128], ident[:],
                           [('xin', b), 'ident'], [('pbq', pi, j)])
                    A(lambda e, half=half, pi=pi: e.copy(out=xT[:, half * 4:half * 4 + 4, tile_cols(t)],
                                                        in_=pbv(pi, 4)),
                      [('pbq', pi, j) for j in range(4)], [('xT', t)] + [('pbq', pi, j) for j in range(4)])
            S.barrier()

        def xkeys(s0, n):
            return [('xT', t) for t in range(s0 // 128, (s0 + n + 127) // 128)]

        def hkeys(s0, n):
            return [('hT', t) for t in range(s0 // 128, (s0 + n + 127) // 128)]

        def make_norm(psb, W, nbuf=2):
            sq = [psb("sq%d" % i, [128, KC, W], BF16) for i in range(nbuf)]
            ntmp = [psb("ntmp%d" % i, [128, KC, W], F32) for i in range(nbuf)]
            rs = [psb("rs%d" % i, [128, W], F32) for i in range(nbuf)]
            ctr = [0]

            def norm_range(gain, gkey, s0, n, hdst, hk, xv=None, xk=None):
                i = ctr[0] % nbuf
                ctr[0] += 1
                if xv is None:
                    xv = xTh[0][:, :, s0:s0 + n]
                    xk = xkeys(s0, n)
                A(lambda e: e.activation(out=sq[i][:, :, :n], in_=xv, func=AF.Square),
                  xk, [('sq', i)])
                for kc in range(KC):
                    MM(pb[5][:, :n], ones_bf[:], sq[i][:, kc, :n], kc == 0, kc == KC - 1,
                       [('sq', i), 'ones_bf'], PK(5))
                A(lambda e: e.activation(out=rs[i][:, :n], in_=pb[5][:, :n], func=AF.Sqrt,
                                         bias=EPS, scale=1.0 / D), PK(5), [('rs', i)])
                V(lambda e: e.reciprocal(out=rs[i][:, :n], in_=rs[i][:, :n]), [('rs', i)], [('rs', i)])
                G(lambda e: e.tensor_tensor(out=ntmp[i][:, :, :n], in0=xv,
                                            in1=rs[i][:, :n].unsqueeze(1).to_broadcast([128, KC, n]),
                                            op=ALU.mult), xk + [('rs', i)], [('ntmp', i)])
                V(lambda e: e.tensor_tensor(out=hdst, in0=ntmp[i][:, :, :n],
                                            in1=gain.unsqueeze(2).to_broadcast([128, KC, n]),
                                            op=ALU.mult), [('ntmp', i), gkey], hk)
            return norm_range

        def mlp_layer(l):
            xT = xTh[0]
            with ExitStack() as ph:
                psb = mk_sb(ph)
                hT = psb("hT", [128, KC, TT], BF16)
                norm_range = make_norm(psb, 256)
                wu = [psb("wu%d" % i, [128, KC, 512], BF16) for i in range(2)]
                wd = [psb("wd%d" % i, [128, 4, D], BF16) for i in range(2)]
                h1 = [psb("h1_%d" % i, [128, 4, 512], BF16) for i in range(2)]
                rtmp = [psb("rtmp%d" % i, [128, 512], F32) for i in range(2)]
                ctr = {'up': 0, 'dn': 0, 'h1': 0, 'w': 0}
                for (s0, n) in blocks(TT, 256):
                    norm_range(gmlp[:, l, :], 'gmlp', s0, n, hT[:, :, s0:s0 + n], hkeys(s0, n))
                wup_v = w_up[l].rearrange("(kc p) f -> p kc f", p=128)
                wdn_v = w_down[l].rearrange("(fc p) d -> p fc d", p=128)
                for g in range(8):
                    wb = ctr['w'] % 2
                    ctr['w'] += 1
                    S.dma('pool', wu[wb][:], wup_v[:, :, g * 512:(g + 1) * 512], writes=[('wu', wb)], stream='w')
                    S.dma('pool', wd[wb][:], wdn_v[:, g * 4:(g + 1) * 4, :], writes=[('wd', wb)], stream='w')
                    for (s0, n) in blocks(TT, 512):
                        hb = ctr['h1'] % 2
                        ctr['h1'] += 1
                        tl = list(range(s0 // 128, (s0 + n + 127) // 128))
                        for fc in range(4):
                            pi = ctr['up'] % 2
                            ctr['up'] += 1
                            for kc in range(KC):
                                MM(pb[pi][:, :n], wu[wb][:, kc, fc * 128:(fc + 1) * 128], hT[:, kc, s0:s0 + n],
                                   kc == 0, kc == KC - 1, [('wu', wb)] + hkeys(s0, n), PK(pi))
                            A(lambda e, pi=pi: e.activation(out=rtmp[pi][:, :n], in_=pb[pi][:, :n], func=AF.Relu),
                              PK(pi), [('rtmp', pi)])
                            G(lambda e, pi=pi, fc=fc: e.tensor_tensor(out=h1[hb][:, fc, :n], in0=rtmp[pi][:, :n],
                                                                      in1=rtmp[pi][:, :n], op=ALU.mult),
                              [('rtmp', pi)], [('h1', hb, fc)])
                        for dc in range(KC):
                            pi = 2 + ctr['dn'] % 3
                            ctr['dn'] += 1
                            for fc in range(4):
                                MM(pb[pi][:, :n], wd[wb][:, fc, dc * 128:(dc + 1) * 128], h1[hb][:, fc, :n],
                                   fc == 0, fc == 3, [('wd', wb), ('h1', hb, fc)], PK(pi))
                            V(lambda e, dc=dc, pi=pi: e.tensor_tensor(out=xT[:, dc, s0:s0 + n],
                                                                      in0=xT[:, dc, s0:s0 + n],
                                                                      in1=pb[pi][:, :n], op=ALU.add),
                              PK(pi) + [('xTd', dc, t) for t in tl], [('xTd', dc, t) for t in tl])
                S.barrier()

        def hgrn_layer():
            xT = xTh[0]
            with ExitStack() as ph:
                psb = mk_sb(ph)
                norm_range = make_norm(psb, 128, 1)
                hTt = psb("hTt", [128, KC, 128], BF16)
                wic = psb("wic", [128, KC, 4096], BF16)
                woc = psb("woc", [128, KC, D], BF16)
                wic_v = w_in_c.rearrange("(kc p) f -> p kc f", p=128)
                woc_v = w_out_c.rearrange("(kc p) f -> p kc f", p=128)
                for kc in range(KC):
                    S.dma('pool', wic[:, kc, :], wic_v[:, kc, :], writes=['wic'], stream='w')
                S.dma('pool', woc[:], woc_v, writes=['woc'], stream='w')
                lbt = psb("lbt", [128, 2, 8], F32)
                lb = psb("lb", [128, 8], F32)
                oml = psb("oml", [128, 8], F32)
                gnc = psb("gnc", [128, 1], F32)
                S.dma('sp', lbt[:], lb_fm.rearrange("l p k -> p l k"), writes=['lbt'], stream='c')
                S.dma('sp', gnc[:], g_norm_c, writes=['gnc'], stream='c')
                V(lambda e: e.tensor_tensor(out=lb[:], in0=lbt[:, 1, :], in1=lbt[:, 0, :], op=ALU.subtract),
                  ['lbt'], ['lb'])
                A(lambda e: e.activation(out=lb[:], in_=lb[:], func=AF.Sigmoid), ['lb'], ['lb'])
                V(lambda e: e.tensor_scalar(out=oml[:], in0=lb[:], scalar1=-1.0, scalar2=1.0, op0=ALU.mult,
                                            op1=ALU.add), ['lb'], ['oml'])
                SC = 128 ** -0.5
                mask_p = psb("mask_p", [128, 128], F32)
                mask_s = psb("mask_s", [128, 128], F32)
                for mk, nm in ((mask_p, 'mask_p'), (mask_s, 'mask_s')):
                    G(lambda e, mk=mk: e.memset(mk[:], SC), [], [nm])
                    G(lambda e, mk=mk: e.affine_select(out=mk[:], in_=mk[:], pattern=[[1, 128]],
                                                       compare_op=ALU.is_ge, fill=0.0, base=0,
                                                       channel_multiplier=-1), [nm], [nm])
                G(lambda e: e.affine_select(out=mask_s[:].rearrange("p (a b) -> p a b", a=16),
                                            in_=mask_s[:].rearrange("p (a b) -> p a b", a=16),
                                            pattern=[[-8, 16], [0, 8]], compare_op=ALU.is_ge, fill=0.0,
                                            base=0, channel_multiplier=1), ['mask_s'], ['mask_s'])
                rm_p = psb("rm_p", [128, 8, 128], F32)
                rm_s = rm_p
                S.alias('rm_s', 'rm_p')
                G(lambda e: e.memset(rm_p[:], 1.0), [], ['rm_p'])
                G(lambda e: e.memset(rm_p[:, :, 0:1], 0.0), ['rm_p'], ['rm_p'])
                seqmask = psb("seqmask", [128, 16], F32)
                G(lambda e: e.memset(seqmask[:], 1.0), [], ['seqmask'])
                G(lambda e: e.affine_select(out=seqmask[:], in_=seqmask[:], pattern=[[-8, 16]],
                                            compare_op=ALU.is_ge, fill=0.0, base=0, channel_multiplier=1),
                  ['seqmask'], ['seqmask'])
                G(lambda e: e.affine_select(out=seqmask[:], in_=seqmask[:], pattern=[[8, 16]],
                                            compare_op=ALU.is_ge, fill=0.0, base=7, channel_multiplier=-1),
                  ['seqmask'], ['seqmask'])

                f4 = [128, 8, 128]
                qs = psb("qs", f4)
                ff = psb("ff", f4)
                kk = psb("kk", f4)
                gc = psb("gc", f4)
                sgate = psb("sgate", f4)
                v_tm = psb("v_tm", f4, BF16)
                tA = psb("tA", [128, 8, 32])
                tB = ff
                S.alias('tB', 'ff')
                qI = psb("qI", [128, 8, 32], BF16)
                kI = psb("kI", f4, BF16)
                intra = psb("intra", f4, BF16)
                qg = psb("qg", f4, BF16)
                khT = psb("khT", f4, BF16)
                kh_tm = psb("kh_tm", f4, BF16)
                khm = kI
                S.alias('khm', 'kI')
                egl = psb("egl", [128, 8, 16])
                Sst = psb("Sst", f4)
                Sbf = psb("Sbf", f4, BF16)
                S0 = ff
                S.alias('S0', 'ff')
                osq = khT
                S.alias('osq', 'khT')
                rn = qs[:].rearrange("p h t -> p (h t)")
                S.alias('rn', 'qs')
                onb = psb("onb", f4, BF16)
                G(lambda e: e.memset(intra[:], 0.0), [], ['intra'])
                G(lambda e: e.memset(Sst[:], 0.0), [], ['Sst'])
                G(lambda e: e.memset(Sbf[:], 0.0), [], ['Sbf'])

                def proj_fm(col0, banks):
                    for h in range(8):
                        bi = banks[h // 4]
                        for kc in range(KC):
                            MM(pb[bi][:, (h % 4) * 128:(h % 4 + 1) * 128],
                               wic[:, kc, col0 + h * 128:col0 + (h + 1) * 128], hTt[:, kc, :],
                               kc == 0, kc == KC - 1, ['wic', 'hTt'], [('pbq', bi, h % 4)])

                bq = PK

                order = list(range(NT)) + [NT]
                for t in order:
                    smp = (t == NT)
                    cols = tile_cols(t)
                    mask = mask_s if smp else mask_p
                    mkey = 'mask_s' if smp else 'mask_p'
                    rm = rm_s if smp else rm_p
                    if smp:
                        G(lambda e: e.memset(rm_s[:].rearrange("p h (s t) -> p h s t", t=8)[:, :, :, 0:1], 0.0),
                          ['rm_s'], ['rm_s'])
                    norm_range(gmix[:, 1, :], 'gmix', t * 128, 128, hTt[:], ['hTt'])
                    proj_fm(0, (0, 1))
                    for b2 in range(2):
                        A(lambda e, b2=b2: e.activation(out=qs[:, b2 * 4:b2 * 4 + 4, :], in_=pbv(b2, 4), func=AF.Silu),
                          bq(b2), ['qs'] + bq(b2))
                    if cfg.get('cstop', 99) <= 1:
                        continue
                    proj_fm(1024, (2, 3))
                    for b2 in range(2):
                        A(lambda e, b2=b2: e.activation(out=ff[:, b2 * 4:b2 * 4 + 4, :], in_=pbv(2 + b2, 4),
                                                        func=AF.Sigmoid), bq(2 + b2), ['ff'] + bq(2 + b2))
                    V(lambda e: e.tensor_tensor(out=ff[:], in0=ff[:], in1=oml[:].unsqueeze(2).to_broadcast(f4),
                                                op=ALU.mult), ['ff', 'oml'], ['ff'])
                    V(lambda e: e.tensor_tensor(out=ff[:], in0=ff[:], in1=lb[:].unsqueeze(2).to_broadcast(f4),
                                                op=ALU.add), ['ff', 'lb'], ['ff'])
                    V(lambda e: e.tensor_scalar(out=kk[:], in0=ff[:], scalar1=-1.0, scalar2=1.0, op0=ALU.mult,
                                                op1=ALU.add), ['ff'], ['kk'])
                    logf = ff
                    A(lambda e: e.activation(out=ff[:], in_=ff[:], func=AF.Ln), ['ff'], ['ff'])
                    V(lambda e: e.tensor_tensor_scan(out=gc[:].rearrange("p h t -> p (h t)"),
                                                     data0=rm[:].rearrange("p h t -> p (h t)"),
                                                     data1=logf[:].rearrange("p h t -> p (h t)"),
                                                     initial=0.0, op0=ALU.mult, op1=ALU.add),
                      ['ff', 'rm_p'], ['gc'])
                    if cfg.get('cstop', 99) <= 2:
                        continue
                    proj_fm(3072, (4, 5))
                    for b2 in range(2):
                        A(lambda e, b2=b2: e.activation(out=sgate[:, b2 * 4:b2 * 4 + 4, :], in_=pbv(4 + b2, 4),
                                                        func=AF.Silu), bq(4 + b2), ['sgate'] + bq(4 + b2))
                    for b2 in range(2):
                        for kc in range(KC):
                            MM(pb[6 + b2][:], hTt[:, kc, :], wic[:, kc, 2048 + b2 * 512:2048 + (b2 + 1) * 512],
                               kc == 0, kc == KC - 1, ['wic', 'hTt'], PK(6 + b2))
                        A(lambda e, b2=b2: e.copy(out=v_tm[:, b2 * 4:b2 * 4 + 4, :], in_=pbv(6 + b2, 4)),
                          PK(6 + b2), ['v_tm'])
                    if cfg.get('cstop', 99) <= 3:
                        continue
                    for I in range(4):
                        blk = slice(32 * I, 32 * I + 32)
                        nj = 32 * (I + 1)
                        if I == 0:
                            A(lambda e: e.activation(out=tA[:], in_=gc[:, :, blk], func=AF.Exp), ['gc'], ['tA'])
                            A(lambda e: e.activation(out=tB[:, :, :nj], in_=gc[:, :, :nj], func=AF.Exp, scale=-1.0),
                              ['gc'], ['tB'])
                        else:
                            ref = gc[:, :, 32 * I - 1:32 * I]
                            V(lambda e: e.tensor_tensor(out=tA[:], in0=gc[:, :, blk],
                                                        in1=ref.to_broadcast([128, 8, 32]), op=ALU.subtract),
                              ['gc'], ['tA'])
                            A(lambda e: e.activation(out=tA[:], in_=tA[:], func=AF.Exp), ['tA'], ['tA'])
                            V(lambda e: e.tensor_tensor(out=tB[:, :, :nj], in0=ref.to_broadcast([128, 8, nj]),
                                                        in1=gc[:, :, :nj], op=ALU.subtract), ['gc'], ['tB'])
                            A(lambda e: e.activation(out=tB[:, :, :nj], in_=tB[:, :, :nj], func=AF.Exp),
                              ['tB'], ['tB'])
                        V(lambda e: e.tensor_tensor(out=qI[:], in0=qs[:, :, blk], in1=tA[:], op=ALU.mult),
                          ['qs', 'tA'], ['qI'])
                        G(lambda e: e.tensor_tensor(out=kI[:, :, :nj], in0=kk[:, :, :nj], in1=tB[:, :, :nj],
                                                    op=ALU.mult), ['kk', 'tB'], ['kI'])
                        for h in range(8):
                            bi = h // 4
                            c0 = (h % 4) * 128 + 32 * I
                            MM(pb[bi][:nj, c0:c0 + 32], kI[:, h, :nj], qI[:, h, :], True, True,
                               ['kI', 'qI'], [('pbq', bi, h % 4)])
                        for b2 in range(2):
                            V(lambda e, b2=b2: e.tensor_tensor(
                                out=intra[:nj, b2 * 4:b2 * 4 + 4, blk], in0=pbv(b2, 4)[:nj, :, blk],
                                in1=mask[:nj, blk].unsqueeze(1).to_broadcast([nj, 4, 32]), op=ALU.mult),
                                bq(b2) + [mkey], ['intra'] + bq(b2))
                    if cfg.get('cstop', 99) <= 4:
                        continue
                    A(lambda e: e.activation(out=tB[:], in_=gc[:], func=AF.Exp), ['gc'], ['tB'])
                    V(lambda e: e.scalar_tensor_tensor(out=qg[:], in0=qs[:], scalar=SC, in1=tB[:], op0=ALU.mult, op1=ALU.mult),
                      ['qs', 'tB'], ['qg'])
                    if smp:
                        gl = gc[:].rearrange("p h (s t) -> p h s t", t=8)[:, :, :, 7:8]
                        V(lambda e: e.tensor_tensor(out=tB[:].rearrange("p h (s t) -> p h s t", t=8),
                                                    in0=gl.to_broadcast([128, 8, 16, 8]),
                                                    in1=gc[:].rearrange("p h (s t) -> p h s t", t=8),
                                                    op=ALU.subtract), ['gc'], ['tB'])
                        A(lambda e: e.activation(out=egl[:], in_=gc[:].rearrange("p h (s t) -> p h s t", t=8)[:, :, :, 7],
                                                 func=AF.Exp), ['gc'], ['egl'])
                    else:
                        gl = gc[:, :, 127:128]
                        V(lambda e: e.tensor_tensor(out=tB[:], in0=gl.to_broadcast(f4), in1=gc[:],
                                                    op=ALU.subtract), ['gc'], ['tB'])
                        A(lambda e: e.activation(out=egl[:, :, 0], in_=gc[:, :, 127], func=AF.Exp), ['gc'], ['egl'])
                    A(lambda e: e.activation(out=tB[:], in_=tB[:], func=AF.Exp), ['tB'], ['tB'])
                    G(lambda e: e.tensor_tensor(out=khT[:], in0=kk[:], in1=tB[:], op=ALU.mult), ['kk', 'tB'], ['khT'])
                    for h in range(8):
                        TR(pbbf(2)[:, h * 128:(h + 1) * 128], khT[:, h, :], ident_bf[:], ['khT', 'ident_bf'],
                           [('pbq', 2, h // 2)])
                    A(lambda e: e.copy(out=kh_tm[:].rearrange("p h t -> p (h t)"), in_=pbbf(2)),
                      PK(2), ['kh_tm'] + PK(2))
                    if cfg.get('cstop', 99) <= 5:
                        continue
                    if not smp and cfg.get('dbg', '') == 'nop':
                        pass
                    elif smp and cfg.get('dbg', '') == 'nos':
                        pass
                    elif not smp:
                        for h in range(8):
                            bi = 4 + h // 4
                            o_out = pb[bi][:, (h % 4) * 128:(h % 4 + 1) * 128]
                            MM(o_out, Sbf[:, h, :], qg[:, h, :], True, False, ['Sbf', 'qg'], [('pbq', bi, h % 4)])
                            MM(o_out, v_tm[:, h, :], intra[:, h, :], False, True, ['v_tm', 'intra'],
                               [('pbq', bi, h % 4)])
                        for h in range(8):
                            if cfg.get('sub', 9) < 2:
                                break
                            bi = 6 + h // 4
                            MM(pb[bi][:, (h % 4) * 128:(h % 4 + 1) * 128], kh_tm[:, h, :], v_tm[:, h, :], True, True,
                               ['kh_tm', 'v_tm'], [('pbq', bi, h % 4)])
                        V(lambda e: e.tensor_tensor(out=Sst[:], in0=Sst[:], in1=egl[:, :, 0:1].to_broadcast(f4),
                                                    op=ALU.mult), ['Sst', 'egl'], ['Sst'])
                        for b2 in range(2):
                            V(lambda e, b2=b2: e.tensor_tensor(out=Sst[:, b2 * 4:b2 * 4 + 4, :],
                                                               in0=Sst[:, b2 * 4:b2 * 4 + 4, :], in1=pbv(6 + b2, 4),
                                                               op=ALU.add), ['Sst'] + PK(6 + b2), ['Sst'] + PK(6 + b2))
                        if cfg.get('sub', 9) >= 4:
                            A(lambda e: e.copy(out=Sbf[:], in_=Sst[:]), ['Sst'], ['Sbf'])
                        if t == NT - 1 and cfg.get('sub', 9) >= 5:
                            S.dma('sp', hgrn_p.rearrange("h k v -> k h v"), Sst[:], reads=['Sst'], stream='o')
                    else:
                        for b2 in range(2):
                            V(lambda e, b2=b2: e.memset(pb[4 + b2][:], 0.0), [], PK(4 + b2))
                        for s in range(16):
                            S.dma('sp', S0[:], hgrn_in[s].rearrange("h k v -> k h v"), writes=['S0'], stream='s0')
                            A(lambda e: e.copy(out=Sbf[:], in_=S0[:]), ['S0'], ['Sbf'])
                            for h in range(8):
                                bi = 4 + h // 4
                                c0 = (h % 4) * 128 + s * 8
                                MM(pb[bi][:, c0:c0 + 8], Sbf[:, h, :], qg[:, h, s * 8:(s + 1) * 8], False, False,
                                   ['Sbf', 'qg'], [('pbq', bi, h % 4)])
                            V(lambda e, s=s: e.tensor_scalar(out=khm[:], in0=kh_tm[:], scalar1=seqmask[:, s:s + 1],
                                                            scalar2=None, op0=ALU.mult),
                              ['kh_tm', 'seqmask'], ['khm'])
                            for h in range(8):
                                bi = 6 + h // 4
                                MM(pb[bi][:, (h % 4) * 128:(h % 4 + 1) * 128], khm[:, h, :], v_tm[:, h, :],
                                   True, True, ['khm', 'v_tm'], [('pbq', bi, h % 4)])
                            V(lambda e, s=s: e.tensor_tensor(out=S0[:], in0=S0[:],
                                                             in1=egl[:, :, s:s + 1].to_broadcast(f4), op=ALU.mult),
                              ['S0', 'egl'], ['S0'])
                            for b2 in range(2):
                                V(lambda e, b2=b2: e.tensor_tensor(out=S0[:, b2 * 4:b2 * 4 + 4, :],
                                                                   in0=S0[:, b2 * 4:b2 * 4 + 4, :], in1=pbv(6 + b2, 4),
                                                                   op=ALU.add), ['S0'] + PK(6 + b2), ['S0'] + PK(6 + b2))
                            S.dma('sp', hgrn_s[s].rearrange("h k v -> k h v"), S0[:], reads=['S0'], stream='o')
                        for h in range(8):
                            bi = 4 + h // 4
                            MM(pb[bi][:, (h % 4) * 128:(h % 4 + 1) * 128], v_tm[:, h, :], intra[:, h, :],
                               False, True, ['v_tm', 'intra'], [('pbq', bi, h % 4)])
                    if cfg.get('cstop', 99) <= 6:
                        continue
                    for b2 in range(2):
                        A(lambda e, b2=b2: e.activation(out=osq[:, b2 * 4:b2 * 4 + 4, :], in_=pbv(4 + b2, 4),
                                                        func=AF.Square), bq(4 + b2), ['osq'])
                    for b2 in range(2):
                        MM(pb[b2][:], ones_bf[:], osq[:, b2 * 4:b2 * 4 + 4, :].rearrange("p h t -> p (h t)"),
                           True, True, ['osq', 'ones_bf'], bq(b2))
                        A(lambda e, b2=b2: e.activation(out=rn[:, b2 * 512:(b2 + 1) * 512], in_=pb[b2][:],
                                                        func=AF.Sqrt, bias=EPS, scale=1.0 / 128),
                          bq(b2), ['rn'] + bq(b2))
                    V(lambda e: e.reciprocal(out=rn[:], in_=rn[:]), ['rn'], ['rn'])
                    for b2 in range(2):
                        V(lambda e, b2=b2: e.scalar_tensor_tensor(
                            out=rn[:, b2 * 512:(b2 + 1) * 512].rearrange("p (h t) -> p h t", h=4),
                            in0=pbv(4 + b2, 4), scalar=gnc[:, 0:1],
                            in1=rn[:, b2 * 512:(b2 + 1) * 512].rearrange("p (h t) -> p h t", h=4),
                            op0=ALU.mult, op1=ALU.mult), bq(4 + b2) + ['gnc', 'rn'], ['rn'] + bq(4 + b2))
                    G(lambda e: e.tensor_tensor(out=onb[:], in0=rn[:].rearrange("p (h t) -> p h t", h=8), in1=sgate[:],
                                                op=ALU.mult), ['rn', 'sgate'], ['onb'])
                    if cfg.get('cstop', 99) <= 7:
                        continue
                    for dc in range(KC):
                        bi = 2 + dc // 4
                        for kc in range(KC):
                            MM(pb[bi][:, (dc % 4) * 128:(dc % 4 + 1) * 128], woc[:, kc, dc * 128:(dc + 1) * 128],
                               onb[:, kc, :], kc == 0, kc == KC - 1, ['woc', 'onb'],
                               [('pbq', bi, dc % 4)])
                    for b2 in range(2):
                        V(lambda e, b2=b2: e.tensor_tensor(out=xT[:, b2 * 4:b2 * 4 + 4, cols],
                                                           in0=xT[:, b2 * 4:b2 * 4 + 4, cols],
                                                           in1=pbv(2 + b2, 4), op=ALU.add),
                          bq(2 + b2) + [('xT', t)], [('xT', t)] + bq(2 + b2))
                S.barrier()

        def ab_layer():
            with ExitStack() as ph:
                psb = mk_sb(ph)
                norm_range = make_norm(psb, 128, 1)
                STG = cfg.get('astage', 99)
                NEG = -1.0e30
                SCq = 128 ** -0.5
                SCm = 96 ** -0.5
                f4 = [128, 4, 128]
                wia = psb("wia", [128, KC, 2600], BF16)
                woa = psb("woa", [128, KC, D], BF16)
                wuq = psb("wuq", [128, 2, 768], BF16)
                wuk = psb("wuk", [128, 2, 512], BF16)
                wuv = psb("wuv", [128, 2, 512], BF16)
                wia_v = w_in_ab.rearrange("(kc p) f -> p kc f", p=128)
                for kc in range(KC):
                    for (c0, cn) in blocks(2600, 1300):
                        S.dma('pool', wia[:, kc, c0:c0 + cn], wia_v[:, kc, c0:c0 + cn], writes=['wia'], stream='w')
                S.dma('pool', woa[:], w_out_ab.rearrange("(kc p) f -> p kc f", p=128), writes=['woa'], stream='w')
                S.dma('pool', wuq[:], w_uq.rearrange("(c p) f -> p c f", p=128), writes=['wuq'], stream='w')
                S.dma('pool', wuk[:], w_uk.rearrange("(c p) f -> p c f", p=128), writes=['wuk'], stream='w')
                S.dma('pool', wuv[:], w_uv.rearrange("(c p) f -> p c f", p=128), writes=['wuv'], stream='w')
                cw = psb("cw", [128, 12, 4])
                S.dma('sp', cw[:], conv_w_fm, writes=['cw'], stream='c')
                negA = psb("negA", [128, 4])
                dtb = psb("dtb", [128, 4])
                S.dma('sp', negA[:], a_log.partition_broadcast(128), writes=['negA'], stream='c')
                S.dma('sp', dtb[:], dt_bias.partition_broadcast(128), writes=['dtb'], stream='c')
                A(lambda e: e.activation(out=negA[:], in_=negA[:], func=AF.Exp), ['negA'], ['negA'])
                V(lambda e: e.tensor_scalar(out=negA[:], in0=negA[:], scalar1=-1.0, scalar2=None, op0=ALU.mult),
                  ['negA'], ['negA'])
                gdnn = psb("gdnn", [128, 1])
                S.dma('sp', gdnn[:], gdn_norm, writes=['gdnn'], stream='c')
                qnb = psb("qnb", [128, 256])
                kvnb = psb("kvnb", [128, 256])
                S.dma('sp', qnb[:], q_norm.partition_broadcast(128), writes=['qnb'], stream='c')
                S.dma('sp', kvnb[:], kv_norm.partition_broadcast(128), writes=['kvnb'], stream='c')
                cosb = psb("cosb", [128, NT + 1, 16])
                sinb = psb("sinb", [128, NT + 1, 16])
                S.dma('sp', cosb[:], rope_cos.rearrange("(t p) f -> p t f", p=128), writes=['cosb'], stream='c')
                S.dma('sp', sinb[:], rope_sin.rearrange("(t p) f -> p t f", p=128), writes=['sinb'], stream='c')
                ones_f = psb("ones_f", [128, 128])
                G(lambda e: e.memset(ones_f[:], 1.0), [], ['ones_f'])

                def build_masks(tag, blk8, xsb):
                    m = {}
                    for nm in ('tri', 'blk', 'nms', 'nmi', 'nmsT', 'cmask'):
                        m[nm] = xsb(nm + tag, [128, 128])

                    def sel(t, pattern, op, fill, base, cm, view=None):
                        key = m_key(t)
                        ap = t[:] if view is None else t[:].rearrange("p (a b) -> p a b", a=16)
                        G(lambda e: e.affine_select(out=ap, in_=ap, pattern=pattern, compare_op=op, fill=fill,
                                                    base=base, channel_multiplier=cm), [key], [key])

                    def m_key(t):
                        return 'mask' + tag

                    def same_block(t, fill):
                        sel(t, [[-8, 16], [0, 8]], ALU.is_ge, fill, 0, 1, view=True)
                        sel(t, [[8, 16], [0, 8]], ALU.is_ge, fill, 7, -1, view=True)
                    G(lambda e: e.memset(m['tri'][:], 1.0), [], ['mask' + tag])
                    sel(m['tri'], [[1, 128]], ALU.is_ge, 0.0, 0, -1)
                    G(lambda e: e.memset(m['blk'][:], 1.0), ['mask' + tag], ['mask' + tag])
                    G(lambda e: e.memset(m['nms'][:], 0.0), ['mask' + tag], ['mask' + tag])
                    sel(m['nms'], [[1, 128]], ALU.is_gt, NEG, 0, -1)
                    G(lambda e: e.memset(m['nmi'][:], 0.0), ['mask' + tag], ['mask' + tag])
                    sel(m['nmi'], [[1, 128]], ALU.is_ge, NEG, 0, -1)
                    G(lambda e: e.memset(m['nmsT'][:], 0.0), ['mask' + tag], ['mask' + tag])
                    sel(m['nmsT'], [[-1, 128]], ALU.is_gt, NEG, 0, 1)
                    G(lambda e: e.memset(m['cmask'][:], 1.0), ['mask' + tag], ['mask' + tag])
                    sel(m['cmask'], [[1, 128]], ALU.is_ge, 0.0, 0, -1)
                    if blk8:
                        same_block(m['tri'], 0.0)
                        same_block(m['blk'], 0.0)
                        same_block(m['nms'], NEG)
                        same_block(m['nmi'], NEG)
                        same_block(m['nmsT'], NEG)
                        same_block(m['cmask'], 0.0)
                    return m
                seqmask = psb("seqmask", [128, 16], F32)
                G(lambda e: e.memset(seqmask[:], 1.0), [], ['seqmask'])
                G(lambda e: e.affine_select(out=seqmask[:], in_=seqmask[:], pattern=[[-8, 16]],
                                            compare_op=ALU.is_ge, fill=0.0, base=0, channel_multiplier=1),
                  ['seqmask'], ['seqmask'])
                G(lambda e: e.affine_select(out=seqmask[:], in_=seqmask[:], pattern=[[8, 16]],
                                            compare_op=ALU.is_ge, fill=0.0, base=7, channel_multiplier=-1),
                  ['seqmask'], ['seqmask'])

                H = {}
                xTt = psb("xTt", [128, KC, 128])
                hTt = psb("hTt", [128, KC, 128], BF16)
                acc = psb("acc", [128, 12, 128])
                ctmp = psb("ctmp", [128, 12, 128])
                qkvs = acc
                S.alias('qkvs', 'acc')
                qkvtm = ctmp[:].rearrange("p c t -> p (c t)")
                S.alias('qkvtm', 'ctmp')
                cst = ctmp[:].rearrange("p c t -> p (c t)")[0:48, :]
                S.alias('cst', 'ctmp')
                xin = ctmp[:].rearrange("p c t -> p (c t)")[:, 0:D]
                S.alias('xin', 'ctmp')
                zs = psb("zs", f4)
                tm = psb("tm", [128, 552])
                sq8 = psb("sq8", [128, 8, 128], BF16)
                rn8 = psb("rn8", [128, 8, 128])
                qkn = rn8
                S.alias('qkn', 'rn8')
                ktm = psb("ktm", f4)
                vtm = psb("vtm", f4)
                gt = psb("gt", [128, 16])
                gcc = psb("gcc", [128, 8])
                G4 = psb("G4", f4)
                gcr = psb("gcr", f4)
                br = psb("br", f4)
                a1 = psb("a1", f4)
                E1s = psb("E1s", f4)
                E1i = psb("E1i", f4)
                E2s = psb("E2s", f4)
                Xa = [psb("Xa%d" % i, f4) for i in range(2)]
                XTa = [psb("XTa%d" % i, f4) for i in range(2)]
                Pm = psb("Pm", f4)
                itr = psb("itr", f4)
                qg = psb("qg", f4)
                vb, kbg, kdec, uT, wT = E1s, E2s, E1i, br, G4
                vnT, vn, kdm, ona = Xa[0], Xa[1], XTa[0], XTa[1]
                for al, cn in (('vb', 'E1s'), ('kbg', 'E2s'), ('kdec', 'E1i'), ('uT', 'br'), ('wT', 'G4'),
                               ('vnT', ('Xa', 0)), ('vn', ('Xa', 1)), ('kdm', ('XTa', 0)), ('ona', ('XTa', 1))):
                    S.alias(al, cn)
                eglr = psb("eglr", [128, 4, 16])
                Sg = psb("Sg", f4)
                Sgs = psb("Sgs", f4)
                osq = psb("osq", f4, BF16)
                rno = psb("rno", [128, 512])
                onb = psb("onb", [128, KC, 128], BF16)
                ssn = psb("ssn", [128, 4])
                cqn = psb("cqn", [128, 256])
                ckvn = psb("ckvn", [128, 256])
                kst = psb("kst", [128, 128])
                rt = psb("rt", [128, 8, 64])
                cqnT = psb("cqnT", [128, 2, 128], BF16)
                ckvnT = psb("ckvnT", [128, 2, 128], BF16)
                qtm = psb("qtm", [128, 8, 96])
                qn2 = psb("qn2", [128, 4, 128], BF16)
                qp2 = psb("qp2", [128, 8, 32], BF16)
                QTn = psb("QTn", [128, 4, 128], BF16)
                QTp = psb("QTp", [32, 8, 128], BF16)
                G(lambda e: e.memset(Sg[:], 0.0), [], ['Sg'])
                G(lambda e: e.memset(kst[:], 0.0), [], ['kst'])
                G(lambda e: e.memset(onb[:], 0.0), [], [('onb', 0), ('onb', 1)])


                def sample_attention():
                    NPG = cfg['NPG']
                    NPOOL = cfg['NPOOL']
                    R = NPG
                    SB = 128 // R
                    RG = min(8, R)
                    NG = R // RG
                    BPP = 128 // RG
                    ssb = H['ssb2']
                    rep = ssb("rep", [NPG, 128])
                    gb = ssb("gb", [128, NG + 1])
                    cm8 = ssb("cm8", [128, 64])
                    S.dma('sp', rep[:], rep_in, writes=['rep'], stream='c')
                    S.dma('sp', gb[:], gbase_in, writes=['gb'], stream='c')
                    S.dma('sp', cm8[:], cm8_in, writes=['cm8'], stream='c')
                    pti = ssb("pti", [16, NPG], I32)
                    ptf = ssb("ptf", [16, NPG])
                    ptT = ssb("ptT", [NPG, 16])
                    ptrep = ssb("ptrep", [128, 16])
                    S.dma('sp', pti[:], page_table, writes=['pti'], stream='c')
                    V(lambda e: e.tensor_copy(out=ptf[:], in_=pti[:]), ['pti'], ['ptf'])
                    TR(pb[0][0:NPG, 0:16], ptf[:], ident[0:16, 0:16], ['ptf', 'ident'], PK(0))
                    A(lambda e: e.copy(out=ptT[:], in_=pb[0][0:NPG, 0:16]), PK(0), ['ptT'] + PK(0))
                    MM(pb[0][:, 0:16], rep[:], ptT[:], True, True, ['rep', 'ptT'], PK(0))
                    A(lambda e: e.copy(out=ptrep[:], in_=pb[0][:, 0:16]), PK(0), ['ptrep'] + PK(0))
                    idxf = ssb("idxf", [128, 16, NG + 1])
                    idx = ssb("idx", [128, 16, NG + 1], I32)
                    for g in range(NG + 1):
                        V(lambda e, g=g: e.tensor_scalar(out=idxf[:, :, g], in0=ptrep[:],
                                                        scalar1=float(BPP if g < NG else SB), scalar2=gb[:, g:g + 1],
                                                        op0=ALU.mult, op1=ALU.add), ['ptrep', 'gb', 'idxf'], ['idxf'])
                    V(lambda e: e.tensor_copy(out=idx[:], in_=idxf[:]), ['idxf'], ['idx'])
                    SST = cfg.get('sstop', 99)
                    if SST <= 1:
                        return
                    lat_blk = cache_lat.rearrange("n (b r) d -> (n b) (r d)", r=RG)
                    kr_blk = cache_kr.rearrange("n (b r) d -> (n b) (r d)", r=R)
                    wk_ol = ssb("wk_ol", [128, 2048], BF16)
                    wukT = wk_ol[:, :].rearrange("p (h l) -> p h l", h=8)
                    S.alias('wukT', 'wk_ol')
                    S.alias('olatT', 'wk_ol')
                    for hh in range(2):
                        S.dma('pool', wk_ol[:, hh * 1024:(hh + 1) * 1024], w_ukT[:, hh * 1024:(hh + 1) * 1024], writes=['wukT'], stream='w')
                    if cfg.get('sq', 9) <= 0:
                        return
                    qlatT = ssb("qlatT", [128, 2, 8, 128], BF16)
                    for c in range(2):
                        for h in range(8):
                            bi = h // 4
                            MM(pb[bi][:, (h % 4) * 128:(h % 4 + 1) * 128], wukT[:, h, c * 128:(c + 1) * 128],
                               QTn[:, h // 2, :], True, True, ['wukT', 'QT'], [('pbq', bi, h % 4)])
                        for b2 in range(2):
                            A(lambda e, c=c, b2=b2: e.copy(out=qlatT[:, c, b2 * 4:b2 * 4 + 4, :], in_=pbv(b2, 4)),
                              PK(b2), ['qlatT'] + PK(b2))
                    if cfg.get('sq', 9) <= 1:
                        return
                    qpeT = QTp
                    S.alias('qpeT', 'QT')
                    if cfg.get('sq', 9) <= 2:
                        return
                    kpTn = ssb("kpTn", [32, 128], BF16)
                    TR(pb[3][0:32, 0:128], kst[:, 0:32], ident[:], ['kst', 'ident'], PK(3))
                    A(lambda e: e.copy(out=kpTn[:], in_=pb[3][0:32, 0:128]), PK(3), ['kpTn'] + PK(3))
                    Lnew = ssb("Lnew", [128, 257], BF16)
                    V(lambda e: e.memset(Lnew[:, 256:257], 1.0), [], ['Lnew'])
                    V(lambda e: e.tensor_copy(out=Lnew[:, 0:256], in_=ckvn[:]), ['ckvn', 'Lnew'], ['Lnew'])
                    if SST <= 2:
                        return
                    NLB = cfg.get("nlb", 3)
                    Lb = [ssb("Lb%d" % i, [128, RG, 256], BF16) for i in range(NLB)]
                    Kb = [ssb("Kb0", [128, R, 32], BF16)] * 2
                    latT = [ssb("latT0", [128, 2, RG, 128], BF16)] * 2
                    kpT = [ssb("kpT0", [32, RG, 128], BF16)] * 2
                    PTs = [ssb("PTs%d" % i, [128, RG, 64], BF16) for i in range(2)]
                    oacc = ssb("oacc", [64, 257])
                    olat = ssb("olat", [64, 256])
                    olatT = wk_ol[:, :].rearrange("p (c s q) -> p c s q", c=2, s=16)
                    mst = ssb("mst", [128, 8])
                    m11 = ssb("m11", [1, 4])
                    gq = []
                    for s in range(16):
                        for g in range(NG + 1):
                            gq.append((s, g))
                    ctr = {'n': 0}

                    def stage1(s, g):
                        n = ctr['n']
                        ctr['n'] += 1
                        b = n % 2
                        lb = n % NLB
                        sbk = 6 + b
                        if g == 0:
                            kb_ = s % 2
                            S.dma_custom('pool', lambda e: e.indirect_dma_start(
                                out=Kb[kb_][:].rearrange("p r d -> p (r d)"), out_offset=None, in_=kr_blk,
                                in_offset=bass.IndirectOffsetOnAxis(ap=idx[:, s, NG:NG + 1], axis=0)),
                                ['idx'], [('Kb', 0)], stream='gk', nrr=2)
                        if g < NG:
                            S.dma_custom('pool', lambda e: e.indirect_dma_start(
                                out=Lb[lb][:].rearrange("p r d -> p (r d)"), out_offset=None, in_=lat_blk,
                                in_offset=bass.IndirectOffsetOnAxis(ap=idx[:, s, g:g + 1], axis=0)),
                                ['idx'], [('Lb', lb)], stream='gl', nrr=NLB)
                            for r in range(RG):
                                rr = g * RG + r
                                for c in range(2):
                                    blk = (r % 4) * 2 + c
                                    TR(pbbf(4 + (r // 4) % 2)[:, blk * 128:(blk + 1) * 128], Lb[lb][:, r, c * 128:(c + 1) * 128],
                                       ident_bf[:], [('Lb', lb), 'ident_bf'], [('pbq', 4 + (r // 4) % 2, blk // 2)])
                                TR(pbbf(3)[0:32, r * 128:(r + 1) * 128], Kb[s % 2][:, rr, :], ident_bf[:],
                                   [('Kb', 0), 'ident_bf'], [('pbq', 3, r // 2)])
                                if r % 4 == 3:
                                    bk = 4 + (r // 4) % 2
                                    r0 = r - 3
                                    V(lambda e, bk=bk, r0=r0: e.tensor_copy(
                                        out=latT[b][:, :, r0:r0 + 4, :].rearrange("p c r k -> p r c k"),
                                        in_=pbbf(bk).rearrange("p (r c k) -> p r c k", r=4, c=2)),
                                        PK(bk), [('latT', 0)] + PK(bk))
                            A(lambda e: e.copy(out=kpT[b][:].rearrange("p r k -> p (r k)"), in_=pbbf(3)[0:32, 0:RG * 128]),
                              PK(3), [('kpT', 0)] + PK(3))
                            for r in range(RG):
                                o = pb[sbk][:, r * 64:(r + 1) * 64]
                                MM(o, latT[b][:, 0, r, :], qlatT[:, 0, :, s * 8:(s + 1) * 8], True, False, [('latT', 0), 'qlatT'],
                                   [('pbq', sbk, r // 2)])
                                MM(o, latT[b][:, 1, r, :], qlatT[:, 1, :, s * 8:(s + 1) * 8], False, False, [('latT', 0), 'qlatT'],
                                   [('pbq', sbk, r // 2)])
                                MM(o, kpT[b][:, r, :], qpeT[:, :, s * 8:(s + 1) * 8], False, True, [('kpT', 0), 'qpeT'],
                                   [('pbq', sbk, r // 2)])
                            ncol = RG * 64
                        else:
                            o = pb[sbk][:, 0:64]
                            MM(o, ckvnT[:, 0, :], qlatT[:, 0, :, s * 8:(s + 1) * 8], True, False, ['ckvnT', 'qlatT'], [('pbq', sbk, 0)])
                            MM(o, ckvnT[:, 1, :], qlatT[:, 1, :, s * 8:(s + 1) * 8], False, False, ['ckvnT', 'qlatT'], [('pbq', sbk, 0)])
                            MM(o, kpTn[:], qpeT[:, :, s * 8:(s + 1) * 8], False, True, ['kpTn', 'qpeT'], [('pbq', sbk, 0)])
                            ncol = 64
                        return (s, g, b, sbk, ncol, lb)

                    def stage2(st_):
                        s, g, b, sbk, ncol, lb = st_
                        keys = [('pbq', sbk, j) for j in range((ncol + 127) // 128)]
                        if g == 0:
                            V(lambda e: e.reduce_max(out=mst[:, 0:1], in_=pb[sbk][:, :ncol], axis=AX.X), keys, ['mst0'])
                            TR(pb[2][0:1, 0:128], mst[:, 0:1], ident[:], ['mst0', 'ident'], PK(2))
                            V(lambda e: e.reduce_max(out=m11[:, 1:2], in_=pb[2][0:1, 0:128], axis=AX.X), PK(2),
                              ['m11'] + PK(2))
                            MM(pb[2][:, 128:129], ones_f[0:1, :], m11[:, 1:2], True, True, ['ones_f', 'm11'], PK(2))
                            A(lambda e: e.mul(out=mst[:, 3:4], in_=pb[2][:, 128:129], mul=-SCm), PK(2), ['mst3'] + PK(2))
                        first = (g == 0)
                        if g < NG:
                            A(lambda e: e.activation(out=PTs[b][:].rearrange("p r q -> p (r q)"), in_=pb[sbk][:, :ncol],
                                                     func=AF.Exp, bias=mst[:, 3:4], scale=SCm),
                              keys + ['mst3'], [('PTs', b)] + keys)
                            for r in range(RG):
                                MM(pb[1][0:64, 0:256], PTs[b][:, r, :], Lb[lb][:, r, :], first and r == 0, False,
                                   [('PTs', b), ('Lb', lb)], PK(1), inc=(r == RG - 1))
                                MM(pb[1][0:64, 256:257], PTs[b][:, r, :], ones_bf[:, 0:1], False, False,
                                   [('PTs', b), 'ones_bf'], PK(1), inc=(r == RG - 1))
                        else:
                            A(lambda e: e.activation(out=PTs[b][:, 0, :], in_=pb[sbk][:, :64], func=AF.Exp,
                                                     bias=mst[:, 3:4], scale=SCm), keys + ['mst3'], [('PTs', b)] + keys)
                            V(lambda e: e.scalar_tensor_tensor(out=PTs[b][:, 0, :], in0=PTs[b][:, 0, :],
                                                               scalar=seqmask[:, s:s + 1], in1=cm8[:], op0=ALU.mult,
                                                               op1=ALU.mult), [('PTs', b), 'seqmask', 'cm8'], [('PTs', b)])
                            MM(pb[1][0:64, 0:257], PTs[b][:, 0, :], Lnew[:], False, True, [('PTs', b), 'Lnew'], PK(1))
                            V(lambda e: e.reciprocal(out=mst[0:64, 5:6], in_=pb[1][0:64, 256:257]), PK(1), ['mst5'])
                            V(lambda e: e.tensor_scalar(out=olat[:], in0=pb[1][0:64, 0:256], scalar1=mst[0:64, 5:6],
                                                        scalar2=None, op0=ALU.mult), PK(1) + ['mst5'], ['olat'] + PK(1))
                            for c in range(2):
                                TR(pb[0][:, c * 64:(c + 1) * 64], olat[:, c * 128:(c + 1) * 128], ident[0:64, 0:64],
                                   ['olat', 'ident'], PK(0))
                            A(lambda e: e.copy(out=olatT[:, :, s, :], in_=pb[0][:, 0:128].rearrange("p (c q) -> p c q", c=2)),
                              PK(0), ['olatT'] + PK(0))

                    prev = None
                    if SST <= 5:
                        gq = gq[:SST - 2]
                    for (s, g) in gq:
                        cur = stage1(s, g)
                        if prev is not None and SST != 3:
                            stage2(prev)
                        prev = cur
                    if SST != 3:
                        stage2(prev)
                    if cfg.get('dbg_s'):
                        S.dma('sp', dbg_mst, mst[:], reads=['mst0', 'mst3', 'mst5'], stream='o')
                        S.dma('sp', dbg_lb, Lb[0][:, :, :].rearrange('p r d -> p (r d)'), reads=[('Lb', 0)], stream='o')
                        S.dma('sp', dbg_kb, Kb[0][:].rearrange('p r d -> p (r d)'), reads=[('Kb', 0)], stream='o')
                        S.dma('sp', dbg_olat, olat[:], reads=['olat'], stream='o')
                    if SST <= 6:
                        return
                    for h in range(8):
                        rh = (h % 2) * 64
                        pr = (h // 2) * 128
                        for c in range(2):
                            MM(pb[0][:, 0:128], wuv[:, c, pr:pr + 128],
                               olatT[:, c, :, h * 8:(h + 1) * 8], c == 0, c == 1,
                               ['wuv', 'olatT'], PK(0))
                        V(lambda e, h=h, rh=rh: e.tensor_copy(out=onb[rh:rh + 64, 4 + h // 2, :], in_=pb[0][rh:rh + 64, 0:128]),
                          PK(0), [('onb', 1)] + PK(0))


                def tile_body(t):
                    smp = (t == NT)
                    M = H['M']
                    mk = 'maskS' if smp else 'maskP'
                    S.dma('sp', xin, tile_rows(t), writes=['xin'], stream='x')
                    for half in range(2):
                        for j in range(4):
                            kc = half * 4 + j
                            TR(pb[half][:, j * 128:(j + 1) * 128], xin[:, kc * 128:(kc + 1) * 128], ident[:],
                               ['xin', 'ident'], [('pbq', half, j)])
                        A(lambda e, half=half: e.copy(out=xTt[:, half * 4:half * 4 + 4, :], in_=pbv(half, 4)),
                          PK(half), ['xTt'] + PK(half))
                    norm_range(gmix[:, 0, :], 'gmix', 0, 128, hTt[:], ['hTt'], xv=xTt[:], xk=['xTt'])
                    for c in range(16):
                        bi = c // 4
                        for kc in range(KC):
                            MM(pb[bi][:, (c % 4) * 128:(c % 4 + 1) * 128], wia[:, kc, c * 128:(c + 1) * 128],
                               hTt[:, kc, :], kc == 0, kc == KC - 1, ['wia', 'hTt'], [('pbq', bi, c % 4)])
                    for b2 in range(3):
                        if smp:
                            A(lambda e, b2=b2: e.copy(
                                out=H['cbufS'][:, b2 * 4:b2 * 4 + 4, :, 3:11],
                                in_=pb[b2][:].rearrange("p (c s t) -> p c s t", c=4, s=16)),
                              PK(b2), ['cbufS'] + PK(b2))
                        else:
                            A(lambda e, b2=b2: e.copy(out=H['cbuf'][:, b2 * 4:b2 * 4 + 4, 3:131], in_=pbv(b2, 4)),
                              PK(b2), ['cbuf'] + PK(b2))
                    A(lambda e: e.activation(out=zs[:], in_=pbv(3, 4), func=AF.Silu), PK(3), ['zs'] + PK(3))
                    for (bi, c0, cn) in ((4, 2048, 512), (5, 2560, 40)):
                        for kc in range(KC):
                            MM(pb[bi][:, :cn], hTt[:, kc, :], wia[:, kc, c0:c0 + cn], kc == 0, kc == KC - 1,
                               ['wia', 'hTt'], PK(bi))
                        A(lambda e, bi=bi, c0=c0, cn=cn: e.copy(out=tm[:, c0 - 2048:c0 - 2048 + cn], in_=pb[bi][:, :cn]),
                          PK(bi), ['tm'] + PK(bi))
                    if smp or t == NT - 1:
                        for j3 in range(3):
                            bi = 5 + j3
                            for kc in range(KC):
                                MM(pb[bi][:], hTt[:, kc, :], wia[:, kc, j3 * 512:(j3 + 1) * 512], kc == 0, kc == KC - 1,
                                   ['wia', 'hTt'], PK(bi))
                            A(lambda e, bi=bi, j3=j3: e.copy(out=qkvtm[:, j3 * 512:(j3 + 1) * 512], in_=pb[bi][:]),
                              PK(bi), ['qkvtm'] + PK(bi))
                        if smp:
                            for s in range(16):
                                S.dma('sp', conv_s_o[s], qkvtm[s * 8 + 5:s * 8 + 8, :], reads=['qkvtm'], stream='o')
                        else:
                            S.dma('sp', conv_p_o, qkvtm[125:128, :], reads=['qkvtm'], stream='o')
                    if STG <= 1:
                        return
                    if smp:
                        S.dma('sp', cst, conv_s_in, writes=['cst'], stream='c')
                        for c in range(12):
                            bi = c // 8
                            cc = c % 8
                            TR(pb[bi][:, cc * 48:(cc + 1) * 48], cst[:, c * 128:(c + 1) * 128], ident[0:48, 0:48],
                               ['cst', 'ident'], PK(bi))
                        A(lambda e: e.copy(out=H['cbufS'][:, 0:8, :, 0:3],
                                           in_=pb[0][:, 0:384].rearrange("p (c s r) -> p c s r", c=8, s=16)),
                          PK(0), ['cbufS'] + PK(0))
                        A(lambda e: e.copy(out=H['cbufS'][:, 8:12, :, 0:3],
                                           in_=pb[1][:, 0:192].rearrange("p (c s r) -> p c s r", c=4, s=16)),
                          PK(1), ['cbufS'] + PK(1))
                        accv = acc[:].rearrange("p c (s t) -> p c s t", s=16)
                        ctv = ctmp[:].rearrange("p c (s t) -> p c s t", s=16)

                        def win(i):
                            return H['cbufS'][:, :, :, i:i + 8]

                        def cwb(i):
                            return cw[:, :, i:i + 1].unsqueeze(3).to_broadcast([128, 12, 16, 8])
                        ck = 'cbufS'
                    else:
                        accv = acc[:]
                        ctv = ctmp[:]

                        def win(i):
                            return H['cbuf'][:, :, i:i + 128]

                        def cwb(i):
                            return cw[:, :, i:i + 1].to_broadcast([128, 12, 128])
                        ck = 'cbuf'
                    V(lambda e: e.tensor_tensor(out=accv, in0=win(0), in1=cwb(0), op=ALU.mult), [ck, 'cw'], ['acc'])
                    for i in range(1, 4):
                        G(lambda e, i=i: e.tensor_tensor(out=ctv, in0=win(i), in1=cwb(i), op=ALU.mult),
                          [ck, 'cw'], ['ctmp'])
                        V(lambda e: e.tensor_tensor(out=accv, in0=accv, in1=ctv, op=ALU.add), ['acc', 'ctmp'], ['acc'])
                    if not smp:
                        V(lambda e: e.tensor_copy(out=H['cbuf'][:, :, 0:3], in_=H['cbuf'][:, :, 128:131]), ['cbuf'], ['cbuf'])
                    A(lambda e: e.activation(out=acc[:], in_=acc[:], func=AF.Silu), ['acc'], ['acc'])
                    A(lambda e: e.activation(out=sq8[:], in_=qkvs[:, 0:8, :], func=AF.Square), ['qkvs'], ['sq8'])
                    for b2 in range(2):
                        MM(pb[b2][:], ones_bf[:], sq8[:, b2 * 4:b2 * 4 + 4, :].rearrange("p h t -> p (h t)"), True, True,
                           ['sq8', 'ones_bf'], PK(b2))
                        A(lambda e, b2=b2: e.activation(out=rn8[:, b2 * 4:b2 * 4 + 4, :], in_=pbv(b2, 4), func=AF.Sqrt,
                                                        bias=EPS, scale=1.0), PK(b2), ['rn8'] + PK(b2))
                    V(lambda e: e.reciprocal(out=rn8[:], in_=rn8[:]), ['rn8'], ['rn8'])
                    V(lambda e: e.tensor_tensor(out=qkn[:], in0=qkvs[:, 0:8, :], in1=rn8[:], op=ALU.mult),
                      ['qkvs', 'rn8'], ['qkn'])
                    for h in range(4):
                        TR(pb[2][:, h * 128:(h + 1) * 128], qkn[:, 4 + h, :], ident[:], ['qkn', 'ident'], [('pbq', 2, h)])
                        TR(pb[3][:, h * 128:(h + 1) * 128], qkvs[:, 8 + h, :], ident[:], ['qkvs', 'ident'],
                           [('pbq', 3, h)])
                    A(lambda e: e.copy(out=ktm[:], in_=pbv(2, 4)), PK(2), ['ktm'] + PK(2))
                    A(lambda e: e.copy(out=vtm[:], in_=pbv(3, 4)), PK(3), ['vtm'] + PK(3))
                    A(lambda e: e.activation(out=gt[:, 0:4], in_=tm[:, 0:4], func=AF.Sigmoid), ['tm'], ['gt'])
                    V(lambda e: e.tensor_tensor(out=gt[:, 8:12], in0=tm[:, 4:8], in1=dtb[:], op=ALU.add),
                      ['tm', 'dtb', 'gt'], ['gt'])
                    A(lambda e: e.activation(out=gt[:, 8:12], in_=gt[:, 8:12], func=AF.Exp), ['gt'], ['gt'])
                    A(lambda e: e.activation(out=gt[:, 8:12], in_=gt[:, 8:12], func=AF.Ln, bias=1.0), ['gt'], ['gt'])
                    V(lambda e: e.tensor_tensor(out=gt[:, 4:8], in0=gt[:, 8:12], in1=negA[:], op=ALU.mult),
                      ['gt', 'negA'], ['gt'])
                    MM(pb[4][:, 0:4], M['tri'][:], gt[:, 4:8], True, True, [mk, 'gt'], PK(4))
                    A(lambda e: e.copy(out=gcc[:, 0:4], in_=pb[4][:, 0:4]), PK(4), ['gcc'] + PK(4))
                    MM(pb[4][:, 0:4], M['blk'][:], gt[:, 4:8], True, True, [mk, 'gt'], PK(4))
                    A(lambda e: e.copy(out=gcc[:, 4:8], in_=pb[4][:, 0:4]), PK(4), ['gcc'] + PK(4))
                    V(lambda e: e.tensor_tensor(out=G4[:], in0=M['tri'][:].unsqueeze(1).to_broadcast(f4),
                                                in1=gt[:, 4:8].unsqueeze(2).to_broadcast(f4), op=ALU.mult),
                      [mk, 'gt'], ['G4'])
                    MM(pb[5][:], ones_f[:], G4[:].rearrange("p h t -> p (h t)"), True, True, ['ones_f', 'G4'], PK(5))
                    A(lambda e: e.copy(out=gcr[:], in_=pbv(5, 4)), PK(5), ['gcr'] + PK(5))
                    V(lambda e: e.tensor_tensor(out=G4[:], in0=ident[:].unsqueeze(1).to_broadcast(f4),
                                                in1=gt[:, 0:4].unsqueeze(2).to_broadcast(f4), op=ALU.mult),
                      ['ident', 'gt', 'G4'], ['G4'])
                    MM(pb[6][:], ones_f[:], G4[:].rearrange("p h t -> p (h t)"), True, True, ['ones_f', 'G4'], PK(6))
                    A(lambda e: e.copy(out=br[:], in_=pbv(6, 4)), PK(6), ['br'] + PK(6))
                    V(lambda e: e.tensor_tensor(out=a1[:], in0=gcr[:], in1=gcc[:, 0:4].unsqueeze(2).to_broadcast(f4),
                                                op=ALU.subtract), ['gcr', 'gcc'], ['a1'])
                    V(lambda e: e.tensor_tensor(out=E1s[:], in0=a1[:], in1=M['nms'][:].unsqueeze(1).to_broadcast(f4),
                                                op=ALU.add), ['a1', mk], ['E1s'])
                    A(lambda e: e.activation(out=E1s[:], in_=E1s[:], func=AF.Exp), ['E1s'], ['E1s'])
                    V(lambda e: e.tensor_tensor(out=E1i[:], in0=a1[:], in1=M['nmi'][:].unsqueeze(1).to_broadcast(f4),
                                                op=ALU.add), ['a1', mk], ['E1i'])
                    A(lambda e: e.activation(out=E1i[:], in_=E1i[:], func=AF.Exp), ['E1i'], ['E1i'])
                    V(lambda e: e.tensor_tensor(out=a1[:], in0=gcc[:, 0:4].unsqueeze(2).to_broadcast(f4), in1=gcr[:],
                                                op=ALU.subtract), ['gcr', 'gcc', 'a1'], ['a1'])
                    V(lambda e: e.tensor_tensor(out=E2s[:], in0=a1[:], in1=M['nmsT'][:].unsqueeze(1).to_broadcast(f4),
                                                op=ALU.add), ['a1', mk], ['E2s'])
                    A(lambda e: e.activation(out=E2s[:], in_=E2s[:], func=AF.Exp), ['E2s'], ['E2s'])
                    V(lambda e: e.scalar_tensor_tensor(out=E1s[:], in0=E1s[:], scalar=-1.0, in1=br[:], op0=ALU.mult,
                                                       op1=ALU.mult), ['E1s', 'br'], ['E1s'])
                    V(lambda e: e.scalar_tensor_tensor(out=E2s[:], in0=E2s[:], scalar=-1.0,
                                                       in1=gt[:, 0:4].unsqueeze(2).to_broadcast(f4), op0=ALU.mult,
                                                       op1=ALU.mult), ['E2s', 'gt'], ['E2s'])
                    for h in range(4):
                        MM(pb[0][:, h * 128:(h + 1) * 128], qkn[:, 4 + h, :], qkn[:, 4 + h, :], True, True, ['qkn'],
                           [('pbq', 0, h)])
                        MM(pb[1][:, h * 128:(h + 1) * 128], qkn[:, 4 + h, :], qkn[:, h, :], True, True, ['qkn'],
                           [('pbq', 1, h)])
                    V(lambda e: e.tensor_tensor(out=Xa[0][:], in0=pbv(0, 4), in1=E1s[:], op=ALU.mult),
                      PK(0) + ['E1s'], [('Xa', 0)])
                    V(lambda e: e.tensor_tensor(out=XTa[0][:], in0=pbv(0, 4), in1=E2s[:], op=ALU.mult),
                      PK(0) + ['E2s'], [('XTa', 0)] + PK(0))
                    V(lambda e: e.scalar_tensor_tensor(out=itr[:], in0=pbv(1, 4), scalar=SCq, in1=E1i[:], op0=ALU.mult,
                                                       op1=ALU.mult), PK(1) + ['E1i'], ['itr'] + PK(1))
                    V(lambda e: e.tensor_tensor(out=Pm[:], in0=Xa[0][:], in1=ident[:].unsqueeze(1).to_broadcast(f4),
                                                op=ALU.add), [('Xa', 0), 'ident'], ['Pm'])
                    nlev = 2 if smp else 6
                    for lv in range(nlev):
                        a, b = lv % 2, (lv + 1) % 2
                        last = (lv == nlev - 1)
                        for h in range(4):
                            hs = slice(h * 128, (h + 1) * 128)
                            if not last:
                                MM(pb[2][:, hs], XTa[a][:, h, :], Xa[a][:, h, :], True, True, [('XTa', a), ('Xa', a)],
                                   [('pbq', 2, h)])
                            MM(pb[3][:, hs], Xa[a][:, h, :], XTa[a][:, h, :], True, True, [('XTa', a), ('Xa', a)],
                               [('pbq', 3, h)])
                        if not last:
                            A(lambda e, b=b: e.copy(out=Xa[b][:], in_=pbv(2, 4)), PK(2), [('Xa', b)] + PK(2))
                        A(lambda e, b=b: e.copy(out=XTa[b][:], in_=pbv(3, 4)), PK(3), [('XTa', b)] + PK(3))
                        for h in range(4):
                            hs = slice(h * 128, (h + 1) * 128)
                            MM(pb[4][:, hs], XTa[b][:, h, :], Pm[:, h, :], True, True, [('XTa', b), 'Pm'], [('pbq', 4, h)])
                        V(lambda e: e.tensor_tensor(out=Pm[:], in0=Pm[:], in1=pbv(4, 4), op=ALU.add),
                          ['Pm'] + PK(4), ['Pm'] + PK(4))
                    V(lambda e: e.tensor_tensor(out=vb[:], in0=vtm[:], in1=gt[:, 0:4].unsqueeze(2).to_broadcast(f4),
                                                op=ALU.mult), ['vtm', 'gt'], ['vb'])
                    A(lambda e: e.activation(out=gt[:, 8:12], in_=gcc[:, 0:4], func=AF.Exp), ['gcc', 'gt'], ['gt'])
                    V(lambda e: e.tensor_tensor(out=gt[:, 12:16], in0=gt[:, 8:12], in1=gt[:, 0:4], op=ALU.mult),
                      ['gt'], ['gt'])
                    V(lambda e: e.tensor_tensor(out=kbg[:], in0=ktm[:], in1=gt[:, 12:16].unsqueeze(2).to_broadcast(f4),
                                                op=ALU.mult), ['ktm', 'gt'], ['kbg'])
                    V(lambda e: e.tensor_tensor(out=gt[:, 8:12], in0=gcc[:, 4:8], in1=gcc[:, 0:4], op=ALU.subtract),
                      ['gcc', 'gt'], ['gt'])
                    A(lambda e: e.activation(out=gt[:, 8:12], in_=gt[:, 8:12], func=AF.Exp), ['gt'], ['gt'])
                    V(lambda e: e.tensor_tensor(out=kdec[:], in0=ktm[:], in1=gt[:, 8:12].unsqueeze(2).to_broadcast(f4),
                                                op=ALU.mult), ['ktm', 'gt'], ['kdec'])
                    for h in range(4):
                        hs = slice(h * 128, (h + 1) * 128)
                        MM(pb[5][:, hs], vb[:, h, :], Pm[:, h, :], True, True, ['vb', 'Pm'], [('pbq', 5, h)])
                        MM(pb[6][:, hs], kbg[:, h, :], Pm[:, h, :], True, True, ['kbg', 'Pm'], [('pbq', 6, h)])
                    A(lambda e: e.copy(out=uT[:], in_=pbv(5, 4)), PK(5), ['uT'] + PK(5))
                    A(lambda e: e.copy(out=wT[:], in_=pbv(6, 4)), PK(6), ['wT'] + PK(6))
                    A(lambda e: e.activation(out=a1[:], in_=gcr[:], func=AF.Exp), ['gcr', 'a1'], ['a1'])
                    V(lambda e: e.scalar_tensor_tensor(out=qg[:], in0=qkn[:, 0:4, :], scalar=SCq, in1=a1[:], op0=ALU.mult,
                                                       op1=ALU.mult), ['qkn', 'a1'], ['qg'])
                    if smp:
                        A(lambda e: e.activation(out=eglr[:], in_=gcr[:].rearrange("p h (s t) -> p h s t", t=8)[:, :, :, 7],
                                                 func=AF.Exp), ['gcr'], ['eglr'])
                        V(lambda e: e.memset(pb[7][:], 0.0), [], PK(7))
                        V(lambda e: e.memset(pb[3][:], 0.0), [], PK(3))
                        for s in range(16):
                            ss_ = slice(s * 8, (s + 1) * 8)
                            S.dma('sp', Sgs[:], gdn_s_in[s].rearrange("h k v -> k h v"), writes=['Sgs'], stream='s0')
                            for h in range(4):
                                MM(pb[7][:, h * 128 + s * 8:h * 128 + s * 8 + 8], Sgs[:, h, :], wT[:, h, ss_],
                                   False, False, ['Sgs', 'wT'], [('pbq', 7, h)])
                                MM(pb[3][:, h * 128 + s * 8:h * 128 + s * 8 + 8], Sgs[:, h, :], qg[:, h, ss_],
                                   False, False, ['Sgs', 'qg'], [('pbq', 3, h)], inc=(h == 3))
                    else:
                        A(lambda e: e.activation(out=eglr[:, :, 0], in_=gcr[:, :, 127], func=AF.Exp), ['gcr'], ['eglr'])
                        V(lambda e: e.memset(pb[3][:], 0.0), [], PK(3))
                        for h in range(4):
                            hs = slice(h * 128, (h + 1) * 128)
                            MM(pb[7][:, hs], Sg[:, h, :], wT[:, h, :], True, True, ['Sg', 'wT'], [('pbq', 7, h)])
                            MM(pb[3][:, hs], Sg[:, h, :], qg[:, h, :], False, False, ['Sg', 'qg'], [('pbq', 3, h)])
                    V(lambda e: e.tensor_tensor(out=vnT[:], in0=uT[:], in1=pbv(7, 4), op=ALU.subtract),
                      ['uT'] + PK(7), ['vnT'] + PK(7))
                    for h in range(4):
                        TR(pb[2][:, h * 128:(h + 1) * 128], vnT[:, h, :], ident[:], ['vnT', 'ident'], [('pbq', 2, h)])
                    A(lambda e: e.copy(out=vn[:], in_=pbv(2, 4)), PK(2), ['vn'] + PK(2))
                    for h in range(4):
                        hs = slice(h * 128, (h + 1) * 128)
                        MM(pb[3][:, hs], vn[:, h, :], itr[:, h, :], False, True, ['vn', 'itr'], [('pbq', 3, h)])
                    if smp:
                        for s in range(16):
                            V(lambda e, s=s: e.tensor_scalar(out=kdm[:], in0=kdec[:], scalar1=seqmask[:, s:s + 1],
                                                            scalar2=None, op0=ALU.mult), ['kdec', 'seqmask'], ['kdm'])
                            for h in range(4):
                                MM(pb[4][:, h * 128:(h + 1) * 128], kdm[:, h, :], vn[:, h, :], True, True, ['kdm', 'vn'],
                                   [('pbq', 4, h)])
                            S.dma('sp', Sgs[:], gdn_s_in[s].rearrange("h k v -> k h v"), writes=['Sgs'], stream='s0')
                            V(lambda e, s=s: e.tensor_tensor(out=Sgs[:], in0=Sgs[:],
                                                             in1=eglr[:, :, s:s + 1].to_broadcast(f4), op=ALU.mult),
                              ['Sgs', 'eglr'], ['Sgs'])
                            V(lambda e: e.tensor_tensor(out=Sgs[:], in0=Sgs[:], in1=pbv(4, 4), op=ALU.add),
                              ['Sgs'] + PK(4), ['Sgs'] + PK(4))
                            S.dma('sp', gdn_s_o[s].rearrange("h k v -> k h v"), Sgs[:], reads=['Sgs'], stream='o')
                    else:
                        for h in range(4):
                            MM(pb[4][:, h * 128:(h + 1) * 128], kdec[:, h, :], vn[:, h, :], True, True, ['kdec', 'vn'],
                               [('pbq', 4, h)])
                        V(lambda e: e.tensor_tensor(out=Sg[:], in0=Sg[:], in1=eglr[:, :, 0:1].to_broadcast(f4),
                                                    op=ALU.mult), ['Sg', 'eglr'], ['Sg'])
                        V(lambda e: e.tensor_tensor(out=Sg[:], in0=Sg[:], in1=pbv(4, 4), op=ALU.add),
                          ['Sg'] + PK(4), ['Sg'] + PK(4))
                        if t == NT - 1:
                            S.dma('sp', gdn_p_o.rearrange("h k v -> k h v"), Sg[:], reads=['Sg'], stream='o')
                    A(lambda e: e.activation(out=osq[:], in_=pbv(3, 4), func=AF.Square), PK(3), ['osq'])
                    MM(pb[0][:], ones_bf[:], osq[:].rearrange("p h t -> p (h t)"), True, True, ['osq', 'ones_bf'], PK(0))
                    A(lambda e: e.activation(out=rno[:], in_=pb[0][:], func=AF.Sqrt, bias=EPS, scale=1.0 / 128),
                      PK(0), ['rno'] + PK(0))
                    V(lambda e: e.reciprocal(out=rno[:], in_=rno[:]), ['rno'], ['rno'])
                    V(lambda e: e.scalar_tensor_tensor(out=ona[:], in0=pbv(3, 4), scalar=gdnn[:, 0:1],
                                                       in1=rno[:].rearrange("p (h t) -> p h t", h=4), op0=ALU.mult,
                                                       op1=ALU.mult), PK(3) + ['gdnn', 'rno'], ['ona'] + PK(3))
                    G(lambda e: e.tensor_tensor(out=onb[:, 0:4, :], in0=ona[:], in1=zs[:], op=ALU.mult),
                      ['ona', 'zs'], [('onb', 0)])
                    if STG <= 2:
                        return
                    A(lambda e: e.activation(out=rt[:, 0:4, :].rearrange("p a b -> p (a b)"), in_=tm[:, 8:264],
                                             func=AF.Square, accum_out=ssn[:, 0:1]), ['tm'], ['rt', 'ssn'])
                    A(lambda e: e.activation(out=rt[:, 4:8, :].rearrange("p a b -> p (a b)"), in_=tm[:, 264:520],
                                             func=AF.Square, accum_out=ssn[:, 1:2]), ['tm'], ['rt', 'ssn'])
                    A(lambda e: e.activation(out=ssn[:, 2:4], in_=ssn[:, 0:2], func=AF.Sqrt, bias=EPS, scale=1.0 / 256),
                      ['ssn'], ['ssn'])
                    V(lambda e: e.reciprocal(out=ssn[:, 2:4], in_=ssn[:, 2:4]), ['ssn'], ['ssn'])
                    V(lambda e: e.scalar_tensor_tensor(out=cqn[:], in0=tm[:, 8:264], scalar=ssn[:, 2:3], in1=qnb[:],
                                                       op0=ALU.mult, op1=ALU.mult), ['tm', 'ssn', 'qnb'], ['cqn'])
                    V(lambda e: e.scalar_tensor_tensor(out=ckvn[:], in0=tm[:, 264:520], scalar=ssn[:, 3:4], in1=kvnb[:],
                                                       op0=ALU.mult, op1=ALU.mult), ['tm', 'ssn', 'kvnb'], ['ckvn'])
                    lat_dst = lat_s_o if smp else lat_p_o[t * 128:(t + 1) * 128, :]
                    kpe_dst = kpe_s_o if smp else kpe_p_o[t * 128:(t + 1) * 128, :]
                    S.dma('sp', lat_dst, ckvn[:], reads=['ckvn'], stream='o')
                    cs = cosb[:, t, :]
                    sn = sinb[:, t, :]
                    x1 = tm[:, 520:536]
                    x2 = tm[:, 536:552]
                    V(lambda e: e.tensor_tensor(out=kst[:, 0:16], in0=x1, in1=cs, op=ALU.mult), ['tm', 'cosb', 'kst'], ['kst'])
                    V(lambda e: e.tensor_tensor(out=rt[:, 0, 0:16], in0=x2, in1=sn, op=ALU.mult), ['tm', 'sinb', 'rt'], ['rt'])
                    V(lambda e: e.tensor_tensor(out=kst[:, 0:16], in0=kst[:, 0:16], in1=rt[:, 0, 0:16], op=ALU.subtract),
                      ['kst', 'rt'], ['kst'])
                    V(lambda e: e.tensor_tensor(out=kst[:, 16:32], in0=x2, in1=cs, op=ALU.mult), ['tm', 'cosb', 'kst'], ['kst'])
                    V(lambda e: e.tensor_tensor(out=rt[:, 0, 16:32], in0=x1, in1=sn, op=ALU.mult), ['tm', 'sinb', 'rt'], ['rt'])
                    V(lambda e: e.tensor_tensor(out=kst[:, 16:32], in0=kst[:, 16:32], in1=rt[:, 0, 16:32], op=ALU.add),
                      ['kst', 'rt'], ['kst'])
                    S.dma('sp', kpe_dst, kst[:, 0:32], reads=['kst'], stream='o')
                    if STG <= 3:
                        return
                    for c in range(2):
                        TR(pb[6][:, c * 128:(c + 1) * 128], cqn[:, c * 128:(c + 1) * 128], ident[:], ['cqn', 'ident'],
                           [('pbq', 6, c)])
                        TR(pb[6][:, 256 + c * 128:256 + (c + 1) * 128], ckvn[:, c * 128:(c + 1) * 128], ident[:],
                           ['ckvn', 'ident'], [('pbq', 6, 2 + c)])
                    TR(pb[7][:, 0:128], kst[:], ident[:], ['kst', 'ident'], [('pbq', 7, 0)])
                    A(lambda e: e.copy(out=cqnT[:], in_=pb[6][:, 0:256].rearrange("p (c t) -> p c t", c=2)),
                      PK(6), ['cqnT'])
                    A(lambda e: e.copy(out=ckvnT[:], in_=pb[6][:, 256:512].rearrange("p (c t) -> p c t", c=2)),
                      PK(6), ['ckvnT'] + PK(6))
                    kcols = tile_cols(t)
                    kkey = ('KT', t)
                    if not smp:
                        A(lambda e: e.copy(out=H['KTp'][:, kcols], in_=pb[7][0:32, 0:128]), PK(7), [kkey] + PK(7))
                    for (bi, c0, cn) in ((0, 0, 512), (1, 512, 256)):
                        for c in range(2):
                            MM(pb[bi][:, :cn], cqnT[:, c, :], wuq[:, c, c0:c0 + cn], c == 0, c == 1, ['cqnT', 'wuq'], PK(bi))
                        A(lambda e, bi=bi, c0=c0, cn=cn: e.copy(out=qtm[:].rearrange("p h d -> p (h d)")[:, c0:c0 + cn],
                                                                in_=pb[bi][:, :cn]), PK(bi), ['qtm'] + PK(bi))
                    csb = cs.unsqueeze(1).to_broadcast([128, 8, 16])
                    snb = sn.unsqueeze(1).to_broadcast([128, 8, 16])
                    q1 = qtm[:, :, 64:80]
                    q2 = qtm[:, :, 80:96]
                    V(lambda e: e.tensor_copy(out=qn2[:].rearrange("p m (a d) -> p (m a) d", a=2), in_=qtm[:, :, 0:64]),
                      ['qtm'], ['qtb'])
                    V(lambda e: e.tensor_tensor(out=rt[:, :, 0:16], in0=q1, in1=csb, op=ALU.mult), ['qtm', 'cosb', 'rt'], ['rt'])
                    V(lambda e: e.tensor_tensor(out=rt[:, :, 16:32], in0=q2, in1=snb, op=ALU.mult), ['qtm', 'sinb', 'rt'], ['rt'])
                    V(lambda e: e.tensor_tensor(out=qp2[:, :, 0:16], in0=rt[:, :, 0:16], in1=rt[:, :, 16:32],
                                                op=ALU.subtract), ['rt', 'qtb'], ['qtb'])
                    V(lambda e: e.tensor_tensor(out=rt[:, :, 32:48], in0=q2, in1=csb, op=ALU.mult), ['qtm', 'cosb', 'rt'], ['rt'])
                    V(lambda e: e.tensor_tensor(out=rt[:, :, 48:64], in0=q1, in1=snb, op=ALU.mult), ['qtm', 'sinb', 'rt'], ['rt'])
                    V(lambda e: e.tensor_tensor(out=qp2[:, :, 16:32], in0=rt[:, :, 32:48], in1=rt[:, :, 48:64],
                                                op=ALU.add), ['rt', 'qtb'], ['qtb'])
                    for m in range(4):
                        TR(pbbf(2)[:, m * 128:(m + 1) * 128], qn2[:, m, :], ident_bf[:], ['qtb', 'ident_bf'],
                           [('pbq', 2, m // 2)])
                    for h in range(8):
                        TR(pbbf(1)[0:32, h * 128:(h + 1) * 128], qp2[:, h, :], ident_bf[:], ['qtb', 'ident_bf'],
                           [('pbq', 1, h // 2)])
                    A(lambda e: e.copy(out=QTn[:].rearrange("p m t -> p (m t)"), in_=pbbf(2)[:, 0:512]), PK(2),
                      ['QT'] + PK(2))
                    A(lambda e: e.copy(out=QTp[:].rearrange("p h t -> p (h t)"), in_=pbbf(1)[0:32, :]), PK(1),
                      ['QT'] + PK(1))
                    if not smp:
                        for m in range(4):
                            for c in range(2):
                                MM(pb[3][:, m * 128:(m + 1) * 128], wuk[:, c, m * 128:(m + 1) * 128], ckvnT[:, c, :],
                                   c == 0, c == 1, ['wuk', 'ckvnT'], [('pbq', 3, m)])
                        A(lambda e: e.copy(out=H['KTn'][:, :, kcols], in_=pbv(3, 4)), PK(3), [kkey] + PK(3))
                        for c in range(2):
                            MM(pb[5][:], ckvnT[:, c, :], wuv[:, c, :], c == 0, c == 1, ['ckvnT', 'wuv'], PK(5))
                        A(lambda e: e.copy(out=H['Vt'][:, t, :], in_=pb[5][:]), PK(5), [('Vt', t)] + PK(5))
                    if STG <= 4:
                        return
                    if not smp:
                        nkt = t + 1
                        items = [(h, g0) for h in range(8) for g0 in range(0, nkt, 4)]

                        def qk(idx_):
                            h, g0 = items[idx_]
                            rh = (h % 2) * 64
                            gn = min(4, nkt - g0)
                            sbk = 6 + idx_ % 2
                            for j in range(gn):
                                kt = g0 + j
                                MM(pb[sbk][:, j * 128:(j + 1) * 128], H['KTn'][rh:rh + 64, h // 2, kt * 128:(kt + 1) * 128],
                                   QTn[rh:rh + 64, h // 2, :], True, False, [('KT', kt), 'QT'], [('pbq', sbk, j)], inc=True)
                                MM(pb[sbk][:, j * 128:(j + 1) * 128], H['KTp'][:, kt * 128:(kt + 1) * 128],
                                   QTp[:, h, :], False, True, [('KT', kt), 'QT'], [('pbq', sbk, j)], inc=True)

                        def pv(idx_):
                            h, g0 = items[idx_]
                            ob = h // 4
                            hs = slice((h % 4) * 128, (h % 4 + 1) * 128)
                            pr = (h // 2) * 128
                            rh = (h % 2) * 64
                            gn = min(4, nkt - g0)
                            sbk = 6 + idx_ % 2
                            pi = idx_ % 2
                            A(lambda e: e.activation(out=H['PT'][pi][:, :gn * 128], in_=pb[sbk][:, :gn * 128], func=AF.Exp,
                                                     scale=SCm), PK(sbk), [('PT', pi)] + PK(sbk))
                            if g0 + gn == nkt:
                                jd = gn - 1
                                G(lambda e: e.tensor_tensor(out=H['PT'][pi][:, jd * 128:(jd + 1) * 128],
                                                            in0=H['PT'][pi][:, jd * 128:(jd + 1) * 128],
                                                            in1=H['M']['cmask'][:], op=ALU.mult),
                                  [('PT', pi), 'maskP'], [('PT', pi)])
                            for j in range(gn):
                                kt = g0 + j
                                first = (kt == 0)
                                lastk = (kt == nkt - 1)
                                MM(pb[ob][:, hs], H['Vt'][:, kt, pr:pr + 128], H['PT'][pi][:, j * 128:(j + 1) * 128], first, lastk,
                                   [('Vt', kt), ('PT', pi)], [('pbq', ob, h % 4)], inc=True)
                                MM(pb[2 + ob][:, hs], ones_bf[:], H['PT'][pi][:, j * 128:(j + 1) * 128], first, lastk,
                                   [('PT', pi), 'ones_bf'], [('pbq', 2 + ob, h % 4)], inc=True)
                            if g0 + gn == nkt:
                                V(lambda e: e.reciprocal(out=H['rsum'][rh:rh + 64, :], in_=pb[2 + ob][rh:rh + 64, hs]),
                                  [('pbq', 2 + ob, h % 4)], ['rsum', ('pbq', 2 + ob, h % 4)])
                                V(lambda e: e.tensor_tensor(out=onb[rh:rh + 64, 4 + h // 2, :], in0=pb[ob][rh:rh + 64, hs],
                                                            in1=H['rsum'][rh:rh + 64, :], op=ALU.mult),
                                  [('pbq', ob, h % 4), 'rsum'], [('onb', 1), ('pbq', ob, h % 4)])
                        if not cfg.get('lookahead'):
                            for i_ in range(len(items)):
                                qk(i_)
                                pv(i_)
                        else:
                            qk(0)
                            for i_ in range(len(items)):
                                if i_ + 1 < len(items):
                                    qk(i_ + 1)
                                pv(i_)
                    else:
                        sample_attention()
                    if STG <= 5:
                        return
                    for dc in range(KC):
                        bi = 4 + dc // 4
                        for kc in range(KC):
                            MM(pb[bi][:, (dc % 4) * 128:(dc % 4 + 1) * 128], woa[:, kc, dc * 128:(dc + 1) * 128],
                               onb[:, kc, :], kc == 0, kc == KC - 1, ['woa', ('onb', 0), ('onb', 1)],
                               [('pbq', bi, dc % 4)])
                    for b2 in range(2):
                        V(lambda e, b2=b2: e.tensor_tensor(out=xTt[:, b2 * 4:b2 * 4 + 4, :], in0=xTt[:, b2 * 4:b2 * 4 + 4, :],
                                                           in1=pbv(4 + b2, 4), op=ALU.add),
                          PK(4 + b2) + ['xTt'], ['xTt'] + PK(4 + b2))

                with ExitStack() as sph:
                    ssb = mk_sb(sph)
                    H['ssb'] = ssb
                    H['M'] = build_masks('S', True, ssb)
                    H['cbufS'] = ssb("cbufS", [128, 12, 16, 11])
                    H['ssb2'] = ssb
                    tile_body(NT)
                    S.dma('sp', xsp[NT], xTt[:], reads=['xTt'], stream='xs')
                    S.barrier()
                with ExitStack() as pph:
                    qsb = mk_sb(pph)
                    H['M'] = build_masks('P', False, qsb)
                    H['cbuf'] = qsb("cbuf", [128, 12, 131])
                    H['KTn'] = qsb("KTn", [128, 4, TP], BF16)
                    H['KTp'] = qsb("KTp", [32, TP], BF16)
                    H['Vt'] = qsb("Vt", [128, NT, 512], BF16)
                    H['PT'] = [qsb("PT%d" % i, [128, 512], BF16) for i in range(2)]
                    H['rsum'] = qsb("rsum", [128, 128])
                    G(lambda e: e.memset(H['cbuf'][:], 0.0), [], ['cbuf'])
                    for t in range(NT):
                        tile_body(t)
                        S.dma('sp', xsp[t], xTt[:], reads=['xTt'], stream='xs')
                    S.barrier()

        def fold_x():
            for t in range(NT + 1):
                ks = [S.lastw.get(('xTd', dc, t)) for dc in range(KC)]
                ks = [k for k in ks if k is not None]
                if ks:
                    S.lastw[('xT', t)] = max(ks, key=lambda tk: tk[1])
                    S.reads[('xT', t)] = {}

        if 'A' in PHASES:
            ab_layer()
            xTh[0] = sb("xT", [128, KC, TT], F32)
            for t in range(NT + 1):
                S.dma('sp', xTh[0][:, :, tile_cols(t)], xsp[t], writes=[('xT', t)], stream='x')
        else:
            xTh[0] = sb("xT", [128, KC, TT], F32)
            phase0()
        xT = xTh[0]

        for ph_ in PHASES:
            if ph_ == 'B':
                mlp_layer(0)
                fold_x()
            elif ph_ == 'D':
                mlp_layer(1)
                fold_x()
            elif ph_ == 'C':
                hgrn_layer()

        gfin = sb("gfin", [128, D], F32)
        S.dma('sp', gfin[:], final_norm.partition_broadcast(128), writes=['gfin'], stream='c')
        yo = [sb("yo%d" % i, [128, D], F32) for i in range(2)]
        ss2 = [sb("ss2_%d" % i, [128, 4], F32) for i in range(2)]
        junk = sb("junk", [128, 512], F32)
        for t in range(NT + 1):
            b = t % 2
            for half in range(2):
                pi = 6 + half
                for j in range(4):
                    kc = half * 4 + j
                    TR(pb[pi][:, j * 128:(j + 1) * 128], xT[:, kc, tile_cols(t)], ident[:],
                       [('xT', t), 'ident'], [('pbq', pi, j)])
                A(lambda e, half=half, pi=pi: e.activation(out=junk[:], in_=pb[pi][:], func=AF.Square,
                                                           accum_out=ss2[b][:, half:half + 1]),
                  [('pbq', pi, j) for j in range(4)], ['junk', ('ss2', b, half)])
            V(lambda e: e.tensor_tensor(out=ss2[b][:, 2:3], in0=ss2[b][:, 0:1], in1=ss2[b][:, 1:2], op=ALU.add),
              [('ss2', b, 0), ('ss2', b, 1)], [('ss2', b, 2)])
            A(lambda e: e.activation(out=ss2[b][:, 3:4], in_=ss2[b][:, 2:3], func=AF.Sqrt, bias=EPS,
                                     scale=1.0 / D), [('ss2', b, 2)], [('ss2', b, 3)])
            V(lambda e: e.reciprocal(out=ss2[b][:, 3:4], in_=ss2[b][:, 3:4]), [('ss2', b, 3)], [('ss2', b, 3)])
            for half in range(2):
                pi = 6 + half
                V(lambda e, half=half, pi=pi: e.scalar_tensor_tensor(
                    out=yo[b][:, half * 512:(half + 1) * 512], in0=pb[pi][:], scalar=ss2[b][:, 3:4],
                    in1=gfin[:, half * 512:(half + 1) * 512], op0=ALU.mult, op1=ALU.mult),
                    [('pbq', pi, j) for j in range(4)] + [('ss2', b, 3), 'gfin'],
                    [('yo', b, half)] + [('pbq', pi, j) for j in range(4)])
            dst = y_s if t == NT else y_p[t * 128:(t + 1) * 128, :]
            S.dma('sp', dst, yo[b][:], reads=[('yo', b, 0), ('yo', b, 1)], stream='y')
        S.finish('sp')
        dl = S.check_deadlock()
        print("instructions:", S.n_inst, "deadlock:", dl)
        assert not dl, "semaphore deadlock in the emitted program"
    return nc


def core_inputs(cfg, c, inp):
    m = {}
    m["x_p"] = np.ascontiguousarray(inp['x_prompt'][c])
    m["x_s"] = np.ascontiguousarray(inp['x_sample'][16 * c:16 * c + 16].reshape(128, D))
    m["w_up"] = inp['w_up']
    m["w_down"] = inp['w_down']
    m["mlp_norm_fm"] = np.ascontiguousarray(inp['mlp_norm'].reshape(2, KC, 128).transpose(0, 2, 1))
    m["mix_norm_fm"] = np.ascontiguousarray(inp['mix_norm'].reshape(2, KC, 128).transpose(0, 2, 1))
    m["final_norm"] = np.ascontiguousarray(inp['final_norm'].reshape(1, D))
    m["w_in_c"] = inp['w_in_c'][0]
    m["w_out_c"] = inp['w_out_c'][0]
    m["lb_fm"] = np.ascontiguousarray(inp['lb_logits_c'].reshape(2, 8, 128).transpose(0, 2, 1))
    m["g_norm_c"] = np.ascontiguousarray(inp['g_norm_c'][0].reshape(128, 1))
    m["hgrn_in"] = np.ascontiguousarray(inp['state_hgrn'][0, 16 * c:16 * c + 16])
    m["w_in_ab"] = inp['w_in_ab'][0]
    m["w_out_ab"] = inp['w_out_ab'][0]
    m["w_uq"] = np.ascontiguousarray(inp['w_uq_ab'][0].reshape(256, 768))
    m["w_uk"] = np.ascontiguousarray(inp['w_uk_ab'][0].reshape(256, 512))
    m["w_uv"] = np.ascontiguousarray(inp['w_uv_ab'][0].reshape(256, 512))
    m["conv_w_fm"] = np.ascontiguousarray(inp['conv_w_ab'][0].reshape(4, 12, 128).transpose(2, 1, 0))
    m["a_log"] = np.ascontiguousarray(inp['a_log_ab'].reshape(1, 4))
    m["dt_bias"] = np.ascontiguousarray(inp['dt_bias_ab'].reshape(1, 4))
    m["gdn_norm"] = np.ascontiguousarray(inp['gdn_norm_ab'].reshape(128, 1))
    m["q_norm"] = np.ascontiguousarray(inp['q_norm_ab'].reshape(1, 256))
    m["kv_norm"] = np.ascontiguousarray(inp['kv_norm_ab'].reshape(1, 256))
    TP = cfg['NT'] * 128
    past = inp['page_table'].shape[1] * inp['cache_mla_latent'].shape[2]
    pos = np.concatenate([np.arange(TP), np.tile(past + np.arange(8), 16)]).astype(np.float32)
    inv = (np.float32(10000.0) ** (-np.arange(16, dtype=np.float32) / np.float32(16))).astype(np.float32)
    ang = (pos[:, None] * inv[None, :]).astype(np.float32)
    m["rope_cos"] = np.cos(ang).astype(np.float32)
    m["rope_sin"] = np.sin(ang).astype(np.float32)
    NPG = cfg['NPG']
    R = NPG
    SB = 128 // R
    RG = min(8, R)
    NG = R // RG
    m["cache_lat"] = inp['cache_mla_latent'][0]
    m["cache_kr"] = inp['cache_mla_krope'][0]
    m["page_table"] = np.ascontiguousarray(inp['page_table'][16 * c:16 * c + 16]).astype(np.int32)
    wkt = np.zeros((2, 64, 8, 256), np.float32)
    wu = inp['w_uk_ab'][0]
    for h in range(8):
        wkt[h % 2, :, h, :] = wu[:, h, :].T
    m["w_ukT"] = wkt.reshape(128, 2048)
    p = np.arange(128)
    m["rep_in"] = (p[None, :] // SB == np.arange(NPG)[:, None]).astype(np.float32)
    gbase = np.zeros((128, NG + 1), np.float32)
    for g in range(NG):
        gbase[:, g] = (p % SB) * (R // RG) + g
    gbase[:, NG] = p % SB
    m["gbase_in"] = gbase
    tq = np.arange(64) % 8
    m["cm8_in"] = ((p % 8)[:, None] <= tq[None, :]).astype(np.float32)
    m["conv_s_in"] = np.ascontiguousarray(inp['state_gdn_conv'][0, 16 * c:16 * c + 16].reshape(48, 1536))
    m["gdn_s_in"] = np.ascontiguousarray(inp['state_gdn'][0, 16 * c:16 * c + 16])
    return m


def kernel(**inputs):
    inp = {k: np.asarray(v) for k, v in inputs.items()}
    B, SEQ = inp['x_prompt'].shape[:2]
    cfg = {'NT': SEQ // 128, 'NPG': inp['page_table'].shape[1], 'NPOOL': inp['cache_mla_latent'].shape[1]}
    nc = build(cfg)
    in_maps = [core_inputs(cfg, c, inp) for c in range(NCORES)]
    res = run_bass_kernel_spmd(nc, in_maps, core_ids=list(range(NCORES)))
    r = res.results

    def stk(name):
        return np.stack([r[c][name] for c in range(NCORES)], 0)

    def cat(name):
        return np.concatenate([r[c][name] for c in range(NCORES)], 0)
    y_p = stk("y_p")
    y_s = cat("y_s").reshape(16 * NCORES, 8, D)
    return (y_p, y_s,
            stk("gdn_p")[None], stk("conv_p")[None], stk("lat_p")[None], stk("kpe_p")[None], stk("hgrn_p")[None],
            cat("gdn_s")[None], cat("conv_s")[None], cat("lat_s").reshape(1, 16 * NCORES, 8, 256),
            cat("kpe_s").reshape(1, 16 * NCORES, 8, 32), cat("hgrn_s")[None])
```

```python
from contextlib import ExitStack

import numpy as np
import concourse.bass as bass
import concourse.mybir as mybir
from concourse.bass_utils import run_bass_kernel_spmd

F32 = mybir.dt.float32
BF16 = mybir.dt.bfloat16
I32 = mybir.dt.int32
AF = mybir.ActivationFunctionType
ALU = mybir.AluOpType
AX = mybir.AxisListType

D = 1024
DFF = 4096
KC = 8
EPS = 1e-6
NCORES = 8


class Sched:
    def __init__(self, nc, stack, ndma=4):
        self.nc = nc
        self.stack = stack
        self.eng = {'pe': nc.tensor, 'act': nc.scalar, 'dve': nc.vector,
                    'pool': nc.gpsimd, 'sp': nc.sync}
        self.sem = {}
        self.cnt = {}
        self.seen = {e: {} for e in self.eng}
        self.lastw = {}
        self.reads = {}
        self.ndma = ndma
        self.dma_rr = {}
        self.n_inst = 0
        self.alias_map = {}
        self.log = {e: [] for e in self.eng}
        for e in ('pe', 'act', 'dve', 'pool'):
            self._mk(e)

    def _mk(self, q):
        self.sem[q] = self.stack.enter_context(self.nc.semaphore("s_" + q))
        self.cnt[q] = 0

    def _wait(self, e, q, c):
        if self.seen[e].get(q, 0) >= c:
            return
        self.seen[e][q] = c
        self.log[e].append(('w', q, c))
        self.eng[e].wait_ge(self.sem[q], c)

    def alias(self, a, canon):
        self.alias_map[a] = canon

    def _canon(self, keys):
        return [self.alias_map.get(k, k) for k in keys]

    def _deps(self, e, myq, reads, writes, same_war=False, skip_waw_q=None):
        reads = self._canon(reads)
        writes = self._canon(writes)
        need = {}

        def add(t):
            if t is None:
                return
            q, c = t
            if need.get(q, 0) < c:
                need[q] = c
        for k in reads:
            add(self.lastw.get(k))
        for k in writes:
            t = self.lastw.get(k)
            if not (t is not None and skip_waw_q is not None and t[0] == skip_waw_q):
                add(t)
            for q, c in self.reads.get(k, {}).items():
                if q == myq and not same_war:
                    continue
                add((q, c))
        for q, c in need.items():
            self._wait(e, q, c)

    def _commit(self, ticket, reads, writes):
        reads = self._canon(reads)
        writes = self._canon(writes)
        q, c = ticket
        for k in reads:
            d = self.reads.setdefault(k, {})
            if d.get(q, 0) < c:
                d[q] = c
        for k in writes:
            self.lastw[k] = ticket
            self.reads[k] = {}

    def op(self, e, fn, reads=(), writes=(), inc=True, pe_acc=False):
        self._deps(e, e, reads, writes, skip_waw_q=('pe' if pe_acc else None))
        ins = fn(self.eng[e])
        self.n_inst += 1
        ticket = (e, self.cnt[e] + 1)
        if inc:
            ins.then_inc(self.sem[e], 1)
            self.cnt[e] += 1
            self.log[e].append(('i', e, 1))
        self._commit(ticket, reads, writes)
        return ins

    def dma(self, e, out, in_, reads=(), writes=(), stream='d', nrr=None, **kw):
        return self.dma_custom(e, lambda eng: eng.dma_start(out=out, in_=in_, **kw),
                               reads, writes, stream, nrr)

    def dma_custom(self, e, fn, reads=(), writes=(), stream='g', nrr=None):
        i = self.dma_rr.get(stream, 0)
        self.dma_rr[stream] = i + 1
        q = "dma_%s_%d" % (stream, i % (nrr or {'w': 12, 'c': 12}.get(stream, self.ndma)))
        if q not in self.sem:
            self._mk(q)
        self._deps(e, q, reads, writes, same_war=True)
        ins = fn(self.eng[e])
        ins.then_inc(self.sem[q], 16)
        self.n_inst += 1
        self.cnt[q] += 16
        self.log[e].append(('i', q, 16))
        self._commit((q, self.cnt[q]), reads, writes)
        return ins

    def barrier(self):
        for e in self.eng:
            for q, c in self.cnt.items():
                if c > 0:
                    self._wait(e, q, c)

    def check_deadlock(self):
        pos = {e: 0 for e in self.log}
        val = {}
        progress = True
        while progress:
            progress = False
            for e, lg in self.log.items():
                while pos[e] < len(lg):
                    k, q, c = lg[pos[e]]
                    if k == 'w':
                        if val.get(q, 0) >= c:
                            pos[e] += 1
                            progress = True
                        else:
                            break
                    else:
                        val[q] = val.get(q, 0) + c
                        pos[e] += 1
                        progress = True
        stuck = {e: (pos[e], lg[pos[e]], val.get(lg[pos[e]][1], 0)) for e, lg in self.log.items()
                 if pos[e] < len(lg)}
        return stuck

    def finish(self, e='sp'):
        for q, c in self.cnt.items():
            if c > 0:
                self._wait(e, q, c)


def blocks(total, step):
    return [(s, min(step, total - s)) for s in range(0, total, step)]


def build(cfg):
    NT = cfg['NT']
    TP = 128 * NT
    TT = TP + 128
    PHASES = cfg.get('phases', 'ABCD')
    nc = bass.Bass("TRN2", target_bir_lowering=False)

    def din(name, shape, dt=F32):
        return nc.dram_tensor(name, list(shape), dt, kind="ExternalInput").ap()

    def dout(name, shape):
        return nc.dram_tensor(name, list(shape), F32, kind="ExternalOutput").ap()

    x_p = din("x_p", [TP, D])
    x_s = din("x_s", [128, D])
    w_up = din("w_up", [2, D, DFF])
    w_down = din("w_down", [2, DFF, D])
    mlp_norm_fm = din("mlp_norm_fm", [2, 128, KC])
    mix_norm_fm = din("mix_norm_fm", [2, 128, KC])
    final_norm = din("final_norm", [1, D])
    w_in_c = din("w_in_c", [D, 4096])
    w_out_c = din("w_out_c", [D, D])
    lb_fm = din("lb_fm", [2, 128, 8])
    g_norm_c = din("g_norm_c", [128, 1])
    hgrn_in = din("hgrn_in", [16, 8, 128, 128])
    w_in_ab = din("w_in_ab", [D, 2600])
    w_out_ab = din("w_out_ab", [D, D])
    w_uq = din("w_uq", [256, 768])
    w_uk = din("w_uk", [256, 512])
    w_uv = din("w_uv", [256, 512])
    conv_w_fm = din("conv_w_fm", [128, 12, 4])
    a_log = din("a_log", [1, 4])
    dt_bias = din("dt_bias", [1, 4])
    gdn_norm = din("gdn_norm", [128, 1])
    q_norm = din("q_norm", [1, 256])
    kv_norm = din("kv_norm", [1, 256])
    rope_cos = din("rope_cos", [TT, 16])
    rope_sin = din("rope_sin", [TT, 16])
    conv_s_in = din("conv_s_in", [48, 1536])
    gdn_s_in = din("gdn_s_in", [16, 4, 128, 128])
    NPG = cfg['NPG']
    NPOOL = cfg['NPOOL']
    NGRP = NPG // min(8, NPG)
    cache_lat = din("cache_lat", [NPOOL, 128, 256])
    cache_kr = din("cache_kr", [NPOOL, 128, 32])
    page_table = din("page_table", [16, NPG], I32)
    w_ukT = din("w_ukT", [128, 2048])
    rep_in = din("rep_in", [NPG, 128])
    gbase_in = din("gbase_in", [128, NGRP + 1])
    cm8_in = din("cm8_in", [128, 64])
    NPG = cfg['NPG']
    if cfg.get('dbg_s'):
        dbg_mst = dout("dbg_mst", [128, 8])
        dbg_oacc = dout("dbg_oacc", [64, 257])
        dbg_lb = nc.dram_tensor("dbg_lb", [128, min(8, NPG) * 256], BF16, kind="ExternalOutput").ap()
        dbg_kb = nc.dram_tensor("dbg_kb", [128, NPG * 32], BF16, kind="ExternalOutput").ap()
        dbg_olat = dout("dbg_olat", [64, 256])
    gdn_p_o = dout("gdn_p", [4, 128, 128])
    conv_p_o = dout("conv_p", [3, 1536])
    lat_p_o = dout("lat_p", [TP, 256])
    kpe_p_o = dout("kpe_p", [TP, 32])
    gdn_s_o = dout("gdn_s", [16, 4, 128, 128])
    conv_s_o = dout("conv_s", [16, 3, 1536])
    lat_s_o = dout("lat_s", [128, 256])
    kpe_s_o = dout("kpe_s", [128, 32])
    xsp = nc.dram_tensor("xsp", [NT + 1, 128, KC, 128], F32, kind="Internal").ap()
    y_p = dout("y_p", [TP, D])
    y_s = dout("y_s", [128, D])
    hgrn_p = dout("hgrn_p", [8, 128, 128])
    hgrn_s = dout("hgrn_s", [16, 8, 128, 128])

    with ExitStack() as st:
        S = Sched(nc, st, ndma=4)

        uniq = [0]

        def mk_sb(stack):
            uniq[0] += 1
            pfx = "p%d_" % uniq[0]

            def sb(name, shape, dt=F32):
                nb = int(np.prod(shape[1:])) * (2 if dt == BF16 else 4)
                if cfg.get('memlog'):
                    print("  alloc %-10s %7d B" % (pfx + name, nb))
                return stack.enter_context(nc.sbuf_tensor(pfx + name, list(shape), dt))
            return sb
        sb = mk_sb(st)

        pb = [st.enter_context(nc.psum_tensor("pb%d" % i, [128, 512], F32)) for i in range(8)]

        def PK(i):
            return [('pbq', i, j) for j in range(4)]

        def pbv(i, a):
            return pb[i][:].rearrange("p (a b) -> p a b", a=a)

        def pbbf(i):
            return pb[i][:].bitcast(BF16)

        def A(fn, r, w):
            return S.op('act', fn, r, w)

        def V(fn, r, w):
            return S.op('dve', fn, r, w)

        def G(fn, r, w):
            return S.op('pool', fn, r, w)

        def MM(out, lhsT, rhs, start, stop, r, w, inc=None):
            return S.op('pe', lambda e: e.matmul(out, lhsT, rhs, start=start, stop=stop), r, w,
                        inc=(stop if inc is None else inc), pe_acc=(not start))

        def TR(out, in_, idt, r, w):
            return S.op('pe', lambda e: e.transpose(out, in_, idt), r, w)

        ident = sb("ident", [128, 128], F32)
        ident_bf = sb("ident_bf", [128, 128], BF16)
        ones_bf = sb("ones_bf", [128, 128], BF16)
        G(lambda e: e.memset(ident[:], 0.0), [], ['ident'])
        G(lambda e: e.affine_select(out=ident[:], in_=ident[:], pattern=[[-1, 128]],
                                    compare_op=ALU.not_equal, fill=1.0, base=0,
                                    channel_multiplier=1), ['ident'], ['ident'])
        G(lambda e: e.tensor_copy(out=ident_bf[:], in_=ident[:]), ['ident'], ['ident_bf'])
        G(lambda e: e.memset(ones_bf[:], 1.0), [], ['ones_bf'])

        gmlp = sb("gmlp", [128, 2, KC], F32)
        S.dma('sp', gmlp[:], mlp_norm_fm.rearrange("l p k -> p l k"), writes=['gmlp'], stream='c')
        gmix = sb("gmix", [128, 2, KC], F32)
        S.dma('sp', gmix[:], mix_norm_fm.rearrange("l p k -> p l k"), writes=['gmix'], stream='c')

        xTh = [None]

        def tile_rows(t):
            return x_s if t == NT else x_p[t * 128:(t + 1) * 128, :]

        def tile_cols(t):
            return slice(t * 128, (t + 1) * 128)

        def phase0():
          xT = xTh[0]
          with ExitStack() as ph:
            psb = mk_sb(ph)
            xin = [psb("xin%d" % i, [128, D], F32) for i in range(2)]
            for t in range(NT + 1):
                b = t % 2
                S.dma('sp', xin[b][:], tile_rows(t), writes=[('xin', b)], stream='x')
                for half in range(2):
                    pi = half
                    for j in range(4):
                        kc = half * 4 + j
                        TR(pb[pi][:, j * 128:(j + 1) * 128], xin[b][:, kc * 128:(kc + 1) * 128], ident[:],
                           [('xin', b), 'ident'], [('pbq', pi, j)])
                    A(lambda e, half=half, pi=pi: e.copy(out=xT[:, half * 4:half * 4 + 4, tile_cols(t)],
                                                        in_=pbv(pi, 4)),
                      [('pbq', pi, j) for j in range(4)], [('xT', t)] + [('pbq', pi, j) for j in range(4)])
            S.barrier()

        def xkeys(s0, n):
            return [('xT', t) for t in range(s0 // 128, (s0 + n + 127) // 128)]

        def hkeys(s0, n):
            return [('hT', t) for t in range(s0 // 128, (s0 + n + 127) // 128)]

        def make_norm(psb, W, nbuf=2):
            sq = [psb("sq%d" % i, [128, KC, W], BF16) for i in range(nbuf)]
            ntmp = [psb("ntmp%d" % i, [128, KC, W], F32) for i in range(nbuf)]
            rs = [psb("rs%d" % i, [128, W], F32) for i in range(nbuf)]
            ctr = [0]

            def norm_range(gain, gkey, s0, n, hdst, hk, xv=None, xk=None):
                i = ctr[0] % nbuf
                ctr[0] += 1
                if xv is None:
                    xv = xTh[0][:, :, s0:s0 + n]
                    xk = xkeys(s0, n)
                A(lambda e: e.activation(out=sq[i][:, :, :n], in_=xv, func=AF.Square),
                  xk, [('sq', i)])
                for kc in range(KC):
                    MM(pb[5][:, :n], ones_bf[:], sq[i][:, kc, :n], kc == 0, kc == KC - 1,
                       [('sq', i), 'ones_bf'], PK(5))
                A(lambda e: e.activation(out=rs[i][:, :n], in_=pb[5][:, :n], func=AF.Sqrt,
                                         bias=EPS, scale=1.0 / D), PK(5), [('rs', i)])
                V(lambda e: e.reciprocal(out=rs[i][:, :n], in_=rs[i][:, :n]), [('rs', i)], [('rs', i)])
                G(lambda e: e.tensor_tensor(out=ntmp[i][:, :, :n], in0=xv,
                                            in1=rs[i][:, :n].unsqueeze(1).to_broadcast([128, KC, n]),
                                            op=ALU.mult), xk + [('rs', i)], [('ntmp', i)])
                V(lambda e: e.tensor_tensor(out=hdst, in0=ntmp[i][:, :, :n],
                                            in1=gain.unsqueeze(2).to_broadcast([128, KC, n]),
                                            op=ALU.mult), [('ntmp', i), gkey], hk)
            return norm_range

        def mlp_layer(l):
            xT = xTh[0]
            with ExitStack() as ph:
                psb = mk_sb(ph)
                hT = psb("hT", [128, KC, TT], BF16)
                norm_range = make_norm(psb, 256)
                wu = [psb("wu%d" % i, [128, KC, 512], BF16) for i in range(2)]
                wd = [psb("wd%d" % i, [128, 4, D], BF16) for i in range(2)]
                h1 = [psb("h1_%d" % i, [128, 4, 512], BF16) for i in range(2)]
                rtmp = [psb("rtmp%d" % i, [128, 512], F32) for i in range(2)]
                ctr = {'up': 0, 'dn': 0, 'h1': 0, 'w': 0}
                for (s0, n) in blocks(TT, 256):
                    norm_range(gmlp[:, l, :], 'gmlp', s0, n, hT[:, :, s0:s0 + n], hkeys(s0, n))
                wup_v = w_up[l].rearrange("(kc p) f -> p kc f", p=128)
                wdn_v = w_down[l].rearrange("(fc p) d -> p fc d", p=128)
                for g in range(8):
                    wb = ctr['w'] % 2
                    ctr['w'] += 1
                    S.dma('pool', wu[wb][:], wup_v[:, :, g * 512:(g + 1) * 512], writes=[('wu', wb)], stream='w')
                    S.dma('pool', wd[wb][:], wdn_v[:, g * 4:(g + 1) * 4, :], writes=[('wd', wb)], stream='w')
                    for (s0, n) in blocks(TT, 512):
                        hb = ctr['h1'] % 2
                        ctr['h1'] += 1
                        tl = list(range(s0 // 128, (s0 + n + 127) // 128))
                        for fc in range(4):
                            pi = ctr['up'] % 2
                            ctr['up'] += 1
                            for kc in range(KC):
                                MM(pb[pi][:, :n], wu[wb][:, kc, fc * 128:(fc + 1) * 128], hT[:, kc, s0:s0 + n],
                                   kc == 0, kc == KC - 1, [('wu', wb)] + hkeys(s0, n), PK(pi))
                            A(lambda e, pi=pi: e.activation(out=rtmp[pi][:, :n], in_=pb[pi][:, :n], func=AF.Relu),
                              PK(pi), [('rtmp', pi)])
                            G(lambda e, pi=pi, fc=fc: e.tensor_tensor(out=h1[hb][:, fc, :n], in0=rtmp[pi][:, :n],
                                                                      in1=rtmp[pi][:, :n], op=ALU.mult),
                              [('rtmp', pi)], [('h1', hb, fc)])
                        for dc in range(KC):
                            pi = 2 + ctr['dn'] % 3
                            ctr['dn'] += 1
                            for fc in range(4):
                                MM(pb[pi][:, :n], wd[wb][:, fc, dc * 128:(dc + 1) * 128], h1[hb][:, fc, :n],
                                   fc == 0, fc == 3, [('wd', wb), ('h1', hb, fc)], PK(pi))
                            V(lambda e, dc=dc, pi=pi: e.tensor_tensor(out=xT[:, dc, s0:s0 + n],
                                                                      in0=xT[:, dc, s0:s0 + n],
                                                                      in1=pb[pi][:, :n], op=ALU.add),
                              PK(pi) + [('xTd', dc, t) for t in tl], [('xTd', dc, t) for t in tl])
                S.barrier()

        def hgrn_layer():
            xT = xTh[0]
            with ExitStack() as ph:
                psb = mk_sb(ph)
                norm_range = make_norm(psb, 128, 1)
                hTt = psb("hTt", [128, KC, 128], BF16)
                wic = psb("wic", [128, KC, 4096], BF16)
                woc = psb("woc", [128, KC, D], BF16)
                wic_v = w_in_c.rearrange("(kc p) f -> p kc f", p=128)
                woc_v = w_out_c.rearrange("(kc p) f -> p kc f", p=128)
                for kc in range(KC):
                    S.dma('pool', wic[:, kc, :], wic_v[:, kc, :], writes=['wic'], stream='w')
                S.dma('pool', woc[:], woc_v, writes=['woc'], stream='w')
                lbt = psb("lbt", [128, 2, 8], F32)
                lb = psb("lb", [128, 8], F32)
                oml = psb("oml", [128, 8], F32)
                gnc = psb("gnc", [128, 1], F32)
                S.dma('sp', lbt[:], lb_fm.rearrange("l p k -> p l k"), writes=['lbt'], stream='c')
                S.dma('sp', gnc[:], g_norm_c, writes=['gnc'], stream='c')
                V(lambda e: e.tensor_tensor(out=lb[:], in0=lbt[:, 1, :], in1=lbt[:, 0, :], op=ALU.subtract),
                  ['lbt'], ['lb'])
                A(lambda e: e.activation(out=lb[:], in_=lb[:], func=AF.Sigmoid), ['lb'], ['lb'])
                V(lambda e: e.tensor_scalar(out=oml[:], in0=lb[:], scalar1=-1.0, scalar2=1.0, op0=ALU.mult,
                                            op1=ALU.add), ['lb'], ['oml'])
                SC = 128 ** -0.5
                mask_p = psb("mask_p", [128, 128], F32)
                mask_s = psb("mask_s", [128, 128], F32)
                for mk, nm in ((mask_p, 'mask_p'), (mask_s, 'mask_s')):
                    G(lambda e, mk=mk: e.memset(mk[:], SC), [], [nm])
                    G(lambda e, mk=mk: e.affine_select(out=mk[:], in_=mk[:], pattern=[[1, 128]],
                                                       compare_op=ALU.is_ge, fill=0.0, base=0,
                                                       channel_multiplier=-1), [nm], [nm])
                G(lambda e: e.affine_select(out=mask_s[:].rearrange("p (a b) -> p a b", a=16),
                                            in_=mask_s[:].rearrange("p (a b) -> p a b", a=16),
                                            pattern=[[-8, 16], [0, 8]], compare_op=ALU.is_ge, fill=0.0,
                                            base=0, channel_multiplier=1), ['mask_s'], ['mask_s'])
                rm_p = psb("rm_p", [128, 8, 128], F32)
                rm_s = rm_p
                S.alias('rm_s', 'rm_p')
                G(lambda e: e.memset(rm_p[:], 1.0), [], ['rm_p'])
                G(lambda e: e.memset(rm_p[:, :, 0:1], 0.0), ['rm_p'], ['rm_p'])
                seqmask = psb("seqmask", [128, 16], F32)
                G(lambda e: e.memset(seqmask[:], 1.0), [], ['seqmask'])
                G(lambda e: e.affine_select(out=seqmask[:], in_=seqmask[:], pattern=[[-8, 16]],
                                            compare_op=ALU.is_ge, fill=0.0, base=0, channel_multiplier=1),
                  ['seqmask'], ['seqmask'])
                G(lambda e: e.affine_select(out=seqmask[:], in_=seqmask[:], pattern=[[8, 16]],
                                            compare_op=ALU.is_ge, fill=0.0, base=7, channel_multiplier=-1),
                  ['seqmask'], ['seqmask'])

                f4 = [128, 8, 128]
                qs = psb("qs", f4)
                ff = psb("ff", f4)
                kk = psb("kk", f4)
                gc = psb("gc", f4)
                sgate = psb("sgate", f4)
                v_tm = psb("v_tm", f4, BF16)
                tA = psb("tA", [128, 8, 32])
                tB = ff
                S.alias('tB', 'ff')
                qI = psb("qI", [128, 8, 32], BF16)
                kI = psb("kI", f4, BF16)
                intra = psb("intra", f4, BF16)
                qg = psb("qg", f4, BF16)
                khT = psb("khT", f4, BF16)
                kh_tm = psb("kh_tm", f4, BF16)
                khm = kI
                S.alias('khm', 'kI')
                egl = psb("egl", [128, 8, 16])
                Sst = psb("Sst", f4)
                Sbf = psb("Sbf", f4, BF16)
                S0 = ff
                S.alias('S0', 'ff')
                osq = khT
                S.alias('osq', 'khT')
                rn = qs[:].rearrange("p h t -> p (h t)")
                S.alias('rn', 'qs')
                onb = psb("onb", f4, BF16)
                G(lambda e: e.memset(intra[:], 0.0), [], ['intra'])
                G(lambda e: e.memset(Sst[:], 0.0), [], ['Sst'])
                G(lambda e: e.memset(Sbf[:], 0.0), [], ['Sbf'])

                def proj_fm(col0, banks):
                    for h in range(8):
                        bi = banks[h // 4]
                        for kc in range(KC):
                            MM(pb[bi][:, (h % 4) * 128:(h % 4 + 1) * 128],
                               wic[:, kc, col0 + h * 128:col0 + (h + 1) * 128], hTt[:, kc, :],
                               kc == 0, kc == KC - 1, ['wic', 'hTt'], [('pbq', bi, h % 4)])

                bq = PK

                order = list(range(NT)) + [NT]
                for t in order:
                    smp = (t == NT)
                    cols = tile_cols(t)
                    mask = mask_s if smp else mask_p
                    mkey = 'mask_s' if smp else 'mask_p'
                    rm = rm_s if smp else rm_p
                    if smp:
                        G(lambda e: e.memset(rm_s[:].rearrange("p h (s t) -> p h s t", t=8)[:, :, :, 0:1], 0.0),
                          ['rm_s'], ['rm_s'])
                    norm_range(gmix[:, 1, :], 'gmix', t * 128, 128, hTt[:], ['hTt'])
                    proj_fm(0, (0, 1))
                    for b2 in range(2):
                        A(lambda e, b2=b2: e.activation(out=qs[:, b2 * 4:b2 * 4 + 4, :], in_=pbv(b2, 4), func=AF.Silu),
                          bq(b2), ['qs'] + bq(b2))
                    if cfg.get('cstop', 99) <= 1:
                        continue
                    proj_fm(1024, (2, 3))
                    for b2 in range(2):
                        A(lambda e, b2=b2: e.activation(out=ff[:, b2 * 4:b2 * 4 + 4, :], in_=pbv(2 + b2, 4),
                                                        func=AF.Sigmoid), bq(2 + b2), ['ff'] + bq(2 + b2))
                    V(lambda e: e.tensor_tensor(out=ff[:], in0=ff[:], in1=oml[:].unsqueeze(2).to_broadcast(f4),
                                                op=ALU.mult), ['ff', 'oml'], ['ff'])
                    V(lambda e: e.tensor_tensor(out=ff[:], in0=ff[:], in1=lb[:].unsqueeze(2).to_broadcast(f4),
                                                op=ALU.add), ['ff', 'lb'], ['ff'])
                    V(lambda e: e.tensor_scalar(out=kk[:], in0=ff[:], scalar1=-1.0, scalar2=1.0, op0=ALU.mult,
                                                op1=ALU.add), ['ff'], ['kk'])
                    logf = ff
                    A(lambda e: e.activation(out=ff[:], in_=ff[:], func=AF.Ln), ['ff'], ['ff'])
                    V(lambda e: e.tensor_tensor_scan(out=gc[:].rearrange("p h t -> p (h t)"),
                                                     data0=rm[:].rearrange("p h t -> p (h t)"),
                                                     data1=logf[:].rearrange("p h t -> p (h t)"),
                                                     initial=0.0, op0=ALU.mult, op1=ALU.add),
                      ['ff', 'rm_p'], ['gc'])
                    if cfg.get('cstop', 99) <= 2:
                        continue
                    proj_fm(3072, (4, 5))
                    for b2 in range(2):
                        A(lambda e, b2=b2: e.activation(out=sgate[:, b2 * 4:b2 * 4 + 4, :], in_=pbv(4 + b2, 4),
                                                        func=AF.Silu), bq(4 + b2), ['sgate'] + bq(4 + b2))
                    for b2 in range(2):
                        for kc in range(KC):
                            MM(pb[6 + b2][:], hTt[:, kc, :], wic[:, kc, 2048 + b2 * 512:2048 + (b2 + 1) * 512],
                               kc == 0, kc == KC - 1, ['wic', 'hTt'], PK(6 + b2))
                        A(lambda e, b2=b2: e.copy(out=v_tm[:, b2 * 4:b2 * 4 + 4, :], in_=pbv(6 + b2, 4)),
                          PK(6 + b2), ['v_tm'])
                    if cfg.get('cstop', 99) <= 3:
                        continue
                    for I in range(4):
                        blk = slice(32 * I, 32 * I + 32)
                        nj = 32 * (I + 1)
                        if I == 0:
                            A(lambda e: e.activation(out=tA[:], in_=gc[:, :, blk], func=AF.Exp), ['gc'], ['tA'])
                            A(lambda e: e.activation(out=tB[:, :, :nj], in_=gc[:, :, :nj], func=AF.Exp, scale=-1.0),
                              ['gc'], ['tB'])
                        else:
                            ref = gc[:, :, 32 * I - 1:32 * I]
                            V(lambda e: e.tensor_tensor(out=tA[:], in0=gc[:, :, blk],
                                                        in1=ref.to_broadcast([128, 8, 32]), op=ALU.subtract),
                              ['gc'], ['tA'])
                            A(lambda e: e.activation(out=tA[:], in_=tA[:], func=AF.Exp), ['tA'], ['tA'])
                            V(lambda e: e.tensor_tensor(out=tB[:, :, :nj], in0=ref.to_broadcast([128, 8, nj]),
                                                        in1=gc[:, :, :nj], op=ALU.subtract), ['gc'], ['tB'])
                            A(lambda e: e.activation(out=tB[:, :, :nj], in_=tB[:, :, :nj], func=AF.Exp),
                              ['tB'], ['tB'])
                        V(lambda e: e.tensor_tensor(out=qI[:], in0=qs[:, :, blk], in1=tA[:], op=ALU.mult),
                          ['qs', 'tA'], ['qI'])
                        G(lambda e: e.tensor_tensor(out=kI[:, :, :nj], in0=kk[:, :, :nj], in1=tB[:, :, :nj],
                                                    op=ALU.mult), ['kk', 'tB'], ['kI'])
                        for h in range(8):
                            bi = h // 4
                            c0 = (h % 4) * 128 + 32 * I
                            MM(pb[bi][:nj, c0:c0 + 32], kI[:, h, :nj], qI[:, h, :], True, True,
                               ['kI', 'qI'], [('pbq', bi, h % 4)])
                        for b2 in range(2):
                            V(lambda e, b2=b2: e.tensor_tensor(
                                out=intra[:nj, b2 * 4:b2 * 4 + 4, blk], in0=pbv(b2, 4)[:nj, :, blk],
                                in1=mask[:nj, blk].unsqueeze(1).to_broadcast([nj, 4, 32]), op=ALU.mult),
                                bq(b2) + [mkey], ['intra'] + bq(b2))
                    if cfg.get('cstop', 99) <= 4:
                        continue
                    A(lambda e: e.activation(out=tB[:], in_=gc[:], func=AF.Exp), ['gc'], ['tB'])
                    V(lambda e: e.scalar_tensor_tensor(out=qg[:], in0=qs[:], scalar=SC, in1=tB[:], op0=ALU.mult, op1=ALU.mult),
                      ['qs', 'tB'], ['qg'])
                    if smp:
                        gl = gc[:].rearrange("p h (s t) -> p h s t", t=8)[:, :, :, 7:8]
                        V(lambda e: e.tensor_tensor(out=tB[:].rearrange("p h (s t) -> p h s t", t=8),
                                                    in0=gl.to_broadcast([128, 8, 16, 8]),
                                                    in1=gc[:].rearrange("p h (s t) -> p h s t", t=8),
                                                    op=ALU.subtract), ['gc'], ['tB'])
                        A(lambda e: e.activation(out=egl[:], in_=gc[:].rearrange("p h (s t) -> p h s t", t=8)[:, :, :, 7],
                                                 func=AF.Exp), ['gc'], ['egl'])
                    else:
                        gl = gc[:, :, 127:128]
                        V(lambda e: e.tensor_tensor(out=tB[:], in0=gl.to_broadcast(f4), in1=gc[:],
                                                    op=ALU.subtract), ['gc'], ['tB'])
                        A(lambda e: e.activation(out=egl[:, :, 0], in_=gc[:, :, 127], func=AF.Exp), ['gc'], ['egl'])
                    A(lambda e: e.activation(out=tB[:], in_=tB[:], func=AF.Exp), ['tB'], ['tB'])
                    G(lambda e: e.tensor_tensor(out=khT[:], in0=kk[:], in1=tB[:], op=ALU.mult), ['kk', 'tB'], ['khT'])
                    for h in range(8):
                        TR(pbbf(2)[:, h * 128:(h + 1) * 128], khT[:, h, :], ident_bf[:], ['khT', 'ident_bf'],
                           [('pbq', 2, h // 2)])
                    A(lambda e: e.copy(out=kh_tm[:].rearrange("p h t -> p (h t)"), in_=pbbf(2)),
                      PK(2), ['kh_tm'] + PK(2))
                    if cfg.get('cstop', 99) <= 5:
                        continue
                    if not smp and cfg.get('dbg', '') == 'nop':
                        pass
                    elif smp and cfg.get('dbg', '') == 'nos':
                        pass
                    elif not smp:
                        for h in range(8):
                            bi = 4 + h // 4
                            o_out = pb[bi][:, (h % 4) * 128:(h % 4 + 1) * 128]
                            MM(o_out, Sbf[:, h, :], qg[:, h, :], True, False, ['Sbf', 'qg'], [('pbq', bi, h % 4)])
                            MM(o_out, v_tm[:, h, :], intra[:, h, :], False, True, ['v_tm', 'intra'],
                               [('pbq', bi, h % 4)])
                        for h in range(8):
                            if cfg.get('sub', 9) < 2:
                                break
                            bi = 6 + h // 4
                            MM(pb[bi][:, (h % 4) * 128:(h % 4 + 1) * 128], kh_tm[:, h, :], v_tm[:, h, :], True, True,
                               ['kh_tm', 'v_tm'], [('pbq', bi, h % 4)])
                        V(lambda e: e.tensor_tensor(out=Sst[:], in0=Sst[:], in1=egl[:, :, 0:1].to_broadcast(f4),
                                                    op=ALU.mult), ['Sst', 'egl'], ['Sst'])
                        for b2 in range(2):
                            V(lambda e, b2=b2: e.tensor_tensor(out=Sst[:, b2 * 4:b2 * 4 + 4, :],
                                                               in0=Sst[:, b2 * 4:b2 * 4 + 4, :], in1=pbv(6 + b2, 4),
                                                               op=ALU.add), ['Sst'] + PK(6 + b2), ['Sst'] + PK(6 + b2))
                        if cfg.get('sub', 9) >= 4:
                            A(lambda e: e.copy(out=Sbf[:], in_=Sst[:]), ['Sst'], ['Sbf'])
                        if t == NT - 1 and cfg.get('sub', 9) >= 5:
                            S.dma('sp', hgrn_p.rearrange("h k v -> k h v"), Sst[:], reads=['Sst'], stream='o')
                    else:
                        for b2 in range(2):
                            V(lambda e, b2=b2: e.memset(pb[4 + b2][:], 0.0), [], PK(4 + b2))
                        for s in range(16):
                            S.dma('sp', S0[:], hgrn_in[s].rearrange("h k v -> k h v"), writes=['S0'], stream='s0')
                            A(lambda e: e.copy(out=Sbf[:], in_=S0[:]), ['S0'], ['Sbf'])
                            for h in range(8):
                                bi = 4 + h // 4
                                c0 = (h % 4) * 128 + s * 8
                                MM(pb[bi][:, c0:c0 + 8], Sbf[:, h, :], qg[:, h, s * 8:(s + 1) * 8], False, False,
                                   ['Sbf', 'qg'], [('pbq', bi, h % 4)])
                            V(lambda e, s=s: e.tensor_scalar(out=khm[:], in0=kh_tm[:], scalar1=seqmask[:, s:s + 1],
                                                            scalar2=None, op0=ALU.mult),
                              ['kh_tm', 'seqmask'], ['khm'])
                            for h in range(8):
                                bi = 6 + h // 4
                                MM(pb[bi][:, (h % 4) * 128:(h % 4 + 1) * 128], khm[:, h, :], v_tm[:, h, :],
                                   True, True, ['khm', 'v_tm'], [('pbq', bi, h % 4)])
                            V(lambda e, s=s: e.tensor_tensor(out=S0[:], in0=S0[:],
                                                             in1=egl[:, :, s:s + 1].to_broadcast(f4), op=ALU.mult),
                              ['S0', 'egl'], ['S0'])
                            for b2 in range(2):
                                V(lambda e, b2=b2: e.tensor_tensor(out=S0[:, b2 * 4:b2 * 4 + 4, :],
                                                                   in0=S0[:, b2 * 4:b2 * 4 + 4, :], in1=pbv(6 + b2, 4),
                                                                   op=ALU.add), ['S0'] + PK(6 + b2), ['S0'] + PK(6 + b2))
                            S.dma('sp', hgrn_s[s].rearrange("h k v -> k h v"), S0[:], reads=['S0'], stream='o')
                        for h in range(8):
                            bi = 4 + h // 4
                            MM(pb[bi][:, (h % 4) * 128:(h % 4 + 1) * 128], v_tm[:, h, :], intra[:, h, :],
                               False, True, ['v_tm', 'intra'], [('pbq', bi, h % 4)])
                    if cfg.get('cstop', 99) <= 6:
                        continue
                    for b2 in range(2):
                        A(lambda e, b2=b2: e.activation(out=osq[:, b2 * 4:b2 * 4 + 4, :], in_=pbv(4 + b2, 4),
                                                        func=AF.Square), bq(4 + b2), ['osq'])
                    for b2 in range(2):
                        MM(pb[b2][:], ones_bf[:], osq[:, b2 * 4:b2 * 4 + 4, :].rearrange("p h t -> p (h t)"),
                           True, True, ['osq', 'ones_bf'], bq(b2))
                        A(lambda e, b2=b2: e.activation(out=rn[:, b2 * 512:(b2 + 1) * 512], in_=pb[b2][:],
                                                        func=AF.Sqrt, bias=EPS, scale=1.0 / 128),
                          bq(b2), ['rn'] + bq(b2))
                    V(lambda e: e.reciprocal(out=rn[:], in_=rn[:]), ['rn'], ['rn'])
                    for b2 in range(2):
                        V(lambda e, b2=b2: e.scalar_tensor_tensor(
                            out=rn[:, b2 * 512:(b2 + 1) * 512].rearrange("p (h t) -> p h t", h=4),
                            in0=pbv(4 + b2, 4), scalar=gnc[:, 0:1],
                            in1=rn[:, b2 * 512:(b2 + 1) * 512].rearrange("p (h t) -> p h t", h=4),
                            op0=ALU.mult, op1=ALU.mult), bq(4 + b2) + ['gnc', 'rn'], ['rn'] + bq(4 + b2))
                    G(lambda e: e.tensor_tensor(out=onb[:], in0=rn[:].rearrange("p (h t) -> p h t", h=8), in1=sgate[:],
                                                op=ALU.mult), ['rn', 'sgate'], ['onb'])
                    if cfg.get('cstop', 99) <= 7:
                        continue
                    for dc in range(KC):
                        bi = 2 + dc // 4
                        for kc in range(KC):
                            MM(pb[bi][:, (dc % 4) * 128:(dc % 4 + 1) * 128], woc[:, kc, dc * 128:(dc + 1) * 128],
                               onb[:, kc, :], kc == 0, kc == KC - 1, ['woc', 'onb'],
                               [('pbq', bi, dc % 4)])
                    for b2 in range(2):
                        V(lambda e, b2=b2: e.tensor_tensor(out=xT[:, b2 * 4:b2 * 4 + 4, cols],
                                                           in0=xT[:, b2 * 4:b2 * 4 + 4, cols],
                                                           in1=pbv(2 + b2, 4), op=ALU.add),
                          bq(2 + b2) + [('xT', t)], [('xT', t)] + bq(2 + b2))
                S.barrier()

        def ab_layer():
            with ExitStack() as ph:
                psb = mk_sb(ph)
                norm_range = make_norm(psb, 128, 1)
                STG = cfg.get('astage', 99)
                NEG = -1.0e30
                SCq = 128 ** -0.5
                SCm = 96 ** -0.5
                f4 = [128, 4, 128]
                wia = psb("wia", [128, KC, 2600], BF16)
                woa = psb("woa", [128, KC, D], BF16)
                wuq = psb("wuq", [128, 2, 768], BF16)
                wuk = psb("wuk", [128, 2, 512], BF16)
                wuv = psb("wuv", [128, 2, 512], BF16)
                wia_v = w_in_ab.rearrange("(kc p) f -> p kc f", p=128)
                for kc in range(KC):
                    for (c0, cn) in blocks(2600, 1300):
                        S.dma('pool', wia[:, kc, c0:c0 + cn], wia_v[:, kc, c0:c0 + cn], writes=['wia'], stream='w')
                S.dma('pool', woa[:], w_out_ab.rearrange("(kc p) f -> p kc f", p=128), writes=['woa'], stream='w')
                S.dma('pool', wuq[:], w_uq.rearrange("(c p) f -> p c f", p=128), writes=['wuq'], stream='w')
                S.dma('pool', wuk[:], w_uk.rearrange("(c p) f -> p c f", p=128), writes=['wuk'], stream='w')
                S.dma('pool', wuv[:], w_uv.rearrange("(c p) f -> p c f", p=128), writes=['wuv'], stream='w')
                cw = psb("cw", [128, 12, 4])
                S.dma('sp', cw[:], conv_w_fm, writes=['cw'], stream='c')
                negA = psb("negA", [128, 4])
                dtb = psb("dtb", [128, 4])
                S.dma('sp', negA[:], a_log.partition_broadcast(128), writes=['negA'], stream='c')
                S.dma('sp', dtb[:], dt_bias.partition_broadcast(128), writes=['dtb'], stream='c')
                A(lambda e: e.activation(out=negA[:], in_=negA[:], func=AF.Exp), ['negA'], ['negA'])
                V(lambda e: e.tensor_scalar(out=negA[:], in0=negA[:], scalar1=-1.0, scalar2=None, op0=ALU.mult),
                  ['negA'], ['negA'])
                gdnn = psb("gdnn", [128, 1])
                S.dma('sp', gdnn[:], gdn_norm, writes=['gdnn'], stream='c')
                qnb = psb("qnb", [128, 256])
                kvnb = psb("kvnb", [128, 256])
                S.dma('sp', qnb[:], q_norm.partition_broadcast(128), writes=['qnb'], stream='c')
                S.dma('sp', kvnb[:], kv_norm.partition_broadcast(128), writes=['kvnb'], stream='c')
                cosb = psb("cosb", [128, NT + 1, 16])
                sinb = psb("sinb", [128, NT + 1, 16])
                S.dma('sp', cosb[:], rope_cos.rearrange("(t p) f -> p t f", p=128), writes=['cosb'], stream='c')
                S.dma('sp', sinb[:], rope_sin.rearrange("(t p) f -> p t f", p=128), writes=['sinb'], stream='c')
                ones_f = psb("ones_f", [128, 128])
                G(lambda e: e.memset(ones_f[:], 1.0), [], ['ones_f'])

                def build_masks(tag, blk8, xsb):
                    m = {}
                    for nm in ('tri', 'blk', 'nms', 'nmi', 'nmsT', 'cmask'):
                        m[nm] = xsb(nm + tag, [128, 128])

                    def sel(t, pattern, op, fill, base, cm, view=None):
                        key = m_key(t)
                        ap = t[:] if view is None else t[:].rearrange("p (a b) -> p a b", a=16)
                        G(lambda e: e.affine_select(out=ap, in_=ap, pattern=pattern, compare_op=op, fill=fill,
                                                    base=base, channel_multiplier=cm), [key], [key])

                    def m_key(t):
                        return 'mask' + tag

                    def same_block(t, fill):
                        sel(t, [[-8, 16], [0, 8]], ALU.is_ge, fill, 0, 1, view=True)
                        sel(t, [[8, 16], [0, 8]], ALU.is_ge, fill, 7, -1, view=True)
                    G(lambda e: e.memset(m['tri'][:], 1.0), [], ['mask' + tag])
                    sel(m['tri'], [[1, 128]], ALU.is_ge, 0.0, 0, -1)
                    G(lambda e: e.memset(m['blk'][:], 1.0), ['mask' + tag], ['mask' + tag])
                    G(lambda e: e.memset(m['nms'][:], 0.0), ['mask' + tag], ['mask' + tag])
                    sel(m['nms'], [[1, 128]], ALU.is_gt, NEG, 0, -1)
                    G(lambda e: e.memset(m['nmi'][:], 0.0), ['mask' + tag], ['mask' + tag])
                    sel(m['nmi'], [[1, 128]], ALU.is_ge, NEG, 0, -1)
                    G(lambda e: e.memset(m['nmsT'][:], 0.0), ['mask' + tag], ['mask' + tag])
                    sel(m['nmsT'], [[-1, 128]], ALU.is_gt, NEG, 0, 1)
                    G(lambda e: e.memset(m['cmask'][:], 1.0), ['mask' + tag], ['mask' + tag])
                    sel(m['cmask'], [[1, 128]], ALU.is_ge, 0.0, 0, -1)
                    if blk8:
                        same_block(m['tri'], 0.0)
                        same_block(m['blk'], 0.0)
                        same_block(m['nms'], NEG)
                        same_block(m['nmi'], NEG)
                        same_block(m['nmsT'], NEG)
                        same_block(m['cmask'], 0.0)
                    return m
                seqmask = psb("seqmask", [128, 16], F32)
                G(lambda e: e.memset(seqmask[:], 1.0), [], ['seqmask'])
                G(lambda e: e.affine_select(out=seqmask[:], in_=seqmask[:], pattern=[[-8, 16]],
                                            compare_op=ALU.is_ge, fill=0.0, base=0, channel_multiplier=1),
                  ['seqmask'], ['seqmask'])
                G(lambda e: e.affine_select(out=seqmask[:], in_=seqmask[:], pattern=[[8, 16]],
                                            compare_op=ALU.is_ge, fill=0.0, base=7, channel_multiplier=-1),
                  ['seqmask'], ['seqmask'])

                H = {}
                xTt = psb("xTt", [128, KC, 128])
                hTt = psb("hTt", [128, KC, 128], BF16)
                acc = psb("acc", [128, 12, 128])
                ctmp = psb("ctmp", [128, 12, 128])
                qkvs = acc
                S.alias('qkvs', 'acc')
                qkvtm = ctmp[:].rearrange("p c t -> p (c t)")
                S.alias('qkvtm', 'ctmp')
                cst = ctmp[:].rearrange("p c t -> p (c t)")[0:48, :]
                S.alias('cst', 'ctmp')
                xin = ctmp[:].rearrange("p c t -> p (c t)")[:, 0:D]
                S.alias('xin', 'ctmp')
                zs = psb("zs", f4)
                tm = psb("tm", [128, 552])
                sq8 = psb("sq8", [128, 8, 128], BF16)
                rn8 = psb("rn8", [128, 8, 128])
                qkn = rn8
                S.alias('qkn', 'rn8')
                ktm = psb("ktm", f4)
                vtm = psb("vtm", f4)
                gt = psb("gt", [128, 16])
                gcc = psb("gcc", [128, 8])
                G4 = psb("G4", f4)
                gcr = psb("gcr", f4)
                br = psb("br", f4)
                a1 = psb("a1", f4)
                E1s = psb("E1s", f4)
                E1i = psb("E1i", f4)
                E2s = psb("E2s", f4)
                Xa = [psb("Xa%d" % i, f4) for i in range(2)]
                XTa = [psb("XTa%d" % i, f4) for i in range(2)]
                Pm = psb("Pm", f4)
                itr = psb("itr", f4)
                qg = psb("qg", f4)
                vb, kbg, kdec, uT, wT = E1s, E2s, E1i, br, G4
                vnT, vn, kdm, ona = Xa[0], Xa[1], XTa[0], XTa[1]
                for al, cn in (('vb', 'E1s'), ('kbg', 'E2s'), ('kdec', 'E1i'), ('uT', 'br'), ('wT', 'G4'),
                               ('vnT', ('Xa', 0)), ('vn', ('Xa', 1)), ('kdm', ('XTa', 0)), ('ona', ('XTa', 1))):
                    S.alias(al, cn)
                eglr = psb("eglr", [128, 4, 16])
                Sg = psb("Sg", f4)
                Sgs = psb("Sgs", f4)
                osq = psb("osq", f4, BF16)
                rno = psb("rno", [128, 512])
                onb = psb("onb", [128, KC, 128], BF16)
                ssn = psb("ssn", [128, 4])
                cqn = psb("cqn", [128, 256])
                ckvn = psb("ckvn", [128, 256])
                kst = psb("kst", [128, 128])
                rt = psb("rt", [128, 8, 64])
                cqnT = psb("cqnT", [128, 2, 128], BF16)
                ckvnT = psb("ckvnT", [128, 2, 128], BF16)
                qtm = psb("qtm", [128, 8, 96])
                qn2 = psb("qn2", [128, 4, 128], BF16)
                qp2 = psb("qp2", [128, 8, 32], BF16)
                QTn = psb("QTn", [128, 4, 128], BF16)
                QTp = psb("QTp", [32, 8, 128], BF16)
                G(lambda e: e.memset(Sg[:], 0.0), [], ['Sg'])
                G(lambda e: e.memset(kst[:], 0.0), [], ['kst'])
                G(lambda e: e.memset(onb[:], 0.0), [], [('onb', 0), ('onb', 1)])


                def sample_attention():
                    NPG = cfg['NPG']
                    NPOOL = cfg['NPOOL']
                    R = NPG
                    SB = 128 // R
                    RG = min(8, R)
                    NG = R // RG
                    BPP = 128 // RG
                    ssb = H['ssb2']
                    rep = ssb("rep", [NPG, 128])
                    gb = ssb("gb", [128, NG + 1])
                    cm8 = ssb("cm8", [128, 64])
                    S.dma('sp', rep[:], rep_in, writes=['rep'], stream='c')
                    S.dma('sp', gb[:], gbase_in, writes=['gb'], stream='c')
                    S.dma('sp', cm8[:], cm8_in, writes=['cm8'], stream='c')
                    pti = ssb("pti", [16, NPG], I32)
                    ptf = ssb("ptf", [16, NPG])
                    ptT = ssb("ptT", [NPG, 16])
                    ptrep = ssb("ptrep", [128, 16])
                    S.dma('sp', pti[:], page_table, writes=['pti'], stream='c')
                    V(lambda e: e.tensor_copy(out=ptf[:], in_=pti[:]), ['pti'], ['ptf'])
                    TR(pb[0][0:NPG, 0:16], ptf[:], ident[0:16, 0:16], ['ptf', 'ident'], PK(0))
                    A(lambda e: e.copy(out=ptT[:], in_=pb[0][0:NPG, 0:16]), PK(0), ['ptT'] + PK(0))
                    MM(pb[0][:, 0:16], rep[:], ptT[:], True, True, ['rep', 'ptT'], PK(0))
                    A(lambda e: e.copy(out=ptrep[:], in_=pb[0][:, 0:16]), PK(0), ['ptrep'] + PK(0))
                    idxf = ssb("idxf", [128, 16, NG + 1])
                    idx = ssb("idx", [128, 16, NG + 1], I32)
                    for g in range(NG + 1):
                        V(lambda e, g=g: e.tensor_scalar(out=idxf[:, :, g], in0=ptrep[:],
                                                        scalar1=float(BPP if g < NG else SB), scalar2=gb[:, g:g + 1],
                                                        op0=ALU.mult, op1=ALU.add), ['ptrep', 'gb', 'idxf'], ['idxf'])
                    V(lambda e: e.tensor_copy(out=idx[:], in_=idxf[:]), ['idxf'], ['idx'])
                    SST = cfg.get('sstop', 99)
                    if SST <= 1:
                        return
                    lat_blk = cache_lat.rearrange("n (b r) d -> (n b) (r d)", r=RG)
                    kr_blk = cache_kr.rearrange("n (b r) d -> (n b) (r d)", r=R)
                    wk_ol = ssb("wk_ol", [128, 2048], BF16)
                    wukT = wk_ol[:, :].rearrange("p (h l) -> p h l", h=8)
                    S.alias('wukT', 'wk_ol')
                    S.alias('olatT', 'wk_ol')
                    for hh in range(2):
                        S.dma('pool', wk_ol[:, hh * 1024:(hh + 1) * 1024], w_ukT[:, hh * 1024:(hh + 1) * 1024], writes=['wukT'], stream='w')
                    if cfg.get('sq', 9) <= 0:
                        return
                    qlatT = ssb("qlatT", [128, 2, 8, 128], BF16)
                    for c in range(2):
                        for h in range(8):
                            bi = h // 4
                            MM(pb[bi][:, (h % 4) * 128:(h % 4 + 1) * 128], wukT[:, h, c * 128:(c + 1) * 128],
                               QTn[:, h // 2, :], True, True, ['wukT', 'QT'], [('pbq', bi, h % 4)])
                        for b2 in range(2):
                            A(lambda e, c=c, b2=b2: e.copy(out=qlatT[:, c, b2 * 4:b2 * 4 + 4, :], in_=pbv(b2, 4)),
                              PK(b2), ['qlatT'] + PK(b2))
                    if cfg.get('sq', 9) <= 1:
                        return
                    qpeT = QTp
                    S.alias('qpeT', 'QT')
                    if cfg.get('sq', 9) <= 2:
                        return
                    kpTn = ssb("kpTn", [32, 128], BF16)
                    TR(pb[3][0:32, 0:128], kst[:, 0:32], ident[:], ['kst', 'ident'], PK(3))
                    A(lambda e: e.copy(out=kpTn[:], in_=pb[3][0:32, 0:128]), PK(3), ['kpTn'] + PK(3))
                    Lnew = ssb("Lnew", [128, 257], BF16)
                    V(lambda e: e.memset(Lnew[:, 256:257], 1.0), [], ['Lnew'])
                    V(lambda e: e.tensor_copy(out=Lnew[:, 0:256], in_=ckvn[:]), ['ckvn', 'Lnew'], ['Lnew'])
                    if SST <= 2:
                        return
                    NLB = cfg.get("nlb", 3)
                    Lb = [ssb("Lb%d" % i, [128, RG, 256], BF16) for i in range(NLB)]
                    Kb = [ssb("Kb0", [128, R, 32], BF16)] * 2
                    latT = [ssb("latT0", [128, 2, RG, 128], BF16)] * 2
                    kpT = [ssb("kpT0", [32, RG, 128], BF16)] * 2
                    PTs = [ssb("PTs%d" % i, [128, RG, 64], BF16) for i in range(2)]
                    oacc = ssb("oacc", [64, 257])
                    olat = ssb("olat", [64, 256])
                    olatT = wk_ol[:, :].rearrange("p (c s q) -> p c s q", c=2, s=16)
                    mst = ssb("mst", [128, 8])
                    m11 = ssb("m11", [1, 4])
                    gq = []
                    for s in range(16):
                        for g in range(NG + 1):
                            gq.append((s, g))
                    ctr = {'n': 0}

                    def stage1(s, g):
                        n = ctr['n']
                        ctr['n'] += 1
                        b = n % 2
                        lb = n % NLB
                        sbk = 6 + b
                        if g == 0:
                            kb_ = s % 2
                            S.dma_custom('pool', lambda e: e.indirect_dma_start(
                                out=Kb[kb_][:].rearrange("p r d -> p (r d)"), out_offset=None, in_=kr_blk,
                                in_offset=bass.IndirectOffsetOnAxis(ap=idx[:, s, NG:NG + 1], axis=0)),
                                ['idx'], [('Kb', 0)], stream='gk', nrr=2)
                        if g < NG:
                            S.dma_custom('pool', lambda e: e.indirect_dma_start(
                                out=Lb[lb][:].rearrange("p r d -> p (r d)"), out_offset=None, in_=lat_blk,
                                in_offset=bass.IndirectOffsetOnAxis(ap=idx[:, s, g:g + 1], axis=0)),
                                ['idx'], [('Lb', lb)], stream='gl', nrr=NLB)
                            for r in range(RG):
                                rr = g * RG + r
                                for c in range(2):
                                    blk = (r % 4) * 2 + c
                                    TR(pbbf(4 + (r // 4) % 2)[:, blk * 128:(blk + 1) * 128], Lb[lb][:, r, c * 128:(c + 1) * 128],
                                       ident_bf[:], [('Lb', lb), 'ident_bf'], [('pbq', 4 + (r // 4) % 2, blk // 2)])
                                TR(pbbf(3)[0:32, r * 128:(r + 1) * 128], Kb[s % 2][:, rr, :], ident_bf[:],
                                   [('Kb', 0), 'ident_bf'], [('pbq', 3, r // 2)])
                                if r % 4 == 3:
                                    bk = 4 + (r // 4) % 2
                                    r0 = r - 3
                                    V(lambda e, bk=bk, r0=r0: e.tensor_copy(
                                        out=latT[b][:, :, r0:r0 + 4, :].rearrange("p c r k -> p r c k"),
                                        in_=pbbf(bk).rearrange("p (r c k) -> p r c k", r=4, c=2)),
                                        PK(bk), [('latT', 0)] + PK(bk))
                            A(lambda e: e.copy(out=kpT[b][:].rearrange("p r k -> p (r k)"), in_=pbbf(3)[0:32, 0:RG * 128]),
                              PK(3), [('kpT', 0)] + PK(3))
                            for r in range(RG):
                                o = pb[sbk][:, r * 64:(r + 1) * 64]
                                MM(o, latT[b][:, 0, r, :], qlatT[:, 0, :, s * 8:(s + 1) * 8], True, False, [('latT', 0), 'qlatT'],
                                   [('pbq', sbk, r // 2)])
                                MM(o, latT[b][:, 1, r, :], qlatT[:, 1, :, s * 8:(s + 1) * 8], False, False, [('latT', 0), 'qlatT'],
                                   [('pbq', sbk, r // 2)])
                                MM(o, kpT[b][:, r, :], qpeT[:, :, s * 8:(s + 1) * 8], False, True, [('kpT', 0), 'qpeT'],
                                   [('pbq', sbk, r // 2)])
                            ncol = RG * 64
                        else:
                            o = pb[sbk][:, 0:64]
                            MM(o, ckvnT[:, 0, :], qlatT[:, 0, :, s * 8:(s + 1) * 8], True, False, ['ckvnT', 'qlatT'], [('pbq', sbk, 0)])
                            MM(o, ckvnT[:, 1, :], qlatT[:, 1, :, s * 8:(s + 1) * 8], False, False, ['ckvnT', 'qlatT'], [('pbq', sbk, 0)])
                            MM(o, kpTn[:], qpeT[:, :, s * 8:(s + 1) * 8], False, True, ['kpTn', 'qpeT'], [('pbq', sbk, 0)])
                            ncol = 64
                        return (s, g, b, sbk, ncol, lb)

                    def stage2(st_):
                        s, g, b, sbk, ncol, lb = st_
                        keys = [('pbq', sbk, j) for j in range((ncol + 127) // 128)]
                        if g == 0:
                            V(lambda e: e.reduce_max(out=mst[:, 0:1], in_=pb[sbk][:, :ncol], axis=AX.X), keys, ['mst0'])
                            TR(pb[2][0:1, 0:128], mst[:, 0:1], ident[:], ['mst0', 'ident'], PK(2))
                            V(lambda e: e.reduce_max(out=m11[:, 1:2], in_=pb[2][0:1, 0:128], axis=AX.X), PK(2),
                              ['m11'] + PK(2))
                            MM(pb[2][:, 128:129], ones_f[0:1, :], m11[:, 1:2], True, True, ['ones_f', 'm11'], PK(2))
                            A(lambda e: e.mul(out=mst[:, 3:4], in_=pb[2][:, 128:129], mul=-SCm), PK(2), ['mst3'] + PK(2))
                        first = (g == 0)
                        if g < NG:
                            A(lambda e: e.activation(out=PTs[b][:].rearrange("p r q -> p (r q)"), in_=pb[sbk][:, :ncol],
                                                     func=AF.Exp, bias=mst[:, 3:4], scale=SCm),
                              keys + ['mst3'], [('PTs', b)] + keys)
                            for r in range(RG):
                                MM(pb[1][0:64, 0:256], PTs[b][:, r, :], Lb[lb][:, r, :], first and r == 0, False,
                                   [('PTs', b), ('Lb', lb)], PK(1), inc=(r == RG - 1))
                                MM(pb[1][0:64, 256:257], PTs[b][:, r, :], ones_bf[:, 0:1], False, False,
                                   [('PTs', b), 'ones_bf'], PK(1), inc=(r == RG - 1))
                        else:
                            A(lambda e: e.activation(out=PTs[b][:, 0, :], in_=pb[sbk][:, :64], func=AF.Exp,
                                                     bias=mst[:, 3:4], scale=SCm), keys + ['mst3'], [('PTs', b)] + keys)
                            V(lambda e: e.scalar_tensor_tensor(out=PTs[b][:, 0, :], in0=PTs[b][:, 0, :],
                                                               scalar=seqmask[:, s:s + 1], in1=cm8[:], op0=ALU.mult,
                                                               op1=ALU.mult), [('PTs', b), 'seqmask', 'cm8'], [('PTs', b)])
                            MM(pb[1][0:64, 0:257], PTs[b][:, 0, :], Lnew[:], False, True, [('PTs', b), 'Lnew'], PK(1))
                            V(lambda e: e.reciprocal(out=mst[0:64, 5:6], in_=pb[1][0:64, 256:257]), PK(1), ['mst5'])
                            V(lambda e: e.tensor_scalar(out=olat[:], in0=pb[1][0:64, 0:256], scalar1=mst[0:64, 5:6],
                                                        scalar2=None, op0=ALU.mult), PK(1) + ['mst5'], ['olat'] + PK(1))
                            for c in range(2):
                                TR(pb[0][:, c * 64:(c + 1) * 64], olat[:, c * 128:(c + 1) * 128], ident[0:64, 0:64],
                                   ['olat', 'ident'], PK(0))
                            A(lambda e: e.copy(out=olatT[:, :, s, :], in_=pb[0][:, 0:128].rearrange("p (c q) -> p c q", c=2)),
                              PK(0), ['olatT'] + PK(0))

                    prev = None
                    if SST <= 5:
                        gq = gq[:SST - 2]
                    for (s, g) in gq:
                        cur = stage1(s, g)
                        if prev is not None and SST != 3:
                            stage2(prev)
                        prev = cur
                    if SST != 3:
                        stage2(prev)
                    if cfg.get('dbg_s'):
                        S.dma('sp', dbg_mst, mst[:], reads=['mst0', 'mst3', 'mst5'], stream='o')
                        S.dma('sp', dbg_lb, Lb[0][:, :, :].rearrange('p r d -> p (r d)'), reads=[('Lb', 0)], stream='o')
                        S.dma('sp', dbg_kb, Kb[0][:].rearrange('p r d -> p (r d)'), reads=[('Kb', 0)], stream='o')
                        S.dma('sp', dbg_olat, olat[:], reads=['olat'], stream='o')
                    if SST <= 6:
                        return
                    for h in range(8):
                        rh = (h % 2) * 64
                        pr = (h // 2) * 128
                        for c in range(2):
                            MM(pb[0][:, 0:128], wuv[:, c, pr:pr + 128],
                               olatT[:, c, :, h * 8:(h + 1) * 8], c == 0, c == 1,
                               ['wuv', 'olatT'], PK(0))
                        V(lambda e, h=h, rh=rh: e.tensor_copy(out=onb[rh:rh + 64, 4 + h // 2, :], in_=pb[0][rh:rh + 64, 0:128]),
                          PK(0), [('onb', 1)] + PK(0))


                def tile_body(t):
                    smp = (t == NT)
                    M = H['M']
                    mk = 'maskS' if smp else 'maskP'
                    S.dma('sp', xin, tile_rows(t), writes=['xin'], stream='x')
                    for half in range(2):
                        for j in range(4):
                            kc = half * 4 + j
                            TR(pb[half][:, j * 128:(j + 1) * 128], xin[:, kc * 128:(kc + 1) * 128], ident[:],
                               ['xin', 'ident'], [('pbq', half, j)])
                        A(lambda e, half=half: e.copy(out=xTt[:, half * 4:half * 4 + 4, :], in_=pbv(half, 4)),
                          PK(half), ['xTt'] + PK(half))
                    norm_range(gmix[:, 0, :], 'gmix', 0, 128, hTt[:], ['hTt'], xv=xTt[:], xk=['xTt'])
                    for c in range(16):
                        bi = c // 4
                        for kc in range(KC):
                            MM(pb[bi][:, (c % 4) * 128:(c % 4 + 1) * 128], wia[:, kc, c * 128:(c + 1) * 128],
                               hTt[:, kc, :], kc == 0, kc == KC - 1, ['wia', 'hTt'], [('pbq', bi, c % 4)])
                    for b2 in range(3):
                        if smp:
                            A(lambda e, b2=b2: e.copy(
                                out=H['cbufS'][:, b2 * 4:b2 * 4 + 4, :, 3:11],
                                in_=pb[b2][:].rearrange("p (c s t) -> p c s t", c=4, s=16)),
                              PK(b2), ['cbufS'] + PK(b2))
                        else:
                            A(lambda e, b2=b2: e.copy(out=H['cbuf'][:, b2 * 4:b2 * 4 + 4, 3:131], in_=pbv(b2, 4)),
                              PK(b2), ['cbuf'] + PK(b2))
                    A(lambda e: e.activation(out=zs[:], in_=pbv(3, 4), func=AF.Silu), PK(3), ['zs'] + PK(3))
                    for (bi, c0, cn) in ((4, 2048, 512), (5, 2560, 40)):
                        for kc in range(KC):
                            MM(pb[bi][:, :cn], hTt[:, kc, :], wia[:, kc, c0:c0 + cn], kc == 0, kc == KC - 1,
                               ['wia', 'hTt'], PK(bi))
                        A(lambda e, bi=bi, c0=c0, cn=cn: e.copy(out=tm[:, c0 - 2048:c0 - 2048 + cn], in_=pb[bi][:, :cn]),
                          PK(bi), ['tm'] + PK(bi))
                    if smp or t == NT - 1:
                        for j3 in range(3):
                            bi = 5 + j3
                            for kc in range(KC):
                                MM(pb[bi][:], hTt[:, kc, :], wia[:, kc, j3 * 512:(j3 + 1) * 512], kc == 0, kc == KC - 1,
                                   ['wia', 'hTt'], PK(bi))
                            A(lambda e, bi=bi, j3=j3: e.copy(out=qkvtm[:, j3 * 512:(j3 + 1) * 512], in_=pb[bi][:]),
                              PK(bi), ['qkvtm'] + PK(bi))
                        if smp:
                            for s in range(16):
                                S.dma('sp', conv_s_o[s], qkvtm[s * 8 + 5:s * 8 + 8, :], reads=['qkvtm'], stream='o')
                        else:
                            S.dma('sp', conv_p_o, qkvtm[125:128, :], reads=['qkvtm'], stream='o')
                    if STG <= 1:
                        return
                    if smp:
                        S.dma('sp', cst, conv_s_in, writes=['cst'], stream='c')
                        for c in range(12):
                            bi = c // 8
                            cc = c % 8
                            TR(pb[bi][:, cc * 48:(cc + 1) * 48], cst[:, c * 128:(c + 1) * 128], ident[0:48, 0:48],
                               ['cst', 'ident'], PK(bi))
                        A(lambda e: e.copy(out=H['cbufS'][:, 0:8, :, 0:3],
                                           in_=pb[0][:, 0:384].rearrange("p (c s r) -> p c s r", c=8, s=16)),
                          PK(0), ['cbufS'] + PK(0))
                        A(lambda e: e.copy(out=H['cbufS'][:, 8:12, :, 0:3],
                                           in_=pb[1][:, 0:192].rearrange("p (c s r) -> p c s r", c=4, s=16)),
                          PK(1), ['cbufS'] + PK(1))
                        accv = acc[:].rearrange("p c (s t) -> p c s t", s=16)
                        ctv = ctmp[:].rearrange("p c (s t) -> p c s t", s=16)

                        def win(i):
                            return H['cbufS'][:, :, :, i:i + 8]

                        def cwb(i):
                            return cw[:, :, i:i + 1].unsqueeze(3).to_broadcast([128, 12, 16, 8])
                        ck = 'cbufS'
                    else:
                        accv = acc[:]
                        ctv = ctmp[:]

                        def win(i):
                            return H['cbuf'][:, :, i:i + 128]

                        def cwb(i):
                            return cw[:, :, i:i + 1].to_broadcast([128, 12, 128])
                        ck = 'cbuf'
                    V(lambda e: e.tensor_tensor(out=accv, in0=win(0), in1=cwb(0), op=ALU.mult), [ck, 'cw'], ['acc'])
                    for i in range(1, 4):
                        G(lambda e, i=i: e.tensor_tensor(out=ctv, in0=win(i), in1=cwb(i), op=ALU.mult),
                          [ck, 'cw'], ['ctmp'])
                        V(lambda e: e.tensor_tensor(out=accv, in0=accv, in1=ctv, op=ALU.add), ['acc', 'ctmp'], ['acc'])
                    if not smp:
                        V(lambda e: e.tensor_copy(out=H['cbuf'][:, :, 0:3], in_=H['cbuf'][:, :, 128:131]), ['cbuf'], ['cbuf'])
                    A(lambda e: e.activation(out=acc[:], in_=acc[:], func=AF.Silu), ['acc'], ['acc'])
                    A(lambda e: e.activation(out=sq8[:], in_=qkvs[:, 0:8, :], func=AF.Square), ['qkvs'], ['sq8'])
                    for b2 in range(2):
                        MM(pb[b2][:], ones_bf[:], sq8[:, b2 * 4:b2 * 4 + 4, :].rearrange("p h t -> p (h t)"), True, True,
                           ['sq8', 'ones_bf'], PK(b2))
                        A(lambda e, b2=b2: e.activation(out=rn8[:, b2 * 4:b2 * 4 + 4, :], in_=pbv(b2, 4), func=AF.Sqrt,
                                                        bias=EPS, scale=1.0), PK(b2), ['rn8'] + PK(b2))
                    V(lambda e: e.reciprocal(out=rn8[:], in_=rn8[:]), ['rn8'], ['rn8'])
                    V(lambda e: e.tensor_tensor(out=qkn[:], in0=qkvs[:, 0:8, :], in1=rn8[:], op=ALU.mult),
                      ['qkvs', 'rn8'], ['qkn'])
                    for h in range(4):
                        TR(pb[2][:, h * 128:(h + 1) * 128], qkn[:, 4 + h, :], ident[:], ['qkn', 'ident'], [('pbq', 2, h)])
                        TR(pb[3][:, h * 128:(h + 1) * 128], qkvs[:, 8 + h, :], ident[:], ['qkvs', 'ident'],
                           [('pbq', 3, h)])
                    A(lambda e: e.copy(out=ktm[:], in_=pbv(2, 4)), PK(2), ['ktm'] + PK(2))
                    V(lambda e: e.tensor_copy(out=vtm[:], in_=pbv(3, 4)), PK(3), ['vtm'] + PK(3))
                    A(lambda e: e.activation(out=gt[:, 0:4], in_=tm[:, 0:4], func=AF.Sigmoid), ['tm'], ['gt'])
                    V(lambda e: e.tensor_tensor(out=gt[:, 8:12], in0=tm[:, 4:8], in1=dtb[:], op=ALU.add),
                      ['tm', 'dtb', 'gt'], ['gt'])
                    A(lambda e: e.activation(out=gt[:, 8:12], in_=gt[:, 8:12], func=AF.Exp), ['gt'], ['gt'])
                    A(lambda e: e.activation(out=gt[:, 8:12], in_=gt[:, 8:12], func=AF.Ln, bias=1.0), ['gt'], ['gt'])
                    V(lambda e: e.tensor_tensor(out=gt[:, 4:8], in0=gt[:, 8:12], in1=negA[:], op=ALU.mult),
                      ['gt', 'negA'], ['gt'])
                    MM(pb[4][:, 0:4], M['tri'][:], gt[:, 4:8], True, True, [mk, 'gt'], PK(4))
                    A(lambda e: e.copy(out=gcc[:, 0:4], in_=pb[4][:, 0:4]), PK(4), ['gcc'] + PK(4))
                    MM(pb[4][:, 0:4], M['blk'][:], gt[:, 4:8], True, True, [mk, 'gt'], PK(4))
                    A(lambda e: e.copy(out=gcc[:, 4:8], in_=pb[4][:, 0:4]), PK(4), ['gcc'] + PK(4))
                    V(lambda e: e.tensor_tensor(out=G4[:], in0=M['tri'][:].unsqueeze(1).to_broadcast(f4),
                                                in1=gt[:, 4:8].unsqueeze(2).to_broadcast(f4), op=ALU.mult),
                      [mk, 'gt'], ['G4'])
                    MM(pb[5][:], ones_f[:], G4[:].rearrange("p h t -> p (h t)"), True, True, ['ones_f', 'G4'], PK(5))
                    A(lambda e: e.copy(out=gcr[:], in_=pbv(5, 4)), PK(5), ['gcr'] + PK(5))
                    V(lambda e: e.tensor_tensor(out=G4[:], in0=ident[:].unsqueeze(1).to_broadcast(f4),
                                                in1=gt[:, 0:4].unsqueeze(2).to_broadcast(f4), op=ALU.mult),
                      ['ident', 'gt', 'G4'], ['G4'])
                    MM(pb[6][:], ones_f[:], G4[:].rearrange("p h t -> p (h t)"), True, True, ['ones_f', 'G4'], PK(6))
                    A(lambda e: e.copy(out=br[:], in_=pbv(6, 4)), PK(6), ['br'] + PK(6))
                    V(lambda e: e.tensor_tensor(out=a1[:], in0=gcr[:], in1=gcc[:, 0:4].unsqueeze(2).to_broadcast(f4),
                                                op=ALU.subtract), ['gcr', 'gcc'], ['a1'])
                    V(lambda e: e.tensor_tensor(out=E1s[:], in0=a1[:], in1=M['nms'][:].unsqueeze(1).to_broadcast(f4),
                                                op=ALU.add), ['a1', mk], ['E1s'])
                    A(lambda e: e.activation(out=E1s[:], in_=E1s[:], func=AF.Exp), ['E1s'], ['E1s'])
                    V(lambda e: e.tensor_tensor(out=E1i[:], in0=a1[:], in1=M['nmi'][:].unsqueeze(1).to_broadcast(f4),
                                                op=ALU.add), ['a1', mk], ['E1i'])
                    A(lambda e: e.activation(out=E1i[:], in_=E1i[:], func=AF.Exp), ['E1i'], ['E1i'])
                    V(lambda e: e.tensor_tensor(out=a1[:], in0=gcc[:, 0:4].unsqueeze(2).to_broadcast(f4), in1=gcr[:],
                                                op=ALU.subtract), ['gcr', 'gcc', 'a1'], ['a1'])
                    V(lambda e: e.tensor_tensor(out=E2s[:], in0=a1[:], in1=M['nmsT'][:].unsqueeze(1).to_broadcast(f4),
                                                op=ALU.add), ['a1', mk], ['E2s'])
                    A(lambda e: e.activation(out=E2s[:], in_=E2s[:], func=AF.Exp), ['E2s'], ['E2s'])
                    V(lambda e: e.scalar_tensor_tensor(out=E1s[:], in0=E1s[:], scalar=-1.0, in1=br[:], op0=ALU.mult,
                                                       op1=ALU.mult), ['E1s', 'br'], ['E1s'])
                    V(lambda e: e.scalar_tensor_tensor(out=E2s[:], in0=E2s[:], scalar=-1.0,
                                                       in1=gt[:, 0:4].unsqueeze(2).to_broadcast(f4), op0=ALU.mult,
                                                       op1=ALU.mult), ['E2s', 'gt'], ['E2s'])
                    for h in range(4):
                        MM(pb[0][:, h * 128:(h + 1) * 128], qkn[:, 4 + h, :], qkn[:, 4 + h, :], True, True, ['qkn'],
                           [('pbq', 0, h)])
                        MM(pb[1][:, h * 128:(h + 1) * 128], qkn[:, 4 + h, :], qkn[:, h, :], True, True, ['qkn'],
                           [('pbq', 1, h)])
                    V(lambda e: e.tensor_tensor(out=Xa[0][:], in0=pbv(0, 4), in1=E1s[:], op=ALU.mult),
                      PK(0) + ['E1s'], [('Xa', 0)])
                    V(lambda e: e.tensor_tensor(out=XTa[0][:], in0=pbv(0, 4), in1=E2s[:], op=ALU.mult),
                      PK(0) + ['E2s'], [('XTa', 0)] + PK(0))
                    V(lambda e: e.scalar_tensor_tensor(out=itr[:], in0=pbv(1, 4), scalar=SCq, in1=E1i[:], op0=ALU.mult,
                                                       op1=ALU.mult), PK(1) + ['E1i'], ['itr'] + PK(1))
                    V(lambda e: e.tensor_tensor(out=Pm[:], in0=Xa[0][:], in1=ident[:].unsqueeze(1).to_broadcast(f4),
                                                op=ALU.add), [('Xa', 0), 'ident'], ['Pm'])
                    nlev = 2 if smp else 6
                    for lv in range(nlev):
                        a, b = lv % 2, (lv + 1) % 2
                        last = (lv == nlev - 1)
                        for h in range(4):
                            hs = slice(h * 128, (h + 1) * 128)
                            if not last:
                                MM(pb[2][:, hs], XTa[a][:, h, :], Xa[a][:, h, :], True, True, [('XTa', a), ('Xa', a)],
                                   [('pbq', 2, h)])
                            MM(pb[3][:, hs], Xa[a][:, h, :], XTa[a][:, h, :], True, True, [('XTa', a), ('Xa', a)],
                               [('pbq', 3, h)])
                        if not last:
                            V(lambda e, b=b: e.tensor_copy(out=Xa[b][:], in_=pbv(2, 4)), PK(2), [('Xa', b)] + PK(2))
                        A(lambda e, b=b: e.copy(out=XTa[b][:], in_=pbv(3, 4)), PK(3), [('XTa', b)] + PK(3))
                        for h in range(4):
                            hs = slice(h * 128, (h + 1) * 128)
                            MM(pb[4][:, hs], XTa[b][:, h, :], Pm[:, h, :], True, True, [('XTa', b), 'Pm'], [('pbq', 4, h)])
                        V(lambda e: e.tensor_tensor(out=Pm[:], in0=Pm[:], in1=pbv(4, 4), op=ALU.add),
                          ['Pm'] + PK(4), ['Pm'] + PK(4))
                    V(lambda e: e.tensor_tensor(out=vb[:], in0=vtm[:], in1=gt[:, 0:4].unsqueeze(2).to_broadcast(f4),
                                                op=ALU.mult), ['vtm', 'gt'], ['vb'])
                    A(lambda e: e.activation(out=gt[:, 8:12], in_=gcc[:, 0:4], func=AF.Exp), ['gcc', 'gt'], ['gt'])
                    V(lambda e: e.tensor_tensor(out=gt[:, 12:16], in0=gt[:, 8:12], in1=gt[:, 0:4], op=ALU.mult),
                      ['gt'], ['gt'])
                    V(lambda e: e.tensor_tensor(out=kbg[:], in0=ktm[:], in1=gt[:, 12:16].unsqueeze(2).to_broadcast(f4),
                                                op=ALU.mult), ['ktm', 'gt'], ['kbg'])
                    V(lambda e: e.tensor_tensor(out=gt[:, 8:12], in0=gcc[:, 4:8], in1=gcc[:, 0:4], op=ALU.subtract),
                      ['gcc', 'gt'], ['gt'])
                    A(lambda e: e.activation(out=gt[:, 8:12], in_=gt[:, 8:12], func=AF.Exp), ['gt'], ['gt'])
                    V(lambda e: e.tensor_tensor(out=kdec[:], in0=ktm[:], in1=gt[:, 8:12].unsqueeze(2).to_broadcast(f4),
                                                op=ALU.mult), ['ktm', 'gt'], ['kdec'])
                    for h in range(4):
                        hs = slice(h * 128, (h + 1) * 128)
                        MM(pb[5][:, hs], vb[:, h, :], Pm[:, h, :], True, True, ['vb', 'Pm'], [('pbq', 5, h)])
                        MM(pb[6][:, hs], kbg[:, h, :], Pm[:, h, :], True, True, ['kbg', 'Pm'], [('pbq', 6, h)])
                    A(lambda e: e.copy(out=uT[:], in_=pbv(5, 4)), PK(5), ['uT'] + PK(5))
                    V(lambda e: e.tensor_copy(out=wT[:], in_=pbv(6, 4)), PK(6), ['wT'] + PK(6))
                    A(lambda e: e.activation(out=a1[:], in_=gcr[:], func=AF.Exp), ['gcr', 'a1'], ['a1'])
                    V(lambda e: e.scalar_tensor_tensor(out=qg[:], in0=qkn[:, 0:4, :], scalar=SCq, in1=a1[:], op0=ALU.mult,
                                                       op1=ALU.mult), ['qkn', 'a1'], ['qg'])
                    if smp:
                        A(lambda e: e.activation(out=eglr[:], in_=gcr[:].rearrange("p h (s t) -> p h s t", t=8)[:, :, :, 7],
                                                 func=AF.Exp), ['gcr'], ['eglr'])
                        V(lambda e: e.memset(pb[7][:], 0.0), [], PK(7))
                        V(lambda e: e.memset(pb[3][:], 0.0), [], PK(3))
                        for s in range(16):
                            ss_ = slice(s * 8, (s + 1) * 8)
                            S.dma('sp', Sgs[:], gdn_s_in[s].rearrange("h k v -> k h v"), writes=['Sgs'], stream='s0')
                            for h in range(4):
                                MM(pb[7][:, h * 128 + s * 8:h * 128 + s * 8 + 8], Sgs[:, h, :], wT[:, h, ss_],
                                   False, False, ['Sgs', 'wT'], [('pbq', 7, h)])
                                MM(pb[3][:, h * 128 + s * 8:h * 128 + s * 8 + 8], Sgs[:, h, :], qg[:, h, ss_],
                                   False, False, ['Sgs', 'qg'], [('pbq', 3, h)], inc=(h == 3))
                    else:
                        A(lambda e: e.activation(out=eglr[:, :, 0], in_=gcr[:, :, 127], func=AF.Exp), ['gcr'], ['eglr'])
                        V(lambda e: e.memset(pb[3][:], 0.0), [], PK(3))
                        for h in range(4):
                            hs = slice(h * 128, (h + 1) * 128)
                            MM(pb[7][:, hs], Sg[:, h, :], wT[:, h, :], True, True, ['Sg', 'wT'], [('pbq', 7, h)])
                            MM(pb[3][:, hs], Sg[:, h, :], qg[:, h, :], False, False, ['Sg', 'qg'], [('pbq', 3, h)])
                    V(lambda e: e.tensor_tensor(out=vnT[:], in0=uT[:], in1=pbv(7, 4), op=ALU.subtract),
                      ['uT'] + PK(7), ['vnT'] + PK(7))
                    for h in range(4):
                        TR(pb[2][:, h * 128:(h + 1) * 128], vnT[:, h, :], ident[:], ['vnT', 'ident'], [('pbq', 2, h)])
                    A(lambda e: e.copy(out=vn[:], in_=pbv(2, 4)), PK(2), ['vn'] + PK(2))
                    for h in range(4):
                        hs = slice(h * 128, (h + 1) * 128)
                        MM(pb[3][:, hs], vn[:, h, :], itr[:, h, :], False, True, ['vn', 'itr'], [('pbq', 3, h)])
                    if smp:
                        for s in range(16):
                            V(lambda e, s=s: e.tensor_scalar(out=kdm[:], in0=kdec[:], scalar1=seqmask[:, s:s + 1],
                                                            scalar2=None, op0=ALU.mult), ['kdec', 'seqmask'], ['kdm'])
                            for h in range(4):
                                MM(pb[4][:, h * 128:(h + 1) * 128], kdm[:, h, :], vn[:, h, :], True, True, ['kdm', 'vn'],
                                   [('pbq', 4, h)])
                            S.dma('sp', Sgs[:], gdn_s_in[s].rearrange("h k v -> k h v"), writes=['Sgs'], stream='s0')
                            V(lambda e, s=s: e.tensor_tensor(out=Sgs[:], in0=Sgs[:],
                                                             in1=eglr[:, :, s:s + 1].to_broadcast(f4), op=ALU.mult),
                              ['Sgs', 'eglr'], ['Sgs'])
                            V(lambda e: e.tensor_tensor(out=Sgs[:], in0=Sgs[:], in1=pbv(4, 4), op=ALU.add),
                              ['Sgs'] + PK(4), ['Sgs'] + PK(4))
                            S.dma('sp', gdn_s_o[s].rearrange("h k v -> k h v"), Sgs[:], reads=['Sgs'], stream='o')
                    else:
                        for h in range(4):
                            MM(pb[4][:, h * 128:(h + 1) * 128], kdec[:, h, :], vn[:, h, :], True, True, ['kdec', 'vn'],
                               [('pbq', 4, h)])
                        V(lambda e: e.tensor_tensor(out=Sg[:], in0=Sg[:], in1=eglr[:, :, 0:1].to_broadcast(f4),
                                                    op=ALU.mult), ['Sg', 'eglr'], ['Sg'])
                        V(lambda e: e.tensor_tensor(out=Sg[:], in0=Sg[:], in1=pbv(4, 4), op=ALU.add),
                          ['Sg'] + PK(4), ['Sg'] + PK(4))
                        if t == NT - 1:
                            S.dma('sp', gdn_p_o.rearrange("h k v -> k h v"), Sg[:], reads=['Sg'], stream='o')
                    A(lambda e: e.activation(out=osq[:], in_=pbv(3, 4), func=AF.Square), PK(3), ['osq'])
                    MM(pb[0][:], ones_bf[:], osq[:].rearrange("p h t -> p (h t)"), True, True, ['osq', 'ones_bf'], PK(0))
                    A(lambda e: e.activation(out=rno[:], in_=pb[0][:], func=AF.Sqrt, bias=EPS, scale=1.0 / 128),
                      PK(0), ['rno'] + PK(0))
                    V(lambda e: e.reciprocal(out=rno[:], in_=rno[:]), ['rno'], ['rno'])
                    V(lambda e: e.scalar_tensor_tensor(out=ona[:], in0=pbv(3, 4), scalar=gdnn[:, 0:1],
                                                       in1=rno[:].rearrange("p (h t) -> p h t", h=4), op0=ALU.mult,
                                                       op1=ALU.mult), PK(3) + ['gdnn', 'rno'], ['ona'] + PK(3))
                    G(lambda e: e.tensor_tensor(out=onb[:, 0:4, :], in0=ona[:], in1=zs[:], op=ALU.mult),
                      ['ona', 'zs'], [('onb', 0)])
                    if STG <= 2:
                        return
                    A(lambda e: e.activation(out=rt[:, 0:4, :].rearrange("p a b -> p (a b)"), in_=tm[:, 8:264],
                                             func=AF.Square, accum_out=ssn[:, 0:1]), ['tm'], ['rt', 'ssn'])
                    A(lambda e: e.activation(out=rt[:, 4:8, :].rearrange("p a b -> p (a b)"), in_=tm[:, 264:520],
                                             func=AF.Square, accum_out=ssn[:, 1:2]), ['tm'], ['rt', 'ssn'])
                    A(lambda e: e.activation(out=ssn[:, 2:4], in_=ssn[:, 0:2], func=AF.Sqrt, bias=EPS, scale=1.0 / 256),
                      ['ssn'], ['ssn'])
                    V(lambda e: e.reciprocal(out=ssn[:, 2:4], in_=ssn[:, 2:4]), ['ssn'], ['ssn'])
                    V(lambda e: e.scalar_tensor_tensor(out=cqn[:], in0=tm[:, 8:264], scalar=ssn[:, 2:3], in1=qnb[:],
                                                       op0=ALU.mult, op1=ALU.mult), ['tm', 'ssn', 'qnb'], ['cqn'])
                    V(lambda e: e.scalar_tensor_tensor(out=ckvn[:], in0=tm[:, 264:520], scalar=ssn[:, 3:4], in1=kvnb[:],
                                                       op0=ALU.mult, op1=ALU.mult), ['tm', 'ssn', 'kvnb'], ['ckvn'])
                    lat_dst = lat_s_o if smp else lat_p_o[t * 128:(t + 1) * 128, :]
                    kpe_dst = kpe_s_o if smp else kpe_p_o[t * 128:(t + 1) * 128, :]
                    S.dma('sp', lat_dst, ckvn[:], reads=['ckvn'], stream='o')
                    cs = cosb[:, t, :]
                    sn = sinb[:, t, :]
                    x1 = tm[:, 520:536]
                    x2 = tm[:, 536:552]
                    V(lambda e: e.tensor_tensor(out=kst[:, 0:16], in0=x1, in1=cs, op=ALU.mult), ['tm', 'cosb', 'kst'], ['kst'])
                    V(lambda e: e.tensor_tensor(out=rt[:, 0, 0:16], in0=x2, in1=sn, op=ALU.mult), ['tm', 'sinb', 'rt'], ['rt'])
                    V(lambda e: e.tensor_tensor(out=kst[:, 0:16], in0=kst[:, 0:16], in1=rt[:, 0, 0:16], op=ALU.subtract),
                      ['kst', 'rt'], ['kst'])
                    V(lambda e: e.tensor_tensor(out=kst[:, 16:32], in0=x2, in1=cs, op=ALU.mult), ['tm', 'cosb', 'kst'], ['kst'])
                    V(lambda e: e.tensor_tensor(out=rt[:, 0, 16:32], in0=x1, in1=sn, op=ALU.mult), ['tm', 'sinb', 'rt'], ['rt'])
                    V(lambda e: e.tensor_tensor(out=kst[:, 16:32], in0=kst[:, 16:32], in1=rt[:, 0, 16:32], op=ALU.add),
                      ['kst', 'rt'], ['kst'])
                    S.dma('sp', kpe_dst, kst[:, 0:32], reads=['kst'], stream='o')
                    if STG <= 3:
                        return
                    for c in range(2):
                        TR(pb[6][:, c * 128:(c + 1) * 128], cqn[:, c * 128:(c + 1) * 128], ident[:], ['cqn', 'ident'],
                           [('pbq', 6, c)])
                        TR(pb[6][:, 256 + c * 128:256 + (c + 1) * 128], ckvn[:, c * 128:(c + 1) * 128], ident[:],
                           ['ckvn', 'ident'], [('pbq', 6, 2 + c)])
                    TR(pb[7][:, 0:128], kst[:], ident[:], ['kst', 'ident'], [('pbq', 7, 0)])
                    A(lambda e: e.copy(out=cqnT[:], in_=pb[6][:, 0:256].rearrange("p (c t) -> p c t", c=2)),
                      PK(6), ['cqnT'])
                    A(lambda e: e.copy(out=ckvnT[:], in_=pb[6][:, 256:512].rearrange("p (c t) -> p c t", c=2)),
                      PK(6), ['ckvnT'] + PK(6))
                    kcols = tile_cols(t)
                    kkey = ('KT', t)
                    if not smp:
                        A(lambda e: e.copy(out=H['KTp'][:, kcols], in_=pb[7][0:32, 0:128]), PK(7), [kkey] + PK(7))
                    for (bi, c0, cn) in ((0, 0, 512), (1, 512, 256)):
                        for c in range(2):
                            MM(pb[bi][:, :cn], cqnT[:, c, :], wuq[:, c, c0:c0 + cn], c == 0, c == 1, ['cqnT', 'wuq'], PK(bi))
                        A(lambda e, bi=bi, c0=c0, cn=cn: e.copy(out=qtm[:].rearrange("p h d -> p (h d)")[:, c0:c0 + cn],
                                                                in_=pb[bi][:, :cn]), PK(bi), ['qtm'] + PK(bi))
                    csb = cs.unsqueeze(1).to_broadcast([128, 8, 16])
                    snb = sn.unsqueeze(1).to_broadcast([128, 8, 16])
                    q1 = qtm[:, :, 64:80]
                    q2 = qtm[:, :, 80:96]
                    V(lambda e: e.tensor_copy(out=qn2[:].rearrange("p m (a d) -> p (m a) d", a=2), in_=qtm[:, :, 0:64]),
                      ['qtm'], ['qtb'])
                    V(lambda e: e.tensor_tensor(out=rt[:, :, 0:16], in0=q1, in1=csb, op=ALU.mult), ['qtm', 'cosb', 'rt'], ['rt'])
                    V(lambda e: e.tensor_tensor(out=rt[:, :, 16:32], in0=q2, in1=snb, op=ALU.mult), ['qtm', 'sinb', 'rt'], ['rt'])
                    V(lambda e: e.tensor_tensor(out=qp2[:, :, 0:16], in0=rt[:, :, 0:16], in1=rt[:, :, 16:32],
                                                op=ALU.subtract), ['rt', 'qtb'], ['qtb'])
                    V(lambda e: e.tensor_tensor(out=rt[:, :, 32:48], in0=q2, in1=csb, op=ALU.mult), ['qtm', 'cosb', 'rt'], ['rt'])
                    V(lambda e: e.tensor_tensor(out=rt[:, :, 48:64], in0=q1, in1=snb, op=ALU.mult), ['qtm', 'sinb', 'rt'], ['rt'])
                    V(lambda e: e.tensor_tensor(out=qp2[:, :, 16:32], in0=rt[:, :, 32:48], in1=rt[:, :, 48:64],
                                                op=ALU.add), ['rt', 'qtb'], ['qtb'])
                    for m in range(4):
                        TR(pbbf(2)[:, m * 128:(m + 1) * 128], qn2[:, m, :], ident_bf[:], ['qtb', 'ident_bf'],
                           [('pbq', 2, m // 2)])
                    for h in range(8):
                        TR(pbbf(1)[0:32, h * 128:(h + 1) * 128], qp2[:, h, :], ident_bf[:], ['qtb', 'ident_bf'],
                           [('pbq', 1, h // 2)])
                    A(lambda e: e.copy(out=QTn[:].rearrange("p m t -> p (m t)"), in_=pbbf(2)[:, 0:512]), PK(2),
                      ['QT'] + PK(2))
                    A(lambda e: e.copy(out=QTp[:].rearrange("p h t -> p (h t)"), in_=pbbf(1)[0:32, :]), PK(1),
                      ['QT'] + PK(1))
                    if not smp:
                        for m in range(4):
                            for c in range(2):
                                MM(pb[3][:, m * 128:(m + 1) * 128], wuk[:, c, m * 128:(m + 1) * 128], ckvnT[:, c, :],
                                   c == 0, c == 1, ['wuk', 'ckvnT'], [('pbq', 3, m)])
                        A(lambda e: e.copy(out=H['KTn'][:, :, kcols], in_=pbv(3, 4)), PK(3), [kkey] + PK(3))
                        for c in range(2):
                            MM(pb[5][:], ckvnT[:, c, :], wuv[:, c, :], c == 0, c == 1, ['ckvnT', 'wuv'], PK(5))
                        A(lambda e: e.copy(out=H['Vt'][:, t, :], in_=pb[5][:]), PK(5), [('Vt', t)] + PK(5))
                    if STG <= 4:
                        return
                    if not smp:
                        nkt = t + 1
                        items = [(h, g0) for h in range(8) for g0 in range(0, nkt, 4)]

                        def qk(idx_):
                            h, g0 = items[idx_]
                            rh = (h % 2) * 64
                            gn = min(4, nkt - g0)
                            sbk = 6 + idx_ % 2
                            for j in range(gn):
                                kt = g0 + j
                                MM(pb[sbk][:, j * 128:(j + 1) * 128], H['KTn'][rh:rh + 64, h // 2, kt * 128:(kt + 1) * 128],
                                   QTn[rh:rh + 64, h // 2, :], True, False, [('KT', kt), 'QT'], [('pbq', sbk, j)], inc=True)
                                MM(pb[sbk][:, j * 128:(j + 1) * 128], H['KTp'][:, kt * 128:(kt + 1) * 128],
                                   QTp[:, h, :], False, True, [('KT', kt), 'QT'], [('pbq', sbk, j)], inc=True)

                        def pv(idx_):
                            h, g0 = items[idx_]
                            ob = h // 4
                            hs = slice((h % 4) * 128, (h % 4 + 1) * 128)
                            pr = (h // 2) * 128
                            rh = (h % 2) * 64
                            gn = min(4, nkt - g0)
                            sbk = 6 + idx_ % 2
                            pi = idx_ % 2
                            A(lambda e: e.activation(out=H['PT'][pi][:, :gn * 128], in_=pb[sbk][:, :gn * 128], func=AF.Exp,
                                                     scale=SCm), PK(sbk), [('PT', pi)] + PK(sbk))
                            if g0 + gn == nkt:
                                jd = gn - 1
                                G(lambda e: e.tensor_tensor(out=H['PT'][pi][:, jd * 128:(jd + 1) * 128],
                                                            in0=H['PT'][pi][:, jd * 128:(jd + 1) * 128],
                                                            in1=H['M']['cmask'][:], op=ALU.mult),
                                  [('PT', pi), 'maskP'], [('PT', pi)])
                            for j in range(gn):
                                kt = g0 + j
                                first = (kt == 0)
                                lastk = (kt == nkt - 1)
                                MM(pb[ob][:, hs], H['Vt'][:, kt, pr:pr + 128], H['PT'][pi][:, j * 128:(j + 1) * 128], first, lastk,
                                   [('Vt', kt), ('PT', pi)], [('pbq', ob, h % 4)], inc=True)
                                MM(pb[2 + ob][:, hs], ones_bf[:], H['PT'][pi][:, j * 128:(j + 1) * 128], first, lastk,
                                   [('PT', pi), 'ones_bf'], [('pbq', 2 + ob, h % 4)], inc=True)
                            if g0 + gn == nkt:
                                V(lambda e: e.reciprocal(out=H['rsum'][rh:rh + 64, :], in_=pb[2 + ob][rh:rh + 64, hs]),
                                  [('pbq', 2 + ob, h % 4)], ['rsum', ('pbq', 2 + ob, h % 4)])
                                V(lambda e: e.tensor_tensor(out=onb[rh:rh + 64, 4 + h // 2, :], in0=pb[ob][rh:rh + 64, hs],
                                                            in1=H['rsum'][rh:rh + 64, :], op=ALU.mult),
                                  [('pbq', ob, h % 4), 'rsum'], [('onb', 1), ('pbq', ob, h % 4)])
                        if not cfg.get('lookahead'):
                            for i_ in range(len(items)):
                                qk(i_)
                                pv(i_)
                        else:
                            qk(0)
                            for i_ in range(len(items)):
                                if i_ + 1 < len(items):
                                    qk(i_ + 1)
                                pv(i_)
                    else:
                        sample_attention()
                    if STG <= 5:
                        return
                    for dc in range(KC):
                        bi = 4 + dc // 4
                        for kc in range(KC):
                            MM(pb[bi][:, (dc % 4) * 128:(dc % 4 + 1) * 128], woa[:, kc, dc * 128:(dc + 1) * 128],
                               onb[:, kc, :], kc == 0, kc == KC - 1, ['woa', ('onb', 0), ('onb', 1)],
                               [('pbq', bi, dc % 4)])
                    for b2 in range(2):
                        V(lambda e, b2=b2: e.tensor_tensor(out=xTt[:, b2 * 4:b2 * 4 + 4, :], in0=xTt[:, b2 * 4:b2 * 4 + 4, :],
                                                           in1=pbv(4 + b2, 4), op=ALU.add),
                          PK(4 + b2) + ['xTt'], ['xTt'] + PK(4 + b2))

                with ExitStack() as sph:
                    ssb = mk_sb(sph)
                    H['ssb'] = ssb
                    H['M'] = build_masks('S', True, ssb)
                    H['cbufS'] = ssb("cbufS", [128, 12, 16, 11])
                    H['ssb2'] = ssb
                    tile_body(NT)
                    S.dma('sp', xsp[NT], xTt[:], reads=['xTt'], stream='xs')
                    S.barrier()
                with ExitStack() as pph:
                    qsb = mk_sb(pph)
                    H['M'] = build_masks('P', False, qsb)
                    H['cbuf'] = qsb("cbuf", [128, 12, 131])
                    H['KTn'] = qsb("KTn", [128, 4, TP], BF16)
                    H['KTp'] = qsb("KTp", [32, TP], BF16)
                    H['Vt'] = qsb("Vt", [128, NT, 512], BF16)
                    H['PT'] = [qsb("PT%d" % i, [128, 512], BF16) for i in range(2)]
                    H['rsum'] = qsb("rsum", [128, 128])
                    G(lambda e: e.memset(H['cbuf'][:], 0.0), [], ['cbuf'])
                    for t in range(NT):
                        tile_body(t)
                        S.dma('sp', xsp[t], xTt[:], reads=['xTt'], stream='xs')
                    S.barrier()

        def fold_x():
            for t in range(NT + 1):
                ks = [S.lastw.get(('xTd', dc, t)) for dc in range(KC)]
                ks = [k for k in ks if k is not None]
                if ks:
                    S.lastw[('xT', t)] = max(ks, key=lambda tk: tk[1])
                    S.reads[('xT', t)] = {}

        if 'A' in PHASES:
            ab_layer()
            xTh[0] = sb("xT", [128, KC, TT], F32)
            for t in range(NT + 1):
                S.dma('sp', xTh[0][:, :, tile_cols(t)], xsp[t], writes=[('xT', t)], stream='x')
        else:
            xTh[0] = sb("xT", [128, KC, TT], F32)
            phase0()
        xT = xTh[0]

        for ph_ in PHASES:
            if ph_ == 'B':
                mlp_layer(0)
                fold_x()
            elif ph_ == 'D':
                mlp_layer(1)
                fold_x()
            elif ph_ == 'C':
                hgrn_layer()

        gfin = sb("gfin", [128, D], F32)
        S.dma('sp', gfin[:], final_norm.partition_broadcast(128), writes=['gfin'], stream='c')
        yo = [sb("yo%d" % i, [128, D], F32) for i in range(2)]
        ss2 = [sb("ss2_%d" % i, [128, 4], F32) for i in range(2)]
        junk = sb("junk", [128, 512], F32)
        for t in range(NT + 1):
            b = t % 2
            for half in range(2):
                pi = 6 + half
                for j in range(4):
                    kc = half * 4 + j
                    TR(pb[pi][:, j * 128:(j + 1) * 128], xT[:, kc, tile_cols(t)], ident[:],
                       [('xT', t), 'ident'], [('pbq', pi, j)])
                A(lambda e, half=half, pi=pi: e.activation(out=junk[:], in_=pb[pi][:], func=AF.Square,
                                                           accum_out=ss2[b][:, half:half + 1]),
                  [('pbq', pi, j) for j in range(4)], ['junk', ('ss2', b, half)])
            V(lambda e: e.tensor_tensor(out=ss2[b][:, 2:3], in0=ss2[b][:, 0:1], in1=ss2[b][:, 1:2], op=ALU.add),
              [('ss2', b, 0), ('ss2', b, 1)], [('ss2', b, 2)])
            A(lambda e: e.activation(out=ss2[b][:, 3:4], in_=ss2[b][:, 2:3], func=AF.Sqrt, bias=EPS,
                                     scale=1.0 / D), [('ss2', b, 2)], [('ss2', b, 3)])
            V(lambda e: e.reciprocal(out=ss2[b][:, 3:4], in_=ss2[b][:, 3:4]), [('ss2', b, 3)], [('ss2', b, 3)])
            for half in range(2):
                pi = 6 + half
                V(lambda e, half=half, pi=pi: e.scalar_tensor_tensor(
                    out=yo[b][:, half * 512:(half + 1) * 512], in0=pb[pi][:], scalar=ss2[b][:, 3:4],
                    in1=gfin[:, half * 512:(half + 1) * 512], op0=ALU.mult, op1=ALU.mult),
                    [('pbq', pi, j) for j in range(4)] + [('ss2', b, 3), 'gfin'],
                    [('yo', b, half)] + [('pbq', pi, j) for j in range(4)])
            dst = y_s if t == NT else y_p[t * 128:(t + 1) * 128, :]
            S.dma('sp', dst, yo[b][:], reads=[('yo', b, 0), ('yo', b, 1)], stream='y')
        S.finish('sp')
        dl = S.check_deadlock()
        print("instructions:", S.n_inst, "deadlock:", dl)
        assert not dl, "semaphore deadlock in the emitted program"
    return nc


def core_inputs(cfg, c, inp):
    m = {}
    m["x_p"] = np.ascontiguousarray(inp['x_prompt'][c])
    m["x_s"] = np.ascontiguousarray(inp['x_sample'][16 * c:16 * c + 16].reshape(128, D))
    m["w_up"] = inp['w_up']
    m["w_down"] = inp['w_down']
    m["mlp_norm_fm"] = np.ascontiguousarray(inp['mlp_norm'].reshape(2, KC, 128).transpose(0, 2, 1))
    m["mix_norm_fm"] = np.ascontiguousarray(inp['mix_norm'].reshape(2, KC, 128).transpose(0, 2, 1))
    m["final_norm"] = np.ascontiguousarray(inp['final_norm'].reshape(1, D))
    m["w_in_c"] = inp['w_in_c'][0]
    m["w_out_c"] = inp['w_out_c'][0]
    m["lb_fm"] = np.ascontiguousarray(inp['lb_logits_c'].reshape(2, 8, 128).transpose(0, 2, 1))
    m["g_norm_c"] = np.ascontiguousarray(inp['g_norm_c'][0].reshape(128, 1))
    m["hgrn_in"] = np.ascontiguousarray(inp['state_hgrn'][0, 16 * c:16 * c + 16])
    m["w_in_ab"] = inp['w_in_ab'][0]
    m["w_out_ab"] = inp['w_out_ab'][0]
    m["w_uq"] = np.ascontiguousarray(inp['w_uq_ab'][0].reshape(256, 768))
    m["w_uk"] = np.ascontiguousarray(inp['w_uk_ab'][0].reshape(256, 512))
    m["w_uv"] = np.ascontiguousarray(inp['w_uv_ab'][0].reshape(256, 512))
    m["conv_w_fm"] = np.ascontiguousarray(inp['conv_w_ab'][0].reshape(4, 12, 128).transpose(2, 1, 0))
    m["a_log"] = np.ascontiguousarray(inp['a_log_ab'].reshape(1, 4))
    m["dt_bias"] = np.ascontiguousarray(inp['dt_bias_ab'].reshape(1, 4))
    m["gdn_norm"] = np.ascontiguousarray(inp['gdn_norm_ab'].reshape(128, 1))
    m["q_norm"] = np.ascontiguousarray(inp['q_norm_ab'].reshape(1, 256))
    m["kv_norm"] = np.ascontiguousarray(inp['kv_norm_ab'].reshape(1, 256))
    TP = cfg['NT'] * 128
    past = inp['page_table'].shape[1] * inp['cache_mla_latent'].shape[2]
    pos = np.concatenate([np.arange(TP), np.tile(past + np.arange(8), 16)]).astype(np.float32)
    inv = (np.float32(10000.0) ** (-np.arange(16, dtype=np.float32) / np.float32(16))).astype(np.float32)
    ang = (pos[:, None] * inv[None, :]).astype(np.float32)
    m["rope_cos"] = np.cos(ang).astype(np.float32)
    m["rope_sin"] = np.sin(ang).astype(np.float32)
    NPG = cfg['NPG']
    R = NPG
    SB = 128 // R
    RG = min(8, R)
    NG = R // RG
    m["cache_lat"] = inp['cache_mla_latent'][0]
    m["cache_kr"] = inp['cache_mla_krope'][0]
    m["page_table"] = np.ascontiguousarray(inp['page_table'][16 * c:16 * c + 16]).astype(np.int32)
    wkt = np.zeros((2, 64, 8, 256), np.float32)
    wu = inp['w_uk_ab'][0]
    for h in range(8):
        wkt[h % 2, :, h, :] = wu[:, h, :].T
    m["w_ukT"] = wkt.reshape(128, 2048)
    p = np.arange(128)
    m["rep_in"] = (p[None, :] // SB == np.arange(NPG)[:, None]).astype(np.float32)
    gbase = np.zeros((128, NG + 1), np.float32)
    for g in range(NG):
        gbase[:, g] = (p % SB) * (R // RG) + g
    gbase[:, NG] = p % SB
    m["gbase_in"] = gbase
    tq = np.arange(64) % 8
    m["cm8_in"] = ((p % 8)[:, None] <= tq[None, :]).astype(np.float32)
    m["conv_s_in"] = np.ascontiguousarray(inp['state_gdn_conv'][0, 16 * c:16 * c + 16].reshape(48, 1536))
    m["gdn_s_in"] = np.ascontiguousarray(inp['state_gdn'][0, 16 * c:16 * c + 16])
    return m


def kernel(**inputs):
    inp = {k: np.asarray(v) for k, v in inputs.items()}
    B, SEQ = inp['x_prompt'].shape[:2]
    cfg = {'NT': SEQ // 128, 'NPG': inp['page_table'].shape[1], 'NPOOL': inp['cache_mla_latent'].shape[1]}
    nc = build(cfg)
    in_maps = [core_inputs(cfg, c, inp) for c in range(NCORES)]
    res = run_bass_kernel_spmd(nc, in_maps, core_ids=list(range(NCORES)))
    r = res.results

    def stk(name):
        return np.stack([r[c][name] for c in range(NCORES)], 0)

    def cat(name):
        return np.concatenate([r[c][name] for c in range(NCORES)], 0)
    y_p = stk("y_p")
    y_s = cat("y_s").reshape(16 * NCORES, 8, D)
    return (y_p, y_s,
            stk("gdn_p")[None], stk("conv_p")[None], stk("lat_p")[None], stk("kpe_p")[None], stk("hgrn_p")[None],
            cat("gdn_s")[None], cat("conv_s")[None], cat("lat_s").reshape(1, 16 * NCORES, 8, 256),
            cat("kpe_s").reshape(1, 16 * NCORES, 8, 32), cat("hgrn_s")[None])
```
